# Optimizing a Trainium2 kernel written in Bass

```python
import math
import jax, jax.numpy as jnp
from jax import lax
import numpy as np

D_MODEL = 1024
BATCH = 4
SEQ = 8192
DEPTH = 2

GLA_HEADS = 4
GLA_DK = 64
GLA_DV = 128
GLA_GATE_RANK = 16
GLA_TAU = 16.0
GLA_CHUNK = 64
MLA_HEADS = 4
MLA_NOPE = 128
MLA_ROPE = 64
MLA_V = 128
MLA_Q_RANK = 256
MLA_KV_RANK = 128
ROPE_THETA = 10000.0
ATTN_BLOCK = 128
D_GLA = GLA_HEADS * GLA_DV
D_MLA = MLA_HEADS * MLA_V
D_MIX = D_GLA + D_MLA
D_FF = 4 * D_MODEL
N_MOD = 6
RMS_EPS = 1e-6
IN_SIZES = (GLA_HEADS * GLA_DK, GLA_HEADS * GLA_DK, D_GLA, D_GLA, GLA_GATE_RANK,
            MLA_Q_RANK, MLA_KV_RANK, MLA_ROPE)
D_IN = sum(IN_SIZES)

kernel_name = 'hybrid_gla_mla_adaln_block'


def split_cols(t, sizes):
    idx = []
    acc = 0
    for s in sizes[:-1]:
        acc += s
        idx.append(acc)
    return jnp.split(t, idx, axis=-1)


def rms_norm(x, gain=None):
    xf = x.astype(jnp.float32)
    y = xf * lax.rsqrt(jnp.mean(xf * xf, axis=-1, keepdims=True) + RMS_EPS)
    if gain is not None:
        y = y * gain.astype(jnp.float32)
    return y.astype(x.dtype)


def rope_tables(positions):
    inv_freq = ROPE_THETA ** (-jnp.arange(0, MLA_ROPE, 2, dtype=jnp.float32) / MLA_ROPE)
    ang = positions.astype(jnp.float32)[..., None] * inv_freq
    return jnp.cos(ang), jnp.sin(ang)


def apply_rope(t, cos, sin):
    t = t.astype(jnp.float32)
    half = t.shape[-1] // 2
    t1, t2 = t[..., :half], t[..., half:]
    return jnp.concatenate([t1 * cos - t2 * sin, t2 * cos + t1 * sin], axis=-1)


def gla_chunked(q, k, v, log_a):
    B, S, H, dk = q.shape
    dv = v.shape[-1]
    C = GLA_CHUNK
    N = S // C

    def to_chunks(t):
        return t.reshape(B, N, C, H, t.shape[-1]).transpose(1, 0, 3, 2, 4)

    causal = jnp.tril(jnp.ones((C, C), dtype=bool))

    def step(state, inp):
        qc, kc, vc, gc = inp
        b = jnp.cumsum(gc, axis=2)
        o_inter = jnp.einsum('bhcd,bhde->bhce', qc * jnp.exp(b), state)
        diff = b[:, :, :, None, :] - b[:, :, None, :, :]
        decay = jnp.exp(jnp.where(causal[:, :, None], diff, -jnp.inf))
        attn = jnp.sum(qc[:, :, :, None, :] * kc[:, :, None, :, :] * decay, axis=-1)
        o_intra = jnp.einsum('bhij,bhje->bhie', attn, vc)
        b_last = b[:, :, -1:, :]
        k_dec = kc * jnp.exp(b_last - b)
        state = jnp.exp(b_last[:, :, 0, :])[..., None] * state + jnp.einsum('bhcd,bhce->bhde', k_dec, vc)
        return state, o_inter + o_intra

    state0 = jnp.zeros((B, H, dk, dv), dtype=jnp.float32)
    _, o = lax.scan(step, state0, (to_chunks(q), to_chunks(k), to_chunks(v), to_chunks(log_a)))
    return o.transpose(1, 0, 3, 2, 4).reshape(B, S, H, dv)


def mla_attention(q_nope, q_pe, k_nope, k_pe, v):
    B, S, H, _ = q_nope.shape
    NB = S // ATTN_BLOCK
    scale = (MLA_NOPE + MLA_ROPE) ** -0.5
    key_idx = jnp.arange(S)

    def blocks(t):
        return t.reshape(B, NB, ATTN_BLOCK, *t.shape[2:]).swapaxes(0, 1)

    def one_block(args):
        qn, qp, blk = args
        s = (jnp.einsum('bqhd,bkhd->bhqk', qn, k_nope).astype(jnp.float32)
             + jnp.einsum('bqhr,bkr->bhqk', qp, k_pe).astype(jnp.float32)) * scale
        q_idx = blk * ATTN_BLOCK + jnp.arange(ATTN_BLOCK)
        mask = key_idx[None, :] <= q_idx[:, None]
        p = jax.nn.softmax(jnp.where(mask, s, -jnp.inf), axis=-1)
        return jnp.einsum('bhqk,bkhe->bqhe', p.astype(v.dtype), v)

    out = lax.map(one_block, (blocks(q_nope), blocks(q_pe), jnp.arange(NB)))
    return out.swapaxes(0, 1).reshape(B, S, H, v.shape[-1])


def hybrid_mixer(h, cos, sin, w_in, w_gate_up, b_gate, gla_out_norm, q_a_norm, w_q_up,
                 kv_a_norm, w_kv_up, q_norm_nope, k_norm_nope, q_norm_rope, k_norm_rope, w_out):
    B, S, _ = h.shape
    proj = h @ w_in
    g_q, g_k, g_v, g_o, g_a, m_q, m_kv, m_kpe = split_cols(proj, IN_SIZES)

    q = g_q.reshape(B, S, GLA_HEADS, GLA_DK).astype(jnp.float32) * (GLA_DK ** -0.5)
    k = g_k.reshape(B, S, GLA_HEADS, GLA_DK).astype(jnp.float32)
    v = g_v.reshape(B, S, GLA_HEADS, GLA_DV).astype(jnp.float32)
    log_a = jax.nn.log_sigmoid((g_a @ w_gate_up + b_gate).astype(jnp.float32)) / GLA_TAU
    log_a = log_a.reshape(B, S, GLA_HEADS, GLA_DK)
    o = gla_chunked(q, k, v, log_a)
    o = rms_norm(o, gla_out_norm) * jax.nn.silu(g_o.reshape(B, S, GLA_HEADS, GLA_DV).astype(jnp.float32))
    gla_out = o.reshape(B, S, D_GLA).astype(h.dtype)

    qh = (rms_norm(m_q, q_a_norm) @ w_q_up).reshape(B, S, MLA_HEADS, MLA_NOPE + MLA_ROPE)
    q_nope, q_pe = qh[..., :MLA_NOPE], qh[..., MLA_NOPE:]
    q_nope = rms_norm(q_nope, q_norm_nope)
    q_pe = apply_rope(rms_norm(q_pe, q_norm_rope), cos[:, :, None, :], sin[:, :, None, :])
    kv = (rms_norm(m_kv, kv_a_norm) @ w_kv_up).reshape(B, S, MLA_HEADS, MLA_NOPE + MLA_V)
    k_nope, mv = kv[..., :MLA_NOPE], kv[..., MLA_NOPE:]
    k_nope = rms_norm(k_nope, k_norm_nope)
    k_pe = apply_rope(rms_norm(m_kpe, k_norm_rope), cos, sin)
    mla_out = mla_attention(q_nope, q_pe, k_nope, k_pe, mv).reshape(B, S, D_MLA)

    mixed = jnp.concatenate([gla_out, mla_out.astype(gla_out.dtype)], axis=-1)
    return mixed @ w_out


def setup_inputs(seed: int = 0) -> dict:
    key = jax.random.key(seed)
    ks = jax.random.split(key, 24)
    f32 = jnp.float32

    def w(k, shape, fan_in):
        return jax.random.normal(k, shape, f32) * (fan_in ** -0.5)

    def gain(k, shape):
        return 1.0 + 0.02 * jax.random.normal(k, shape, f32)

    x = jax.random.normal(ks[0], (BATCH, SEQ, D_MODEL), f32)
    c = jax.random.normal(ks[1], (BATCH, D_MODEL), f32)
    offsets = jax.random.randint(ks[2], (BATCH, 1), 0, 1024, dtype=jnp.int32)
    positions = (offsets + jnp.arange(SEQ, dtype=jnp.int32)[None, :]).astype(jnp.int32)
    return {
        'x': x,
        'c': c,
        'positions': positions,
        'w_ada': w(ks[3], (DEPTH, D_MODEL, N_MOD * D_MODEL), D_MODEL),
        'b_ada': 0.02 * jax.random.normal(ks[4], (DEPTH, N_MOD * D_MODEL), f32),
        'w_in': w(ks[5], (DEPTH, D_MODEL, D_IN), D_MODEL),
        'w_gate_up': w(ks[6], (DEPTH, GLA_GATE_RANK, GLA_HEADS * GLA_DK), GLA_GATE_RANK),
        'b_gate': 0.1 * jax.random.normal(ks[7], (DEPTH, GLA_HEADS * GLA_DK), f32),
        'gla_out_norm': gain(ks[8], (DEPTH, GLA_DV)),
        'q_a_norm': gain(ks[9], (DEPTH, MLA_Q_RANK)),
        'w_q_up': w(ks[10], (DEPTH, MLA_Q_RANK, MLA_HEADS * (MLA_NOPE + MLA_ROPE)), MLA_Q_RANK),
        'kv_a_norm': gain(ks[11], (DEPTH, MLA_KV_RANK)),
        'w_kv_up': w(ks[12], (DEPTH, MLA_KV_RANK, MLA_HEADS * (MLA_NOPE + MLA_V)), MLA_KV_RANK),
        'q_norm_nope': gain(ks[13], (DEPTH, MLA_NOPE)),
        'k_norm_nope': gain(ks[14], (DEPTH, MLA_NOPE)),
        'q_norm_rope': gain(ks[15], (DEPTH, MLA_ROPE)),
        'k_norm_rope': gain(ks[16], (DEPTH, MLA_ROPE)),
        'w_out': w(ks[17], (DEPTH, D_MIX, D_MODEL), D_MIX),
        'w_mlp_up': w(ks[18], (DEPTH, D_MODEL, D_FF), D_MODEL),
        'w_mlp_down': w(ks[19], (DEPTH, D_FF, D_MODEL), D_FF),
    }


def reference(x, c, positions, w_ada, b_ada, w_in, w_gate_up, b_gate, gla_out_norm, q_a_norm,
              w_q_up, kv_a_norm, w_kv_up, q_norm_nope, k_norm_nope, q_norm_rope, k_norm_rope,
              w_out, w_mlp_up, w_mlp_down):
    cos, sin = rope_tables(positions)
    cond = jax.nn.silu(c)
    for l in range(DEPTH):
        mod = (cond @ w_ada[l] + b_ada[l])[:, None, :]
        shift_a, scale_a, gate_a, shift_f, scale_f, gate_f = jnp.split(mod, N_MOD, axis=-1)
        h = rms_norm(x) * (1.0 + scale_a) + shift_a
        mix = hybrid_mixer(h, cos, sin, w_in[l], w_gate_up[l], b_gate[l], gla_out_norm[l],
                           q_a_norm[l], w_q_up[l], kv_a_norm[l], w_kv_up[l], q_norm_nope[l],
                           k_norm_nope[l], q_norm_rope[l], k_norm_rope[l], w_out[l])
        x = x + gate_a * mix
        h = rms_norm(x) * (1.0 + scale_f) + shift_f
        x = x + gate_f * (jnp.square(jax.nn.relu(h @ w_mlp_up[l])) @ w_mlp_down[l])
    return x
```

```python
import contextlib
import math
import numpy as np
import concourse.bass as bass
import concourse.mybir as mybir
from concourse.bass_utils import run_bass_kernel_spmd

F32 = mybir.dt.float32
BF16 = mybir.dt.bfloat16
I32 = mybir.dt.int32
ALU = mybir.AluOpType
AF = mybir.ActivationFunctionType

ENGS = ("pe", "act", "dve", "pool", "sp")
import os as _os
SEM_SHARE = _os.environ.get("SEM_SHARE", "0") == "1"
SEQ_CHAINS = _os.environ.get("SEQ_CHAINS", "0") == "1"
NO_KPR = _os.environ.get("NO_KPR", "0") == "1"
SKIP = set(_os.environ.get("SKIP_PHASES", "").split(","))
OLD_ORDER = _os.environ.get("OLD_ORDER", "0") == "1"
SEM_MAP = {"wm0": "A0", "wm1": "A1", "xt0": "A0", "xt1": "A1", "cst0": "B0", "cst1": "B1",
           "KTs0": "A0", "KTs1": "A1", "Vh0": "B0", "Vh1": "B1", "qns0": "C0", "qns1": "C1",
           "qps0": "D0", "qps1": "D1", "mx0": "B0", "mx1": "B1", "xm0": "A0", "xm1": "A1",
           "ob_st0": "ST0", "ob_st1": "ST1", "xo_st0": "ST0", "xo_st1": "ST1", "yo_st0": "ST0", "yo_st1": "ST1"}


class Buf:
    __slots__ = ("name", "lw", "rd", "rd_dma")

    def __init__(self, name):
        self.name = name
        self.lw = None
        self.rd = {}
        self.rd_dma = []


class Ins:
    __slots__ = ("eng", "fn", "deps", "signal", "cnt", "is_dma", "dsem", "dval")

    def __init__(self, eng, fn, is_dma=False):
        self.eng = eng
        self.fn = fn
        self.deps = []
        self.signal = False
        self.cnt = 0
        self.is_dma = is_dma
        self.dsem = None
        self.dval = 0


class Sched:
    def __init__(self, nc):
        self.nc = nc
        self.q = {e: [] for e in ENGS}
        self.dma_sems = {}
        self.all_dma = []
        self.last_on_sem = {}
        self.bar = None
        self.bar_done = {}
        self.nbuf = 0

    def buf(self, name=None):
        self.nbuf += 1
        return Buf(name or ("b%d" % self.nbuf))

    def _collect(self, ins, reads, writes):
        deps = {}
        pe = (ins.eng == "pe" and not ins.is_dma)

        def add(d):
            if d is None or d is ins:
                return
            if pe and d.eng == "pe" and not d.is_dma:
                return
            if d.is_dma:
                d = self.last_on_sem[d.dsem]
                if d is ins:
                    return
            deps[id(d)] = d

        for b in reads:
            add(b.lw)
        for b in writes:
            add(b.lw)
            for d in b.rd.values():
                add(d)
            for d in b.rd_dma:
                add(d)
        if self.bar is not None and not self.bar_done.get(ins.eng):
            for d in self.bar:
                add(d)
            self.bar_done[ins.eng] = True
        for b in reads:
            if ins.is_dma:
                b.rd_dma.append(ins)
            else:
                b.rd[ins.eng] = ins
        for b in writes:
            b.lw = ins
            b.rd = {}
            b.rd_dma = []
        ins.deps = list(deps.values())

    def op(self, eng, fn, reads=(), writes=()):
        ins = Ins(eng, fn)
        self._collect(ins, reads, writes)
        self.q[eng].append(ins)
        return ins

    def dma(self, eng, out, in_, reads=(), writes=(), sem=None, nobar=False):
        if sem is None:
            sem = (list(writes) + list(reads))[0].name
        if SEM_SHARE:
            sem = SEM_MAP.get(sem, "G_const" if not sem.endswith("_st") else "ST")
        ins = Ins(eng, lambda e: e.dma_start(out=out, in_=in_), is_dma=True)
        tot = self.dma_sems.get(sem, 0) + 16
        self.dma_sems[sem] = tot
        ins.dsem = sem
        ins.dval = tot
        self._collect(ins, reads, writes)
        self.last_on_sem[sem] = ins
        self.q[eng].append(ins)
        if not nobar:
            self.all_dma.append(ins)
        return ins

    def barrier(self):
        deps = []
        for e in ENGS:
            for ins in reversed(self.q[e]):
                if not ins.is_dma:
                    deps.append(ins)
                    break
        deps.extend(self.all_dma)
        self.all_dma = []
        last = {}
        keep = []
        for d in deps:
            if d.is_dma:
                if d.dsem not in last or last[d.dsem].dval < d.dval:
                    last[d.dsem] = d
            else:
                keep.append(d)
        self.bar = keep + list(last.values())
        self.bar_done = {}

    def emit(self, final_waits=()):
        nc = self.nc
        for e in ENGS:
            for ins in self.q[e]:
                for d in ins.deps:
                    if not d.is_dma:
                        d.signal = True
        for e in ENGS:
            c = 0
            for ins in self.q[e]:
                if ins.signal and not ins.is_dma:
                    c += 1
                ins.cnt = c
        with contextlib.ExitStack() as st:
            esem = {e: st.enter_context(nc.semaphore("es_" + e)) for e in ENGS}
            dsem = {k: st.enter_context(nc.semaphore("ds_%d" % i)) for i, k in enumerate(self.dma_sems)}
            block = st.enter_context(nc.Block())
            sched = self

            def run(e, eh):
                waited = {}
                for ins in sched.q[e]:
                    need = {}
                    for d in ins.deps:
                        if d.is_dma:
                            s, v, key = dsem[d.dsem], d.dval, ("d", d.dsem)
                        else:
                            s, v, key = esem[d.eng], d.cnt, ("e", d.eng)
                        if key not in need or need[key][1] < v:
                            need[key] = (s, v)
                    for key, (s, v) in need.items():
                        if waited.get(key, 0) >= v:
                            continue
                        waited[key] = v
                        eh.wait_ge(s, v)
                    bi = ins.fn(eh)
                    if ins.is_dma:
                        bi.then_inc(dsem[ins.dsem], 16)
                    elif ins.signal:
                        bi.then_inc(esem[e], 1)
                if e == "sp":
                    for d in final_waits:
                        eh.wait_ge(dsem[d.dsem], d.dval)

            @block.tensor
            def _(eh):
                run("pe", eh)

            @block.scalar
            def _(eh):
                run("act", eh)

            @block.vector
            def _(eh):
                run("dve", eh)

            @block.gpsimd
            def _(eh):
                run("pool", eh)

            @block.sync
            def _(eh):
                run("sp", eh)


D = 1024
DFF = 4096
EPS = 1e-6
TWO_PI = 2.0 * math.pi
C1 = 6.28125
C2 = TWO_PI - C1


def build(S_LEN, DEPTH, dbg=False, stop=None, split=False):
    NT = S_LEN // 128
    NC4 = S_LEN // 512
    NG = S_LEN // 256
    nc = bass.Bass("TRN2", target_bir_lowering=False)
    S = Sched(nc)

    def din(name, shape, dt=F32):
        return nc.dram_tensor(name, shape, dt, kind="ExternalInput").ap()

    def dscr(name, shape, dt):
        return nc.dram_tensor(name, shape, dt, kind=("ExternalOutput" if dbg else "Internal")).ap()

    x_in = din("x", [S_LEN, D])
    ccol = din("ccol", [128, 8])
    pos_in = din("pos", [128, NT], I32)
    invf_in = din("invf", [128, 32])
    par_in = din("parcol", [128, 2])
    w_ada = din("w_ada", [DEPTH, D, 6 * D])
    bada_col = din("bada_col", [DEPTH, 128, 48])
    bada_gate = din("bada_gate", [DEPTH, 2, 128, D])
    w_in = din("w_in", [DEPTH, D, 2000])
    wgu_in = din("w_gate_up", [DEPTH, 16, 256])
    bgate_in = din("bgate_bc", [DEPTH, 128, 256])
    gon_in = din("gon_bc4", [DEPTH, 128, 512])
    qan_in = din("qan_col", [DEPTH, 128, 2])
    wqu_in = din("w_q_up", [DEPTH, 256, 768])
    kvan_in = din("kvan_col", [DEPTH, 128, 1])
    wkvu_in = din("w_kv_up", [DEPTH, 128, 1024])
    qnn_in = din("qnn_bc", [DEPTH, 128, 128])
    knn_in = din("knn_bc", [DEPTH, 128, 128])
    qnr_in = din("qnr_bc", [DEPTH, 128, 64])
    knr_in = din("knr_bc", [DEPTH, 128, 64])
    wout_in = din("w_out", [DEPTH, D, D])
    w1_in = din("w_mlp_up", [DEPTH, D, DFF])
    w2_in = din("w_mlp_down", [DEPTH, DFF, D])
    y_out = nc.dram_tensor("y", [S_LEN // 2 if split else S_LEN, D], F32, kind="ExternalOutput").ap()

    xs = [dscr("xs%d" % i, [S_LEN, D], F32) for i in range(2)]
    xmid_d = dscr("xmid", [S_LEN, D], F32)
    mixT_d = dscr("mixT", [8, 128, S_LEN], BF16)
    KT_d = dscr("KT", [4, 128, S_LEN], BF16)
    KPE_d = dscr("KPE", [64, S_LEN], BF16)
    V_d = dscr("Vd", [S_LEN, 512], BF16)
    QT_d = dscr("QT", [4, 128, S_LEN], BF16)
    QPE_d = dscr("QPE", [2, 128, S_LEN], BF16)
    cos_d = dscr("cosd", [128, NT, 32], F32)
    sin_d = dscr("sind", [128, NT, 32], F32)
    gate_d = dscr("gated", [DEPTH, 2, 128, D], F32)
    b_xs = [S.buf("xs0"), S.buf("xs1")]
    b_xmid = S.buf("xmid")
    b_mixT = S.buf("mixT")
    b_KT, b_KPE, b_V, b_QT, b_QPE = S.buf("KT"), S.buf("KPE"), S.buf("V"), S.buf("QT"), S.buf("QPE")
    b_cos, b_sin, b_gate = S.buf("cosd"), S.buf("sind"), S.buf("gated")
    b_y = S.buf("y")
    out_dmas = []

    def mm(out, lhsT, rhs, start, stop, R, W):
        S.op("pe", lambda e: e.matmul(out=out, lhsT=lhsT, rhs=rhs, start=start, stop=stop), R, W)

    def tr(out, in_, ident, R, W):
        S.op("pe", lambda e: e.transpose(out=out, in_=in_, identity=ident), R, W)

    def act(out, in_, func, R, W, scale=1.0, bias=0.0, accum=None):
        if accum is None:
            S.op("act", lambda e: e.activation(out=out, in_=in_, func=func, bias=bias, scale=scale), R, W)
        else:
            S.op("act", lambda e: e.activation(out=out, in_=in_, func=func, bias=bias, scale=scale, accum_out=accum), R, W)

    def tt(eng, out, in0, in1, op, R, W):
        S.op(eng, lambda e: e.tensor_tensor(out=out, in0=in0, in1=in1, op=op), R, W)

    def ts(eng, out, in0, s1, s2, op0, op1, R, W):
        if s2 is None:
            S.op(eng, lambda e: e.tensor_scalar(out=out, in0=in0, scalar1=s1, scalar2=None, op0=op0), R, W)
        else:
            S.op(eng, lambda e: e.tensor_scalar(out=out, in0=in0, scalar1=s1, scalar2=s2, op0=op0, op1=op1), R, W)

    def stt(eng, out, in0, scalar, in1, op0, op1, R, W, accum=None):
        if accum is None:
            S.op(eng, lambda e: e.scalar_tensor_tensor(out=out, in0=in0, scalar=scalar, in1=in1, op0=op0, op1=op1), R, W)
        else:
            S.op(eng, lambda e: e.scalar_tensor_tensor(out=out, in0=in0, scalar=scalar, in1=in1, op0=op0, op1=op1, accum_out=accum), R, W)

    def cp(eng, out, in_, R, W):
        if eng == "act":
            S.op("act", lambda e: e.copy(out=out, in_=in_), R, W)
        else:
            S.op(eng, lambda e: e.tensor_copy(out=out, in_=in_), R, W)

    def recip(out, in_, R, W):
        S.op("dve", lambda e: e.reciprocal(out=out, in_=in_), R, W)

    def memset(eng, ap, val, W):
        S.op(eng, lambda e: e.memset(ap, val), (), W)

    def asel(out, in_, pattern, cmp, fill, base, cm, R, W):
        S.op("pool", lambda e: e.affine_select(out=out, in_=in_, pattern=pattern, compare_op=cmp, fill=fill,
                                               base=base, channel_multiplier=cm), R, W)

    def rsqrt_cols(dst, src, scale, R, W):
        act(dst, src, AF.Ln, R, W, scale=scale, bias=EPS)
        act(dst, dst, AF.Exp, W, W, scale=-0.5)

    with contextlib.ExitStack() as top:
        uid = [0]

        def SB(stack, name, shape, dt):
            uid[0] += 1
            t = stack.enter_context(nc.sbuf_tensor("%s_s%d" % (name, uid[0]), shape, dt))
            return t, S.buf(name)

        def PS(stack, name, shape, dt):
            uid[0] += 1
            t = stack.enter_context(nc.psum_tensor("%s_p%d" % (name, uid[0]), shape, dt))
            return t, S.buf(name)

        ident, b_ident = SB(top, "ident", [128, 128], BF16)
        maskU4, b_maskU4 = SB(top, "maskU4", [128, 4, 128], BF16)
        ones_f, b_onesf = SB(top, "ones_f", [128, 128], F32)
        ones_b, b_onesb = SB(top, "ones_b", [128, 128], BF16)
        maskD, b_maskD = SB(top, "maskD", [128, 4, 512], BF16)
        modcol, b_modcol = SB(top, "modcol", [128, DEPTH * 32], F32)

        par, b_par = SB(top, "par", [128, 2], F32)
        S.dma("sp", par[:], par_in, writes=[b_par])
        memset("pool", ident[:], 0.0, [b_ident])
        asel(ident[:], ident[:], [[-1, 128]], ALU.not_equal, 1.0, 0, 1, [b_ident], [b_ident])
        memset("pool", maskU4[:], 1.0, [b_maskU4])
        for h in range(4):
            asel(maskU4[:, h, :], maskU4[:, h, :], [[1, 128]], ALU.is_ge, 0.0, 0, -1, [b_maskU4], [b_maskU4])
        memset("pool", ones_f[:], 1.0, [b_onesf])
        memset("pool", ones_b[:], 1.0, [b_onesb])
        memset("pool", maskD[:], 1.0, [b_maskD])
        for r in range(4):
            asel(maskD[:, r, :], maskD[:, r, :], [[1, 512]], ALU.is_ge, 0.0, -128 * r, -1, [b_maskD], [b_maskD])

        with contextlib.ExitStack() as ph:
            posi, b_posi = SB(ph, "posi", [128, NT], I32)
            posf, b_posf = SB(ph, "posf", [128, NT], F32)
            invf, b_invf = SB(ph, "invf", [128, 32], F32)
            ang, b_ang = SB(ph, "ang", [128, NT, 32], F32)
            uu, b_uu = SB(ph, "uu", [128, NT, 32], F32)
            ni, b_ni = SB(ph, "ni", [128, NT, 32], I32)
            nf, b_nf = SB(ph, "nf", [128, NT, 32], F32)
            mk, b_mk = SB(ph, "mk", [128, NT, 32], F32)
            sn, b_sn = SB(ph, "sn", [128, NT, 32], F32)
            cs, b_cs = SB(ph, "cs", [128, NT, 32], F32)
            S.dma("sp", posi[:], pos_in, writes=[b_posi])
            S.dma("sp", invf[:], invf_in, writes=[b_invf])
            cp("dve", posf[:], posi[:], [b_posi], [b_posf])
            for t in range(NT):
                ts("dve", ang[:, t, :], invf[:], posf[:, t:t + 1], None, ALU.mult, None, [b_invf, b_posf], [b_ang])
            ts("dve", uu[:], ang[:], 1.0 / TWO_PI, None, ALU.mult, None, [b_ang], [b_uu])
            cp("dve", ni[:], uu[:], [b_uu], [b_ni])
            cp("dve", nf[:], ni[:], [b_ni], [b_nf])
            stt("dve", ang[:], nf[:], -C1, ang[:], ALU.mult, ALU.add, [b_nf, b_ang], [b_ang])
            stt("dve", ang[:], nf[:], -C2, ang[:], ALU.mult, ALU.add, [b_nf, b_ang], [b_ang])
            ts("dve", mk[:], ang[:], math.pi, None, ALU.is_gt, None, [b_ang], [b_mk])
            stt("dve", ang[:], mk[:], -TWO_PI, ang[:], ALU.mult, ALU.add, [b_mk, b_ang], [b_ang])
            ts("dve", mk[:], ang[:], -math.pi, None, ALU.is_lt, None, [b_ang], [b_mk])
            stt("dve", ang[:], mk[:], TWO_PI, ang[:], ALU.mult, ALU.add, [b_mk, b_ang], [b_ang])
            ts("dve", ang[:], ang[:], math.pi, -math.pi, ALU.min, ALU.max, [b_ang], [b_ang])
            act(sn[:], ang[:], AF.Sin, [b_ang], [b_sn])
            stt("dve", uu[:], ang[:], -1.0, ang[:], ALU.mult, ALU.max, [b_ang], [b_uu])
            ts("dve", uu[:], uu[:], -1.0, math.pi / 2, ALU.mult, ALU.add, [b_uu], [b_uu])
            act(cs[:], uu[:], AF.Sin, [b_uu], [b_cs])
            S.dma("sp", cos_d, cs[:], reads=[b_cs], writes=[b_cos], sem="cs_st")
            S.dma("sp", sin_d, sn[:], reads=[b_sn], writes=[b_sin], sem="sn_st")

            if stop != 'rope':
                cc, b_cc = SB(ph, "cc", [128, 8], F32)
                ce, b_ce = SB(ph, "ce", [128, 8], F32)
                cond, b_cond = SB(ph, "cond", [128, 8], F32)
                cond_rep, b_crep = SB(ph, "cond_rep", [128, 8, 128], BF16)
                condb, b_condb = SB(ph, "condb", [128, 16], BF16)
                wm = [SB(ph, "wm%d" % i, [128, 8, D], BF16) for i in range(2)]
                bcol, b_bcol = SB(ph, "bcol", [128, DEPTH * 48], F32)
                bgt, b_bgt = SB(ph, "bgt", [128, D], F32)
                gsb, b_gsb = SB(ph, "gsb", [128, D], F32)
                pg = [PS(ph, "pg%d" % i, [128, 512], F32) for i in range(2)]
                pc, b_pc = PS(ph, "pc", [128, 8], F32)
                S.dma("sp", cc[:], ccol, writes=[b_cc])
                for l in range(DEPTH):
                    S.dma("sp", bcol[:, l * 48:(l + 1) * 48], bada_col[l], writes=[b_bcol])
                act(ce[:], cc[:], AF.Exp, [b_cc], [b_ce], scale=-1.0)
                ts("dve", ce[:], ce[:], 1.0, None, ALU.add, None, [b_ce], [b_ce])
                recip(ce[:], ce[:], [b_ce], [b_ce])
                tt("dve", cond[:], cc[:], ce[:], ALU.mult, [b_cc, b_ce], [b_cond])
                memset("dve", condb[:], 0.0, [b_condb])
                cp("dve", condb[:, 0:8], cond[:], [b_cond, b_condb], [b_condb])
                for k in range(8):
                    ts("dve", cond_rep[:, k, :], ones_f[:], cond[:, k:k + 1], None, ALU.mult, None, [b_onesf, b_cond], [b_crep])
                li = 0
                for l in range(DEPTH):
                    for m in range(6):
                        wt, b_wt = wm[li % 2]
                        li += 1
                        S.dma("pool", wt[:], w_ada[l, :, m * D:(m + 1) * D].rearrange("(k p) n -> p k n", p=128), writes=[b_wt])
                        if m in (2, 5):
                            gi = 0 if m == 2 else 1
                            S.dma("sp", bgt[:], bada_gate[l, gi], writes=[b_bgt])
                            for half in range(2):
                                pgt, b_pg = pg[half]
                                for k in range(8):
                                    mm(pgt[:], cond_rep[:, k, :], wt[:, k, half * 512:(half + 1) * 512], k == 0, k == 7,
                                       [b_crep, b_wt], [b_pg])
                                tt("dve", gsb[:, half * 512:(half + 1) * 512], pgt[:], bgt[:, half * 512:(half + 1) * 512],
                                   ALU.add, [b_pg, b_bgt], [b_gsb])
                            S.dma("sp", gate_d[l, gi], gsb[:], reads=[b_gsb], writes=[b_gate], sem="gate_st")
                        else:
                            mi = {0: 0, 1: 1, 3: 2, 4: 3}[m]
                            for ko in range(8):
                                for k in range(8):
                                    mm(pc[:, ko:ko + 1], wt[:, k, ko * 128:(ko + 1) * 128], condb[:, k:k + 1], k == 0, k == 7,
                                       [b_wt, b_condb], [b_pc])
                            dst = modcol[:, l * 32 + mi * 8: l * 32 + mi * 8 + 8]
                            tt("dve", dst, pc[:], bcol[:, l * 48 + m * 8: l * 48 + m * 8 + 8], ALU.add, [b_pc, b_bcol], [b_modcol])
                            if m in (1, 4):
                                ts("dve", dst, dst, 1.0, None, ALU.add, None, [b_modcol], [b_modcol])
        S.barrier()

        for l in range(DEPTH if stop not in ('setup', 'rope') else 0):
            x_cur = x_in if l == 0 else xs[(l - 1) % 2]
            b_xcur = None if l == 0 else b_xs[(l - 1) % 2]
            last = (l == DEPTH - 1)
            x_nxt = y_out if last else xs[l % 2]
            b_xnxt = b_y if last else b_xs[l % 2]
            mc = l * 32

            with contextlib.ExitStack() as ph:
                win, b_win = SB(ph, "win", [128, 8, 2000], BF16)
                wgu, b_wgu = SB(ph, "wgu", [16, 256], BF16)
                wqu, b_wqu = SB(ph, "wqu", [128, 2, 768], BF16)
                wkvu, b_wkvu = SB(ph, "wkvu", [128, 1024], BF16)
                qan, b_qan = SB(ph, "qan", [128, 2], F32)
                kvan, b_kvan = SB(ph, "kvan", [128, 1], F32)
                bgate, b_bgate = SB(ph, "bgate", [128, 256], F32)
                gon4, b_gon4 = SB(ph, "gon4", [128, 512], F32)
                qnn, b_qnn = SB(ph, "qnn", [128, 128], F32)
                knn, b_knn = SB(ph, "knn", [128, 128], F32)
                qnr, b_qnr = SB(ph, "qnr", [128, 64], F32)
                knr, b_knr = SB(ph, "knr", [128, 64], F32)
                b_wink = [S.buf("win%d" % k) for k in range(8)]
                for k in range(8):
                    S.dma("pool", win[:, k, :], w_in[l, k * 128:(k + 1) * 128, :], writes=[b_wink[k]], sem="win_ld")
                S.dma("pool", wgu[:], wgu_in[l], writes=[b_wgu])
                S.dma("pool", wqu[:], wqu_in[l].rearrange("(k p) n -> p k n", p=128), writes=[b_wqu])
                S.dma("pool", wkvu[:], wkvu_in[l], writes=[b_wkvu])
                S.dma("sp", qan[:], qan_in[l], writes=[b_qan])
                S.dma("sp", kvan[:], kvan_in[l], writes=[b_kvan])
                S.dma("sp", bgate[:], bgate_in[l], writes=[b_bgate])
                S.dma("sp", gon4[:], gon_in[l], writes=[b_gon4])
                S.dma("sp", qnn[:], qnn_in[l], writes=[b_qnn])
                S.dma("sp", knn[:], knn_in[l], writes=[b_knn])
                S.dma("sp", qnr[:], qnr_in[l], writes=[b_qnr])
                S.dma("sp", knr[:], knr_in[l], writes=[b_knr])
                for c in range(2):
                    ts("dve", wqu[:, c, :], wqu[:, c, :], qan[:, c:c + 1], None, ALU.mult, None, [b_wqu, b_qan], [b_wqu])
                ts("dve", wkvu[:], wkvu[:], kvan[:, 0:1], None, ALU.mult, None, [b_wkvu, b_kvan], [b_wkvu])
                qsc = 192.0 ** -0.5
                ts("dve", qnn[:], qnn[:], qsc, None, ALU.mult, None, [b_qnn], [b_qnn])
                ts("dve", qnr[:], qnr[:], qsc, None, ALU.mult, None, [b_qnr], [b_qnr])

                xt = [SB(ph, "xt%d" % i, [128, D], F32) for i in range(2)]
                cst = [SB(ph, "cst%d" % i, [128, 2, 32], F32) for i in range(2)]
                sq, b_sq = SB(ph, "sq", [128, D], F32)
                ss, b_ss = SB(ph, "ss", [128, 1], F32)
                xn, b_xn = SB(ph, "xn", [128, D], BF16)
                hT, b_hT = SB(ph, "hT", [128, 8, 128], BF16)
                mD, b_mD = SB(ph, "mD", [128, 400], BF16)
                ss3, b_ss3 = SB(ph, "ss3", [128, 3], F32)
                rs3, b_rs3 = SB(ph, "rs3", [128, 3], F32)
                rq2, b_rq2 = SB(ph, "rq2", [128, 3], F32)
                mT, b_mT = SB(ph, "mT", [128, 4, 128], BF16)
                pre, b_pre = SB(ph, "pre", [128, 256], F32)
                lg, b_lg = SB(ph, "lg", [128, 256], F32)
                lgh, b_lgh = SB(ph, "lgh", [128, 256], BF16)
                lgl, b_lgl = SB(ph, "lgl", [128, 256], BF16)
                eb, b_eb = SB(ph, "eb", [128, 256], F32)
                enb, b_enb = SB(ph, "enb", [128, 256], F32)
                ebl, b_ebl = SB(ph, "ebl", [64, 4], F32)
                qg, b_qg = SB(ph, "qg", [128, 256], BF16)
                kg, b_kg = SB(ph, "kg", [128, 256], BF16)
                qkT, b_qkT = SB(ph, "qkT", [64, 8, 128], BF16)
                vsb, b_vsb = SB(ph, "vsb", [128, 512], BF16)
                eo, b_eo = SB(ph, "eo", [128, 512], F32)
                gog, b_gog = SB(ph, "gog", [128, 512], F32)
                ATs, b_ATs = SB(ph, "ATs", [128, 4, 128], BF16)
                stt_, b_st = SB(ph, "gst", [64, 4, 128], F32)
                stb, b_stb = SB(ph, "gstb", [64, 4, 128], BF16)
                sso, b_sso = SB(ph, "sso", [128, 4], F32)
                go, b_go = SB(ph, "go", [128, 512], BF16)
                gT, b_gT = SB(ph, "gT", [128, 4, 128], BF16)
                ss8, b_ss8 = SB(ph, "ss8", [128, 8], F32)
                fac8, b_fac8 = SB(ph, "fac8", [128, 8], F32)
                qn, b_qn = SB(ph, "qn", [128, 4, 128], BF16)
                zall, b_zall = SB(ph, "zall", [128, 5, 64], F32)
                ra, b_ra = SB(ph, "ra", [128, 5, 32], F32)
                rb, b_rb = SB(ph, "rb", [128, 5, 32], F32)
                rope_o, b_ropeo = SB(ph, "rope_o", [128, 5, 64], BF16)
                qT6, b_qT6 = SB(ph, "qT6", [128, 6, 128], BF16)
                kn, b_kn = SB(ph, "kn", [128, 4, 128], BF16)
                vt, b_vt = SB(ph, "vt", [128, 4, 128], BF16)
                kT5, b_kT5 = SB(ph, "kT5", [128, 5, 128], BF16)
                ssk, b_ssk = SB(ph, "ssk", [128, 4], F32)
                fack, b_fack = SB(ph, "fack", [128, 4], F32)

                pT, b_pT = PS(ph, "pT", [128, 8, 128], BF16)
                pA, b_pA = PS(ph, "pA", [128, 512], F32)
                pB, b_pB = PS(ph, "pB", [128, 512], F32)
                pC, b_pC = PS(ph, "pC", [128, 512], F32)
                pD, b_pD = PS(ph, "pD", [128, 512], F32)
                pM, _ = PS(ph, "pM", [128, 8, 128], BF16)
                b_pMa = b_pMb = S.buf("pM")
                pX, b_pX = PS(ph, "pX", [128, 512], F32)
                pY, b_pY = PS(ph, "pY", [128, 512], F32)
                pX4 = pX[:].rearrange("p (h e) -> p h e", e=128)
                pY4 = pY[:].rearrange("p (h e) -> p h e", e=128)

                memset("dve", stt_[:], 0.0, [b_st])
                memset("dve", stb[:], 0.0, [b_stb])

                sq2, b_sq2 = SB(ph, "sq2", [128, 128], F32)
                sq3, b_sq3 = SB(ph, "sq3", [128, 128], F32)
                b_pM = b_pMa
                kprs = [SB(ph, "kpr%d" % i, [128, 64], F32) for i in range(2)]
                rs3s = [SB(ph, "rs3_%d" % i, [128, 3], F32) for i in range(2)]
                rq2s = [SB(ph, "rq2_%d" % i, [128, 3], F32) for i in range(2)]
                mTs = [SB(ph, "mT%d" % i, [128, 4, 128], BF16) for i in range(2)]
                gqks = [SB(ph, "gqk%d" % i, [128, 512], F32) for i in range(2)]
                vsbs = [SB(ph, "vsb%d" % i, [128, 512], BF16) for i in range(2)]
                gogs = [SB(ph, "gog%d" % i, [128, 512], F32) for i in range(2)]
                pW = [(pA, b_pA), (pB, b_pB)]
                pQ = [(pC, b_pC), (pD, b_pD)]

                def drive(gens):
                    alive = list(gens)
                    while alive:
                        for g in list(alive):
                            try:
                                next(g)
                            except StopIteration:
                                alive.remove(g)

                def prologue(t):
                    sl = t % 2
                    tsl = slice(t * 128, (t + 1) * 128)
                    xtt, b_xt = xt[sl]
                    cs_t, b_cst = cst[sl]
                    kpr, b_kpr = kprs[sl]
                    rs3, b_rs3 = rs3s[sl]
                    rq2, b_rq2 = rq2s[sl]
                    mT, b_mT = mTs[sl]
                    gqk, b_gqk = gqks[sl]
                    vsb, b_vsb = vsbs[sl]
                    gog, b_gog = gogs[sl]
                    S.dma("sp", xtt[:], x_cur[tsl, :], reads=([b_xcur] if b_xcur else []), writes=[b_xt])
                    S.dma("sp", cs_t[:, 0, :], cos_d[:, t, :], reads=[b_cos], writes=[b_cst])
                    S.dma("sp", cs_t[:, 1, :], sin_d[:, t, :], reads=[b_sin], writes=[b_cst])
                    yield
                    act(sq[:], xtt[:], AF.Square, [b_xt], [b_sq, b_ss], accum=ss[:, 0:1])
                    yield
                    rsqrt_cols(ss[:, 0:1], ss[:, 0:1], 1.0 / D, [b_ss], [b_ss])
                    yield
                    ts("dve", xn[:], xtt[:], ss[:, 0:1], None, ALU.mult, None, [b_xt, b_ss], [b_xn])
                    yield
                    for k in range(8):
                        tr(pT[:, k, :], xn[:, k * 128:(k + 1) * 128], ident[:], [b_xn, b_ident], [b_pT])
                    yield
                    for k in range(8):
                        if k % 2 == 0:
                            ts("dve", hT[:, k, :], pT[:, k, :], modcol[:, mc + 8 + k: mc + 9 + k], modcol[:, mc + k: mc + k + 1],
                               ALU.mult, ALU.add, [b_pT, b_modcol], [b_hT])
                        else:
                            act(hT[:, k, :], pT[:, k, :], AF.Identity, [b_pT, b_modcol], [b_hT],
                                scale=modcol[:, mc + 8 + k: mc + 9 + k], bias=modcol[:, mc + k: mc + k + 1])
                        if k % 4 == 3:
                            yield
                    blks = ((1536, 2000), (0, 512), (512, 1024), (1024, 1536))
                    for bi, (c0, c1) in enumerate(blks):
                        pw, b_pw = pW[bi % 2]
                        for k in range(8):
                            mm(pw[:, 0:c1 - c0], hT[:, k, :], win[:, k, c0:c1], k == 0, k == 7, [b_hT, b_wink[k]], [b_pw])
                        yield
                        if bi == 0:
                            cp("act", mD[:], pw[:, 0:400], [b_pw], [b_mD])
                            cp("act", kpr[:], pw[:, 400:464], [b_pw], [b_kpr])
                            yield
                            act(sq[:, 0:256], pw[:, 16:272], AF.Square, [b_pw], [b_sq, b_ss3], accum=ss3[:, 0:1])
                            act(sq[:, 0:128], pw[:, 272:400], AF.Square, [b_pw], [b_sq, b_ss3], accum=ss3[:, 1:2])
                            act(sq[:, 0:64], pw[:, 400:464], AF.Square, [b_pw], [b_sq, b_ss3], accum=ss3[:, 2:3])
                            yield
                        elif bi == 1:
                            cp("act", gqk[:], pw[:], [b_pw], [b_gqk])
                            yield
                        elif bi == 2:
                            cp("act", vsb[:], pw[:], [b_pw], [b_vsb])
                            yield
                        else:
                            act(eo[:], pw[:], AF.Exp, [b_pw], [b_eo], scale=-1.0)
                            yield
                            act(eo[:], eo[:], AF.Ln, [b_eo], [b_eo], bias=1.0)
                            yield
                            act(eo[:], eo[:], AF.Exp, [b_eo], [b_eo], scale=-1.0)
                            yield
                            tt("dve", gog[:], pw[:], eo[:], ALU.mult, [b_pw, b_eo], [b_gog])
                            yield
                            tt("pool", gog[:], gog[:], gon4[:], ALU.mult, [b_gog, b_gon4], [b_gog])
                            yield
                    act(rs3[:, 0:1], ss3[:, 0:1], AF.Ln, [b_ss3], [b_rs3], scale=1.0 / 256, bias=EPS)
                    act(rs3[:, 1:2], ss3[:, 1:2], AF.Ln, [b_ss3], [b_rs3], scale=1.0 / 128, bias=EPS)
                    act(rs3[:, 2:3], ss3[:, 2:3], AF.Ln, [b_ss3], [b_rs3], scale=1.0 / 64, bias=EPS)
                    yield
                    act(rs3[:], rs3[:], AF.Exp, [b_rs3], [b_rs3], scale=-0.5)
                    yield
                    tt("dve", rq2[:], rs3[:], rs3[:], ALU.mult, [b_rs3], [b_rq2])
                    tr(pT[0:16, 0, :], mD[:, 0:16], ident[:], [b_mD, b_ident], [b_pT])
                    tr(pT[:, 1, :], mD[:, 16:144], ident[:], [b_mD, b_ident], [b_pT])
                    tr(pT[:, 2, :], mD[:, 144:272], ident[:], [b_mD, b_ident], [b_pT])
                    tr(pT[:, 3, :], mD[:, 272:400], ident[:], [b_mD, b_ident], [b_pT])
                    yield
                    cp("dve", mT[0:16, 0, :], pT[0:16, 0, :], [b_pT], [b_mT])
                    cp("dve", mT[:, 1:4, :], pT[:, 1:4, :], [b_pT], [b_mT])
                    yield

                def gla_chain(t):
                    sl = t % 2
                    tsl = slice(t * 128, (t + 1) * 128)
                    mT, b_mT = mTs[sl]
                    gqk, b_gqk = gqks[sl]
                    vsb, b_vsb = vsbs[sl]
                    gog, b_gog = gogs[sl]
                    mm(pX[:, 0:256], mT[0:16, 0, :], wgu[:], True, True, [b_mT, b_wgu], [b_pX])
                    tt("dve", pre[:], pX[:, 0:256], bgate[:], ALU.add, [b_pX, b_bgate], [b_pre])
                    yield
                    act(pre[:], pre[:], AF.Exp, [b_pre], [b_pre], scale=-1.0)
                    yield
                    act(lg[:], pre[:], AF.Ln, [b_pre], [b_lg], bias=1.0)
                    yield
                    cp("dve", lgh[:], lg[:], [b_lg], [b_lgh])
                    yield
                    tt("dve", lgl[:], lg[:], lgh[:], ALU.subtract, [b_lg, b_lgh], [b_lgl])
                    yield
                    mm(pX[:, 256:512], maskU4[:, 0, :], lgh[:], True, False, [b_maskU4, b_lgh], [b_pX])
                    mm(pX[:, 256:512], maskU4[:, 0, :], lgl[:], False, True, [b_maskU4, b_lgl], [b_pX])
                    for h in range(4):
                        mm(pY[0:64, h:h + 1], lgh[:, h * 64:(h + 1) * 64], ones_b[:, 0:1], True, False,
                           [b_lgh, b_onesb], [b_pY])
                        mm(pY[0:64, h:h + 1], lgl[:, h * 64:(h + 1) * 64], ones_b[:, 0:1], False, True,
                           [b_lgl, b_onesb], [b_pY])
                    yield
                    act(eb[:], pX[:, 256:512], AF.Exp, [b_pX], [b_eb], scale=-1.0 / 16)
                    act(enb[:], pX[:, 256:512], AF.Exp, [b_pX], [b_enb], scale=1.0 / 16)
                    act(ebl[:], pY[0:64, 0:4], AF.Exp, [b_pY], [b_ebl], scale=-1.0 / 16)
                    yield
                    stt("dve", qg[:], gqk[:, 0:256], 0.125, eb[:], ALU.mult, ALU.mult, [b_gqk, b_eb], [b_qg])
                    tt("dve", kg[:], gqk[:, 256:512], enb[:], ALU.mult, [b_gqk, b_enb], [b_kg])
                    yield
                    for h in range(4):
                        tr(pM[0:64, h, :], qg[:, h * 64:(h + 1) * 64], ident[:], [b_qg, b_ident], [b_pM])
                        tr(pM[0:64, 4 + h, :], kg[:, h * 64:(h + 1) * 64], ident[:], [b_kg, b_ident], [b_pM])
                    yield
                    cp("dve", qkT[:], pM[0:64, :, :], [b_pM], [b_qkT])
                    yield
                    for h in range(4):
                        mm(pY4[:, h, :], qkT[:, 4 + h, :], qkT[:, h, :], True, True, [b_qkT], [b_pY])
                    yield
                    tt("dve", ATs[:], pY4, maskU4[:], ALU.mult, [b_pY, b_maskU4], [b_ATs])
                    yield
                    for h in range(4):
                        mm(pX4[:, h, :], ATs[:, h, :], vsb[:, h * 128:(h + 1) * 128], True, False, [b_ATs, b_vsb], [b_pX])
                        mm(pX4[:, h, :], qkT[:, h, :], stb[:, h, :], False, True, [b_qkT, b_stb], [b_pX])
                    for h in range(4):
                        mm(pY4[0:64, h, :], kg[:, h * 64:(h + 1) * 64], vsb[:, h * 128:(h + 1) * 128], True, True,
                           [b_kg, b_vsb], [b_pY])
                    yield
                    for h in range(4):
                        ts("dve", stt_[:, h, :], stt_[:, h, :], ebl[:, h:h + 1], None, ALU.mult, None, [b_st, b_ebl], [b_st])
                        stt("dve", stt_[:, h, :], pY4[0:64, h, :], ebl[:, h:h + 1], stt_[:, h, :], ALU.mult, ALU.add,
                            [b_pY, b_ebl, b_st], [b_st])
                        if h == 1:
                            yield
                    cp("dve", stb[:], stt_[:], [b_st], [b_stb])
                    yield
                    for h in range(4):
                        act(sq2[:], pX4[:, h, :], AF.Square, [b_pX], [b_sq2, b_sso], accum=sso[:, h:h + 1])
                    yield
                    act(sso[:], sso[:], AF.Ln, [b_sso], [b_sso], scale=1.0 / 128, bias=EPS)
                    yield
                    act(sso[:], sso[:], AF.Exp, [b_sso], [b_sso], scale=-0.5)
                    yield
                    for h in range(4):
                        stt("dve", go[:, h * 128:(h + 1) * 128], pX4[:, h, :], sso[:, h:h + 1], gog[:, h * 128:(h + 1) * 128],
                            ALU.mult, ALU.mult, [b_pX, b_sso, b_gog], [b_go])
                        if h == 1:
                            yield
                    yield
                    for h in range(4):
                        tr(pM[:, h, :], go[:, h * 128:(h + 1) * 128], ident[:], [b_go, b_ident], [b_pM])
                    yield
                    cp("dve", gT[:], pM[:, 0:4, :], [b_pM], [b_gT])
                    yield
                    S.dma("sp", mixT_d[0:4, :, tsl].rearrange("c p s -> p c s"), gT[:], reads=[b_gT], writes=[b_mixT], sem="gT_st")

                def mla_chain(t):
                    sl = t % 2
                    tsl = slice(t * 128, (t + 1) * 128)
                    cs_t, b_cst = cst[sl]
                    kpr, b_kpr = kprs[sl]
                    rs3, b_rs3 = rs3s[sl]
                    rq2, b_rq2 = rq2s[sl]
                    mT, b_mT = mTs[sl]
                    for g, (pt_, bpt) in enumerate(pQ):
                        for c in range(2):
                            mm(pt_[:, 0:384], mT[:, 1 + c, :], wqu[:, c, g * 384:(g + 1) * 384], c == 0, c == 1, [b_mT, b_wqu], [bpt])
                    yield
                    for h in range(4):
                        pt_, bpt = pQ[h // 2]
                        base = (h % 2) * 192
                        act(sq3[:, 0:128], pt_[:, base:base + 128], AF.Square, [bpt], [b_sq3, b_ss8], accum=ss8[:, h:h + 1])
                        act(sq3[:, 0:64], pt_[:, base + 128:base + 192], AF.Square, [bpt], [b_sq3, b_ss8], accum=ss8[:, 4 + h:5 + h])
                        if h == 1:
                            yield
                    yield
                    ts("dve", ss8[:], ss8[:], rq2[:, 0:1], None, ALU.mult, None, [b_ss8, b_rq2], [b_ss8])
                    yield
                    act(fac8[:, 0:4], ss8[:, 0:4], AF.Ln, [b_ss8], [b_fac8], scale=1.0 / 128, bias=EPS)
                    act(fac8[:, 4:8], ss8[:, 4:8], AF.Ln, [b_ss8], [b_fac8], scale=1.0 / 64, bias=EPS)
                    yield
                    act(fac8[:], fac8[:], AF.Exp, [b_fac8], [b_fac8], scale=-0.5)
                    yield
                    ts("dve", fac8[:], fac8[:], rs3[:, 0:1], None, ALU.mult, None, [b_fac8, b_rs3], [b_fac8])
                    yield
                    for h in range(4):
                        pt_, bpt = pQ[h // 2]
                        base = (h % 2) * 192
                        stt("dve", qn[:, h, :], pt_[:, base:base + 128], fac8[:, h:h + 1], qnn[:], ALU.mult, ALU.mult,
                            [bpt, b_fac8, b_qnn], [b_qn])
                        stt("dve", zall[:, h, :], pt_[:, base + 128:base + 192], fac8[:, 4 + h:5 + h], qnr[:], ALU.mult, ALU.mult,
                            [bpt, b_fac8, b_qnr], [b_zall])
                        if h == 1:
                            yield
                    stt("dve", zall[:, 4, :], kpr[:], rs3[:, 2:3], knr[:], ALU.mult, ALU.mult, [b_kpr, b_rs3, b_knr], [b_zall])
                    yield
                    for g, (pt_, bpt) in enumerate(pQ):
                        mm(pt_[:], mT[:, 3, :], wkvu[:, g * 512:(g + 1) * 512], True, True, [b_mT, b_wkvu], [bpt])
                    yield
                    for hh in range(5):
                        z1, z2 = zall[:, hh, 0:32], zall[:, hh, 32:64]
                        cth, sth = cs_t[:, 0, :], cs_t[:, 1, :]
                        eng = "pool" if hh % 4 != 3 else "dve"
                        tt(eng, ra[:, hh, :], z1, cth, ALU.mult, [b_zall, b_cst], [b_ra])
                        tt(eng, rb[:, hh, :], z2, sth, ALU.mult, [b_zall, b_cst], [b_rb])
                        tt(eng, rope_o[:, hh, 0:32], ra[:, hh, :], rb[:, hh, :], ALU.subtract, [b_ra, b_rb], [b_ropeo])
                        yield
                        tt(eng, ra[:, hh, :], z2, cth, ALU.mult, [b_zall, b_cst], [b_ra])
                        tt(eng, rb[:, hh, :], z1, sth, ALU.mult, [b_zall, b_cst], [b_rb])
                        tt(eng, rope_o[:, hh, 32:64], ra[:, hh, :], rb[:, hh, :], ALU.add, [b_ra, b_rb], [b_ropeo])
                        yield
                    for h in range(4):
                        pt_, bpt = pQ[h // 2]
                        base = (h % 2) * 256
                        act(sq3[:, 0:128], pt_[:, base:base + 128], AF.Square, [bpt], [b_sq3, b_ssk], accum=ssk[:, h:h + 1])
                        if h == 1:
                            yield
                    yield
                    ts("dve", ssk[:], ssk[:], rq2[:, 1:2], None, ALU.mult, None, [b_ssk, b_rq2], [b_ssk])
                    yield
                    act(fack[:], ssk[:], AF.Ln, [b_ssk], [b_fack], scale=1.0 / 128, bias=EPS)
                    yield
                    act(fack[:], fack[:], AF.Exp, [b_fack], [b_fack], scale=-0.5)
                    yield
                    ts("dve", fack[:], fack[:], rs3[:, 1:2], None, ALU.mult, None, [b_fack, b_rs3], [b_fack])
                    yield
                    for h in range(4):
                        pt_, bpt = pQ[h // 2]
                        base = (h % 2) * 256
                        stt("dve", kn[:, h, :], pt_[:, base:base + 128], fack[:, h:h + 1], knn[:], ALU.mult, ALU.mult,
                            [bpt, b_fack, b_knn], [b_kn])
                        ts("dve", vt[:, h, :], pt_[:, base + 128:base + 256], rs3[:, 1:2], None, ALU.mult, None, [bpt, b_rs3], [b_vt])
                        if h == 1:
                            yield
                    yield
                    S.dma("sp", V_d[tsl, :], vt[:].rearrange("p h e -> p (h e)"), reads=[b_vt], writes=[b_V], sem="vt_st")
                    for h in range(4):
                        tr(pM[:, h, :], qn[:, h, :], ident[:], [b_qn, b_ident], [b_pM])
                    for hp in range(2):
                        tr(pM[:, 4 + hp, :], rope_o[:, 2 * hp:2 * hp + 2, :].rearrange("p h r -> p (h r)"), ident[:],
                           [b_ropeo, b_ident], [b_pM])
                    tr(pM[0:64, 6, :], rope_o[:, 4, :], ident[:], [b_ropeo, b_ident], [b_pM])
                    yield
                    cp("dve", qT6[:], pM[:, 0:6, :], [b_pM], [b_qT6])
                    cp("dve", kT5[0:64, 4, :], pM[0:64, 6, :], [b_pM], [b_kT5])
                    yield
                    S.dma("sp", QT_d[:, :, tsl].rearrange("c p s -> p c s"), qT6[:, 0:4, :], reads=[b_qT6], writes=[b_QT], sem="qT_st")
                    S.dma("sp", QPE_d[:, :, tsl].rearrange("c p s -> p c s"), qT6[:, 4:6, :], reads=[b_qT6], writes=[b_QPE], sem="qT_st2")
                    for h in range(4):
                        tr(pM[:, h, :], kn[:, h, :], ident[:], [b_kn, b_ident], [b_pM])
                    yield
                    cp("dve", kT5[:, 0:4, :], pM[:, 0:4, :], [b_pM], [b_kT5])
                    yield
                    S.dma("sp", KT_d[:, :, tsl].rearrange("c p s -> p c s"), kT5[:, 0:4, :], reads=[b_kT5], writes=[b_KT], sem="kT_st")
                    S.dma("sp", KPE_d[:, tsl], kT5[0:64, 4, :], reads=[b_kT5], writes=[b_KPE], sem="kT_st2")

                if "p1" not in SKIP:
                    drive([prologue(0)])
                for t in range(NT if "p1" not in SKIP else 0):
                    gens = [gla_chain(t), mla_chain(t)]
                    if t + 1 < NT:
                        gens.append(prologue(t + 1))
                    drive(gens)
            S.barrier()
            if stop and (stop.startswith('p1') or stop.startswith('q') or stop.startswith('c')):
                break

            ph23 = contextlib.ExitStack()
            phw = contextlib.ExitStack()
            uid[0] += 1
            wout = phw.enter_context(nc.sbuf_tensor("wout_s%d" % uid[0], [128, 8, D], BF16, side="right"))
            b_woutk = [S.buf("wout%d" % k) for k in range(8)]
            for k in range(8):
                S.dma("pool", wout[:, k, :], wout_in[l, k * 128:(k + 1) * 128, :], writes=[b_woutk[k]], sem="wout_ld", nobar=True)
            with contextlib.ExitStack() as ph:
                KTs = [SB(ph, "KTs%d" % i, [128, S_LEN], BF16) for i in range(2)]
                Vh = [SB(ph, "Vh%d" % i, [128, NT, 128], BF16) for i in range(2)]
                KPEs, b_KPEs = SB(ph, "KPEs", [64, S_LEN], BF16)
                qns = [SB(ph, "qns%d" % i, [128, 512], BF16) for i in range(2)]
                qps = [SB(ph, "qps%d" % i, [64, 512], BF16) for i in range(2)]
                pTs = [SB(ph, "pTs%d" % i, [128, 512], BF16) for i in range(3)]
                rl, b_rl = SB(ph, "rl", [128, 512], F32)
                ob = [SB(ph, "ob%d" % i, [128, 512], BF16) for i in range(2)]
                pS = [PS(ph, "pS%d" % i, [128, 512], F32) for i in range(3)]
                pO = [PS(ph, "pO%d" % i, [128, 512], F32) for i in range(2)]
                pL = [PS(ph, "pL%d" % i, [128, 512], F32) for i in range(2)]
                S.dma("sp", KPEs[:], KPE_d, reads=[b_KPE], writes=[b_KPEs])
                spl2 = split and last
                if spl2:
                    maskSel, b_maskSel = SB(ph, "maskSel", [128, 8, 512], BF16)
                    for r in range(4):
                        act(maskSel[:, r, :], maskD[:, r, :], AF.Identity, [b_maskD, b_par], [b_maskSel],
                            scale=par[:, 0:1], bias=par[:, 1:2])
                        act(maskSel[:, 4 + r, :], maskD[:, r, :], AF.Identity, [b_maskD, b_par], [b_maskSel], scale=par[:, 1:2])
                    qna = [SB(ph, "qna%d" % i, [128, 512], BF16) for i in range(2)]
                    qnb = [SB(ph, "qnb%d" % i, [128, 512], BF16) for i in range(2)]
                    qpa = [SB(ph, "qpa%d" % i, [64, 512], BF16) for i in range(2)]
                    qpb = [SB(ph, "qpb%d" % i, [64, 512], BF16) for i in range(2)]
                NPOS = NC4 // 2 if spl2 else NC4

                def nkb_of(c):
                    return 8 * c + 8 if spl2 else 4 * c + 4

                chunks = [(h, c) for h in range(4 if "p2" not in SKIP else 0) for c in range(NPOS)]
                blocks = []
                for idx, (h, c) in enumerate(chunks):
                    for kb in range(nkb_of(c)):
                        blocks.append((idx, kb, nkb_of(c)))

                def load_head(h):
                    KTh, b_KTh = KTs[h % 2]
                    Vhh, b_Vhh = Vh[h % 2]
                    S.dma("sp", KTh[:], KT_d[h], reads=[b_KT], writes=[b_KTh])
                    S.dma("sp", Vhh[:], V_d[:, h * 128:(h + 1) * 128].rearrange("(t p) e -> p t e", p=128), reads=[b_V], writes=[b_Vhh])

                def load_q(idx):
                    h, c = chunks[idx]
                    hp, off = h // 2, 64 * (h % 2)
                    if not spl2:
                        csl = slice(c * 512, (c + 1) * 512)
                        S.dma("sp", qns[idx % 2][0][:], QT_d[h, :, csl], reads=[b_QT], writes=[qns[idx % 2][1]])
                        S.dma("sp", qps[idx % 2][0][:], QPE_d[hp, off:off + 64, csl], reads=[b_QPE], writes=[qps[idx % 2][1]])
                        return
                    sla = slice(2 * c * 512, (2 * c + 1) * 512)
                    slb = slice((2 * c + 1) * 512, (2 * c + 2) * 512)
                    qa, b_qa = qna[idx % 2]
                    qb, b_qb = qnb[idx % 2]
                    pa, b_pa = qpa[idx % 2]
                    pb, b_pb = qpb[idx % 2]
                    qs, b_qs = qns[idx % 2]
                    ps_, b_ps = qps[idx % 2]
                    S.dma("sp", qa[:], QT_d[h, :, sla], reads=[b_QT], writes=[b_qa])
                    S.dma("sp", qb[:], QT_d[h, :, slb], reads=[b_QT], writes=[b_qb])
                    S.dma("sp", pa[:], QPE_d[hp, off:off + 64, sla], reads=[b_QPE], writes=[b_pa])
                    S.dma("sp", pb[:], QPE_d[hp, off:off + 64, slb], reads=[b_QPE], writes=[b_pb])
                    ts("dve", qs[:], qa[:], par[:, 0:1], None, ALU.mult, None, [b_qa, b_par], [b_qs])
                    stt("dve", qs[:], qb[:], par[:, 1:2], qs[:], ALU.mult, ALU.add, [b_qb, b_par, b_qs], [b_qs])
                    ts("dve", ps_[:], pa[:], par[0:64, 0:1], None, ALU.mult, None, [b_pa, b_par], [b_ps])
                    stt("dve", ps_[:], pb[:], par[0:64, 1:2], ps_[:], ALU.mult, ALU.add, [b_pb, b_par, b_ps], [b_ps])

                LA = 2
                if chunks:
                    load_head(0)
                    load_q(0)
                for i in range((len(blocks) + LA) if blocks else 0):
                    if i < len(blocks):
                        idx, kb, nkb = blocks[i]
                        h, c = chunks[idx]
                        if kb == 0 and idx + 1 < len(chunks):
                            load_q(idx + 1)
                        KTh, b_KTh = KTs[h % 2]
                        qn_, b_qn_ = qns[idx % 2]
                        qp_, b_qp_ = qps[idx % 2]
                        ksl = slice(kb * 128, (kb + 1) * 128)
                        pSt, b_pSt = pS[i % 3]
                        pTt, b_pTt = pTs[i % 3]
                        mm(pSt[:], KTh[:, ksl], qn_[:], True, False, [b_KTh, b_qn_], [b_pSt])
                        mm(pSt[:], KPEs[:, ksl], qp_[:], False, True, [b_KPEs, b_qp_], [b_pSt])
                        act(pTt[:], pSt[:], AF.Exp, [b_pSt], [b_pTt])
                        if spl2:
                            r = kb - 8 * c
                            if r >= 0:
                                tt("pool", pTt[:], pTt[:], maskSel[:, r, :], ALU.mult, [b_pTt, b_maskSel], [b_pTt])
                        else:
                            r = kb - 4 * c
                            if r >= 0:
                                tt("pool", pTt[:], pTt[:], maskD[:, r, :], ALU.mult, [b_pTt, b_maskD], [b_pTt])
                    if i >= LA:
                        j = i - LA
                        idx, kb, nkb = blocks[j]
                        h, c = chunks[idx]
                        Vhh, b_Vhh = Vh[h % 2]
                        pTt, b_pTt = pTs[j % 3]
                        pOt, b_pOt = pO[idx % 2]
                        pLt, b_pLt = pL[idx % 2]
                        mm(pOt[:], Vhh[:, kb, :], pTt[:], kb == 0, kb == nkb - 1, [b_Vhh, b_pTt], [b_pOt])
                        mm(pLt[:], ones_b[:], pTt[:], kb == 0, kb == nkb - 1, [b_onesb, b_pTt], [b_pLt])
                        if kb == 0 and c == 0 and h + 1 < 4:
                            load_head(h + 1)
                        if kb == nkb - 1:
                            obt, b_obt = ob[idx % 2]
                            csl = slice(c * 512, (c + 1) * 512)
                            recip(rl[:], pLt[:], [b_pLt], [b_rl])
                            tt("dve", obt[:], pOt[:], rl[:], ALU.mult, [b_pOt, b_rl], [b_obt])
                            S.dma("pool", mixT_d[4 + h, :, csl], obt[:], reads=[b_obt], writes=[b_mixT], sem="ob_st%d" % (idx % 2))
            S.barrier()
            if stop == 'p2':
                phw.close()
                ph23.close()
                break
            w1, _ = SB(ph23, "w1", [128, 8, DFF], BF16)
            w2, _ = SB(ph23, "w2", [128, 32, D], BF16)
            b_w1k = [S.buf("w1_%d" % k) for k in range(8)]
            b_w2f = [S.buf("w2_%d" % f) for f in range(32)]
            for k in range(8):
                for q in range(2):
                    S.dma("pool", w1[:, k, q * 2048:(q + 1) * 2048], w1_in[l, k * 128:(k + 1) * 128, q * 2048:(q + 1) * 2048],
                          writes=[b_w1k[k]], sem="w1_ld", nobar=True)
            for f in range(32):
                S.dma("pool", w2[:, f, :], w2_in[l, f * 128:(f + 1) * 128, :], writes=[b_w2f[f]], sem="w2_ld", nobar=True)

            with contextlib.ExitStack() as ph:
                gta, b_gta = SB(ph, "gta", [128, D], F32)
                S.dma("sp", gta[:], gate_d[l, 0], reads=[b_gate], writes=[b_gta])
                mx = [SB(ph, "mx%d" % i, [128, 8, 128], BF16) for i in range(2)]
                xt = [SB(ph, "xt%d" % i, [128, D], F32) for i in range(2)]
                xo = [SB(ph, "xo%d" % i, [128, D], F32) for i in range(2)]
                pW = [PS(ph, "pW%d" % i, [128, 512], F32) for i in range(4)]
                spl = split and last
                if spl:
                    mxb = [SB(ph, "mxb%d" % i, [128, 8, 128], BF16) for i in range(2)]
                    xtb = [SB(ph, "xtb%d" % i, [128, D], F32) for i in range(2)]
                    mxs = [SB(ph, "mxs%d" % i, [128, 8, 128], BF16) for i in range(2)]
                    xss = [SB(ph, "xss%d" % i, [128, D], F32) for i in range(2)]
                NT3 = (NT // 2 if spl else NT) if "p3a" not in SKIP else 0
                for t in range(NT3):
                    tsl = slice(t * 128, (t + 1) * 128)
                    tg = (8 * (t // 4) + t % 4) if spl else t
                    tsl0 = slice(tg * 128, (tg + 1) * 128)
                    tsl1 = slice((tg + 4) * 128, (tg + 5) * 128)
                    mxt, b_mx = mx[t % 2]
                    xtt, b_xt = xt[t % 2]
                    xot, b_xo = xo[t % 2]
                    S.dma("sp", mxt[:], mixT_d[:, :, tsl0].rearrange("c p s -> p c s"), reads=[b_mixT], writes=[b_mx])
                    S.dma("sp", xtt[:], x_cur[tsl0, :], reads=([b_xcur] if b_xcur else []), writes=[b_xt])
                    if spl:
                        mxbt, b_mxb = mxb[t % 2]
                        xtbt, b_xtb = xtb[t % 2]
                        mxst, b_mxs = mxs[t % 2]
                        xsst, b_xss = xss[t % 2]
                        S.dma("sp", mxbt[:], mixT_d[:, :, tsl1].rearrange("c p s -> p c s"), reads=[b_mixT], writes=[b_mxb])
                        S.dma("sp", xtbt[:], x_cur[tsl1, :], reads=([b_xcur] if b_xcur else []), writes=[b_xtb])
                        act(mxst[:, 0:4, :], mxt[:, 0:4, :], AF.Identity, [b_mx, b_par], [b_mxs], scale=par[:, 0:1])
                        stt("dve", mxst[:, 0:4, :], mxbt[:, 0:4, :], par[:, 1:2], mxst[:, 0:4, :], ALU.mult, ALU.add,
                            [b_mxb, b_par, b_mxs], [b_mxs])
                        S.dma("sp", mxst[:, 4:8, :], mixT_d[4:8, :, tsl].rearrange("c p s -> p c s"), reads=[b_mixT], writes=[b_mxs])
                        act(xsst[:], xtt[:], AF.Identity, [b_xt, b_par], [b_xss], scale=par[:, 0:1])
                        stt("dve", xsst[:], xtbt[:], par[:, 1:2], xsst[:], ALU.mult, ALU.add, [b_xtb, b_par, b_xss], [b_xss])
                        mxt, b_mx = mxst, b_mxs
                        xtt, b_xt = xsst, b_xss
                    for half in range(2):
                        pw, b_pw = pW[(t % 2) * 2 + half]
                        hs = slice(half * 512, (half + 1) * 512)
                        for c in range(8):
                            mm(pw[:], mxt[:, c, :], wout[:, c, hs], c == 0, c == 7, [b_mx, b_woutk[c]], [b_pw])
                        tt("dve", xot[:, hs], pw[:], gta[:, hs], ALU.mult, [b_pw, b_gta], [b_xo])
                        tt("pool", xot[:, hs], xot[:, hs], xtt[:, hs], ALU.add, [b_xo, b_xt], [b_xo])
                    S.dma("pool", xmid_d[tsl, :], xot[:], reads=[b_xo], writes=[b_xmid], sem="xo_st%d" % (t % 2))
            S.barrier()
            phw.close()
            if stop == 'p3a':
                ph23.close()
                break

            with contextlib.ExitStack() as ph:
                gtf, b_gtf = SB(ph, "gtf", [128, D], F32)
                S.dma("sp", gtf[:], gate_d[l, 1], reads=[b_gate], writes=[b_gtf])
                xm = [SB(ph, "xm%d" % i, [128, 2, D], F32) for i in range(2)]
                sq, b_sq = SB(ph, "sq", [128, D], F32)
                ss, b_ss = SB(ph, "ss", [128, 2], F32)
                xn2 = [SB(ph, "xn2_%d" % i, [128, 2, D], BF16) for i in range(2)]
                h2Ts = [SB(ph, "h2T%d" % i, [128, 8, 256], BF16) for i in range(2)]
                aT, b_aT = SB(ph, "aT", [128, 32, 256], BF16)
                rt = [SB(ph, "rt%d" % i, [128, 256], F32) for i in range(2)]
                yo = [SB(ph, "yo%d" % i, [128, D], F32) for i in range(2)]
                pT, b_pT = PS(ph, "pT", [128, 8, 128], BF16)
                pU = [PS(ph, "pU%d" % i, [128, 256], F32) for i in range(2)]
                pDn = [PS(ph, "pDn%d" % i, [128, 512], F32) for i in range(4)]

                def prep_norm(g):
                    xmt, b_xm = xm[g % 2]
                    xnt, b_xnt = xn2[g % 2]
                    for j in range(2):
                        t = 2 * g + j
                        S.dma("sp", xmt[:, j, :], xmid_d[t * 128:(t + 1) * 128, :], reads=[b_xmid], writes=[b_xm])
                    for j in range(2):
                        act(sq[:], xmt[:, j, :], AF.Square, [b_xm], [b_sq, b_ss], accum=ss[:, j:j + 1])
                    rsqrt_cols(ss[:], ss[:], 1.0 / D, [b_ss], [b_ss])
                    for j in range(2):
                        ts("dve", xnt[:, j, :], xmt[:, j, :], ss[:, j:j + 1], None, ALU.mult, None, [b_xm, b_ss], [b_xnt])

                def prep_tr(g):
                    xnt, b_xnt = xn2[g % 2]
                    h2T, b_h2T = h2Ts[g % 2]
                    for j in range(2):
                        for k in range(8):
                            tr(pT[:, k, :], xnt[:, j, k * 128:(k + 1) * 128], ident[:], [b_xnt, b_ident], [b_pT])
                        for k in range(8):
                            if k % 2 == 0:
                                ts("dve", h2T[:, k, j * 128:(j + 1) * 128], pT[:, k, :], modcol[:, mc + 24 + k: mc + 25 + k],
                                   modcol[:, mc + 16 + k: mc + 17 + k], ALU.mult, ALU.add, [b_pT, b_modcol], [b_h2T])
                            else:
                                act(h2T[:, k, j * 128:(j + 1) * 128], pT[:, k, :], AF.Identity, [b_pT, b_modcol], [b_h2T],
                                    scale=modcol[:, mc + 24 + k: mc + 25 + k], bias=modcol[:, mc + 16 + k: mc + 17 + k])

                yi = 0
                NGE = NG // 2 if (split and last) else NG
                prep_norm(0)
                prep_tr(0)
                for g in range(NGE):
                    xmt, b_xm = xm[g % 2]
                    h2T, b_h2T = h2Ts[g % 2]
                    if g + 1 < NGE:
                        prep_norm(g + 1)
                    for f in range(32):
                        pu, b_pu = pU[f % 2]
                        rtt, b_rt = rt[f % 2]
                        for k in range(8):
                            mm(pu[:], w1[:, k, f * 128:(f + 1) * 128], h2T[:, k, :], k == 0, k == 7, [b_w1k[k], b_h2T], [b_pu])
                        act(rtt[:], pu[:], AF.Relu, [b_pu], [b_rt])
                        tt("dve" if f % 2 == 0 else "pool", aT[:, f, :], rtt[:], rtt[:], ALU.mult, [b_rt], [b_aT])
                    if g + 1 < NGE:
                        prep_tr(g + 1)
                    for j in range(2):
                        t = 2 * g + j
                        tsl = slice(t * 128, (t + 1) * 128)
                        yot, b_yo = yo[yi % 2]
                        yi += 1
                        for half in range(2):
                            pd, b_pd = pDn[j * 2 + half]
                            hs = slice(half * 512, (half + 1) * 512)
                            for f in range(32):
                                mm(pd[:], aT[:, f, j * 128:(j + 1) * 128], w2[:, f, hs], f == 0, f == 31, [b_aT, b_w2f[f]], [b_pd])
                            tt("dve", yot[:, hs], pd[:], gtf[:, hs], ALU.mult, [b_pd, b_gtf], [b_yo])
                            tt("pool", yot[:, hs], yot[:, hs], xmt[:, j, hs], ALU.add, [b_yo, b_xm], [b_yo])
                        d = S.dma("pool", x_nxt[tsl, :], yot[:], reads=[b_yo], writes=[b_xnxt], sem="yo_st%d" % ((yi - 1) % 2))
                        if last:
                            out_dmas.append(d)
            S.barrier()
            ph23.close()

        S.emit(final_waits=out_dmas)
    return nc


def host_inputs(b, S_LEN, DEPTH, x, c, positions, _parity=0, *, w_ada, b_ada, w_in, w_gate_up, b_gate, gla_out_norm, q_a_norm,
                w_q_up, kv_a_norm, w_kv_up, q_norm_nope, k_norm_nope, q_norm_rope, k_norm_rope,
                w_out, w_mlp_up, w_mlp_down):
    f32 = np.float32
    NT = S_LEN // 128
    A = lambda a: np.ascontiguousarray(np.asarray(a))

    def bc(v, n=128):
        v = np.asarray(v, dtype=f32)
        return A(np.broadcast_to(v[:, None, :], (v.shape[0], n, v.shape[1])))

    inv_freq = (10000.0 ** (-np.arange(0, 64, 2, dtype=f32) / f32(64))).astype(f32)
    b_ada = np.asarray(b_ada, dtype=f32)
    d = {
        "x": A(np.asarray(x[b], dtype=f32)),
        "ccol": A(np.asarray(c[b], dtype=f32).reshape(8, 128).T),
        "pos": A(np.asarray(positions[b]).astype(np.int32).reshape(NT, 128).T),
        "invf": A(np.broadcast_to(inv_freq[None, :], (128, 32))),
        "parcol": A(np.broadcast_to(np.array([[1.0 - _parity, float(_parity)]], dtype=f32), (128, 2))),
        "w_ada": A(np.asarray(w_ada, dtype=f32)),
        "bada_col": A(b_ada.reshape(DEPTH, 48, 128).transpose(0, 2, 1)),
        "bada_gate": A(np.broadcast_to(b_ada.reshape(DEPTH, 6, 1, D)[:, [2, 5]], (DEPTH, 2, 128, D))),
        "w_in": A(np.asarray(w_in, dtype=f32)),
        "w_gate_up": A(np.asarray(w_gate_up, dtype=f32)),
        "bgate_bc": bc(b_gate),
        "gon_bc4": bc(np.tile(np.asarray(gla_out_norm, dtype=f32), (1, 4))),
        "qan_col": A(np.asarray(q_a_norm, dtype=f32).reshape(DEPTH, 2, 128).transpose(0, 2, 1)),
        "w_q_up": A(np.asarray(w_q_up, dtype=f32)),
        "kvan_col": A(np.asarray(kv_a_norm, dtype=f32).reshape(DEPTH, 128, 1)),
        "w_kv_up": A(np.asarray(w_kv_up, dtype=f32)),
        "qnn_bc": bc(q_norm_nope), "knn_bc": bc(k_norm_nope),
        "qnr_bc": bc(q_norm_rope), "knr_bc": bc(k_norm_rope),
        "w_out": A(np.asarray(w_out, dtype=f32)),
        "w_mlp_up": A(np.asarray(w_mlp_up, dtype=f32)),
        "w_mlp_down": A(np.asarray(w_mlp_down, dtype=f32)),
    }
    return d


_NC_CACHE = {}


def kernel(**inputs):
    x = np.asarray(inputs["x"])
    B, S_LEN, _ = x.shape
    DEPTH = np.asarray(inputs["w_ada"]).shape[0]
    key = (S_LEN, DEPTH)
    if key not in _NC_CACHE:
        _NC_CACHE[key] = build(S_LEN, DEPTH, split=True)
    nc = _NC_CACHE[key]
    in_maps = []
    for b in range(B):
        base = host_inputs(b, S_LEN, DEPTH, _parity=0, **inputs)
        in_maps.append(base)
        m1 = dict(base)
        m1["parcol"] = host_inputs_par(1)
        in_maps.append(m1)
    res = run_bass_kernel_spmd(nc, in_maps, core_ids=list(range(2 * B)))
    out = np.empty((B, S_LEN, D), dtype=np.float32)
    ov = out.reshape(B, S_LEN // 1024, 2, 512, D)
    for b in range(B):
        for p in range(2):
            ov[b, :, p] = np.asarray(res.results[2 * b + p]["y"], dtype=np.float32).reshape(S_LEN // 1024, 512, D)
    return out


def host_inputs_par(p):
    return np.ascontiguousarray(np.broadcast_to(np.array([[1.0 - p, float(p)]], dtype=np.float32), (128, 2)))
```

```python
import contextlib
import math
import numpy as np
import concourse.bass as bass
import concourse.mybir as mybir
from concourse.bass_utils import run_bass_kernel_spmd

F32 = mybir.dt.float32
BF16 = mybir.dt.bfloat16
I32 = mybir.dt.int32
ALU = mybir.AluOpType
AF = mybir.ActivationFunctionType

ENGS = ("pe", "act", "dve", "pool", "sp")
EST_DUR = {"pe": 0.15, "act": 0.3, "dve": 0.3, "pool": 0.5, "sp": 0.1}
import os as _os
SEM_SHARE = _os.environ.get("SEM_SHARE", "0") == "1"
SEQ_CHAINS = _os.environ.get("SEQ_CHAINS", "0") == "1"
NO_KPR = _os.environ.get("NO_KPR", "0") == "1"
SKIP = set(_os.environ.get("SKIP_PHASES", "").split(","))
OLD_ORDER = _os.environ.get("OLD_ORDER", "0") == "1"
SEM_MAP = {"wm0": "A0", "wm1": "A1", "xt0": "A0", "xt1": "A1", "cst0": "B0", "cst1": "B1",
           "KTs0": "A0", "KTs1": "A1", "Vh0": "B0", "Vh1": "B1", "qns0": "C0", "qns1": "C1",
           "qps0": "D0", "qps1": "D1", "mx0": "B0", "mx1": "B1", "xm0": "A0", "xm1": "A1",
           "ob_st0": "ST0", "ob_st1": "ST1", "xo_st0": "ST0", "xo_st1": "ST1", "yo_st0": "ST0", "yo_st1": "ST1"}


class Buf:
    __slots__ = ("name", "lw", "rd", "rd_dma")

    def __init__(self, name):
        self.name = name
        self.lw = None
        self.rd = {}
        self.rd_dma = []


class Ins:
    __slots__ = ("eng", "fn", "deps", "signal", "cnt", "is_dma", "dsem", "dval", "tfin")

    def __init__(self, eng, fn, is_dma=False):
        self.eng = eng
        self.fn = fn
        self.deps = []
        self.signal = False
        self.cnt = 0
        self.is_dma = is_dma
        self.dsem = None
        self.dval = 0
        self.tfin = 0.0


class Sched:
    def __init__(self, nc):
        self.nc = nc
        self.q = {e: [] for e in ENGS}
        self.dma_sems = {}
        self.all_dma = []
        self.last_on_sem = {}
        self.bar = None
        self.bar_done = {}
        self.nbuf = 0
        self.eng_free = {e: 0.0 for e in ENGS}
        self.step_max = 0.0

    def buf(self, name=None):
        self.nbuf += 1
        return Buf(name or ("b%d" % self.nbuf))

    def _collect(self, ins, reads, writes):
        deps = {}
        pe = (ins.eng == "pe" and not ins.is_dma)

        def add(d):
            if d is None or d is ins:
                return
            if pe and d.eng == "pe" and not d.is_dma:
                return
            if d.is_dma:
                d = self.last_on_sem[d.dsem]
                if d is ins:
                    return
            deps[id(d)] = d

        for b in reads:
            add(b.lw)
        for b in writes:
            add(b.lw)
            for d in b.rd.values():
                add(d)
            for d in b.rd_dma:
                add(d)
        if self.bar is not None and not self.bar_done.get(ins.eng):
            for d in self.bar:
                add(d)
            self.bar_done[ins.eng] = True
        for b in reads:
            if ins.is_dma:
                b.rd_dma.append(ins)
            else:
                b.rd[ins.eng] = ins
        for b in writes:
            b.lw = ins
            b.rd = {}
            b.rd_dma = []
        ins.deps = list(deps.values())
        ready = max([d.tfin for d in ins.deps], default=0.0) + (0.25 if ins.deps else 0.0)
        start = max(ready, self.eng_free[ins.eng])
        if ins.is_dma:
            self.eng_free[ins.eng] = start + 0.1
            ins.tfin = start + 2.5
        else:
            ins.tfin = start + EST_DUR[ins.eng]
            self.eng_free[ins.eng] = ins.tfin
        if ins.tfin > self.step_max:
            self.step_max = ins.tfin

    def op(self, eng, fn, reads=(), writes=()):
        ins = Ins(eng, fn)
        self._collect(ins, reads, writes)
        self.q[eng].append(ins)
        return ins

    def dma(self, eng, out, in_, reads=(), writes=(), sem=None, nobar=False):
        if sem is None:
            sem = (list(writes) + list(reads))[0].name
        if SEM_SHARE:
            sem = SEM_MAP.get(sem, "G_const" if not sem.endswith("_st") else "ST")
        ins = Ins(eng, lambda e: e.dma_start(out=out, in_=in_), is_dma=True)
        tot = self.dma_sems.get(sem, 0) + 16
        self.dma_sems[sem] = tot
        ins.dsem = sem
        ins.dval = tot
        self._collect(ins, reads, writes)
        self.last_on_sem[sem] = ins
        self.q[eng].append(ins)
        if not nobar:
            self.all_dma.append(ins)
        return ins

    def barrier(self):
        deps = []
        for e in ENGS:
            for ins in reversed(self.q[e]):
                if not ins.is_dma:
                    deps.append(ins)
                    break
        deps.extend(self.all_dma)
        self.all_dma = []
        last = {}
        keep = []
        for d in deps:
            if d.is_dma:
                if d.dsem not in last or last[d.dsem].dval < d.dval:
                    last[d.dsem] = d
            else:
                keep.append(d)
        self.bar = keep + list(last.values())
        self.bar_done = {}

    def emit(self, final_waits=()):
        nc = self.nc
        for e in ENGS:
            for ins in self.q[e]:
                for d in ins.deps:
                    if not d.is_dma:
                        d.signal = True
        for e in ENGS:
            c = 0
            for ins in self.q[e]:
                if ins.signal and not ins.is_dma:
                    c += 1
                ins.cnt = c
        with contextlib.ExitStack() as st:
            esem = {e: st.enter_context(nc.semaphore("es_" + e)) for e in ENGS}
            dsem = {k: st.enter_context(nc.semaphore("ds_%d" % i)) for i, k in enumerate(self.dma_sems)}
            block = st.enter_context(nc.Block())
            sched = self

            def run(e, eh):
                waited = {}
                for ins in sched.q[e]:
                    need = {}
                    for d in ins.deps:
                        if d.is_dma:
                            s, v, key = dsem[d.dsem], d.dval, ("d", d.dsem)
                        else:
                            s, v, key = esem[d.eng], d.cnt, ("e", d.eng)
                        if key not in need or need[key][1] < v:
                            need[key] = (s, v)
                    for key, (s, v) in need.items():
                        if waited.get(key, 0) >= v:
                            continue
                        waited[key] = v
                        eh.wait_ge(s, v)
                    bi = ins.fn(eh)
                    if ins.is_dma:
                        bi.then_inc(dsem[ins.dsem], 16)
                    elif ins.signal:
                        bi.then_inc(esem[e], 1)
                if e == "sp":
                    for d in final_waits:
                        eh.wait_ge(dsem[d.dsem], d.dval)

            @block.tensor
            def _(eh):
                run("pe", eh)

            @block.scalar
            def _(eh):
                run("act", eh)

            @block.vector
            def _(eh):
                run("dve", eh)

            @block.gpsimd
            def _(eh):
                run("pool", eh)

            @block.sync
            def _(eh):
                run("sp", eh)


D = 1024
DFF = 4096
EPS = 1e-6
TWO_PI = 2.0 * math.pi
C1 = 6.28125
C2 = TWO_PI - C1


def build(S_LEN, DEPTH, dbg=False, stop=None, split=False):
    NT = S_LEN // 128
    NC4 = S_LEN // 512
    NG = S_LEN // 256
    nc = bass.Bass("TRN2", target_bir_lowering=False)
    S = Sched(nc)

    def din(name, shape, dt=F32):
        return nc.dram_tensor(name, shape, dt, kind="ExternalInput").ap()

    def dscr(name, shape, dt):
        return nc.dram_tensor(name, shape, dt, kind=("ExternalOutput" if dbg else "Internal")).ap()

    x_in = din("x", [S_LEN, D])
    ccol = din("ccol", [128, 8])
    pos_in = din("pos", [128, NT], I32)
    invf_in = din("invf", [128, 32])
    par_in = din("parcol", [128, 2])
    w_ada = din("w_ada", [DEPTH, D, 6 * D])
    bada_col = din("bada_col", [DEPTH, 128, 48])
    bada_gate = din("bada_gate", [DEPTH, 2, 128, D])
    w_in = din("w_in", [DEPTH, D, 2000])
    wgu_in = din("w_gate_up", [DEPTH, 16, 256])
    bgate_in = din("bgate_bc", [DEPTH, 128, 256])
    gon_in = din("gon_bc4", [DEPTH, 128, 512])
    qan_in = din("qan_col", [DEPTH, 128, 2])
    wqu_in = din("w_q_up", [DEPTH, 256, 768])
    kvan_in = din("kvan_col", [DEPTH, 128, 1])
    wkvu_in = din("w_kv_up", [DEPTH, 128, 1024])
    qnn_in = din("qnn_bc", [DEPTH, 128, 128])
    knn_in = din("knn_bc", [DEPTH, 128, 128])
    qnr_in = din("qnr_bc", [DEPTH, 128, 64])
    knr_in = din("knr_bc", [DEPTH, 128, 64])
    wout_in = din("w_out", [DEPTH, D, D])
    w1_in = din("w_mlp_up", [DEPTH, D, DFF])
    w2_in = din("w_mlp_down", [DEPTH, DFF, D])
    y_out = nc.dram_tensor("y", [S_LEN // 2 if split else S_LEN, D], F32, kind="ExternalOutput").ap()

    xs = [dscr("xs%d" % i, [S_LEN, D], F32) for i in range(2)]
    xmid_d = dscr("xmid", [S_LEN, D], F32)
    mixT_d = dscr("mixT", [8, 128, S_LEN], BF16)
    KT_d = dscr("KT", [4, 128, S_LEN], BF16)
    KPE_d = dscr("KPE", [64, S_LEN], BF16)
    V_d = dscr("Vd", [S_LEN, 512], BF16)
    QT_d = dscr("QT", [4, 128, S_LEN], BF16)
    QPE_d = dscr("QPE", [2, 128, S_LEN], BF16)
    cos_d = dscr("cosd", [128, NT, 32], F32)
    sin_d = dscr("sind", [128, NT, 32], F32)
    gate_d = dscr("gated", [DEPTH, 2, 128, D], F32)
    b_xs = [S.buf("xs0"), S.buf("xs1")]
    b_xmid = S.buf("xmid")
    b_mixT = S.buf("mixT")
    b_KT, b_KPE, b_V, b_QT, b_QPE = S.buf("KT"), S.buf("KPE"), S.buf("V"), S.buf("QT"), S.buf("QPE")
    b_cos, b_sin, b_gate = S.buf("cosd"), S.buf("sind"), S.buf("gated")
    b_y = S.buf("y")
    out_dmas = []

    def mm(out, lhsT, rhs, start, stop, R, W):
        S.op("pe", lambda e: e.matmul(out=out, lhsT=lhsT, rhs=rhs, start=start, stop=stop), R, W)

    def tr(out, in_, ident, R, W):
        S.op("pe", lambda e: e.transpose(out=out, in_=in_, identity=ident), R, W)

    def act(out, in_, func, R, W, scale=1.0, bias=0.0, accum=None):
        if accum is None:
            S.op("act", lambda e: e.activation(out=out, in_=in_, func=func, bias=bias, scale=scale), R, W)
        else:
            S.op("act", lambda e: e.activation(out=out, in_=in_, func=func, bias=bias, scale=scale, accum_out=accum), R, W)

    def tt(eng, out, in0, in1, op, R, W):
        S.op(eng, lambda e: e.tensor_tensor(out=out, in0=in0, in1=in1, op=op), R, W)

    def ts(eng, out, in0, s1, s2, op0, op1, R, W):
        if s2 is None:
            S.op(eng, lambda e: e.tensor_scalar(out=out, in0=in0, scalar1=s1, scalar2=None, op0=op0), R, W)
        else:
            S.op(eng, lambda e: e.tensor_scalar(out=out, in0=in0, scalar1=s1, scalar2=s2, op0=op0, op1=op1), R, W)

    def stt(eng, out, in0, scalar, in1, op0, op1, R, W, accum=None):
        if accum is None:
            S.op(eng, lambda e: e.scalar_tensor_tensor(out=out, in0=in0, scalar=scalar, in1=in1, op0=op0, op1=op1), R, W)
        else:
            S.op(eng, lambda e: e.scalar_tensor_tensor(out=out, in0=in0, scalar=scalar, in1=in1, op0=op0, op1=op1, accum_out=accum), R, W)

    def cp(eng, out, in_, R, W):
        if eng == "act":
            S.op("act", lambda e: e.copy(out=out, in_=in_), R, W)
        else:
            S.op(eng, lambda e: e.tensor_copy(out=out, in_=in_), R, W)

    def recip(out, in_, R, W):
        S.op("dve", lambda e: e.reciprocal(out=out, in_=in_), R, W)

    def memset(eng, ap, val, W):
        S.op(eng, lambda e: e.memset(ap, val), (), W)

    def asel(out, in_, pattern, cmp, fill, base, cm, R, W):
        S.op("pool", lambda e: e.affine_select(out=out, in_=in_, pattern=pattern, compare_op=cmp, fill=fill,
                                               base=base, channel_multiplier=cm), R, W)

    def rsqrt_cols(dst, src, scale, R, W):
        act(dst, src, AF.Ln, R, W, scale=scale, bias=EPS)
        act(dst, dst, AF.Exp, W, W, scale=-0.5)

    with contextlib.ExitStack() as top:
        uid = [0]

        def SB(stack, name, shape, dt):
            uid[0] += 1
            t = stack.enter_context(nc.sbuf_tensor("%s_s%d" % (name, uid[0]), shape, dt))
            return t, S.buf(name)

        def PS(stack, name, shape, dt):
            uid[0] += 1
            t = stack.enter_context(nc.psum_tensor("%s_p%d" % (name, uid[0]), shape, dt))
            return t, S.buf(name)

        ident, b_ident = SB(top, "ident", [128, 128], BF16)
        maskU4, b_maskU4 = SB(top, "maskU4", [128, 4, 128], BF16)
        ones_f, b_onesf = SB(top, "ones_f", [128, 128], F32)
        ones_b, b_onesb = SB(top, "ones_b", [128, 128], BF16)
        maskD, b_maskD = SB(top, "maskD", [128, 4, 512], BF16)
        modcol, b_modcol = SB(top, "modcol", [128, DEPTH * 32], F32)

        par, b_par = SB(top, "par", [128, 2], F32)
        S.dma("sp", par[:], par_in, writes=[b_par])
        memset("pool", ident[:], 0.0, [b_ident])
        asel(ident[:], ident[:], [[-1, 128]], ALU.not_equal, 1.0, 0, 1, [b_ident], [b_ident])
        memset("pool", maskU4[:], 1.0, [b_maskU4])
        for h in range(4):
            asel(maskU4[:, h, :], maskU4[:, h, :], [[1, 128]], ALU.is_ge, 0.0, 0, -1, [b_maskU4], [b_maskU4])
        memset("pool", ones_f[:], 1.0, [b_onesf])
        memset("pool", ones_b[:], 1.0, [b_onesb])
        memset("pool", maskD[:], 1.0, [b_maskD])
        for r in range(4):
            asel(maskD[:, r, :], maskD[:, r, :], [[1, 512]], ALU.is_ge, 0.0, -128 * r, -1, [b_maskD], [b_maskD])

        with contextlib.ExitStack() as ph:
            posi, b_posi = SB(ph, "posi", [128, NT], I32)
            posf, b_posf = SB(ph, "posf", [128, NT], F32)
            invf, b_invf = SB(ph, "invf", [128, 32], F32)
            ang, b_ang = SB(ph, "ang", [128, NT, 32], F32)
            uu, b_uu = SB(ph, "uu", [128, NT, 32], F32)
            ni, b_ni = SB(ph, "ni", [128, NT, 32], I32)
            nf, b_nf = SB(ph, "nf", [128, NT, 32], F32)
            mk, b_mk = SB(ph, "mk", [128, NT, 32], F32)
            sn, b_sn = SB(ph, "sn", [128, NT, 32], F32)
            cs, b_cs = SB(ph, "cs", [128, NT, 32], F32)
            S.dma("sp", posi[:], pos_in, writes=[b_posi])
            S.dma("sp", invf[:], invf_in, writes=[b_invf])
            cp("dve", posf[:], posi[:], [b_posi], [b_posf])
            for t in range(NT):
                ts("dve", ang[:, t, :], invf[:], posf[:, t:t + 1], None, ALU.mult, None, [b_invf, b_posf], [b_ang])
            ts("dve", uu[:], ang[:], 1.0 / TWO_PI, None, ALU.mult, None, [b_ang], [b_uu])
            cp("dve", ni[:], uu[:], [b_uu], [b_ni])
            cp("dve", nf[:], ni[:], [b_ni], [b_nf])
            stt("dve", ang[:], nf[:], -C1, ang[:], ALU.mult, ALU.add, [b_nf, b_ang], [b_ang])
            stt("dve", ang[:], nf[:], -C2, ang[:], ALU.mult, ALU.add, [b_nf, b_ang], [b_ang])
            ts("dve", mk[:], ang[:], math.pi, None, ALU.is_gt, None, [b_ang], [b_mk])
            stt("dve", ang[:], mk[:], -TWO_PI, ang[:], ALU.mult, ALU.add, [b_mk, b_ang], [b_ang])
            ts("dve", mk[:], ang[:], -math.pi, None, ALU.is_lt, None, [b_ang], [b_mk])
            stt("dve", ang[:], mk[:], TWO_PI, ang[:], ALU.mult, ALU.add, [b_mk, b_ang], [b_ang])
            ts("dve", ang[:], ang[:], math.pi, -math.pi, ALU.min, ALU.max, [b_ang], [b_ang])
            act(sn[:], ang[:], AF.Sin, [b_ang], [b_sn])
            stt("dve", uu[:], ang[:], -1.0, ang[:], ALU.mult, ALU.max, [b_ang], [b_uu])
            ts("dve", uu[:], uu[:], -1.0, math.pi / 2, ALU.mult, ALU.add, [b_uu], [b_uu])
            act(cs[:], uu[:], AF.Sin, [b_uu], [b_cs])
            S.dma("sp", cos_d, cs[:], reads=[b_cs], writes=[b_cos], sem="cs_st")
            S.dma("sp", sin_d, sn[:], reads=[b_sn], writes=[b_sin], sem="sn_st")

            if stop != 'rope':
                cc, b_cc = SB(ph, "cc", [128, 8], F32)
                ce, b_ce = SB(ph, "ce", [128, 8], F32)
                cond, b_cond = SB(ph, "cond", [128, 8], F32)
                cond_rep, b_crep = SB(ph, "cond_rep", [128, 8, 128], BF16)
                condb, b_condb = SB(ph, "condb", [128, 16], BF16)
                wm = [SB(ph, "wm%d" % i, [128, 8, D], BF16) for i in range(2)]
                bcol, b_bcol = SB(ph, "bcol", [128, DEPTH * 48], F32)
                bgt, b_bgt = SB(ph, "bgt", [128, D], F32)
                gsb, b_gsb = SB(ph, "gsb", [128, D], F32)
                pg = [PS(ph, "pg%d" % i, [128, 512], F32) for i in range(2)]
                pc, b_pc = PS(ph, "pc", [128, 8], F32)
                S.dma("sp", cc[:], ccol, writes=[b_cc])
                for l in range(DEPTH):
                    S.dma("sp", bcol[:, l * 48:(l + 1) * 48], bada_col[l], writes=[b_bcol])
                act(ce[:], cc[:], AF.Exp, [b_cc], [b_ce], scale=-1.0)
                ts("dve", ce[:], ce[:], 1.0, None, ALU.add, None, [b_ce], [b_ce])
                recip(ce[:], ce[:], [b_ce], [b_ce])
                tt("dve", cond[:], cc[:], ce[:], ALU.mult, [b_cc, b_ce], [b_cond])
                memset("dve", condb[:], 0.0, [b_condb])
                cp("dve", condb[:, 0:8], cond[:], [b_cond, b_condb], [b_condb])
                for k in range(8):
                    ts("dve", cond_rep[:, k, :], ones_f[:], cond[:, k:k + 1], None, ALU.mult, None, [b_onesf, b_cond], [b_crep])
                li = 0
                for l in range(DEPTH):
                    for m in range(6):
                        wt, b_wt = wm[li % 2]
                        li += 1
                        S.dma("pool", wt[:], w_ada[l, :, m * D:(m + 1) * D].rearrange("(k p) n -> p k n", p=128), writes=[b_wt])
                        if m in (2, 5):
                            gi = 0 if m == 2 else 1
                            S.dma("sp", bgt[:], bada_gate[l, gi], writes=[b_bgt])
                            for half in range(2):
                                pgt, b_pg = pg[half]
                                for k in range(8):
                                    mm(pgt[:], cond_rep[:, k, :], wt[:, k, half * 512:(half + 1) * 512], k == 0, k == 7,
                                       [b_crep, b_wt], [b_pg])
                                tt("dve", gsb[:, half * 512:(half + 1) * 512], pgt[:], bgt[:, half * 512:(half + 1) * 512],
                                   ALU.add, [b_pg, b_bgt], [b_gsb])
                            S.dma("sp", gate_d[l, gi], gsb[:], reads=[b_gsb], writes=[b_gate], sem="gate_st")
                        else:
                            mi = {0: 0, 1: 1, 3: 2, 4: 3}[m]
                            for ko in range(8):
                                for k in range(8):
                                    mm(pc[:, ko:ko + 1], wt[:, k, ko * 128:(ko + 1) * 128], condb[:, k:k + 1], k == 0, k == 7,
                                       [b_wt, b_condb], [b_pc])
                            dst = modcol[:, l * 32 + mi * 8: l * 32 + mi * 8 + 8]
                            tt("dve", dst, pc[:], bcol[:, l * 48 + m * 8: l * 48 + m * 8 + 8], ALU.add, [b_pc, b_bcol], [b_modcol])
                            if m in (1, 4):
                                ts("dve", dst, dst, 1.0, None, ALU.add, None, [b_modcol], [b_modcol])
        S.barrier()

        for l in range(DEPTH if stop not in ('setup', 'rope') else 0):
            x_cur = x_in if l == 0 else xs[(l - 1) % 2]
            b_xcur = None if l == 0 else b_xs[(l - 1) % 2]
            last = (l == DEPTH - 1)
            x_nxt = y_out if last else xs[l % 2]
            b_xnxt = b_y if last else b_xs[l % 2]
            mc = l * 32

            with contextlib.ExitStack() as ph:
                win, b_win = SB(ph, "win", [128, 8, 2000], BF16)
                wgu, b_wgu = SB(ph, "wgu", [16, 256], BF16)
                wqu, b_wqu = SB(ph, "wqu", [128, 2, 768], BF16)
                wkvu, b_wkvu = SB(ph, "wkvu", [128, 1024], BF16)
                qan, b_qan = SB(ph, "qan", [128, 2], F32)
                kvan, b_kvan = SB(ph, "kvan", [128, 1], F32)
                bgate, b_bgate = SB(ph, "bgate", [128, 256], F32)
                gon4, b_gon4 = SB(ph, "gon4", [128, 512], F32)
                qnn, b_qnn = SB(ph, "qnn", [128, 128], F32)
                knn, b_knn = SB(ph, "knn", [128, 128], F32)
                qnr, b_qnr = SB(ph, "qnr", [128, 64], F32)
                knr, b_knr = SB(ph, "knr", [128, 64], F32)
                b_wink = [S.buf("win%d" % k) for k in range(8)]
                for k in range(8):
                    S.dma("pool", win[:, k, :], w_in[l, k * 128:(k + 1) * 128, :], writes=[b_wink[k]], sem="win_ld")
                S.dma("pool", wgu[:], wgu_in[l], writes=[b_wgu])
                S.dma("pool", wqu[:], wqu_in[l].rearrange("(k p) n -> p k n", p=128), writes=[b_wqu])
                S.dma("pool", wkvu[:], wkvu_in[l], writes=[b_wkvu])
                S.dma("sp", qan[:], qan_in[l], writes=[b_qan])
                S.dma("sp", kvan[:], kvan_in[l], writes=[b_kvan])
                S.dma("sp", bgate[:], bgate_in[l], writes=[b_bgate])
                S.dma("sp", gon4[:], gon_in[l], writes=[b_gon4])
                S.dma("sp", qnn[:], qnn_in[l], writes=[b_qnn])
                S.dma("sp", knn[:], knn_in[l], writes=[b_knn])
                S.dma("sp", qnr[:], qnr_in[l], writes=[b_qnr])
                S.dma("sp", knr[:], knr_in[l], writes=[b_knr])
                for c in range(2):
                    ts("dve", wqu[:, c, :], wqu[:, c, :], qan[:, c:c + 1], None, ALU.mult, None, [b_wqu, b_qan], [b_wqu])
                ts("dve", wkvu[:], wkvu[:], kvan[:, 0:1], None, ALU.mult, None, [b_wkvu, b_kvan], [b_wkvu])
                qsc = 192.0 ** -0.5
                ts("dve", qnn[:], qnn[:], qsc, None, ALU.mult, None, [b_qnn], [b_qnn])
                ts("dve", qnr[:], qnr[:], qsc, None, ALU.mult, None, [b_qnr], [b_qnr])

                xt = [SB(ph, "xt%d" % i, [128, D], F32) for i in range(2)]
                cst = [SB(ph, "cst%d" % i, [128, 2, 32], F32) for i in range(2)]
                sq, b_sq = SB(ph, "sq", [128, D], F32)
                ss, b_ss = SB(ph, "ss", [128, 1], F32)
                xn, b_xn = SB(ph, "xn", [128, D], BF16)
                hT, b_hT = SB(ph, "hT", [128, 8, 128], BF16)
                mD, b_mD = SB(ph, "mD", [128, 400], BF16)
                ss3, b_ss3 = SB(ph, "ss3", [128, 3], F32)
                rs3, b_rs3 = SB(ph, "rs3", [128, 3], F32)
                rq2, b_rq2 = SB(ph, "rq2", [128, 3], F32)
                mT, b_mT = SB(ph, "mT", [128, 4, 128], BF16)
                pre, b_pre = SB(ph, "pre", [128, 256], F32)
                lg, b_lg = SB(ph, "lg", [128, 256], F32)
                lgh, b_lgh = SB(ph, "lgh", [128, 256], BF16)
                lgl, b_lgl = SB(ph, "lgl", [128, 256], BF16)
                eb, b_eb = SB(ph, "eb", [128, 256], F32)
                enb, b_enb = SB(ph, "enb", [128, 256], F32)
                ebl, b_ebl = SB(ph, "ebl", [64, 4], F32)
                qg, b_qg = SB(ph, "qg", [128, 256], BF16)
                kg, b_kg = SB(ph, "kg", [128, 256], BF16)
                qkT, b_qkT = SB(ph, "qkT", [64, 8, 128], BF16)
                vsb, b_vsb = SB(ph, "vsb", [128, 512], BF16)
                eo, b_eo = SB(ph, "eo", [128, 512], F32)
                gog, b_gog = SB(ph, "gog", [128, 512], F32)
                ATs, b_ATs = SB(ph, "ATs", [128, 4, 128], BF16)
                stt_, b_st = SB(ph, "gst", [64, 4, 128], F32)
                stb, b_stb = SB(ph, "gstb", [64, 4, 128], BF16)
                sso, b_sso = SB(ph, "sso", [128, 4], F32)
                go, b_go = SB(ph, "go", [128, 512], BF16)
                gT, b_gT = SB(ph, "gT", [128, 4, 128], BF16)
                ss8, b_ss8 = SB(ph, "ss8", [128, 8], F32)
                fac8, b_fac8 = SB(ph, "fac8", [128, 8], F32)
                qn, b_qn = SB(ph, "qn", [128, 4, 128], BF16)
                zall, b_zall = SB(ph, "zall", [128, 5, 64], F32)
                ra, b_ra = SB(ph, "ra", [128, 5, 32], F32)
                rb, b_rb = SB(ph, "rb", [128, 5, 32], F32)
                rope_o, b_ropeo = SB(ph, "rope_o", [128, 5, 64], BF16)
                qT6, b_qT6 = SB(ph, "qT6", [128, 6, 128], BF16)
                kn, b_kn = SB(ph, "kn", [128, 4, 128], BF16)
                vt, b_vt = SB(ph, "vt", [128, 4, 128], BF16)
                kT5, b_kT5 = SB(ph, "kT5", [128, 5, 128], BF16)
                ssk, b_ssk = SB(ph, "ssk", [128, 4], F32)
                fack, b_fack = SB(ph, "fack", [128, 4], F32)

                pT, b_pT = PS(ph, "pT", [128, 8, 128], BF16)
                pA, b_pA = PS(ph, "pA", [128, 512], F32)
                pB, b_pB = PS(ph, "pB", [128, 512], F32)
                pC, b_pC = PS(ph, "pC", [128, 512], F32)
                pD, b_pD = PS(ph, "pD", [128, 512], F32)
                pM, _ = PS(ph, "pM", [128, 8, 128], BF16)
                b_pMa = b_pMb = S.buf("pM")
                pX, b_pX = PS(ph, "pX", [128, 512], F32)
                pY, b_pY = PS(ph, "pY", [128, 512], F32)
                pX4 = pX[:].rearrange("p (h e) -> p h e", e=128)
                pY4 = pY[:].rearrange("p (h e) -> p h e", e=128)

                memset("dve", stt_[:], 0.0, [b_st])
                memset("dve", stb[:], 0.0, [b_stb])

                sq2, b_sq2 = SB(ph, "sq2", [128, 128], F32)
                sq3, b_sq3 = SB(ph, "sq3", [128, 128], F32)
                b_pM = b_pMa
                kprs = [SB(ph, "kpr%d" % i, [128, 64], F32) for i in range(2)]
                rs3s = [SB(ph, "rs3_%d" % i, [128, 3], F32) for i in range(2)]
                rq2s = [SB(ph, "rq2_%d" % i, [128, 3], F32) for i in range(2)]
                mTs = [SB(ph, "mT%d" % i, [128, 4, 128], BF16) for i in range(2)]
                gqks = [SB(ph, "gqk%d" % i, [128, 512], F32) for i in range(2)]
                vsbs = [SB(ph, "vsb%d" % i, [128, 512], BF16) for i in range(2)]
                gogs = [SB(ph, "gog%d" % i, [128, 512], F32) for i in range(2)]
                pW = [(pA, b_pA), (pB, b_pB)]
                pQ = [(pC, b_pC), (pD, b_pD)]

                def drive(gens):
                    alive = list(gens)
                    while alive:
                        for g in list(alive):
                            try:
                                next(g)
                            except StopIteration:
                                alive.remove(g)

                def prologue(t):
                    sl = t % 2
                    tsl = slice(t * 128, (t + 1) * 128)
                    xtt, b_xt = xt[sl]
                    cs_t, b_cst = cst[sl]
                    kpr, b_kpr = kprs[sl]
                    rs3, b_rs3 = rs3s[sl]
                    rq2, b_rq2 = rq2s[sl]
                    mT, b_mT = mTs[sl]
                    gqk, b_gqk = gqks[sl]
                    vsb, b_vsb = vsbs[sl]
                    gog, b_gog = gogs[sl]
                    S.dma("sp", xtt[:], x_cur[tsl, :], reads=([b_xcur] if b_xcur else []), writes=[b_xt])
                    S.dma("sp", cs_t[:, 0, :], cos_d[:, t, :], reads=[b_cos], writes=[b_cst])
                    S.dma("sp", cs_t[:, 1, :], sin_d[:, t, :], reads=[b_sin], writes=[b_cst])
                    yield
                    act(sq[:], xtt[:], AF.Square, [b_xt], [b_sq, b_ss], accum=ss[:, 0:1])
                    yield
                    rsqrt_cols(ss[:, 0:1], ss[:, 0:1], 1.0 / D, [b_ss], [b_ss])
                    yield
                    ts("dve", xn[:], xtt[:], ss[:, 0:1], None, ALU.mult, None, [b_xt, b_ss], [b_xn])
                    yield
                    for k in range(8):
                        tr(pT[:, k, :], xn[:, k * 128:(k + 1) * 128], ident[:], [b_xn, b_ident], [b_pT])
                    yield
                    for k in range(8):
                        if k % 2 == 0:
                            ts("dve", hT[:, k, :], pT[:, k, :], modcol[:, mc + 8 + k: mc + 9 + k], modcol[:, mc + k: mc + k + 1],
                               ALU.mult, ALU.add, [b_pT, b_modcol], [b_hT])
                        else:
                            act(hT[:, k, :], pT[:, k, :], AF.Identity, [b_pT, b_modcol], [b_hT],
                                scale=modcol[:, mc + 8 + k: mc + 9 + k], bias=modcol[:, mc + k: mc + k + 1])
                        if k % 4 == 3:
                            yield
                    blks = ((1536, 2000), (0, 512), (512, 1024), (1024, 1536))
                    for bi, (c0, c1) in enumerate(blks):
                        pw, b_pw = pW[bi % 2]
                        for k in range(8):
                            mm(pw[:, 0:c1 - c0], hT[:, k, :], win[:, k, c0:c1], k == 0, k == 7, [b_hT, b_wink[k]], [b_pw])
                        yield
                        if bi == 0:
                            cp("act", mD[:], pw[:, 0:400], [b_pw], [b_mD])
                            cp("act", kpr[:], pw[:, 400:464], [b_pw], [b_kpr])
                            yield
                            act(sq[:, 0:256], pw[:, 16:272], AF.Square, [b_pw], [b_sq, b_ss3], accum=ss3[:, 0:1])
                            act(sq[:, 0:128], pw[:, 272:400], AF.Square, [b_pw], [b_sq, b_ss3], accum=ss3[:, 1:2])
                            act(sq[:, 0:64], pw[:, 400:464], AF.Square, [b_pw], [b_sq, b_ss3], accum=ss3[:, 2:3])
                            yield
                        elif bi == 1:
                            cp("act", gqk[:], pw[:], [b_pw], [b_gqk])
                            yield
                        elif bi == 2:
                            cp("act", vsb[:], pw[:], [b_pw], [b_vsb])
                            yield
                        else:
                            act(eo[:], pw[:], AF.Exp, [b_pw], [b_eo], scale=-1.0)
                            yield
                            act(eo[:], eo[:], AF.Ln, [b_eo], [b_eo], bias=1.0)
                            yield
                            act(eo[:], eo[:], AF.Exp, [b_eo], [b_eo], scale=-1.0)
                            yield
                            tt("dve", gog[:], pw[:], eo[:], ALU.mult, [b_pw, b_eo], [b_gog])
                            yield
                            tt("pool", gog[:], gog[:], gon4[:], ALU.mult, [b_gog, b_gon4], [b_gog])
                            yield
                    act(rs3[:, 0:1], ss3[:, 0:1], AF.Ln, [b_ss3], [b_rs3], scale=1.0 / 256, bias=EPS)
                    act(rs3[:, 1:2], ss3[:, 1:2], AF.Ln, [b_ss3], [b_rs3], scale=1.0 / 128, bias=EPS)
                    act(rs3[:, 2:3], ss3[:, 2:3], AF.Ln, [b_ss3], [b_rs3], scale=1.0 / 64, bias=EPS)
                    yield
                    act(rs3[:], rs3[:], AF.Exp, [b_rs3], [b_rs3], scale=-0.5)
                    yield
                    tt("dve", rq2[:], rs3[:], rs3[:], ALU.mult, [b_rs3], [b_rq2])
                    tr(pT[0:16, 0, :], mD[:, 0:16], ident[:], [b_mD, b_ident], [b_pT])
                    tr(pT[:, 1, :], mD[:, 16:144], ident[:], [b_mD, b_ident], [b_pT])
                    tr(pT[:, 2, :], mD[:, 144:272], ident[:], [b_mD, b_ident], [b_pT])
                    tr(pT[:, 3, :], mD[:, 272:400], ident[:], [b_mD, b_ident], [b_pT])
                    yield
                    cp("dve", mT[0:16, 0, :], pT[0:16, 0, :], [b_pT], [b_mT])
                    cp("dve", mT[:, 1:4, :], pT[:, 1:4, :], [b_pT], [b_mT])
                    yield

                def gla_chain(t):
                    sl = t % 2
                    tsl = slice(t * 128, (t + 1) * 128)
                    mT, b_mT = mTs[sl]
                    gqk, b_gqk = gqks[sl]
                    vsb, b_vsb = vsbs[sl]
                    gog, b_gog = gogs[sl]
                    mm(pX[:, 0:256], mT[0:16, 0, :], wgu[:], True, True, [b_mT, b_wgu], [b_pX])
                    tt("dve", pre[:], pX[:, 0:256], bgate[:], ALU.add, [b_pX, b_bgate], [b_pre])
                    yield
                    act(pre[:], pre[:], AF.Exp, [b_pre], [b_pre], scale=-1.0)
                    yield
                    act(lg[:], pre[:], AF.Ln, [b_pre], [b_lg], bias=1.0)
                    yield
                    cp("dve", lgh[:], lg[:], [b_lg], [b_lgh])
                    yield
                    tt("dve", lgl[:], lg[:], lgh[:], ALU.subtract, [b_lg, b_lgh], [b_lgl])
                    yield
                    mm(pX[:, 256:512], maskU4[:, 0, :], lgh[:], True, False, [b_maskU4, b_lgh], [b_pX])
                    mm(pX[:, 256:512], maskU4[:, 0, :], lgl[:], False, True, [b_maskU4, b_lgl], [b_pX])
                    for h in range(4):
                        mm(pY[0:64, h:h + 1], lgh[:, h * 64:(h + 1) * 64], ones_b[:, 0:1], True, False,
                           [b_lgh, b_onesb], [b_pY])
                        mm(pY[0:64, h:h + 1], lgl[:, h * 64:(h + 1) * 64], ones_b[:, 0:1], False, True,
                           [b_lgl, b_onesb], [b_pY])
                    yield
                    act(eb[:], pX[:, 256:512], AF.Exp, [b_pX], [b_eb], scale=-1.0 / 16)
                    act(enb[:], pX[:, 256:512], AF.Exp, [b_pX], [b_enb], scale=1.0 / 16)
                    act(ebl[:], pY[0:64, 0:4], AF.Exp, [b_pY], [b_ebl], scale=-1.0 / 16)
                    yield
                    stt("dve", qg[:], gqk[:, 0:256], 0.125, eb[:], ALU.mult, ALU.mult, [b_gqk, b_eb], [b_qg])
                    tt("dve", kg[:], gqk[:, 256:512], enb[:], ALU.mult, [b_gqk, b_enb], [b_kg])
                    yield
                    for h in range(4):
                        tr(pM[0:64, h, :], qg[:, h * 64:(h + 1) * 64], ident[:], [b_qg, b_ident], [b_pM])
                        tr(pM[0:64, 4 + h, :], kg[:, h * 64:(h + 1) * 64], ident[:], [b_kg, b_ident], [b_pM])
                    yield
                    cp("dve", qkT[:], pM[0:64, :, :], [b_pM], [b_qkT])
                    yield
                    for h in range(4):
                        mm(pY4[:, h, :], qkT[:, 4 + h, :], qkT[:, h, :], True, True, [b_qkT], [b_pY])
                    yield
                    tt("dve", ATs[:], pY4, maskU4[:], ALU.mult, [b_pY, b_maskU4], [b_ATs])
                    yield
                    for h in range(4):
                        mm(pX4[:, h, :], ATs[:, h, :], vsb[:, h * 128:(h + 1) * 128], True, False, [b_ATs, b_vsb], [b_pX])
                        mm(pX4[:, h, :], qkT[:, h, :], stb[:, h, :], False, True, [b_qkT, b_stb], [b_pX])
                    for h in range(4):
                        mm(pY4[0:64, h, :], kg[:, h * 64:(h + 1) * 64], vsb[:, h * 128:(h + 1) * 128], True, True,
                           [b_kg, b_vsb], [b_pY])
                    yield
                    for h in range(4):
                        ts("dve", stt_[:, h, :], stt_[:, h, :], ebl[:, h:h + 1], None, ALU.mult, None, [b_st, b_ebl], [b_st])
                        stt("dve", stt_[:, h, :], pY4[0:64, h, :], ebl[:, h:h + 1], stt_[:, h, :], ALU.mult, ALU.add,
                            [b_pY, b_ebl, b_st], [b_st])
                        if h == 1:
                            yield
                    cp("dve", stb[:], stt_[:], [b_st], [b_stb])
                    yield
                    for h in range(4):
                        act(sq2[:], pX4[:, h, :], AF.Square, [b_pX], [b_sq2, b_sso], accum=sso[:, h:h + 1])
                    yield
                    act(sso[:], sso[:], AF.Ln, [b_sso], [b_sso], scale=1.0 / 128, bias=EPS)
                    yield
                    act(sso[:], sso[:], AF.Exp, [b_sso], [b_sso], scale=-0.5)
                    yield
                    for h in range(4):
                        stt("dve", go[:, h * 128:(h + 1) * 128], pX4[:, h, :], sso[:, h:h + 1], gog[:, h * 128:(h + 1) * 128],
                            ALU.mult, ALU.mult, [b_pX, b_sso, b_gog], [b_go])
                        if h == 1:
                            yield
                    yield
                    for h in range(4):
                        tr(pM[:, h, :], go[:, h * 128:(h + 1) * 128], ident[:], [b_go, b_ident], [b_pM])
                    yield
                    cp("dve", gT[:], pM[:, 0:4, :], [b_pM], [b_gT])
                    yield
                    S.dma("sp", mixT_d[0:4, :, tsl].rearrange("c p s -> p c s"), gT[:], reads=[b_gT], writes=[b_mixT], sem="gT_st")

                qhs, b_qhs = SB(ph, "qhs", [128, 768], F32)
                kvs, b_kvs = SB(ph, "kvs", [128, 1024], F32)
                zk, b_zk = SB(ph, "zk", [128, 64], F32)
                rka, b_rka = SB(ph, "rka", [128, 32], F32)
                rkb, b_rkb = SB(ph, "rkb", [128, 32], F32)
                rope_k, b_ropek = SB(ph, "rope_k", [128, 64], BF16)
                sq4, b_sq4 = SB(ph, "sq4", [128, 128], F32)

                def mla_q(t):
                    sl = t % 2
                    tsl = slice(t * 128, (t + 1) * 128)
                    cs_t, b_cst = cst[sl]
                    rs3, b_rs3 = rs3s[sl]
                    rq2, b_rq2 = rq2s[sl]
                    mT, b_mT = mTs[sl]
                    pt_, bpt = pQ[0]
                    for g in range(2):
                        for c in range(2):
                            mm(pt_[:, 0:384], mT[:, 1 + c, :], wqu[:, c, g * 384:(g + 1) * 384], c == 0, c == 1, [b_mT, b_wqu], [bpt])
                        yield
                        cp("act", qhs[:, g * 384:(g + 1) * 384], pt_[:, 0:384], [bpt], [b_qhs])
                        for hh in range(2):
                            h = 2 * g + hh
                            base = hh * 192
                            act(sq3[:, 0:128], pt_[:, base:base + 128], AF.Square, [bpt], [b_sq3, b_ss8], accum=ss8[:, h:h + 1])
                            act(sq3[:, 0:64], pt_[:, base + 128:base + 192], AF.Square, [bpt], [b_sq3, b_ss8], accum=ss8[:, 4 + h:5 + h])
                        yield
                    ts("dve", ss8[:], ss8[:], rq2[:, 0:1], None, ALU.mult, None, [b_ss8, b_rq2], [b_ss8])
                    yield
                    act(fac8[:, 0:4], ss8[:, 0:4], AF.Ln, [b_ss8], [b_fac8], scale=1.0 / 128, bias=EPS)
                    act(fac8[:, 4:8], ss8[:, 4:8], AF.Ln, [b_ss8], [b_fac8], scale=1.0 / 64, bias=EPS)
                    yield
                    act(fac8[:], fac8[:], AF.Exp, [b_fac8], [b_fac8], scale=-0.5)
                    yield
                    ts("dve", fac8[:], fac8[:], rs3[:, 0:1], None, ALU.mult, None, [b_fac8, b_rs3], [b_fac8])
                    yield
                    for h in range(4):
                        base = h * 192
                        stt("dve", qn[:, h, :], qhs[:, base:base + 128], fac8[:, h:h + 1], qnn[:], ALU.mult, ALU.mult,
                            [b_qhs, b_fac8, b_qnn], [b_qn])
                        stt("dve", zall[:, h, :], qhs[:, base + 128:base + 192], fac8[:, 4 + h:5 + h], qnr[:], ALU.mult, ALU.mult,
                            [b_qhs, b_fac8, b_qnr], [b_zall])
                        if h % 2 == 1:
                            yield
                    for h in range(4):
                        tr(pM[:, h, :], qn[:, h, :], ident[:], [b_qn, b_ident], [b_pM])
                    yield
                    cp("dve", qT6[:, 0:4, :], pM[:, 0:4, :], [b_pM], [b_qT6])
                    yield
                    S.dma("sp", QT_d[:, :, tsl].rearrange("c p s -> p c s"), qT6[:, 0:4, :], reads=[b_qT6], writes=[b_QT], sem="qT_st")
                    for hh in range(4):
                        z1, z2 = zall[:, hh, 0:32], zall[:, hh, 32:64]
                        cth, sth = cs_t[:, 0, :], cs_t[:, 1, :]
                        tt("pool", ra[:, hh, :], z1, cth, ALU.mult, [b_zall, b_cst], [b_ra])
                        tt("pool", rb[:, hh, :], z2, sth, ALU.mult, [b_zall, b_cst], [b_rb])
                        tt("pool", rope_o[:, hh, 0:32], ra[:, hh, :], rb[:, hh, :], ALU.subtract, [b_ra, b_rb], [b_ropeo])
                        yield
                        tt("pool", ra[:, hh, :], z2, cth, ALU.mult, [b_zall, b_cst], [b_ra])
                        tt("pool", rb[:, hh, :], z1, sth, ALU.mult, [b_zall, b_cst], [b_rb])
                        tt("pool", rope_o[:, hh, 32:64], ra[:, hh, :], rb[:, hh, :], ALU.add, [b_ra, b_rb], [b_ropeo])
                        yield
                    for hp in range(2):
                        tr(pM[:, 4 + hp, :], rope_o[:, 2 * hp:2 * hp + 2, :].rearrange("p h r -> p (h r)"), ident[:],
                           [b_ropeo, b_ident], [b_pM])
                    yield
                    cp("dve", qT6[:, 4:6, :], pM[:, 4:6, :], [b_pM], [b_qT6])
                    yield
                    S.dma("sp", QPE_d[:, :, tsl].rearrange("c p s -> p c s"), qT6[:, 4:6, :], reads=[b_qT6], writes=[b_QPE], sem="qT_st2")

                def mla_kv(t):
                    sl = t % 2
                    tsl = slice(t * 128, (t + 1) * 128)
                    cs_t, b_cst = cst[sl]
                    kpr, b_kpr = kprs[sl]
                    rs3, b_rs3 = rs3s[sl]
                    rq2, b_rq2 = rq2s[sl]
                    mT, b_mT = mTs[sl]
                    pt_, bpt = pQ[1]
                    stt("dve", zk[:], kpr[:], rs3[:, 2:3], knr[:], ALU.mult, ALU.mult, [b_kpr, b_rs3, b_knr], [b_zk])
                    yield
                    for g in range(2):
                        mm(pt_[:], mT[:, 3, :], wkvu[:, g * 512:(g + 1) * 512], True, True, [b_mT, b_wkvu], [bpt])
                        yield
                        cp("act", kvs[:, g * 512:(g + 1) * 512], pt_[:], [bpt], [b_kvs])
                        for hh in range(2):
                            h = 2 * g + hh
                            act(sq4[:, 0:128], pt_[:, hh * 256:hh * 256 + 128], AF.Square, [bpt], [b_sq4, b_ssk], accum=ssk[:, h:h + 1])
                        yield
                    z1, z2 = zk[:, 0:32], zk[:, 32:64]
                    cth, sth = cs_t[:, 0, :], cs_t[:, 1, :]
                    tt("dve", rka[:], z1, cth, ALU.mult, [b_zk, b_cst], [b_rka])
                    tt("dve", rkb[:], z2, sth, ALU.mult, [b_zk, b_cst], [b_rkb])
                    tt("dve", rope_k[:, 0:32], rka[:], rkb[:], ALU.subtract, [b_rka, b_rkb], [b_ropek])
                    yield
                    tt("dve", rka[:], z2, cth, ALU.mult, [b_zk, b_cst], [b_rka])
                    tt("dve", rkb[:], z1, sth, ALU.mult, [b_zk, b_cst], [b_rkb])
                    tt("dve", rope_k[:, 32:64], rka[:], rkb[:], ALU.add, [b_rka, b_rkb], [b_ropek])
                    yield
                    ts("dve", ssk[:], ssk[:], rq2[:, 1:2], None, ALU.mult, None, [b_ssk, b_rq2], [b_ssk])
                    yield
                    act(fack[:], ssk[:], AF.Ln, [b_ssk], [b_fack], scale=1.0 / 128, bias=EPS)
                    yield
                    act(fack[:], fack[:], AF.Exp, [b_fack], [b_fack], scale=-0.5)
                    yield
                    ts("dve", fack[:], fack[:], rs3[:, 1:2], None, ALU.mult, None, [b_fack, b_rs3], [b_fack])
                    yield
                    for h in range(4):
                        base = h * 256
                        stt("dve", kn[:, h, :], kvs[:, base:base + 128], fack[:, h:h + 1], knn[:], ALU.mult, ALU.mult,
                            [b_kvs, b_fack, b_knn], [b_kn])
                        if h % 2 == 1:
                            yield
                    for h in range(4):
                        tr(pM[:, h, :], kn[:, h, :], ident[:], [b_kn, b_ident], [b_pM])
                    tr(pM[0:64, 4, :], rope_k[:], ident[:], [b_ropek, b_ident], [b_pM])
                    yield
                    cp("dve", kT5[:, 0:4, :], pM[:, 0:4, :], [b_pM], [b_kT5])
                    cp("dve", kT5[0:64, 4, :], pM[0:64, 4, :], [b_pM], [b_kT5])
                    yield
                    S.dma("sp", KT_d[:, :, tsl].rearrange("c p s -> p c s"), kT5[:, 0:4, :], reads=[b_kT5], writes=[b_KT], sem="kT_st")
                    S.dma("sp", KPE_d[:, tsl], kT5[0:64, 4, :], reads=[b_kT5], writes=[b_KPE], sem="kT_st2")
                    for h in range(4):
                        base = h * 256
                        ts("dve", vt[:, h, :], kvs[:, base + 128:base + 256], rs3[:, 1:2], None, ALU.mult, None, [b_kvs, b_rs3], [b_vt])
                        if h % 2 == 1:
                            yield
                    S.dma("sp", V_d[tsl, :], vt[:].rearrange("p h e -> p (h e)"), reads=[b_vt], writes=[b_V], sem="vt_st")

                if "p1" not in SKIP:
                    drive([prologue(0)])
                for t in range(NT if "p1" not in SKIP else 0):
                    gens = [gla_chain(t), mla_q(t), mla_kv(t)]
                    if t + 1 < NT:
                        gens.append(prologue(t + 1))
                    drive(gens)
            S.barrier()
            if stop and (stop.startswith('p1') or stop.startswith('q') or stop.startswith('c')):
                break

            ph23 = contextlib.ExitStack()
            phw = contextlib.ExitStack()
            uid[0] += 1
            wout = phw.enter_context(nc.sbuf_tensor("wout_s%d" % uid[0], [128, 8, D], BF16, side="right"))
            b_woutk = [S.buf("wout%d" % k) for k in range(8)]
            for k in range(8):
                S.dma("pool", wout[:, k, :], wout_in[l, k * 128:(k + 1) * 128, :], writes=[b_woutk[k]], sem="wout_ld", nobar=True)
            with contextlib.ExitStack() as ph:
                KTs = [SB(ph, "KTs%d" % i, [128, S_LEN], BF16) for i in range(2)]
                Vh = [SB(ph, "Vh%d" % i, [128, NT, 128], BF16) for i in range(2)]
                KPEs, b_KPEs = SB(ph, "KPEs", [64, S_LEN], BF16)
                qns = [SB(ph, "qns%d" % i, [128, 512], BF16) for i in range(2)]
                qps = [SB(ph, "qps%d" % i, [64, 512], BF16) for i in range(2)]
                pTs = [SB(ph, "pTs%d" % i, [128, 512], BF16) for i in range(4)]
                rl, b_rl = SB(ph, "rl", [128, 512], F32)
                ob = [SB(ph, "ob%d" % i, [128, 512], BF16) for i in range(2)]
                pS = [PS(ph, "pS%d" % i, [128, 512], F32) for i in range(4)]
                pO = [PS(ph, "pO%d" % i, [128, 512], F32) for i in range(2)]
                pL = [PS(ph, "pL%d" % i, [128, 512], F32) for i in range(2)]
                S.dma("sp", KPEs[:], KPE_d, reads=[b_KPE], writes=[b_KPEs])
                spl2 = split and last
                if spl2:
                    maskSel, b_maskSel = SB(ph, "maskSel", [128, 8, 512], BF16)
                    for r in range(4):
                        act(maskSel[:, r, :], maskD[:, r, :], AF.Identity, [b_maskD, b_par], [b_maskSel],
                            scale=par[:, 0:1], bias=par[:, 1:2])
                        act(maskSel[:, 4 + r, :], maskD[:, r, :], AF.Identity, [b_maskD, b_par], [b_maskSel], scale=par[:, 1:2])
                    qna = [SB(ph, "qna%d" % i, [128, 512], BF16) for i in range(2)]
                    qnb = [SB(ph, "qnb%d" % i, [128, 512], BF16) for i in range(2)]
                    qpa = [SB(ph, "qpa%d" % i, [64, 512], BF16) for i in range(2)]
                    qpb = [SB(ph, "qpb%d" % i, [64, 512], BF16) for i in range(2)]
                NPOS = NC4 // 2 if spl2 else NC4

                def nkb_of(c):
                    return 8 * c + 8 if spl2 else 4 * c + 4

                chunks = [(h, c) for h in range(4 if "p2" not in SKIP else 0) for c in range(NPOS)]
                blocks = []
                for idx, (h, c) in enumerate(chunks):
                    for kb in range(nkb_of(c)):
                        blocks.append((idx, kb, nkb_of(c)))

                def load_head(h):
                    KTh, b_KTh = KTs[h % 2]
                    Vhh, b_Vhh = Vh[h % 2]
                    S.dma("sp", KTh[:], KT_d[h], reads=[b_KT], writes=[b_KTh])
                    S.dma("sp", Vhh[:], V_d[:, h * 128:(h + 1) * 128].rearrange("(t p) e -> p t e", p=128), reads=[b_V], writes=[b_Vhh])

                def load_q(idx):
                    h, c = chunks[idx]
                    hp, off = h // 2, 64 * (h % 2)
                    if not spl2:
                        csl = slice(c * 512, (c + 1) * 512)
                        S.dma("sp", qns[idx % 2][0][:], QT_d[h, :, csl], reads=[b_QT], writes=[qns[idx % 2][1]])
                        S.dma("sp", qps[idx % 2][0][:], QPE_d[hp, off:off + 64, csl], reads=[b_QPE], writes=[qps[idx % 2][1]])
                        return
                    sla = slice(2 * c * 512, (2 * c + 1) * 512)
                    slb = slice((2 * c + 1) * 512, (2 * c + 2) * 512)
                    qa, b_qa = qna[idx % 2]
                    qb, b_qb = qnb[idx % 2]
                    pa, b_pa = qpa[idx % 2]
                    pb, b_pb = qpb[idx % 2]
                    qs, b_qs = qns[idx % 2]
                    ps_, b_ps = qps[idx % 2]
                    S.dma("sp", qa[:], QT_d[h, :, sla], reads=[b_QT], writes=[b_qa])
                    S.dma("sp", qb[:], QT_d[h, :, slb], reads=[b_QT], writes=[b_qb])
                    S.dma("sp", pa[:], QPE_d[hp, off:off + 64, sla], reads=[b_QPE], writes=[b_pa])
                    S.dma("sp", pb[:], QPE_d[hp, off:off + 64, slb], reads=[b_QPE], writes=[b_pb])
                    ts("dve", qs[:], qa[:], par[:, 0:1], None, ALU.mult, None, [b_qa, b_par], [b_qs])
                    stt("dve", qs[:], qb[:], par[:, 1:2], qs[:], ALU.mult, ALU.add, [b_qb, b_par, b_qs], [b_qs])
                    ts("dve", ps_[:], pa[:], par[0:64, 0:1], None, ALU.mult, None, [b_pa, b_par], [b_ps])
                    stt("dve", ps_[:], pb[:], par[0:64, 1:2], ps_[:], ALU.mult, ALU.add, [b_pb, b_par, b_ps], [b_ps])

                LA = 3
                if chunks:
                    load_head(0)
                    load_q(0)
                for i in range((len(blocks) + LA) if blocks else 0):
                    if i < len(blocks):
                        idx, kb, nkb = blocks[i]
                        h, c = chunks[idx]
                        if kb == 0 and idx + 1 < len(chunks):
                            load_q(idx + 1)
                        KTh, b_KTh = KTs[h % 2]
                        qn_, b_qn_ = qns[idx % 2]
                        qp_, b_qp_ = qps[idx % 2]
                        ksl = slice(kb * 128, (kb + 1) * 128)
                        pSt, b_pSt = pS[i % 4]
                        pTt, b_pTt = pTs[i % 4]
                        mm(pSt[:], KTh[:, ksl], qn_[:], True, False, [b_KTh, b_qn_], [b_pSt])
                        mm(pSt[:], KPEs[:, ksl], qp_[:], False, True, [b_KPEs, b_qp_], [b_pSt])
                        act(pTt[:], pSt[:], AF.Exp, [b_pSt], [b_pTt])
                        if spl2:
                            r = kb - 8 * c
                            if r >= 0:
                                tt("pool", pTt[:], pTt[:], maskSel[:, r, :], ALU.mult, [b_pTt, b_maskSel], [b_pTt])
                        else:
                            r = kb - 4 * c
                            if r >= 0:
                                tt("pool", pTt[:], pTt[:], maskD[:, r, :], ALU.mult, [b_pTt, b_maskD], [b_pTt])
                    if i >= LA:
                        j = i - LA
                        idx, kb, nkb = blocks[j]
                        h, c = chunks[idx]
                        Vhh, b_Vhh = Vh[h % 2]
                        pTt, b_pTt = pTs[j % 4]
                        pOt, b_pOt = pO[idx % 2]
                        pLt, b_pLt = pL[idx % 2]
                        mm(pOt[:], Vhh[:, kb, :], pTt[:], kb == 0, kb == nkb - 1, [b_Vhh, b_pTt], [b_pOt])
                        mm(pLt[:], ones_b[:], pTt[:], kb == 0, kb == nkb - 1, [b_onesb, b_pTt], [b_pLt])
                        if kb == 0 and c == 0 and h + 1 < 4:
                            load_head(h + 1)
                        if kb == nkb - 1:
                            obt, b_obt = ob[idx % 2]
                            csl = slice(c * 512, (c + 1) * 512)
                            recip(rl[:], pLt[:], [b_pLt], [b_rl])
                            tt("dve", obt[:], pOt[:], rl[:], ALU.mult, [b_pOt, b_rl], [b_obt])
                            S.dma("pool", mixT_d[4 + h, :, csl], obt[:], reads=[b_obt], writes=[b_mixT], sem="ob_st%d" % (idx % 2))
            S.barrier()
            if stop == 'p2':
                phw.close()
                ph23.close()
                break
            w1, _ = SB(ph23, "w1", [128, 8, DFF], BF16)
            w2, _ = SB(ph23, "w2", [128, 32, D], BF16)
            b_w1k = [S.buf("w1_%d" % k) for k in range(8)]
            b_w2f = [S.buf("w2_%d" % f) for f in range(32)]
            for k in range(8):
                for q in range(2):
                    S.dma("pool", w1[:, k, q * 2048:(q + 1) * 2048], w1_in[l, k * 128:(k + 1) * 128, q * 2048:(q + 1) * 2048],
                          writes=[b_w1k[k]], sem="w1_ld", nobar=True)
            for f in range(32):
                S.dma("pool", w2[:, f, :], w2_in[l, f * 128:(f + 1) * 128, :], writes=[b_w2f[f]], sem="w2_ld", nobar=True)

            with contextlib.ExitStack() as ph:
                gta, b_gta = SB(ph, "gta", [128, D], F32)
                S.dma("sp", gta[:], gate_d[l, 0], reads=[b_gate], writes=[b_gta])
                mx = [SB(ph, "mx%d" % i, [128, 8, 128], BF16) for i in range(2)]
                xt = [SB(ph, "xt%d" % i, [128, D], F32) for i in range(2)]
                xo = [SB(ph, "xo%d" % i, [128, D], F32) for i in range(2)]
                pW = [PS(ph, "pW%d" % i, [128, 512], F32) for i in range(4)]
                spl = split and last
                if spl:
                    mxb = [SB(ph, "mxb%d" % i, [128, 8, 128], BF16) for i in range(2)]
                    xtb = [SB(ph, "xtb%d" % i, [128, D], F32) for i in range(2)]
                    mxs = [SB(ph, "mxs%d" % i, [128, 8, 128], BF16) for i in range(2)]
                    xss = [SB(ph, "xss%d" % i, [128, D], F32) for i in range(2)]
                NT3 = (NT // 2 if spl else NT) if "p3a" not in SKIP else 0
                for t in range(NT3):
                    tsl = slice(t * 128, (t + 1) * 128)
                    tg = (8 * (t // 4) + t % 4) if spl else t
                    tsl0 = slice(tg * 128, (tg + 1) * 128)
                    tsl1 = slice((tg + 4) * 128, (tg + 5) * 128)
                    mxt, b_mx = mx[t % 2]
                    xtt, b_xt = xt[t % 2]
                    xot, b_xo = xo[t % 2]
                    S.dma("sp", mxt[:], mixT_d[:, :, tsl0].rearrange("c p s -> p c s"), reads=[b_mixT], writes=[b_mx])
                    S.dma("sp", xtt[:], x_cur[tsl0, :], reads=([b_xcur] if b_xcur else []), writes=[b_xt])
                    if spl:
                        mxbt, b_mxb = mxb[t % 2]
                        xtbt, b_xtb = xtb[t % 2]
                        mxst, b_mxs = mxs[t % 2]
                        xsst, b_xss = xss[t % 2]
                        S.dma("sp", mxbt[:], mixT_d[:, :, tsl1].rearrange("c p s -> p c s"), reads=[b_mixT], writes=[b_mxb])
                        S.dma("sp", xtbt[:], x_cur[tsl1, :], reads=([b_xcur] if b_xcur else []), writes=[b_xtb])
                        act(mxst[:, 0:4, :], mxt[:, 0:4, :], AF.Identity, [b_mx, b_par], [b_mxs], scale=par[:, 0:1])
                        stt("dve", mxst[:, 0:4, :], mxbt[:, 0:4, :], par[:, 1:2], mxst[:, 0:4, :], ALU.mult, ALU.add,
                            [b_mxb, b_par, b_mxs], [b_mxs])
                        S.dma("sp", mxst[:, 4:8, :], mixT_d[4:8, :, tsl].rearrange("c p s -> p c s"), reads=[b_mixT], writes=[b_mxs])
                        act(xsst[:], xtt[:], AF.Identity, [b_xt, b_par], [b_xss], scale=par[:, 0:1])
                        stt("dve", xsst[:], xtbt[:], par[:, 1:2], xsst[:], ALU.mult, ALU.add, [b_xtb, b_par, b_xss], [b_xss])
                        mxt, b_mx = mxst, b_mxs
                        xtt, b_xt = xsst, b_xss
                    for half in range(2):
                        pw, b_pw = pW[(t % 2) * 2 + half]
                        hs = slice(half * 512, (half + 1) * 512)
                        for c in range(8):
                            mm(pw[:], mxt[:, c, :], wout[:, c, hs], c == 0, c == 7, [b_mx, b_woutk[c]], [b_pw])
                        tt("dve", xot[:, hs], pw[:], gta[:, hs], ALU.mult, [b_pw, b_gta], [b_xo])
                        tt("pool", xot[:, hs], xot[:, hs], xtt[:, hs], ALU.add, [b_xo, b_xt], [b_xo])
                    S.dma("pool", xmid_d[tsl, :], xot[:], reads=[b_xo], writes=[b_xmid], sem="xo_st%d" % (t % 2))
            S.barrier()
            phw.close()
            if stop == 'p3a':
                ph23.close()
                break

            with contextlib.ExitStack() as ph:
                gtf, b_gtf = SB(ph, "gtf", [128, D], F32)
                S.dma("sp", gtf[:], gate_d[l, 1], reads=[b_gate], writes=[b_gtf])
                xm = [SB(ph, "xm%d" % i, [128, 2, D], F32) for i in range(2)]
                sq, b_sq = SB(ph, "sq", [128, D], F32)
                ss, b_ss = SB(ph, "ss", [128, 2], F32)
                xn2 = [SB(ph, "xn2_%d" % i, [128, 2, D], BF16) for i in range(2)]
                h2Ts = [SB(ph, "h2T%d" % i, [128, 8, 256], BF16) for i in range(2)]
                aT, b_aT = SB(ph, "aT", [128, 32, 256], BF16)
                rt = [SB(ph, "rt%d" % i, [128, 256], F32) for i in range(2)]
                yo = [SB(ph, "yo%d" % i, [128, D], F32) for i in range(2)]
                pT, b_pT = PS(ph, "pT", [128, 8, 128], BF16)
                pU = [PS(ph, "pU%d" % i, [128, 256], F32) for i in range(2)]
                pDn = [PS(ph, "pDn%d" % i, [128, 512], F32) for i in range(4)]

                def prep_norm(g):
                    xmt, b_xm = xm[g % 2]
                    xnt, b_xnt = xn2[g % 2]
                    for j in range(2):
                        t = 2 * g + j
                        S.dma("sp", xmt[:, j, :], xmid_d[t * 128:(t + 1) * 128, :], reads=[b_xmid], writes=[b_xm])
                    for j in range(2):
                        act(sq[:], xmt[:, j, :], AF.Square, [b_xm], [b_sq, b_ss], accum=ss[:, j:j + 1])
                    rsqrt_cols(ss[:], ss[:], 1.0 / D, [b_ss], [b_ss])
                    for j in range(2):
                        ts("dve", xnt[:, j, :], xmt[:, j, :], ss[:, j:j + 1], None, ALU.mult, None, [b_xm, b_ss], [b_xnt])

                def prep_tr(g):
                    xnt, b_xnt = xn2[g % 2]
                    h2T, b_h2T = h2Ts[g % 2]
                    for j in range(2):
                        for k in range(8):
                            tr(pT[:, k, :], xnt[:, j, k * 128:(k + 1) * 128], ident[:], [b_xnt, b_ident], [b_pT])
                        for k in range(8):
                            if k % 2 == 0:
                                ts("dve", h2T[:, k, j * 128:(j + 1) * 128], pT[:, k, :], modcol[:, mc + 24 + k: mc + 25 + k],
                                   modcol[:, mc + 16 + k: mc + 17 + k], ALU.mult, ALU.add, [b_pT, b_modcol], [b_h2T])
                            else:
                                act(h2T[:, k, j * 128:(j + 1) * 128], pT[:, k, :], AF.Identity, [b_pT, b_modcol], [b_h2T],
                                    scale=modcol[:, mc + 24 + k: mc + 25 + k], bias=modcol[:, mc + 16 + k: mc + 17 + k])

                yi = 0
                NGE = NG // 2 if (split and last) else NG
                prep_norm(0)
                prep_tr(0)
                for g in range(NGE):
                    xmt, b_xm = xm[g % 2]
                    h2T, b_h2T = h2Ts[g % 2]
                    if g + 1 < NGE:
                        prep_norm(g + 1)
                    for f in range(32):
                        pu, b_pu = pU[f % 2]
                        rtt, b_rt = rt[f % 2]
                        for k in range(8):
                            mm(pu[:], w1[:, k, f * 128:(f + 1) * 128], h2T[:, k, :], k == 0, k == 7, [b_w1k[k], b_h2T], [b_pu])
                        act(rtt[:], pu[:], AF.Relu, [b_pu], [b_rt])
                        tt("dve" if f % 2 == 0 else "pool", aT[:, f, :], rtt[:], rtt[:], ALU.mult, [b_rt], [b_aT])
                    if g + 1 < NGE:
                        prep_tr(g + 1)
                    for j in range(2):
                        t = 2 * g + j
                        tsl = slice(t * 128, (t + 1) * 128)
                        yot, b_yo = yo[yi % 2]
                        yi += 1
                        for half in range(2):
                            pd, b_pd = pDn[j * 2 + half]
                            hs = slice(half * 512, (half + 1) * 512)
                            for f in range(32):
                                mm(pd[:], aT[:, f, j * 128:(j + 1) * 128], w2[:, f, hs], f == 0, f == 31, [b_aT, b_w2f[f]], [b_pd])
                            tt("dve", yot[:, hs], pd[:], gtf[:, hs], ALU.mult, [b_pd, b_gtf], [b_yo])
                            tt("pool", yot[:, hs], yot[:, hs], xmt[:, j, hs], ALU.add, [b_yo, b_xm], [b_yo])
                        d = S.dma("pool", x_nxt[tsl, :], yot[:], reads=[b_yo], writes=[b_xnxt], sem="yo_st%d" % ((yi - 1) % 2))
                        if last:
                            out_dmas.append(d)
            S.barrier()
            ph23.close()

        S.emit(final_waits=out_dmas)
    return nc


def host_inputs(b, S_LEN, DEPTH, x, c, positions, _parity=0, *, w_ada, b_ada, w_in, w_gate_up, b_gate, gla_out_norm, q_a_norm,
                w_q_up, kv_a_norm, w_kv_up, q_norm_nope, k_norm_nope, q_norm_rope, k_norm_rope,
                w_out, w_mlp_up, w_mlp_down):
    f32 = np.float32
    NT = S_LEN // 128
    A = lambda a: np.ascontiguousarray(np.asarray(a))

    def bc(v, n=128):
        v = np.asarray(v, dtype=f32)
        return A(np.broadcast_to(v[:, None, :], (v.shape[0], n, v.shape[1])))

    inv_freq = (10000.0 ** (-np.arange(0, 64, 2, dtype=f32) / f32(64))).astype(f32)
    b_ada = np.asarray(b_ada, dtype=f32)
    d = {
        "x": A(np.asarray(x[b], dtype=f32)),
        "ccol": A(np.asarray(c[b], dtype=f32).reshape(8, 128).T),
        "pos": A(np.asarray(positions[b]).astype(np.int32).reshape(NT, 128).T),
        "invf": A(np.broadcast_to(inv_freq[None, :], (128, 32))),
        "parcol": A(np.broadcast_to(np.array([[1.0 - _parity, float(_parity)]], dtype=f32), (128, 2))),
        "w_ada": A(np.asarray(w_ada, dtype=f32)),
        "bada_col": A(b_ada.reshape(DEPTH, 48, 128).transpose(0, 2, 1)),
        "bada_gate": A(np.broadcast_to(b_ada.reshape(DEPTH, 6, 1, D)[:, [2, 5]], (DEPTH, 2, 128, D))),
        "w_in": A(np.asarray(w_in, dtype=f32)),
        "w_gate_up": A(np.asarray(w_gate_up, dtype=f32)),
        "bgate_bc": bc(b_gate),
        "gon_bc4": bc(np.tile(np.asarray(gla_out_norm, dtype=f32), (1, 4))),
        "qan_col": A(np.asarray(q_a_norm, dtype=f32).reshape(DEPTH, 2, 128).transpose(0, 2, 1)),
        "w_q_up": A(np.asarray(w_q_up, dtype=f32)),
        "kvan_col": A(np.asarray(kv_a_norm, dtype=f32).reshape(DEPTH, 128, 1)),
        "w_kv_up": A(np.asarray(w_kv_up, dtype=f32)),
        "qnn_bc": bc(q_norm_nope), "knn_bc": bc(k_norm_nope),
        "qnr_bc": bc(q_norm_rope), "knr_bc": bc(k_norm_rope),
        "w_out": A(np.asarray(w_out, dtype=f32)),
        "w_mlp_up": A(np.asarray(w_mlp_up, dtype=f32)),
        "w_mlp_down": A(np.asarray(w_mlp_down, dtype=f32)),
    }
    return d


_NC_CACHE = {}


def kernel(**inputs):
    x = np.asarray(inputs["x"])
    B, S_LEN, _ = x.shape
    DEPTH = np.asarray(inputs["w_ada"]).shape[0]
    key = (S_LEN, DEPTH)
    if key not in _NC_CACHE:
        _NC_CACHE[key] = build(S_LEN, DEPTH, split=True)
    nc = _NC_CACHE[key]
    in_maps = []
    for b in range(B):
        base = host_inputs(b, S_LEN, DEPTH, _parity=0, **inputs)
        in_maps.append(base)
        m1 = dict(base)
        m1["parcol"] = host_inputs_par(1)
        in_maps.append(m1)
    res = run_bass_kernel_spmd(nc, in_maps, core_ids=list(range(2 * B)))
    out = np.empty((B, S_LEN, D), dtype=np.float32)
    ov = out.reshape(B, S_LEN // 1024, 2, 512, D)
    for b in range(B):
        for p in range(2):
            ov[b, :, p] = np.asarray(res.results[2 * b + p]["y"], dtype=np.float32).reshape(S_LEN // 1024, 512, D)
    return out


def host_inputs_par(p):
    return np.ascontiguousarray(np.broadcast_to(np.array([[1.0 - p, float(p)]], dtype=np.float32), (128, 2)))
```

```python
import contextlib
import math
import numpy as np
import concourse.bass as bass
import concourse.mybir as mybir
from concourse.bass_utils import run_bass_kernel_spmd

F32 = mybir.dt.float32
BF16 = mybir.dt.bfloat16
I32 = mybir.dt.int32
ALU = mybir.AluOpType
AF = mybir.ActivationFunctionType

ENGS = ("pe", "act", "dve", "pool", "sp")
EST_DUR = {"pe": 0.15, "act": 0.3, "dve": 0.3, "pool": 0.5, "sp": 0.1}
import os as _os
SEM_SHARE = _os.environ.get("SEM_SHARE", "0") == "1"
SEQ_CHAINS = _os.environ.get("SEQ_CHAINS", "0") == "1"
NO_KPR = _os.environ.get("NO_KPR", "0") == "1"
SKIP = set(_os.environ.get("SKIP_PHASES", "").split(","))
OLD_ORDER = _os.environ.get("OLD_ORDER", "0") == "1"
SEM_MAP = {"wm0": "A0", "wm1": "A1", "xt0": "A0", "xt1": "A1", "cst0": "B0", "cst1": "B1",
           "KTs0": "A0", "KTs1": "A1", "Vh0": "B0", "Vh1": "B1", "qns0": "C0", "qns1": "C1",
           "qps0": "D0", "qps1": "D1", "mx0": "B0", "mx1": "B1", "xm0": "A0", "xm1": "A1", "xm2": "A2",
           "ob_st0": "ST0", "ob_st1": "ST1", "xo_st0": "ST0", "xo_st1": "ST1", "yo_st0": "ST0", "yo_st1": "ST1"}


class Buf:
    __slots__ = ("name", "lw", "rd", "rd_dma")

    def __init__(self, name):
        self.name = name
        self.lw = None
        self.rd = {}
        self.rd_dma = []


class Ins:
    __slots__ = ("eng", "fn", "deps", "signal", "cnt", "is_dma", "dsem", "dval", "tfin")

    def __init__(self, eng, fn, is_dma=False):
        self.eng = eng
        self.fn = fn
        self.deps = []
        self.signal = False
        self.cnt = 0
        self.is_dma = is_dma
        self.dsem = None
        self.dval = 0
        self.tfin = 0.0


class Sched:
    def __init__(self, nc):
        self.nc = nc
        self.q = {e: [] for e in ENGS}
        self.dma_sems = {}
        self.all_dma = []
        self.last_on_sem = {}
        self.bar = None
        self.bar_done = {}
        self.nbuf = 0
        self.eng_free = {e: 0.0 for e in ENGS}
        self.step_max = 0.0

    def buf(self, name=None):
        self.nbuf += 1
        return Buf(name or ("b%d" % self.nbuf))

    def _collect(self, ins, reads, writes):
        deps = {}
        pe = (ins.eng == "pe" and not ins.is_dma)

        def add(d):
            if d is None or d is ins:
                return
            if pe and d.eng == "pe" and not d.is_dma:
                return
            if d.is_dma:
                d = self.last_on_sem[d.dsem]
                if d is ins:
                    return
            deps[id(d)] = d

        for b in reads:
            add(b.lw)
        for b in writes:
            add(b.lw)
            for d in b.rd.values():
                add(d)
            for d in b.rd_dma:
                add(d)
        if self.bar is not None and not self.bar_done.get(ins.eng):
            for d in self.bar:
                add(d)
            self.bar_done[ins.eng] = True
        for b in reads:
            if ins.is_dma:
                b.rd_dma.append(ins)
            else:
                b.rd[ins.eng] = ins
        for b in writes:
            b.lw = ins
            b.rd = {}
            b.rd_dma = []
        ins.deps = list(deps.values())
        ready = max([d.tfin for d in ins.deps], default=0.0) + (0.25 if ins.deps else 0.0)
        start = max(ready, self.eng_free[ins.eng])
        if ins.is_dma:
            self.eng_free[ins.eng] = start + 0.1
            ins.tfin = start + 2.5
        else:
            ins.tfin = start + EST_DUR[ins.eng]
            self.eng_free[ins.eng] = ins.tfin
        if ins.tfin > self.step_max:
            self.step_max = ins.tfin

    def op(self, eng, fn, reads=(), writes=()):
        ins = Ins(eng, fn)
        self._collect(ins, reads, writes)
        self.q[eng].append(ins)
        return ins

    def dma(self, eng, out, in_, reads=(), writes=(), sem=None, nobar=False):
        if sem is None:
            sem = (list(writes) + list(reads))[0].name
        if SEM_SHARE:
            sem = SEM_MAP.get(sem, "G_const" if not sem.endswith("_st") else "ST")
        ins = Ins(eng, lambda e: e.dma_start(out=out, in_=in_), is_dma=True)
        tot = self.dma_sems.get(sem, 0) + 16
        self.dma_sems[sem] = tot
        ins.dsem = sem
        ins.dval = tot
        self._collect(ins, reads, writes)
        self.last_on_sem[sem] = ins
        self.q[eng].append(ins)
        if not nobar:
            self.all_dma.append(ins)
        return ins

    def barrier(self):
        deps = []
        for e in ENGS:
            for ins in reversed(self.q[e]):
                if not ins.is_dma:
                    deps.append(ins)
                    break
        deps.extend(self.all_dma)
        self.all_dma = []
        last = {}
        keep = []
        for d in deps:
            if d.is_dma:
                if d.dsem not in last or last[d.dsem].dval < d.dval:
                    last[d.dsem] = d
            else:
                keep.append(d)
        self.bar = keep + list(last.values())
        self.bar_done = {}

    def emit(self, final_waits=()):
        nc = self.nc
        for e in ENGS:
            for ins in self.q[e]:
                for d in ins.deps:
                    if not d.is_dma:
                        d.signal = True
        for e in ENGS:
            c = 0
            for ins in self.q[e]:
                if ins.signal and not ins.is_dma:
                    c += 1
                ins.cnt = c
        with contextlib.ExitStack() as st:
            esem = {e: st.enter_context(nc.semaphore("es_" + e)) for e in ENGS}
            dsem = {k: st.enter_context(nc.semaphore("ds_%d" % i)) for i, k in enumerate(self.dma_sems)}
            block = st.enter_context(nc.Block())
            sched = self

            def run(e, eh):
                waited = {}
                for ins in sched.q[e]:
                    need = {}
                    for d in ins.deps:
                        if d.is_dma:
                            s, v, key = dsem[d.dsem], d.dval, ("d", d.dsem)
                        else:
                            s, v, key = esem[d.eng], d.cnt, ("e", d.eng)
                        if key not in need or need[key][1] < v:
                            need[key] = (s, v)
                    for key, (s, v) in need.items():
                        if waited.get(key, 0) >= v:
                            continue
                        waited[key] = v
                        eh.wait_ge(s, v)
                    bi = ins.fn(eh)
                    if ins.is_dma:
                        bi.then_inc(dsem[ins.dsem], 16)
                    elif ins.signal:
                        bi.then_inc(esem[e], 1)
                if e == "sp":
                    for d in final_waits:
                        eh.wait_ge(dsem[d.dsem], d.dval)

            @block.tensor
            def _(eh):
                run("pe", eh)

            @block.scalar
            def _(eh):
                run("act", eh)

            @block.vector
            def _(eh):
                run("dve", eh)

            @block.gpsimd
            def _(eh):
                run("pool", eh)

            @block.sync
            def _(eh):
                run("sp", eh)


D = 1024
DFF = 4096
EPS = 1e-6
TWO_PI = 2.0 * math.pi
C1 = 6.28125
C2 = TWO_PI - C1


def build(S_LEN, DEPTH, dbg=False, stop=None, split=False):
    NT = S_LEN // 128
    NC4 = S_LEN // 512
    NG = S_LEN // 256
    nc = bass.Bass("TRN2", target_bir_lowering=False)
    S = Sched(nc)

    def din(name, shape, dt=F32):
        return nc.dram_tensor(name, shape, dt, kind="ExternalInput").ap()

    def dscr(name, shape, dt):
        return nc.dram_tensor(name, shape, dt, kind=("ExternalOutput" if dbg else "Internal")).ap()

    x_in = din("x", [S_LEN, D])
    ccol = din("ccol", [128, 8])
    pos_in = din("pos", [128, NT], I32)
    invf_in = din("invf", [128, 32])
    par_in = din("parcol", [128, 2])
    w_ada = din("w_ada", [DEPTH, D, 6 * D])
    bada_col = din("bada_col", [DEPTH, 128, 48])
    bada_gate = din("bada_gate", [DEPTH, 2, 128, D])
    w_in = din("w_in", [DEPTH, D, 2000])
    wgu_in = din("w_gate_up", [DEPTH, 16, 256])
    bgate_in = din("bgate_bc", [DEPTH, 128, 256])
    gon_in = din("gon_bc4", [DEPTH, 128, 512])
    qan_in = din("qan_col", [DEPTH, 128, 2])
    wqu_in = din("w_q_up", [DEPTH, 256, 768])
    kvan_in = din("kvan_col", [DEPTH, 128, 1])
    wkvu_in = din("w_kv_up", [DEPTH, 128, 1024])
    qnn_in = din("qnn_bc", [DEPTH, 128, 128])
    knn_in = din("knn_bc", [DEPTH, 128, 128])
    qnr_in = din("qnr_bc", [DEPTH, 128, 64])
    knr_in = din("knr_bc", [DEPTH, 128, 64])
    wout_in = din("w_out", [DEPTH, D, D])
    w1_in = din("w_mlp_up", [DEPTH, D, DFF])
    w2_in = din("w_mlp_down", [DEPTH, DFF, D])
    y_out = nc.dram_tensor("y", [S_LEN // 2 if split else S_LEN, D], F32, kind="ExternalOutput").ap()

    xs = [dscr("xs%d" % i, [S_LEN, D], F32) for i in range(2)]
    xmid_d = dscr("xmid", [S_LEN, D], F32)
    mixT_d = dscr("mixT", [8, 128, S_LEN], BF16)
    KT_d = dscr("KT", [4, 128, S_LEN], BF16)
    KPE_d = dscr("KPE", [64, S_LEN], BF16)
    V_d = dscr("Vd", [S_LEN, 512], BF16)
    QT_d = dscr("QT", [4, 128, S_LEN], BF16)
    QPE_d = dscr("QPE", [2, 128, S_LEN], BF16)
    cos_d = dscr("cosd", [128, NT, 32], F32)
    sin_d = dscr("sind", [128, NT, 32], F32)
    gate_d = dscr("gated", [DEPTH, 2, 128, D], F32)
    b_xs = [S.buf("xs0"), S.buf("xs1")]
    b_xmid = S.buf("xmid")
    b_mixT = S.buf("mixT")
    b_KT, b_KPE, b_V, b_QT, b_QPE = S.buf("KT"), S.buf("KPE"), S.buf("V"), S.buf("QT"), S.buf("QPE")
    b_cos, b_sin, b_gate = S.buf("cosd"), S.buf("sind"), S.buf("gated")
    b_y = S.buf("y")
    out_dmas = []

    def mm(out, lhsT, rhs, start, stop, R, W):
        S.op("pe", lambda e: e.matmul(out=out, lhsT=lhsT, rhs=rhs, start=start, stop=stop), R, W)

    def tr(out, in_, ident, R, W):
        S.op("pe", lambda e: e.transpose(out=out, in_=in_, identity=ident), R, W)

    def act(out, in_, func, R, W, scale=1.0, bias=0.0, accum=None):
        if accum is None:
            S.op("act", lambda e: e.activation(out=out, in_=in_, func=func, bias=bias, scale=scale), R, W)
        else:
            S.op("act", lambda e: e.activation(out=out, in_=in_, func=func, bias=bias, scale=scale, accum_out=accum), R, W)

    def tt(eng, out, in0, in1, op, R, W):
        S.op(eng, lambda e: e.tensor_tensor(out=out, in0=in0, in1=in1, op=op), R, W)

    def ts(eng, out, in0, s1, s2, op0, op1, R, W):
        if s2 is None:
            S.op(eng, lambda e: e.tensor_scalar(out=out, in0=in0, scalar1=s1, scalar2=None, op0=op0), R, W)
        else:
            S.op(eng, lambda e: e.tensor_scalar(out=out, in0=in0, scalar1=s1, scalar2=s2, op0=op0, op1=op1), R, W)

    def stt(eng, out, in0, scalar, in1, op0, op1, R, W, accum=None):
        if accum is None:
            S.op(eng, lambda e: e.scalar_tensor_tensor(out=out, in0=in0, scalar=scalar, in1=in1, op0=op0, op1=op1), R, W)
        else:
            S.op(eng, lambda e: e.scalar_tensor_tensor(out=out, in0=in0, scalar=scalar, in1=in1, op0=op0, op1=op1, accum_out=accum), R, W)

    def cp(eng, out, in_, R, W):
        if eng == "act":
            S.op("act", lambda e: e.copy(out=out, in_=in_), R, W)
        else:
            S.op(eng, lambda e: e.tensor_copy(out=out, in_=in_), R, W)

    def recip(out, in_, R, W):
        S.op("dve", lambda e: e.reciprocal(out=out, in_=in_), R, W)

    def memset(eng, ap, val, W):
        S.op(eng, lambda e: e.memset(ap, val), (), W)

    def asel(out, in_, pattern, cmp, fill, base, cm, R, W):
        S.op("pool", lambda e: e.affine_select(out=out, in_=in_, pattern=pattern, compare_op=cmp, fill=fill,
                                               base=base, channel_multiplier=cm), R, W)

    def rsqrt_cols(dst, src, scale, R, W):
        act(dst, src, AF.Ln, R, W, scale=scale, bias=EPS)
        act(dst, dst, AF.Exp, W, W, scale=-0.5)

    with contextlib.ExitStack() as top:
        uid = [0]

        def SB(stack, name, shape, dt):
            uid[0] += 1
            t = stack.enter_context(nc.sbuf_tensor("%s_s%d" % (name, uid[0]), shape, dt))
            return t, S.buf(name)

        def PS(stack, name, shape, dt):
            uid[0] += 1
            t = stack.enter_context(nc.psum_tensor("%s_p%d" % (name, uid[0]), shape, dt))
            return t, S.buf(name)

        ident, b_ident = SB(top, "ident", [128, 128], BF16)
        maskU4, b_maskU4 = SB(top, "maskU4", [128, 4, 128], BF16)
        ones_f, b_onesf = SB(top, "ones_f", [128, 128], F32)
        ones_b, b_onesb = SB(top, "ones_b", [128, 128], BF16)
        maskD, b_maskD = SB(top, "maskD", [128, 4, 512], BF16)
        modcol, b_modcol = SB(top, "modcol", [128, DEPTH * 32], F32)

        par, b_par = SB(top, "par", [128, 2], F32)
        S.dma("sp", par[:], par_in, writes=[b_par])
        memset("pool", ident[:], 0.0, [b_ident])
        asel(ident[:], ident[:], [[-1, 128]], ALU.not_equal, 1.0, 0, 1, [b_ident], [b_ident])
        memset("pool", maskU4[:], 1.0, [b_maskU4])
        for h in range(4):
            asel(maskU4[:, h, :], maskU4[:, h, :], [[1, 128]], ALU.is_ge, 0.0, 0, -1, [b_maskU4], [b_maskU4])
        memset("pool", ones_f[:], 1.0, [b_onesf])
        memset("pool", ones_b[:], 1.0, [b_onesb])
        memset("pool", maskD[:], 1.0, [b_maskD])
        for r in range(4):
            asel(maskD[:, r, :], maskD[:, r, :], [[1, 512]], ALU.is_ge, 0.0, -128 * r, -1, [b_maskD], [b_maskD])

        with contextlib.ExitStack() as ph:
            posi, b_posi = SB(ph, "posi", [128, NT], I32)
            posf, b_posf = SB(ph, "posf", [128, NT], F32)
            invf, b_invf = SB(ph, "invf", [128, 32], F32)
            ang, b_ang = SB(ph, "ang", [128, NT, 32], F32)
            uu, b_uu = SB(ph, "uu", [128, NT, 32], F32)
            ni, b_ni = SB(ph, "ni", [128, NT, 32], I32)
            nf, b_nf = SB(ph, "nf", [128, NT, 32], F32)
            mk, b_mk = SB(ph, "mk", [128, NT, 32], F32)
            sn, b_sn = SB(ph, "sn", [128, NT, 32], F32)
            cs, b_cs = SB(ph, "cs", [128, NT, 32], F32)
            S.dma("sp", posi[:], pos_in, writes=[b_posi])
            S.dma("sp", invf[:], invf_in, writes=[b_invf])
            cp("dve", posf[:], posi[:], [b_posi], [b_posf])
            for t in range(NT):
                ts("dve", ang[:, t, :], invf[:], posf[:, t:t + 1], None, ALU.mult, None, [b_invf, b_posf], [b_ang])
            ts("dve", uu[:], ang[:], 1.0 / TWO_PI, None, ALU.mult, None, [b_ang], [b_uu])
            cp("dve", ni[:], uu[:], [b_uu], [b_ni])
            cp("dve", nf[:], ni[:], [b_ni], [b_nf])
            stt("dve", ang[:], nf[:], -C1, ang[:], ALU.mult, ALU.add, [b_nf, b_ang], [b_ang])
            stt("dve", ang[:], nf[:], -C2, ang[:], ALU.mult, ALU.add, [b_nf, b_ang], [b_ang])
            ts("dve", mk[:], ang[:], math.pi, None, ALU.is_gt, None, [b_ang], [b_mk])
            stt("dve", ang[:], mk[:], -TWO_PI, ang[:], ALU.mult, ALU.add, [b_mk, b_ang], [b_ang])
            ts("dve", mk[:], ang[:], -math.pi, None, ALU.is_lt, None, [b_ang], [b_mk])
            stt("dve", ang[:], mk[:], TWO_PI, ang[:], ALU.mult, ALU.add, [b_mk, b_ang], [b_ang])
            ts("dve", ang[:], ang[:], math.pi, -math.pi, ALU.min, ALU.max, [b_ang], [b_ang])
            act(sn[:], ang[:], AF.Sin, [b_ang], [b_sn])
            stt("dve", uu[:], ang[:], -1.0, ang[:], ALU.mult, ALU.max, [b_ang], [b_uu])
            ts("dve", uu[:], uu[:], -1.0, math.pi / 2, ALU.mult, ALU.add, [b_uu], [b_uu])
            act(cs[:], uu[:], AF.Sin, [b_uu], [b_cs])
            S.dma("sp", cos_d, cs[:], reads=[b_cs], writes=[b_cos], sem="cs_st")
            S.dma("sp", sin_d, sn[:], reads=[b_sn], writes=[b_sin], sem="sn_st")

            if stop != 'rope':
                cc, b_cc = SB(ph, "cc", [128, 8], F32)
                ce, b_ce = SB(ph, "ce", [128, 8], F32)
                cond, b_cond = SB(ph, "cond", [128, 8], F32)
                cond_rep, b_crep = SB(ph, "cond_rep", [128, 8, 128], BF16)
                condb, b_condb = SB(ph, "condb", [128, 16], BF16)
                wm = [SB(ph, "wm%d" % i, [128, 8, D], BF16) for i in range(2)]
                bcol, b_bcol = SB(ph, "bcol", [128, DEPTH * 48], F32)
                bgt, b_bgt = SB(ph, "bgt", [128, D], F32)
                gsb, b_gsb = SB(ph, "gsb", [128, D], F32)
                pg = [PS(ph, "pg%d" % i, [128, 512], F32) for i in range(2)]
                pc, b_pc = PS(ph, "pc", [128, 8], F32)
                S.dma("sp", cc[:], ccol, writes=[b_cc])
                for l in range(DEPTH):
                    S.dma("sp", bcol[:, l * 48:(l + 1) * 48], bada_col[l], writes=[b_bcol])
                act(ce[:], cc[:], AF.Exp, [b_cc], [b_ce], scale=-1.0)
                ts("dve", ce[:], ce[:], 1.0, None, ALU.add, None, [b_ce], [b_ce])
                recip(ce[:], ce[:], [b_ce], [b_ce])
                tt("dve", cond[:], cc[:], ce[:], ALU.mult, [b_cc, b_ce], [b_cond])
                memset("dve", condb[:], 0.0, [b_condb])
                cp("dve", condb[:, 0:8], cond[:], [b_cond, b_condb], [b_condb])
                for k in range(8):
                    ts("dve", cond_rep[:, k, :], ones_f[:], cond[:, k:k + 1], None, ALU.mult, None, [b_onesf, b_cond], [b_crep])
                li = 0
                for l in range(DEPTH):
                    for m in range(6):
                        wt, b_wt = wm[li % 2]
                        li += 1
                        S.dma("pool", wt[:], w_ada[l, :, m * D:(m + 1) * D].rearrange("(k p) n -> p k n", p=128), writes=[b_wt])
                        if m in (2, 5):
                            gi = 0 if m == 2 else 1
                            S.dma("sp", bgt[:], bada_gate[l, gi], writes=[b_bgt])
                            for half in range(2):
                                pgt, b_pg = pg[half]
                                for k in range(8):
                                    mm(pgt[:], cond_rep[:, k, :], wt[:, k, half * 512:(half + 1) * 512], k == 0, k == 7,
                                       [b_crep, b_wt], [b_pg])
                                tt("dve", gsb[:, half * 512:(half + 1) * 512], pgt[:], bgt[:, half * 512:(half + 1) * 512],
                                   ALU.add, [b_pg, b_bgt], [b_gsb])
                            S.dma("sp", gate_d[l, gi], gsb[:], reads=[b_gsb], writes=[b_gate], sem="gate_st")
                        else:
                            mi = {0: 0, 1: 1, 3: 2, 4: 3}[m]
                            for ko in range(8):
                                for k in range(8):
                                    mm(pc[:, ko:ko + 1], wt[:, k, ko * 128:(ko + 1) * 128], condb[:, k:k + 1], k == 0, k == 7,
                                       [b_wt, b_condb], [b_pc])
                            dst = modcol[:, l * 32 + mi * 8: l * 32 + mi * 8 + 8]
                            tt("dve", dst, pc[:], bcol[:, l * 48 + m * 8: l * 48 + m * 8 + 8], ALU.add, [b_pc, b_bcol], [b_modcol])
                            if m in (1, 4):
                                ts("dve", dst, dst, 1.0, None, ALU.add, None, [b_modcol], [b_modcol])
        S.barrier()

        for l in range(DEPTH if stop not in ('setup', 'rope') else 0):
            x_cur = x_in if l == 0 else xs[(l - 1) % 2]
            b_xcur = None if l == 0 else b_xs[(l - 1) % 2]
            last = (l == DEPTH - 1)
            x_nxt = y_out if last else xs[l % 2]
            b_xnxt = b_y if last else b_xs[l % 2]
            mc = l * 32

            with contextlib.ExitStack() as ph:
                win, b_win = SB(ph, "win", [128, 8, 2000], BF16)
                wgu, b_wgu = SB(ph, "wgu", [16, 256], BF16)
                wqu, b_wqu = SB(ph, "wqu", [128, 2, 768], BF16)
                wkvu, b_wkvu = SB(ph, "wkvu", [128, 1024], BF16)
                qan, b_qan = SB(ph, "qan", [128, 2], F32)
                kvan, b_kvan = SB(ph, "kvan", [128, 1], F32)
                bgate, b_bgate = SB(ph, "bgate", [128, 256], F32)
                gon4, b_gon4 = SB(ph, "gon4", [128, 512], F32)
                qnn, b_qnn = SB(ph, "qnn", [128, 128], F32)
                knn, b_knn = SB(ph, "knn", [128, 128], F32)
                qnr, b_qnr = SB(ph, "qnr", [128, 64], F32)
                knr, b_knr = SB(ph, "knr", [128, 64], F32)
                b_wink = [S.buf("win%d" % k) for k in range(8)]
                for k in range(8):
                    S.dma("pool", win[:, k, :], w_in[l, k * 128:(k + 1) * 128, :], writes=[b_wink[k]], sem="win_ld")
                S.dma("pool", wgu[:], wgu_in[l], writes=[b_wgu])
                S.dma("pool", wqu[:], wqu_in[l].rearrange("(k p) n -> p k n", p=128), writes=[b_wqu])
                S.dma("pool", wkvu[:], wkvu_in[l], writes=[b_wkvu])
                S.dma("sp", qan[:], qan_in[l], writes=[b_qan])
                S.dma("sp", kvan[:], kvan_in[l], writes=[b_kvan])
                S.dma("sp", bgate[:], bgate_in[l], writes=[b_bgate])
                S.dma("sp", gon4[:], gon_in[l], writes=[b_gon4])
                S.dma("sp", qnn[:], qnn_in[l], writes=[b_qnn])
                S.dma("sp", knn[:], knn_in[l], writes=[b_knn])
                S.dma("sp", qnr[:], qnr_in[l], writes=[b_qnr])
                S.dma("sp", knr[:], knr_in[l], writes=[b_knr])
                for c in range(2):
                    ts("dve", wqu[:, c, :], wqu[:, c, :], qan[:, c:c + 1], None, ALU.mult, None, [b_wqu, b_qan], [b_wqu])
                ts("dve", wkvu[:], wkvu[:], kvan[:, 0:1], None, ALU.mult, None, [b_wkvu, b_kvan], [b_wkvu])
                qsc = 192.0 ** -0.5
                ts("dve", qnn[:], qnn[:], qsc, None, ALU.mult, None, [b_qnn], [b_qnn])
                ts("dve", qnr[:], qnr[:], qsc, None, ALU.mult, None, [b_qnr], [b_qnr])

                xt = [SB(ph, "xt%d" % i, [128, D], F32) for i in range(2)]
                cst = [SB(ph, "cst%d" % i, [128, 2, 32], F32) for i in range(2)]
                sq, b_sq = SB(ph, "sq", [128, D], F32)
                ss, b_ss = SB(ph, "ss", [128, 1], F32)
                xn, b_xn = SB(ph, "xn", [128, D], BF16)
                hT, b_hT = SB(ph, "hT", [128, 8, 128], BF16)
                mD, b_mD = SB(ph, "mD", [128, 400], BF16)
                ss3, b_ss3 = SB(ph, "ss3", [128, 3], F32)
                rs3, b_rs3 = SB(ph, "rs3", [128, 3], F32)
                rq2, b_rq2 = SB(ph, "rq2", [128, 3], F32)
                mT, b_mT = SB(ph, "mT", [128, 4, 128], BF16)
                pre, b_pre = SB(ph, "pre", [128, 256], F32)
                lg, b_lg = SB(ph, "lg", [128, 256], F32)
                lgh, b_lgh = SB(ph, "lgh", [128, 256], BF16)
                lgl, b_lgl = SB(ph, "lgl", [128, 256], BF16)
                eb, b_eb = SB(ph, "eb", [128, 256], F32)
                enb, b_enb = SB(ph, "enb", [128, 256], F32)
                ebl, b_ebl = SB(ph, "ebl", [64, 4], F32)
                qg, b_qg = SB(ph, "qg", [128, 256], BF16)
                kg, b_kg = SB(ph, "kg", [128, 256], BF16)
                qkT, b_qkT = SB(ph, "qkT", [64, 8, 128], BF16)
                vsb, b_vsb = SB(ph, "vsb", [128, 512], BF16)
                eo, b_eo = SB(ph, "eo", [128, 512], F32)
                gog, b_gog = SB(ph, "gog", [128, 512], F32)
                ATs, b_ATs = SB(ph, "ATs", [128, 4, 128], BF16)
                stt_, b_st = SB(ph, "gst", [64, 4, 128], F32)
                stb, b_stb = SB(ph, "gstb", [64, 4, 128], BF16)
                sso, b_sso = SB(ph, "sso", [128, 4], F32)
                go, b_go = SB(ph, "go", [128, 512], BF16)
                gT, b_gT = SB(ph, "gT", [128, 4, 128], BF16)
                ss8, b_ss8 = SB(ph, "ss8", [128, 8], F32)
                fac8, b_fac8 = SB(ph, "fac8", [128, 8], F32)
                qn, b_qn = SB(ph, "qn", [128, 4, 128], BF16)
                zall, b_zall = SB(ph, "zall", [128, 5, 64], F32)
                ra, b_ra = SB(ph, "ra", [128, 5, 32], F32)
                rb, b_rb = SB(ph, "rb", [128, 5, 32], F32)
                rope_o, b_ropeo = SB(ph, "rope_o", [128, 5, 64], BF16)
                qT6, b_qT6 = SB(ph, "qT6", [128, 6, 128], BF16)
                kn, b_kn = SB(ph, "kn", [128, 4, 128], BF16)
                vt, b_vt = SB(ph, "vt", [128, 4, 128], BF16)
                kT5, b_kT5 = SB(ph, "kT5", [128, 5, 128], BF16)
                ssk, b_ssk = SB(ph, "ssk", [128, 4], F32)
                fack, b_fack = SB(ph, "fack", [128, 4], F32)

                pT, b_pT = PS(ph, "pT", [128, 8, 128], BF16)
                pA, b_pA = PS(ph, "pA", [128, 512], F32)
                pB, b_pB = PS(ph, "pB", [128, 512], F32)
                pC, b_pC = PS(ph, "pC", [128, 512], F32)
                pD, b_pD = PS(ph, "pD", [128, 512], F32)
                pM, _ = PS(ph, "pM", [128, 8, 128], BF16)
                b_pMa = b_pMb = S.buf("pM")
                pX, b_pX = PS(ph, "pX", [128, 512], F32)
                pY, b_pY = PS(ph, "pY", [128, 512], F32)
                pX4 = pX[:].rearrange("p (h e) -> p h e", e=128)
                pY4 = pY[:].rearrange("p (h e) -> p h e", e=128)

                memset("dve", stt_[:], 0.0, [b_st])
                memset("dve", stb[:], 0.0, [b_stb])

                sq2, b_sq2 = SB(ph, "sq2", [128, 128], F32)
                sq3, b_sq3 = SB(ph, "sq3", [128, 128], F32)
                b_pM = b_pMa
                kprs = [SB(ph, "kpr%d" % i, [128, 64], F32) for i in range(2)]
                rs3s = [SB(ph, "rs3_%d" % i, [128, 3], F32) for i in range(2)]
                rq2s = [SB(ph, "rq2_%d" % i, [128, 3], F32) for i in range(2)]
                mTs = [SB(ph, "mT%d" % i, [128, 4, 128], BF16) for i in range(2)]
                gqks = [SB(ph, "gqk%d" % i, [128, 512], F32) for i in range(2)]
                vsbs = [SB(ph, "vsb%d" % i, [128, 512], BF16) for i in range(2)]
                gogs = [SB(ph, "gog%d" % i, [128, 512], F32) for i in range(2)]
                pW = [(pA, b_pA), (pB, b_pB)]
                pQ = [(pC, b_pC), (pD, b_pD)]

                def drive(gens):
                    alive = list(gens)
                    while alive:
                        for g in list(alive):
                            try:
                                next(g)
                            except StopIteration:
                                alive.remove(g)

                def prologue(t):
                    sl = t % 2
                    tsl = slice(t * 128, (t + 1) * 128)
                    xtt, b_xt = xt[sl]
                    cs_t, b_cst = cst[sl]
                    kpr, b_kpr = kprs[sl]
                    rs3, b_rs3 = rs3s[sl]
                    rq2, b_rq2 = rq2s[sl]
                    mT, b_mT = mTs[sl]
                    gqk, b_gqk = gqks[sl]
                    vsb, b_vsb = vsbs[sl]
                    gog, b_gog = gogs[sl]
                    S.dma("sp", xtt[:], x_cur[tsl, :], reads=([b_xcur] if b_xcur else []), writes=[b_xt])
                    S.dma("sp", cs_t[:, 0, :], cos_d[:, t, :], reads=[b_cos], writes=[b_cst])
                    S.dma("sp", cs_t[:, 1, :], sin_d[:, t, :], reads=[b_sin], writes=[b_cst])
                    yield
                    act(sq[:], xtt[:], AF.Square, [b_xt], [b_sq, b_ss], accum=ss[:, 0:1])
                    yield
                    rsqrt_cols(ss[:, 0:1], ss[:, 0:1], 1.0 / D, [b_ss], [b_ss])
                    yield
                    ts("dve", xn[:], xtt[:], ss[:, 0:1], None, ALU.mult, None, [b_xt, b_ss], [b_xn])
                    yield
                    for k in range(8):
                        tr(pT[:, k, :], xn[:, k * 128:(k + 1) * 128], ident[:], [b_xn, b_ident], [b_pT])
                    yield
                    for k in range(8):
                        if k % 2 == 0:
                            ts("dve", hT[:, k, :], pT[:, k, :], modcol[:, mc + 8 + k: mc + 9 + k], modcol[:, mc + k: mc + k + 1],
                               ALU.mult, ALU.add, [b_pT, b_modcol], [b_hT])
                        else:
                            act(hT[:, k, :], pT[:, k, :], AF.Identity, [b_pT, b_modcol], [b_hT],
                                scale=modcol[:, mc + 8 + k: mc + 9 + k], bias=modcol[:, mc + k: mc + k + 1])
                        if k % 4 == 3:
                            yield
                    blks = ((1536, 2000), (0, 512), (512, 1024), (1024, 1536))
                    for bi, (c0, c1) in enumerate(blks):
                        pw, b_pw = pW[bi % 2]
                        for k in range(8):
                            mm(pw[:, 0:c1 - c0], hT[:, k, :], win[:, k, c0:c1], k == 0, k == 7, [b_hT, b_wink[k]], [b_pw])
                        yield
                        if bi == 0:
                            cp("act", mD[:], pw[:, 0:400], [b_pw], [b_mD])
                            cp("act", kpr[:], pw[:, 400:464], [b_pw], [b_kpr])
                            yield
                            act(sq[:, 0:256], pw[:, 16:272], AF.Square, [b_pw], [b_sq, b_ss3], accum=ss3[:, 0:1])
                            act(sq[:, 0:128], pw[:, 272:400], AF.Square, [b_pw], [b_sq, b_ss3], accum=ss3[:, 1:2])
                            act(sq[:, 0:64], pw[:, 400:464], AF.Square, [b_pw], [b_sq, b_ss3], accum=ss3[:, 2:3])
                            yield
                        elif bi == 1:
                            cp("act", gqk[:], pw[:], [b_pw], [b_gqk])
                            yield
                        elif bi == 2:
                            cp("act", vsb[:], pw[:], [b_pw], [b_vsb])
                            yield
                        else:
                            act(eo[:], pw[:], AF.Exp, [b_pw], [b_eo], scale=-1.0)
                            yield
                            act(eo[:], eo[:], AF.Ln, [b_eo], [b_eo], bias=1.0)
                            yield
                            act(eo[:], eo[:], AF.Exp, [b_eo], [b_eo], scale=-1.0)
                            yield
                            tt("dve", gog[:], pw[:], eo[:], ALU.mult, [b_pw, b_eo], [b_gog])
                            yield
                            tt("pool", gog[:], gog[:], gon4[:], ALU.mult, [b_gog, b_gon4], [b_gog])
                            yield
                    act(rs3[:, 0:1], ss3[:, 0:1], AF.Ln, [b_ss3], [b_rs3], scale=1.0 / 256, bias=EPS)
                    act(rs3[:, 1:2], ss3[:, 1:2], AF.Ln, [b_ss3], [b_rs3], scale=1.0 / 128, bias=EPS)
                    act(rs3[:, 2:3], ss3[:, 2:3], AF.Ln, [b_ss3], [b_rs3], scale=1.0 / 64, bias=EPS)
                    yield
                    act(rs3[:], rs3[:], AF.Exp, [b_rs3], [b_rs3], scale=-0.5)
                    yield
                    tt("dve", rq2[:], rs3[:], rs3[:], ALU.mult, [b_rs3], [b_rq2])
                    tr(pT[0:16, 0, :], mD[:, 0:16], ident[:], [b_mD, b_ident], [b_pT])
                    tr(pT[:, 1, :], mD[:, 16:144], ident[:], [b_mD, b_ident], [b_pT])
                    tr(pT[:, 2, :], mD[:, 144:272], ident[:], [b_mD, b_ident], [b_pT])
                    tr(pT[:, 3, :], mD[:, 272:400], ident[:], [b_mD, b_ident], [b_pT])
                    yield
                    cp("dve", mT[0:16, 0, :], pT[0:16, 0, :], [b_pT], [b_mT])
                    cp("dve", mT[:, 1:4, :], pT[:, 1:4, :], [b_pT], [b_mT])
                    yield

                def gla_chain(t):
                    sl = t % 2
                    tsl = slice(t * 128, (t + 1) * 128)
                    mT, b_mT = mTs[sl]
                    gqk, b_gqk = gqks[sl]
                    vsb, b_vsb = vsbs[sl]
                    gog, b_gog = gogs[sl]
                    mm(pX[:, 0:256], mT[0:16, 0, :], wgu[:], True, True, [b_mT, b_wgu], [b_pX])
                    tt("dve", pre[:], pX[:, 0:256], bgate[:], ALU.add, [b_pX, b_bgate], [b_pre])
                    yield
                    act(pre[:], pre[:], AF.Exp, [b_pre], [b_pre], scale=-1.0)
                    yield
                    act(lg[:], pre[:], AF.Ln, [b_pre], [b_lg], bias=1.0)
                    yield
                    cp("dve", lgh[:], lg[:], [b_lg], [b_lgh])
                    yield
                    tt("dve", lgl[:], lg[:], lgh[:], ALU.subtract, [b_lg, b_lgh], [b_lgl])
                    yield
                    mm(pX[:, 256:512], maskU4[:, 0, :], lgh[:], True, False, [b_maskU4, b_lgh], [b_pX])
                    mm(pX[:, 256:512], maskU4[:, 0, :], lgl[:], False, True, [b_maskU4, b_lgl], [b_pX])
                    for h in range(4):
                        mm(pY[0:64, h:h + 1], lgh[:, h * 64:(h + 1) * 64], ones_b[:, 0:1], True, False,
                           [b_lgh, b_onesb], [b_pY])
                        mm(pY[0:64, h:h + 1], lgl[:, h * 64:(h + 1) * 64], ones_b[:, 0:1], False, True,
                           [b_lgl, b_onesb], [b_pY])
                    yield
                    act(eb[:], pX[:, 256:512], AF.Exp, [b_pX], [b_eb], scale=-1.0 / 16)
                    act(enb[:], pX[:, 256:512], AF.Exp, [b_pX], [b_enb], scale=1.0 / 16)
                    act(ebl[:], pY[0:64, 0:4], AF.Exp, [b_pY], [b_ebl], scale=-1.0 / 16)
                    yield
                    stt("dve", qg[:], gqk[:, 0:256], 0.125, eb[:], ALU.mult, ALU.mult, [b_gqk, b_eb], [b_qg])
                    tt("dve", kg[:], gqk[:, 256:512], enb[:], ALU.mult, [b_gqk, b_enb], [b_kg])
                    yield
                    for h in range(4):
                        tr(pM[0:64, h, :], qg[:, h * 64:(h + 1) * 64], ident[:], [b_qg, b_ident], [b_pM])
                        tr(pM[0:64, 4 + h, :], kg[:, h * 64:(h + 1) * 64], ident[:], [b_kg, b_ident], [b_pM])
                    yield
                    cp("dve", qkT[:], pM[0:64, :, :], [b_pM], [b_qkT])
                    yield
                    for h in range(4):
                        mm(pY4[:, h, :], qkT[:, 4 + h, :], qkT[:, h, :], True, True, [b_qkT], [b_pY])
                    yield
                    tt("dve", ATs[:], pY4, maskU4[:], ALU.mult, [b_pY, b_maskU4], [b_ATs])
                    yield
                    for h in range(4):
                        mm(pX4[:, h, :], ATs[:, h, :], vsb[:, h * 128:(h + 1) * 128], True, False, [b_ATs, b_vsb], [b_pX])
                        mm(pX4[:, h, :], qkT[:, h, :], stb[:, h, :], False, True, [b_qkT, b_stb], [b_pX])
                    for h in range(4):
                        mm(pY4[0:64, h, :], kg[:, h * 64:(h + 1) * 64], vsb[:, h * 128:(h + 1) * 128], True, True,
                           [b_kg, b_vsb], [b_pY])
                    yield
                    for h in range(4):
                        ts("dve", stt_[:, h, :], stt_[:, h, :], ebl[:, h:h + 1], None, ALU.mult, None, [b_st, b_ebl], [b_st])
                        stt("dve", stt_[:, h, :], pY4[0:64, h, :], ebl[:, h:h + 1], stt_[:, h, :], ALU.mult, ALU.add,
                            [b_pY, b_ebl, b_st], [b_st])
                        if h == 1:
                            yield
                    cp("dve", stb[:], stt_[:], [b_st], [b_stb])
                    yield
                    for h in range(4):
                        act(sq2[:], pX4[:, h, :], AF.Square, [b_pX], [b_sq2, b_sso], accum=sso[:, h:h + 1])
                    yield
                    act(sso[:], sso[:], AF.Ln, [b_sso], [b_sso], scale=1.0 / 128, bias=EPS)
                    yield
                    act(sso[:], sso[:], AF.Exp, [b_sso], [b_sso], scale=-0.5)
                    yield
                    for h in range(4):
                        stt("dve", go[:, h * 128:(h + 1) * 128], pX4[:, h, :], sso[:, h:h + 1], gog[:, h * 128:(h + 1) * 128],
                            ALU.mult, ALU.mult, [b_pX, b_sso, b_gog], [b_go])
                        if h == 1:
                            yield
                    yield
                    for h in range(4):
                        tr(pM[:, h, :], go[:, h * 128:(h + 1) * 128], ident[:], [b_go, b_ident], [b_pM])
                    yield
                    cp("dve", gT[:], pM[:, 0:4, :], [b_pM], [b_gT])
                    yield
                    S.dma("sp", mixT_d[0:4, :, tsl].rearrange("c p s -> p c s"), gT[:], reads=[b_gT], writes=[b_mixT], sem="gT_st")

                qhs, b_qhs = SB(ph, "qhs", [128, 768], F32)
                kvs, b_kvs = SB(ph, "kvs", [128, 1024], F32)
                zk, b_zk = SB(ph, "zk", [128, 64], F32)
                rka, b_rka = SB(ph, "rka", [128, 32], F32)
                rkb, b_rkb = SB(ph, "rkb", [128, 32], F32)
                rope_k, b_ropek = SB(ph, "rope_k", [128, 64], BF16)
                sq4, b_sq4 = SB(ph, "sq4", [128, 128], F32)

                def mla_q(t):
                    sl = t % 2
                    tsl = slice(t * 128, (t + 1) * 128)
                    cs_t, b_cst = cst[sl]
                    rs3, b_rs3 = rs3s[sl]
                    rq2, b_rq2 = rq2s[sl]
                    mT, b_mT = mTs[sl]
                    pt_, bpt = pQ[0]
                    for g in range(2):
                        for c in range(2):
                            mm(pt_[:, 0:384], mT[:, 1 + c, :], wqu[:, c, g * 384:(g + 1) * 384], c == 0, c == 1, [b_mT, b_wqu], [bpt])
                        yield
                        cp("act", qhs[:, g * 384:(g + 1) * 384], pt_[:, 0:384], [bpt], [b_qhs])
                        for hh in range(2):
                            h = 2 * g + hh
                            base = hh * 192
                            act(sq3[:, 0:128], pt_[:, base:base + 128], AF.Square, [bpt], [b_sq3, b_ss8], accum=ss8[:, h:h + 1])
                            act(sq3[:, 0:64], pt_[:, base + 128:base + 192], AF.Square, [bpt], [b_sq3, b_ss8], accum=ss8[:, 4 + h:5 + h])
                        yield
                    ts("dve", ss8[:], ss8[:], rq2[:, 0:1], None, ALU.mult, None, [b_ss8, b_rq2], [b_ss8])
                    yield
                    act(fac8[:, 0:4], ss8[:, 0:4], AF.Ln, [b_ss8], [b_fac8], scale=1.0 / 128, bias=EPS)
                    act(fac8[:, 4:8], ss8[:, 4:8], AF.Ln, [b_ss8], [b_fac8], scale=1.0 / 64, bias=EPS)
                    yield
                    act(fac8[:], fac8[:], AF.Exp, [b_fac8], [b_fac8], scale=-0.5)
                    yield
                    ts("dve", fac8[:], fac8[:], rs3[:, 0:1], None, ALU.mult, None, [b_fac8, b_rs3], [b_fac8])
                    yield
                    for h in range(4):
                        base = h * 192
                        stt("dve", qn[:, h, :], qhs[:, base:base + 128], fac8[:, h:h + 1], qnn[:], ALU.mult, ALU.mult,
                            [b_qhs, b_fac8, b_qnn], [b_qn])
                        stt("dve", zall[:, h, :], qhs[:, base + 128:base + 192], fac8[:, 4 + h:5 + h], qnr[:], ALU.mult, ALU.mult,
                            [b_qhs, b_fac8, b_qnr], [b_zall])
                        if h % 2 == 1:
                            yield
                    for h in range(4):
                        tr(pM[:, h, :], qn[:, h, :], ident[:], [b_qn, b_ident], [b_pM])
                    yield
                    cp("dve", qT6[:, 0:4, :], pM[:, 0:4, :], [b_pM], [b_qT6])
                    yield
                    S.dma("sp", QT_d[:, :, tsl].rearrange("c p s -> p c s"), qT6[:, 0:4, :], reads=[b_qT6], writes=[b_QT], sem="qT_st")
                    for hh in range(4):
                        z1, z2 = zall[:, hh, 0:32], zall[:, hh, 32:64]
                        cth, sth = cs_t[:, 0, :], cs_t[:, 1, :]
                        tt("pool", ra[:, hh, :], z1, cth, ALU.mult, [b_zall, b_cst], [b_ra])
                        tt("pool", rb[:, hh, :], z2, sth, ALU.mult, [b_zall, b_cst], [b_rb])
                        tt("pool", rope_o[:, hh, 0:32], ra[:, hh, :], rb[:, hh, :], ALU.subtract, [b_ra, b_rb], [b_ropeo])
                        yield
                        tt("pool", ra[:, hh, :], z2, cth, ALU.mult, [b_zall, b_cst], [b_ra])
                        tt("pool", rb[:, hh, :], z1, sth, ALU.mult, [b_zall, b_cst], [b_rb])
                        tt("pool", rope_o[:, hh, 32:64], ra[:, hh, :], rb[:, hh, :], ALU.add, [b_ra, b_rb], [b_ropeo])
                        yield
                    for hp in range(2):
                        tr(pM[:, 4 + hp, :], rope_o[:, 2 * hp:2 * hp + 2, :].rearrange("p h r -> p (h r)"), ident[:],
                           [b_ropeo, b_ident], [b_pM])
                    yield
                    cp("dve", qT6[:, 4:6, :], pM[:, 4:6, :], [b_pM], [b_qT6])
                    yield
                    S.dma("sp", QPE_d[:, :, tsl].rearrange("c p s -> p c s"), qT6[:, 4:6, :], reads=[b_qT6], writes=[b_QPE], sem="qT_st2")

                def mla_kv(t):
                    sl = t % 2
                    tsl = slice(t * 128, (t + 1) * 128)
                    cs_t, b_cst = cst[sl]
                    kpr, b_kpr = kprs[sl]
                    rs3, b_rs3 = rs3s[sl]
                    rq2, b_rq2 = rq2s[sl]
                    mT, b_mT = mTs[sl]
                    pt_, bpt = pQ[1]
                    stt("dve", zk[:], kpr[:], rs3[:, 2:3], knr[:], ALU.mult, ALU.mult, [b_kpr, b_rs3, b_knr], [b_zk])
                    yield
                    for g in range(2):
                        mm(pt_[:], mT[:, 3, :], wkvu[:, g * 512:(g + 1) * 512], True, True, [b_mT, b_wkvu], [bpt])
                        yield
                        cp("act", kvs[:, g * 512:(g + 1) * 512], pt_[:], [bpt], [b_kvs])
                        for hh in range(2):
                            h = 2 * g + hh
                            act(sq4[:, 0:128], pt_[:, hh * 256:hh * 256 + 128], AF.Square, [bpt], [b_sq4, b_ssk], accum=ssk[:, h:h + 1])
                        yield
                    z1, z2 = zk[:, 0:32], zk[:, 32:64]
                    cth, sth = cs_t[:, 0, :], cs_t[:, 1, :]
                    tt("dve", rka[:], z1, cth, ALU.mult, [b_zk, b_cst], [b_rka])
                    tt("dve", rkb[:], z2, sth, ALU.mult, [b_zk, b_cst], [b_rkb])
                    tt("dve", rope_k[:, 0:32], rka[:], rkb[:], ALU.subtract, [b_rka, b_rkb], [b_ropek])
                    yield
                    tt("dve", rka[:], z2, cth, ALU.mult, [b_zk, b_cst], [b_rka])
                    tt("dve", rkb[:], z1, sth, ALU.mult, [b_zk, b_cst], [b_rkb])
                    tt("dve", rope_k[:, 32:64], rka[:], rkb[:], ALU.add, [b_rka, b_rkb], [b_ropek])
                    yield
                    ts("dve", ssk[:], ssk[:], rq2[:, 1:2], None, ALU.mult, None, [b_ssk, b_rq2], [b_ssk])
                    yield
                    act(fack[:], ssk[:], AF.Ln, [b_ssk], [b_fack], scale=1.0 / 128, bias=EPS)
                    yield
                    act(fack[:], fack[:], AF.Exp, [b_fack], [b_fack], scale=-0.5)
                    yield
                    ts("dve", fack[:], fack[:], rs3[:, 1:2], None, ALU.mult, None, [b_fack, b_rs3], [b_fack])
                    yield
                    for h in range(4):
                        base = h * 256
                        stt("dve", kn[:, h, :], kvs[:, base:base + 128], fack[:, h:h + 1], knn[:], ALU.mult, ALU.mult,
                            [b_kvs, b_fack, b_knn], [b_kn])
                        if h % 2 == 1:
                            yield
                    for h in range(4):
                        tr(pM[:, h, :], kn[:, h, :], ident[:], [b_kn, b_ident], [b_pM])
                    tr(pM[0:64, 4, :], rope_k[:], ident[:], [b_ropek, b_ident], [b_pM])
                    yield
                    cp("dve", kT5[:, 0:4, :], pM[:, 0:4, :], [b_pM], [b_kT5])
                    cp("dve", kT5[0:64, 4, :], pM[0:64, 4, :], [b_pM], [b_kT5])
                    yield
                    S.dma("sp", KT_d[:, :, tsl].rearrange("c p s -> p c s"), kT5[:, 0:4, :], reads=[b_kT5], writes=[b_KT], sem="kT_st")
                    S.dma("sp", KPE_d[:, tsl], kT5[0:64, 4, :], reads=[b_kT5], writes=[b_KPE], sem="kT_st2")
                    for h in range(4):
                        base = h * 256
                        ts("dve", vt[:, h, :], kvs[:, base + 128:base + 256], rs3[:, 1:2], None, ALU.mult, None, [b_kvs, b_rs3], [b_vt])
                        if h % 2 == 1:
                            yield
                    S.dma("sp", V_d[tsl, :], vt[:].rearrange("p h e -> p (h e)"), reads=[b_vt], writes=[b_V], sem="vt_st")

                if "p1" not in SKIP:
                    drive([prologue(0)])
                for t in range(NT if "p1" not in SKIP else 0):
                    gens = [gla_chain(t), mla_q(t), mla_kv(t)]
                    if t + 1 < NT:
                        gens.append(prologue(t + 1))
                    drive(gens)
            S.barrier()
            if stop and (stop.startswith('p1') or stop.startswith('q') or stop.startswith('c')):
                break

            ph23 = contextlib.ExitStack()
            phw = contextlib.ExitStack()
            uid[0] += 1
            wout = phw.enter_context(nc.sbuf_tensor("wout_s%d" % uid[0], [128, 8, D], BF16, side="right"))
            b_woutk = [S.buf("wout%d" % k) for k in range(8)]
            for k in range(8):
                S.dma("pool", wout[:, k, :], wout_in[l, k * 128:(k + 1) * 128, :], writes=[b_woutk[k]], sem="wout_ld", nobar=True)
            with contextlib.ExitStack() as ph:
                KTs = [SB(ph, "KTs%d" % i, [128, S_LEN], BF16) for i in range(2)]
                Vh = [SB(ph, "Vh%d" % i, [128, NT, 128], BF16) for i in range(2)]
                KPEs, b_KPEs = SB(ph, "KPEs", [64, S_LEN], BF16)
                qns = [SB(ph, "qns%d" % i, [128, 512], BF16) for i in range(2)]
                qps = [SB(ph, "qps%d" % i, [64, 512], BF16) for i in range(2)]
                pTs = [SB(ph, "pTs%d" % i, [128, 512], BF16) for i in range(4)]
                rl, b_rl = SB(ph, "rl", [128, 512], F32)
                ob = [SB(ph, "ob%d" % i, [128, 512], BF16) for i in range(2)]
                pS = [PS(ph, "pS%d" % i, [128, 512], F32) for i in range(4)]
                pO = [PS(ph, "pO%d" % i, [128, 512], F32) for i in range(2)]
                pL = [PS(ph, "pL%d" % i, [128, 512], F32) for i in range(2)]
                S.dma("sp", KPEs[:], KPE_d, reads=[b_KPE], writes=[b_KPEs])
                spl2 = split and last
                if spl2:
                    maskSel, b_maskSel = SB(ph, "maskSel", [128, 8, 512], BF16)
                    for r in range(4):
                        act(maskSel[:, r, :], maskD[:, r, :], AF.Identity, [b_maskD, b_par], [b_maskSel],
                            scale=par[:, 0:1], bias=par[:, 1:2])
                        act(maskSel[:, 4 + r, :], maskD[:, r, :], AF.Identity, [b_maskD, b_par], [b_maskSel], scale=par[:, 1:2])
                    qna = [SB(ph, "qna%d" % i, [128, 512], BF16) for i in range(2)]
                    qnb = [SB(ph, "qnb%d" % i, [128, 512], BF16) for i in range(2)]
                    qpa = [SB(ph, "qpa%d" % i, [64, 512], BF16) for i in range(2)]
                    qpb = [SB(ph, "qpb%d" % i, [64, 512], BF16) for i in range(2)]
                NPOS = NC4 // 2 if spl2 else NC4

                def nkb_of(c):
                    return 8 * c + 8 if spl2 else 4 * c + 4

                chunks = [(h, c) for h in range(4 if "p2" not in SKIP else 0) for c in range(NPOS)]
                blocks = []
                for idx, (h, c) in enumerate(chunks):
                    for kb in range(nkb_of(c)):
                        blocks.append((idx, kb, nkb_of(c)))

                def load_head(h):
                    KTh, b_KTh = KTs[h % 2]
                    Vhh, b_Vhh = Vh[h % 2]
                    S.dma("sp", KTh[:], KT_d[h], reads=[b_KT], writes=[b_KTh])
                    S.dma("sp", Vhh[:], V_d[:, h * 128:(h + 1) * 128].rearrange("(t p) e -> p t e", p=128), reads=[b_V], writes=[b_Vhh])

                def load_q(idx):
                    h, c = chunks[idx]
                    hp, off = h // 2, 64 * (h % 2)
                    if not spl2:
                        csl = slice(c * 512, (c + 1) * 512)
                        S.dma("sp", qns[idx % 2][0][:], QT_d[h, :, csl], reads=[b_QT], writes=[qns[idx % 2][1]])
                        S.dma("sp", qps[idx % 2][0][:], QPE_d[hp, off:off + 64, csl], reads=[b_QPE], writes=[qps[idx % 2][1]])
                        return
                    sla = slice(2 * c * 512, (2 * c + 1) * 512)
                    slb = slice((2 * c + 1) * 512, (2 * c + 2) * 512)
                    qa, b_qa = qna[idx % 2]
                    qb, b_qb = qnb[idx % 2]
                    pa, b_pa = qpa[idx % 2]
                    pb, b_pb = qpb[idx % 2]
                    qs, b_qs = qns[idx % 2]
                    ps_, b_ps = qps[idx % 2]
                    S.dma("sp", qa[:], QT_d[h, :, sla], reads=[b_QT], writes=[b_qa])
                    S.dma("sp", qb[:], QT_d[h, :, slb], reads=[b_QT], writes=[b_qb])
                    S.dma("sp", pa[:], QPE_d[hp, off:off + 64, sla], reads=[b_QPE], writes=[b_pa])
                    S.dma("sp", pb[:], QPE_d[hp, off:off + 64, slb], reads=[b_QPE], writes=[b_pb])
                    ts("dve", qs[:], qa[:], par[:, 0:1], None, ALU.mult, None, [b_qa, b_par], [b_qs])
                    stt("dve", qs[:], qb[:], par[:, 1:2], qs[:], ALU.mult, ALU.add, [b_qb, b_par, b_qs], [b_qs])
                    ts("dve", ps_[:], pa[:], par[0:64, 0:1], None, ALU.mult, None, [b_pa, b_par], [b_ps])
                    stt("dve", ps_[:], pb[:], par[0:64, 1:2], ps_[:], ALU.mult, ALU.add, [b_pb, b_par, b_ps], [b_ps])

                LA = 3
                if chunks:
                    load_head(0)
                    load_q(0)
                for i in range((len(blocks) + LA) if blocks else 0):
                    if i < len(blocks):
                        idx, kb, nkb = blocks[i]
                        h, c = chunks[idx]
                        if kb == 0 and idx + 1 < len(chunks):
                            load_q(idx + 1)
                        KTh, b_KTh = KTs[h % 2]
                        qn_, b_qn_ = qns[idx % 2]
                        qp_, b_qp_ = qps[idx % 2]
                        ksl = slice(kb * 128, (kb + 1) * 128)
                        pSt, b_pSt = pS[i % 4]
                        pTt, b_pTt = pTs[i % 4]
                        mm(pSt[:], KTh[:, ksl], qn_[:], True, False, [b_KTh, b_qn_], [b_pSt])
                        mm(pSt[:], KPEs[:, ksl], qp_[:], False, True, [b_KPEs, b_qp_], [b_pSt])
                        act(pTt[:], pSt[:], AF.Exp, [b_pSt], [b_pTt])
                        if spl2:
                            r = kb - 8 * c
                            if r >= 0:
                                tt("pool", pTt[:], pTt[:], maskSel[:, r, :], ALU.mult, [b_pTt, b_maskSel], [b_pTt])
                        else:
                            r = kb - 4 * c
                            if r >= 0:
                                tt("pool", pTt[:], pTt[:], maskD[:, r, :], ALU.mult, [b_pTt, b_maskD], [b_pTt])
                    if i >= LA:
                        j = i - LA
                        idx, kb, nkb = blocks[j]
                        h, c = chunks[idx]
                        Vhh, b_Vhh = Vh[h % 2]
                        pTt, b_pTt = pTs[j % 4]
                        pOt, b_pOt = pO[idx % 2]
                        pLt, b_pLt = pL[idx % 2]
                        mm(pOt[:], Vhh[:, kb, :], pTt[:], kb == 0, kb == nkb - 1, [b_Vhh, b_pTt], [b_pOt])
                        mm(pLt[:], ones_b[:], pTt[:], kb == 0, kb == nkb - 1, [b_onesb, b_pTt], [b_pLt])
                        if kb == 0 and c == 0 and h + 1 < 4:
                            load_head(h + 1)
                        if kb == nkb - 1:
                            obt, b_obt = ob[idx % 2]
                            csl = slice(c * 512, (c + 1) * 512)
                            recip(rl[:], pLt[:], [b_pLt], [b_rl])
                            tt("dve", obt[:], pOt[:], rl[:], ALU.mult, [b_pOt, b_rl], [b_obt])
                            S.dma("pool", mixT_d[4 + h, :, csl], obt[:], reads=[b_obt], writes=[b_mixT], sem="ob_st%d" % (idx % 2))
            S.barrier()
            if stop == 'p2':
                phw.close()
                ph23.close()
                break
            w1, _ = SB(ph23, "w1", [128, 8, DFF], BF16)
            w2, _ = SB(ph23, "w2", [128, 32, D], BF16)
            b_w1k = [S.buf("w1_%d" % k) for k in range(8)]
            b_w2f = [S.buf("w2_%d" % f) for f in range(32)]
            for k in range(8):
                for q in range(2):
                    S.dma("pool", w1[:, k, q * 2048:(q + 1) * 2048], w1_in[l, k * 128:(k + 1) * 128, q * 2048:(q + 1) * 2048],
                          writes=[b_w1k[k]], sem="w1_ld", nobar=True)
            for f in range(32):
                S.dma("pool", w2[:, f, :], w2_in[l, f * 128:(f + 1) * 128, :], writes=[b_w2f[f]], sem="w2_ld", nobar=True)

            with contextlib.ExitStack() as ph:
                gta, b_gta = SB(ph, "gta", [128, D], F32)
                S.dma("sp", gta[:], gate_d[l, 0], reads=[b_gate], writes=[b_gta])
                mx = [SB(ph, "mx%d" % i, [128, 8, 128], BF16) for i in range(2)]
                xt = [SB(ph, "xt%d" % i, [128, D], F32) for i in range(2)]
                xo = [SB(ph, "xo%d" % i, [128, D], F32) for i in range(2)]
                pW = [PS(ph, "pW%d" % i, [128, 512], F32) for i in range(4)]
                spl = split and last
                if spl:
                    mxb = [SB(ph, "mxb%d" % i, [128, 8, 128], BF16) for i in range(2)]
                    xtb = [SB(ph, "xtb%d" % i, [128, D], F32) for i in range(2)]
                    mxs = [SB(ph, "mxs%d" % i, [128, 8, 128], BF16) for i in range(2)]
                    xss = [SB(ph, "xss%d" % i, [128, D], F32) for i in range(2)]
                NT3 = (NT // 2 if spl else NT) if "p3a" not in SKIP else 0
                for t in range(NT3):
                    tsl = slice(t * 128, (t + 1) * 128)
                    tg = (8 * (t // 4) + t % 4) if spl else t
                    tsl0 = slice(tg * 128, (tg + 1) * 128)
                    tsl1 = slice((tg + 4) * 128, (tg + 5) * 128)
                    mxt, b_mx = mx[t % 2]
                    xtt, b_xt = xt[t % 2]
                    xot, b_xo = xo[t % 2]
                    S.dma("sp", mxt[:], mixT_d[:, :, tsl0].rearrange("c p s -> p c s"), reads=[b_mixT], writes=[b_mx])
                    S.dma("sp", xtt[:], x_cur[tsl0, :], reads=([b_xcur] if b_xcur else []), writes=[b_xt])
                    if spl:
                        mxbt, b_mxb = mxb[t % 2]
                        xtbt, b_xtb = xtb[t % 2]
                        mxst, b_mxs = mxs[t % 2]
                        xsst, b_xss = xss[t % 2]
                        S.dma("sp", mxbt[:], mixT_d[:, :, tsl1].rearrange("c p s -> p c s"), reads=[b_mixT], writes=[b_mxb])
                        S.dma("sp", xtbt[:], x_cur[tsl1, :], reads=([b_xcur] if b_xcur else []), writes=[b_xtb])
                        act(mxst[:, 0:4, :], mxt[:, 0:4, :], AF.Identity, [b_mx, b_par], [b_mxs], scale=par[:, 0:1])
                        stt("dve", mxst[:, 0:4, :], mxbt[:, 0:4, :], par[:, 1:2], mxst[:, 0:4, :], ALU.mult, ALU.add,
                            [b_mxb, b_par, b_mxs], [b_mxs])
                        S.dma("sp", mxst[:, 4:8, :], mixT_d[4:8, :, tsl].rearrange("c p s -> p c s"), reads=[b_mixT], writes=[b_mxs])
                        act(xsst[:], xtt[:], AF.Identity, [b_xt, b_par], [b_xss], scale=par[:, 0:1])
                        stt("dve", xsst[:], xtbt[:], par[:, 1:2], xsst[:], ALU.mult, ALU.add, [b_xtb, b_par, b_xss], [b_xss])
                        mxt, b_mx = mxst, b_mxs
                        xtt, b_xt = xsst, b_xss
                    for half in range(2):
                        pw, b_pw = pW[(t % 2) * 2 + half]
                        hs = slice(half * 512, (half + 1) * 512)
                        for c in range(8):
                            mm(pw[:], mxt[:, c, :], wout[:, c, hs], c == 0, c == 7, [b_mx, b_woutk[c]], [b_pw])
                        tt("dve", xot[:, hs], pw[:], gta[:, hs], ALU.mult, [b_pw, b_gta], [b_xo])
                        tt("pool", xot[:, hs], xot[:, hs], xtt[:, hs], ALU.add, [b_xo, b_xt], [b_xo])
                    S.dma("pool", xmid_d[tsl, :], xot[:], reads=[b_xo], writes=[b_xmid], sem="xo_st%d" % (t % 2))
            S.barrier()
            phw.close()
            if stop == 'p3a':
                ph23.close()
                break

            with contextlib.ExitStack() as ph:
                gtf, b_gtf = SB(ph, "gtf", [128, D], F32)
                S.dma("sp", gtf[:], gate_d[l, 1], reads=[b_gate], writes=[b_gtf])
                xm = [SB(ph, "xm%d" % i, [128, 2, D], F32) for i in range(3)]
                ss, b_ss = SB(ph, "ss", [128, 2], F32)
                xn2 = [SB(ph, "xn2_%d" % i, [128, 2, D], BF16) for i in range(2)]
                h2Ts = [SB(ph, "h2T%d" % i, [128, 8, 256], BF16) for i in range(2)]
                aT, b_aT = SB(ph, "aT", [128, 32, 256], BF16)
                rt = [SB(ph, "rt%d" % i, [128, 256], F32) for i in range(2)]
                yo = [SB(ph, "yo%d" % i, [128, D], F32) for i in range(2)]
                pTs2 = [PS(ph, "pT%d" % i, [128, 8, 128], BF16) for i in range(2)]
                pU = [PS(ph, "pU%d" % i, [128, 256], F32) for i in range(2)]
                pDn = [PS(ph, "pDn%d" % i, [128, 512], F32) for i in range(4)]

                def prep_norm(g):
                    xmt, b_xm = xm[g % 3]
                    xnt, b_xnt = xn2[g % 2]
                    for j in range(2):
                        t = 2 * g + j
                        S.dma("sp", xmt[:, j, :], xmid_d[t * 128:(t + 1) * 128, :], reads=[b_xmid], writes=[b_xm])
                    for j in range(2):
                        act(xnt[:, j, :], xmt[:, j, :], AF.Square, [b_xm], [b_xnt, b_ss], accum=ss[:, j:j + 1])
                    rsqrt_cols(ss[:], ss[:], 1.0 / D, [b_ss], [b_ss])
                    for j in range(2):
                        ts("dve", xnt[:, j, :], xmt[:, j, :], ss[:, j:j + 1], None, ALU.mult, None, [b_xm, b_ss], [b_xnt])

                def prep_tr(g):
                    xnt, b_xnt = xn2[g % 2]
                    h2T, b_h2T = h2Ts[g % 2]
                    for j in range(2):
                        pT, b_pT = pTs2[j]
                        for k in range(8):
                            tr(pT[:, k, :], xnt[:, j, k * 128:(k + 1) * 128], ident[:], [b_xnt, b_ident], [b_pT])
                        for k in range(8):
                            if k % 2 == 0:
                                ts("dve", h2T[:, k, j * 128:(j + 1) * 128], pT[:, k, :], modcol[:, mc + 24 + k: mc + 25 + k],
                                   modcol[:, mc + 16 + k: mc + 17 + k], ALU.mult, ALU.add, [b_pT, b_modcol], [b_h2T])
                            else:
                                act(h2T[:, k, j * 128:(j + 1) * 128], pT[:, k, :], AF.Identity, [b_pT, b_modcol], [b_h2T],
                                    scale=modcol[:, mc + 24 + k: mc + 25 + k], bias=modcol[:, mc + 16 + k: mc + 17 + k])

                yi = 0
                NGE = NG // 2 if (split and last) else NG
                prep_norm(0)
                prep_tr(0)
                for g in range(NGE):
                    xmt, b_xm = xm[g % 3]
                    h2T, b_h2T = h2Ts[g % 2]
                    if g + 1 < NGE:
                        prep_norm(g + 1)
                    for f in range(32):
                        pu, b_pu = pU[f % 2]
                        rtt, b_rt = rt[f % 2]
                        for k in range(8):
                            mm(pu[:], w1[:, k, f * 128:(f + 1) * 128], h2T[:, k, :], k == 0, k == 7, [b_w1k[k], b_h2T], [b_pu])
                        act(rtt[:], pu[:], AF.Relu, [b_pu], [b_rt])
                        tt("dve" if f % 2 == 0 else "pool", aT[:, f, :], rtt[:], rtt[:], ALU.mult, [b_rt], [b_aT])
                    if g + 1 < NGE:
                        prep_tr(g + 1)
                    for j in range(2):
                        t = 2 * g + j
                        tsl = slice(t * 128, (t + 1) * 128)
                        yot, b_yo = yo[yi % 2]
                        yi += 1
                        for half in range(2):
                            pd, b_pd = pDn[j * 2 + half]
                            hs = slice(half * 512, (half + 1) * 512)
                            for f in range(32):
                                mm(pd[:], aT[:, f, j * 128:(j + 1) * 128], w2[:, f, hs], f == 0, f == 31, [b_aT, b_w2f[f]], [b_pd])
                            tt("dve", yot[:, hs], pd[:], gtf[:, hs], ALU.mult, [b_pd, b_gtf], [b_yo])
                            tt("pool", yot[:, hs], yot[:, hs], xmt[:, j, hs], ALU.add, [b_yo, b_xm], [b_yo])
                        d = S.dma("pool", x_nxt[tsl, :], yot[:], reads=[b_yo], writes=[b_xnxt], sem="yo_st%d" % ((yi - 1) % 2))
                        if last:
                            out_dmas.append(d)
            S.barrier()
            ph23.close()

        S.emit(final_waits=out_dmas)
    return nc


def host_inputs(b, S_LEN, DEPTH, x, c, positions, _parity=0, *, w_ada, b_ada, w_in, w_gate_up, b_gate, gla_out_norm, q_a_norm,
                w_q_up, kv_a_norm, w_kv_up, q_norm_nope, k_norm_nope, q_norm_rope, k_norm_rope,
                w_out, w_mlp_up, w_mlp_down):
    f32 = np.float32
    NT = S_LEN // 128
    A = lambda a: np.ascontiguousarray(np.asarray(a))

    def bc(v, n=128):
        v = np.asarray(v, dtype=f32)
        return A(np.broadcast_to(v[:, None, :], (v.shape[0], n, v.shape[1])))

    inv_freq = (10000.0 ** (-np.arange(0, 64, 2, dtype=f32) / f32(64))).astype(f32)
    b_ada = np.asarray(b_ada, dtype=f32)
    d = {
        "x": A(np.asarray(x[b], dtype=f32)),
        "ccol": A(np.asarray(c[b], dtype=f32).reshape(8, 128).T),
        "pos": A(np.asarray(positions[b]).astype(np.int32).reshape(NT, 128).T),
        "invf": A(np.broadcast_to(inv_freq[None, :], (128, 32))),
        "parcol": A(np.broadcast_to(np.array([[1.0 - _parity, float(_parity)]], dtype=f32), (128, 2))),
        "w_ada": A(np.asarray(w_ada, dtype=f32)),
        "bada_col": A(b_ada.reshape(DEPTH, 48, 128).transpose(0, 2, 1)),
        "bada_gate": A(np.broadcast_to(b_ada.reshape(DEPTH, 6, 1, D)[:, [2, 5]], (DEPTH, 2, 128, D))),
        "w_in": A(np.asarray(w_in, dtype=f32)),
        "w_gate_up": A(np.asarray(w_gate_up, dtype=f32)),
        "bgate_bc": bc(b_gate),
        "gon_bc4": bc(np.tile(np.asarray(gla_out_norm, dtype=f32), (1, 4))),
        "qan_col": A(np.asarray(q_a_norm, dtype=f32).reshape(DEPTH, 2, 128).transpose(0, 2, 1)),
        "w_q_up": A(np.asarray(w_q_up, dtype=f32)),
        "kvan_col": A(np.asarray(kv_a_norm, dtype=f32).reshape(DEPTH, 128, 1)),
        "w_kv_up": A(np.asarray(w_kv_up, dtype=f32)),
        "qnn_bc": bc(q_norm_nope), "knn_bc": bc(k_norm_nope),
        "qnr_bc": bc(q_norm_rope), "knr_bc": bc(k_norm_rope),
        "w_out": A(np.asarray(w_out, dtype=f32)),
        "w_mlp_up": A(np.asarray(w_mlp_up, dtype=f32)),
        "w_mlp_down": A(np.asarray(w_mlp_down, dtype=f32)),
    }
    return d


_NC_CACHE = {}


def kernel(**inputs):
    x = np.asarray(inputs["x"])
    B, S_LEN, _ = x.shape
    DEPTH = np.asarray(inputs["w_ada"]).shape[0]
    key = (S_LEN, DEPTH)
    if key not in _NC_CACHE:
        _NC_CACHE[key] = build(S_LEN, DEPTH, split=True)
    nc = _NC_CACHE[key]
    in_maps = []
    for b in range(B):
        base = host_inputs(b, S_LEN, DEPTH, _parity=0, **inputs)
        in_maps.append(base)
        m1 = dict(base)
        m1["parcol"] = host_inputs_par(1)
        in_maps.append(m1)
    res = run_bass_kernel_spmd(nc, in_maps, core_ids=list(range(2 * B)))
    out = np.empty((B, S_LEN, D), dtype=np.float32)
    ov = out.reshape(B, S_LEN // 1024, 2, 512, D)
    for b in range(B):
        for p in range(2):
            ov[b, :, p] = np.asarray(res.results[2 * b + p]["y"], dtype=np.float32).reshape(S_LEN // 1024, 512, D)
    return out


def host_inputs_par(p):
    return np.ascontiguousarray(np.broadcast_to(np.array([[1.0 - p, float(p)]], dtype=np.float32), (128, 2)))
```

```python
import contextlib
import math
import numpy as np
import concourse.bass as bass
import concourse.mybir as mybir
from concourse.bass_utils import run_bass_kernel_spmd

F32 = mybir.dt.float32
BF16 = mybir.dt.bfloat16
I32 = mybir.dt.int32
ALU = mybir.AluOpType
AF = mybir.ActivationFunctionType

ENGS = ("pe", "act", "dve", "pool", "sp")
EST_DUR = {"pe": 0.15, "act": 0.3, "dve": 0.3, "pool": 0.5, "sp": 0.1}
import os as _os
SEM_SHARE = _os.environ.get("SEM_SHARE", "0") == "1"
SEQ_CHAINS = _os.environ.get("SEQ_CHAINS", "0") == "1"
NO_KPR = _os.environ.get("NO_KPR", "0") == "1"
SKIP = set(_os.environ.get("SKIP_PHASES", "").split(","))
OLD_ORDER = _os.environ.get("OLD_ORDER", "0") == "1"
SEM_MAP = {"wm0": "A0", "wm1": "A1", "xt0": "A0", "xt1": "A1", "xt2": "A2", "cst0": "B0", "cst1": "B1", "cst2": "B2",
           "KTs0": "A0", "KTs1": "A1", "Vh0": "B0", "Vh1": "B1", "qns0": "C0", "qns1": "C1",
           "qps0": "D0", "qps1": "D1", "mx0": "B0", "mx1": "B1", "xm0": "A0", "xm1": "A1", "xm2": "A2",
           "ob_st0": "ST0", "ob_st1": "ST1", "xo_st0": "ST0", "xo_st1": "ST1", "yo_st0": "ST0", "yo_st1": "ST1"}


class Buf:
    __slots__ = ("name", "lw", "rd", "rd_dma")

    def __init__(self, name):
        self.name = name
        self.lw = None
        self.rd = {}
        self.rd_dma = []


class Ins:
    __slots__ = ("eng", "fn", "deps", "signal", "cnt", "is_dma", "dsem", "dval", "tfin")

    def __init__(self, eng, fn, is_dma=False):
        self.eng = eng
        self.fn = fn
        self.deps = []
        self.signal = False
        self.cnt = 0
        self.is_dma = is_dma
        self.dsem = None
        self.dval = 0
        self.tfin = 0.0


class Sched:
    def __init__(self, nc):
        self.nc = nc
        self.q = {e: [] for e in ENGS}
        self.dma_sems = {}
        self.all_dma = []
        self.last_on_sem = {}
        self.bar = None
        self.bar_done = {}
        self.nbuf = 0
        self.eng_free = {e: 0.0 for e in ENGS}
        self.step_max = 0.0

    def buf(self, name=None):
        self.nbuf += 1
        return Buf(name or ("b%d" % self.nbuf))

    def _collect(self, ins, reads, writes):
        deps = {}
        pe = (ins.eng == "pe" and not ins.is_dma)

        def add(d):
            if d is None or d is ins:
                return
            if pe and d.eng == "pe" and not d.is_dma:
                return
            if d.is_dma:
                d = self.last_on_sem[d.dsem]
                if d is ins:
                    return
            deps[id(d)] = d

        for b in reads:
            add(b.lw)
        for b in writes:
            add(b.lw)
            for d in b.rd.values():
                add(d)
            for d in b.rd_dma:
                add(d)
        if self.bar is not None and not self.bar_done.get(ins.eng):
            for d in self.bar:
                add(d)
            self.bar_done[ins.eng] = True
        for b in reads:
            if ins.is_dma:
                b.rd_dma.append(ins)
            else:
                b.rd[ins.eng] = ins
        for b in writes:
            b.lw = ins
            b.rd = {}
            b.rd_dma = []
        ins.deps = list(deps.values())
        ready = max([d.tfin for d in ins.deps], default=0.0) + (0.25 if ins.deps else 0.0)
        start = max(ready, self.eng_free[ins.eng])
        if ins.is_dma:
            self.eng_free[ins.eng] = start + 0.1
            ins.tfin = start + 2.5
        else:
            ins.tfin = start + EST_DUR[ins.eng]
            self.eng_free[ins.eng] = ins.tfin
        if ins.tfin > self.step_max:
            self.step_max = ins.tfin

    def op(self, eng, fn, reads=(), writes=()):
        ins = Ins(eng, fn)
        self._collect(ins, reads, writes)
        self.q[eng].append(ins)
        return ins

    def dma(self, eng, out, in_, reads=(), writes=(), sem=None, nobar=False):
        if sem is None:
            sem = (list(writes) + list(reads))[0].name
        if SEM_SHARE:
            sem = SEM_MAP.get(sem, "G_const" if not sem.endswith("_st") else "ST")
        ins = Ins(eng, lambda e: e.dma_start(out=out, in_=in_), is_dma=True)
        tot = self.dma_sems.get(sem, 0) + 16
        self.dma_sems[sem] = tot
        ins.dsem = sem
        ins.dval = tot
        self._collect(ins, reads, writes)
        self.last_on_sem[sem] = ins
        self.q[eng].append(ins)
        if not nobar:
            self.all_dma.append(ins)
        return ins

    def barrier(self):
        deps = []
        for e in ENGS:
            for ins in reversed(self.q[e]):
                if not ins.is_dma:
                    deps.append(ins)
                    break
        deps.extend(self.all_dma)
        self.all_dma = []
        last = {}
        keep = []
        for d in deps:
            if d.is_dma:
                if d.dsem not in last or last[d.dsem].dval < d.dval:
                    last[d.dsem] = d
            else:
                keep.append(d)
        self.bar = keep + list(last.values())
        self.bar_done = {}

    def emit(self, final_waits=()):
        nc = self.nc
        for e in ENGS:
            for ins in self.q[e]:
                for d in ins.deps:
                    if not d.is_dma:
                        d.signal = True
        for e in ENGS:
            c = 0
            for ins in self.q[e]:
                if ins.signal and not ins.is_dma:
                    c += 1
                ins.cnt = c
        with contextlib.ExitStack() as st:
            esem = {e: st.enter_context(nc.semaphore("es_" + e)) for e in ENGS}
            dsem = {k: st.enter_context(nc.semaphore("ds_%d" % i)) for i, k in enumerate(self.dma_sems)}
            block = st.enter_context(nc.Block())
            sched = self

            def run(e, eh):
                waited = {}
                for ins in sched.q[e]:
                    need = {}
                    for d in ins.deps:
                        if d.is_dma:
                            s, v, key = dsem[d.dsem], d.dval, ("d", d.dsem)
                        else:
                            s, v, key = esem[d.eng], d.cnt, ("e", d.eng)
                        if key not in need or need[key][1] < v:
                            need[key] = (s, v)
                    for key, (s, v) in need.items():
                        if waited.get(key, 0) >= v:
                            continue
                        waited[key] = v
                        eh.wait_ge(s, v)
                    bi = ins.fn(eh)
                    if ins.is_dma:
                        bi.then_inc(dsem[ins.dsem], 16)
                    elif ins.signal:
                        bi.then_inc(esem[e], 1)
                if e == "sp":
                    for d in final_waits:
                        eh.wait_ge(dsem[d.dsem], d.dval)

            @block.tensor
            def _(eh):
                run("pe", eh)

            @block.scalar
            def _(eh):
                run("act", eh)

            @block.vector
            def _(eh):
                run("dve", eh)

            @block.gpsimd
            def _(eh):
                run("pool", eh)

            @block.sync
            def _(eh):
                run("sp", eh)


D = 1024
DFF = 4096
EPS = 1e-6
TWO_PI = 2.0 * math.pi
C1 = 6.28125
C2 = TWO_PI - C1


def build(S_LEN, DEPTH, dbg=False, stop=None, split=False):
    NT = S_LEN // 128
    NC4 = S_LEN // 512
    NG = S_LEN // 256
    nc = bass.Bass("TRN2", target_bir_lowering=False)
    S = Sched(nc)

    def din(name, shape, dt=F32):
        return nc.dram_tensor(name, shape, dt, kind="ExternalInput").ap()

    def dscr(name, shape, dt):
        return nc.dram_tensor(name, shape, dt, kind=("ExternalOutput" if dbg else "Internal")).ap()

    x_in = din("x", [S_LEN, D])
    ccol = din("ccol", [128, 8])
    pos_in = din("pos", [128, NT], I32)
    invf_in = din("invf", [128, 32])
    par_in = din("parcol", [128, 2])
    w_ada = din("w_ada", [DEPTH, D, 6 * D])
    bada_col = din("bada_col", [DEPTH, 128, 48])
    bada_gate = din("bada_gate", [DEPTH, 2, 128, D])
    w_in = din("w_in", [DEPTH, D, 2000])
    wgu_in = din("w_gate_up", [DEPTH, 16, 256])
    bgate_in = din("bgate_bc", [DEPTH, 128, 256])
    gon_in = din("gon_bc4", [DEPTH, 128, 512])
    qan_in = din("qan_col", [DEPTH, 128, 2])
    wqu_in = din("w_q_up", [DEPTH, 256, 768])
    kvan_in = din("kvan_col", [DEPTH, 128, 1])
    wkvu_in = din("w_kv_up", [DEPTH, 128, 1024])
    qnn_in = din("qnn_bc", [DEPTH, 128, 128])
    knn_in = din("knn_bc", [DEPTH, 128, 128])
    qnr_in = din("qnr_bc", [DEPTH, 128, 64])
    knr_in = din("knr_bc", [DEPTH, 128, 64])
    wout_in = din("w_out", [DEPTH, D, D])
    w1_in = din("w_mlp_up", [DEPTH, D, DFF])
    w2_in = din("w_mlp_down", [DEPTH, DFF, D])
    y_out = nc.dram_tensor("y", [S_LEN // 2 if split else S_LEN, D], F32, kind="ExternalOutput").ap()

    xs = [dscr("xs%d" % i, [S_LEN, D], F32) for i in range(2)]
    xmid_d = dscr("xmid", [S_LEN, D], F32)
    mixT_d = dscr("mixT", [8, 128, S_LEN], BF16)
    KT_d = dscr("KT", [4, 128, S_LEN], BF16)
    KPE_d = dscr("KPE", [64, S_LEN], BF16)
    V_d = dscr("Vd", [S_LEN, 512], BF16)
    QT_d = dscr("QT", [4, 128, S_LEN], BF16)
    QPE_d = dscr("QPE", [2, 128, S_LEN], BF16)
    cos_d = dscr("cosd", [128, NT, 32], F32)
    sin_d = dscr("sind", [128, NT, 32], F32)
    gate_d = dscr("gated", [DEPTH, 2, 128, D], F32)
    b_xs = [S.buf("xs0"), S.buf("xs1")]
    b_xmid = S.buf("xmid")
    b_mixT = S.buf("mixT")
    b_KT, b_KPE, b_V, b_QT, b_QPE = S.buf("KT"), S.buf("KPE"), S.buf("V"), S.buf("QT"), S.buf("QPE")
    b_cos, b_sin, b_gate = S.buf("cosd"), S.buf("sind"), S.buf("gated")
    b_y = S.buf("y")
    out_dmas = []

    def mm(out, lhsT, rhs, start, stop, R, W):
        S.op("pe", lambda e: e.matmul(out=out, lhsT=lhsT, rhs=rhs, start=start, stop=stop), R, W)

    def tr(out, in_, ident, R, W):
        S.op("pe", lambda e: e.transpose(out=out, in_=in_, identity=ident), R, W)

    def act(out, in_, func, R, W, scale=1.0, bias=0.0, accum=None):
        if accum is None:
            S.op("act", lambda e: e.activation(out=out, in_=in_, func=func, bias=bias, scale=scale), R, W)
        else:
            S.op("act", lambda e: e.activation(out=out, in_=in_, func=func, bias=bias, scale=scale, accum_out=accum), R, W)

    def tt(eng, out, in0, in1, op, R, W):
        S.op(eng, lambda e: e.tensor_tensor(out=out, in0=in0, in1=in1, op=op), R, W)

    def ts(eng, out, in0, s1, s2, op0, op1, R, W):
        if s2 is None:
            S.op(eng, lambda e: e.tensor_scalar(out=out, in0=in0, scalar1=s1, scalar2=None, op0=op0), R, W)
        else:
            S.op(eng, lambda e: e.tensor_scalar(out=out, in0=in0, scalar1=s1, scalar2=s2, op0=op0, op1=op1), R, W)

    def stt(eng, out, in0, scalar, in1, op0, op1, R, W, accum=None):
        if accum is None:
            S.op(eng, lambda e: e.scalar_tensor_tensor(out=out, in0=in0, scalar=scalar, in1=in1, op0=op0, op1=op1), R, W)
        else:
            S.op(eng, lambda e: e.scalar_tensor_tensor(out=out, in0=in0, scalar=scalar, in1=in1, op0=op0, op1=op1, accum_out=accum), R, W)

    def cp(eng, out, in_, R, W):
        if eng == "act":
            S.op("act", lambda e: e.copy(out=out, in_=in_), R, W)
        else:
            S.op(eng, lambda e: e.tensor_copy(out=out, in_=in_), R, W)

    def recip(out, in_, R, W):
        S.op("dve", lambda e: e.reciprocal(out=out, in_=in_), R, W)

    def memset(eng, ap, val, W):
        S.op(eng, lambda e: e.memset(ap, val), (), W)

    def asel(out, in_, pattern, cmp, fill, base, cm, R, W):
        S.op("pool", lambda e: e.affine_select(out=out, in_=in_, pattern=pattern, compare_op=cmp, fill=fill,
                                               base=base, channel_multiplier=cm), R, W)

    def rsqrt_cols(dst, src, scale, R, W):
        act(dst, src, AF.Ln, R, W, scale=scale, bias=EPS)
        act(dst, dst, AF.Exp, W, W, scale=-0.5)

    with contextlib.ExitStack() as top:
        uid = [0]

        def SB(stack, name, shape, dt):
            uid[0] += 1
            t = stack.enter_context(nc.sbuf_tensor("%s_s%d" % (name, uid[0]), shape, dt))
            return t, S.buf(name)

        def PS(stack, name, shape, dt):
            uid[0] += 1
            t = stack.enter_context(nc.psum_tensor("%s_p%d" % (name, uid[0]), shape, dt))
            return t, S.buf(name)

        ident, b_ident = SB(top, "ident", [128, 128], BF16)
        maskU4, b_maskU4 = SB(top, "maskU4", [128, 4, 128], BF16)
        ones_f, b_onesf = SB(top, "ones_f", [128, 128], F32)
        ones_b, b_onesb = SB(top, "ones_b", [128, 128], BF16)
        maskD, b_maskD = SB(top, "maskD", [128, 4, 512], BF16)
        modcol, b_modcol = SB(top, "modcol", [128, DEPTH * 32], F32)

        par, b_par = SB(top, "par", [128, 2], F32)
        S.dma("sp", par[:], par_in, writes=[b_par])
        memset("pool", ident[:], 0.0, [b_ident])
        asel(ident[:], ident[:], [[-1, 128]], ALU.not_equal, 1.0, 0, 1, [b_ident], [b_ident])
        memset("pool", maskU4[:], 1.0, [b_maskU4])
        for h in range(4):
            asel(maskU4[:, h, :], maskU4[:, h, :], [[1, 128]], ALU.is_ge, 0.0, 0, -1, [b_maskU4], [b_maskU4])
        memset("pool", ones_f[:], 1.0, [b_onesf])
        memset("pool", ones_b[:], 1.0, [b_onesb])
        memset("pool", maskD[:], 1.0, [b_maskD])
        for r in range(4):
            asel(maskD[:, r, :], maskD[:, r, :], [[1, 512]], ALU.is_ge, 0.0, -128 * r, -1, [b_maskD], [b_maskD])

        with contextlib.ExitStack() as ph:
            posi, b_posi = SB(ph, "posi", [128, NT], I32)
            posf, b_posf = SB(ph, "posf", [128, NT], F32)
            invf, b_invf = SB(ph, "invf", [128, 32], F32)
            ang, b_ang = SB(ph, "ang", [128, NT, 32], F32)
            uu, b_uu = SB(ph, "uu", [128, NT, 32], F32)
            ni, b_ni = SB(ph, "ni", [128, NT, 32], I32)
            nf, b_nf = SB(ph, "nf", [128, NT, 32], F32)
            mk, b_mk = SB(ph, "mk", [128, NT, 32], F32)
            sn, b_sn = SB(ph, "sn", [128, NT, 32], F32)
            cs, b_cs = SB(ph, "cs", [128, NT, 32], F32)
            S.dma("sp", posi[:], pos_in, writes=[b_posi])
            S.dma("sp", invf[:], invf_in, writes=[b_invf])
            cp("dve", posf[:], posi[:], [b_posi], [b_posf])
            for t in range(NT):
                ts("dve", ang[:, t, :], invf[:], posf[:, t:t + 1], None, ALU.mult, None, [b_invf, b_posf], [b_ang])
            ts("dve", uu[:], ang[:], 1.0 / TWO_PI, None, ALU.mult, None, [b_ang], [b_uu])
            cp("dve", ni[:], uu[:], [b_uu], [b_ni])
            cp("dve", nf[:], ni[:], [b_ni], [b_nf])
            stt("dve", ang[:], nf[:], -C1, ang[:], ALU.mult, ALU.add, [b_nf, b_ang], [b_ang])
            stt("dve", ang[:], nf[:], -C2, ang[:], ALU.mult, ALU.add, [b_nf, b_ang], [b_ang])
            ts("dve", mk[:], ang[:], math.pi, None, ALU.is_gt, None, [b_ang], [b_mk])
            stt("dve", ang[:], mk[:], -TWO_PI, ang[:], ALU.mult, ALU.add, [b_mk, b_ang], [b_ang])
            ts("dve", mk[:], ang[:], -math.pi, None, ALU.is_lt, None, [b_ang], [b_mk])
            stt("dve", ang[:], mk[:], TWO_PI, ang[:], ALU.mult, ALU.add, [b_mk, b_ang], [b_ang])
            ts("dve", ang[:], ang[:], math.pi, -math.pi, ALU.min, ALU.max, [b_ang], [b_ang])
            act(sn[:], ang[:], AF.Sin, [b_ang], [b_sn])
            stt("dve", uu[:], ang[:], -1.0, ang[:], ALU.mult, ALU.max, [b_ang], [b_uu])
            ts("dve", uu[:], uu[:], -1.0, math.pi / 2, ALU.mult, ALU.add, [b_uu], [b_uu])
            act(cs[:], uu[:], AF.Sin, [b_uu], [b_cs])
            S.dma("sp", cos_d, cs[:], reads=[b_cs], writes=[b_cos], sem="cs_st")
            S.dma("sp", sin_d, sn[:], reads=[b_sn], writes=[b_sin], sem="sn_st")

            if stop != 'rope':
                cc, b_cc = SB(ph, "cc", [128, 8], F32)
                ce, b_ce = SB(ph, "ce", [128, 8], F32)
                cond, b_cond = SB(ph, "cond", [128, 8], F32)
                cond_rep, b_crep = SB(ph, "cond_rep", [128, 8, 128], BF16)
                condb, b_condb = SB(ph, "condb", [128, 16], BF16)
                wm = [SB(ph, "wm%d" % i, [128, 8, D], BF16) for i in range(2)]
                bcol, b_bcol = SB(ph, "bcol", [128, DEPTH * 48], F32)
                bgt, b_bgt = SB(ph, "bgt", [128, D], F32)
                gsb, b_gsb = SB(ph, "gsb", [128, D], F32)
                pg = [PS(ph, "pg%d" % i, [128, 512], F32) for i in range(2)]
                pc, b_pc = PS(ph, "pc", [128, 8], F32)
                S.dma("sp", cc[:], ccol, writes=[b_cc])
                for l in range(DEPTH):
                    S.dma("sp", bcol[:, l * 48:(l + 1) * 48], bada_col[l], writes=[b_bcol])
                act(ce[:], cc[:], AF.Exp, [b_cc], [b_ce], scale=-1.0)
                ts("dve", ce[:], ce[:], 1.0, None, ALU.add, None, [b_ce], [b_ce])
                recip(ce[:], ce[:], [b_ce], [b_ce])
                tt("dve", cond[:], cc[:], ce[:], ALU.mult, [b_cc, b_ce], [b_cond])
                memset("dve", condb[:], 0.0, [b_condb])
                cp("dve", condb[:, 0:8], cond[:], [b_cond, b_condb], [b_condb])
                for k in range(8):
                    ts("dve", cond_rep[:, k, :], ones_f[:], cond[:, k:k + 1], None, ALU.mult, None, [b_onesf, b_cond], [b_crep])
                li = 0
                for l in range(DEPTH):
                    for m in range(6):
                        wt, b_wt = wm[li % 2]
                        li += 1
                        S.dma("pool", wt[:], w_ada[l, :, m * D:(m + 1) * D].rearrange("(k p) n -> p k n", p=128), writes=[b_wt])
                        if m in (2, 5):
                            gi = 0 if m == 2 else 1
                            S.dma("sp", bgt[:], bada_gate[l, gi], writes=[b_bgt])
                            for half in range(2):
                                pgt, b_pg = pg[half]
                                for k in range(8):
                                    mm(pgt[:], cond_rep[:, k, :], wt[:, k, half * 512:(half + 1) * 512], k == 0, k == 7,
                                       [b_crep, b_wt], [b_pg])
                                tt("dve", gsb[:, half * 512:(half + 1) * 512], pgt[:], bgt[:, half * 512:(half + 1) * 512],
                                   ALU.add, [b_pg, b_bgt], [b_gsb])
                            S.dma("sp", gate_d[l, gi], gsb[:], reads=[b_gsb], writes=[b_gate], sem="gate_st")
                        else:
                            mi = {0: 0, 1: 1, 3: 2, 4: 3}[m]
                            for ko in range(8):
                                for k in range(8):
                                    mm(pc[:, ko:ko + 1], wt[:, k, ko * 128:(ko + 1) * 128], condb[:, k:k + 1], k == 0, k == 7,
                                       [b_wt, b_condb], [b_pc])
                            dst = modcol[:, l * 32 + mi * 8: l * 32 + mi * 8 + 8]
                            tt("dve", dst, pc[:], bcol[:, l * 48 + m * 8: l * 48 + m * 8 + 8], ALU.add, [b_pc, b_bcol], [b_modcol])
                            if m in (1, 4):
                                ts("dve", dst, dst, 1.0, None, ALU.add, None, [b_modcol], [b_modcol])
        S.barrier()

        for l in range(DEPTH if stop not in ('setup', 'rope') else 0):
            x_cur = x_in if l == 0 else xs[(l - 1) % 2]
            b_xcur = None if l == 0 else b_xs[(l - 1) % 2]
            last = (l == DEPTH - 1)
            x_nxt = y_out if last else xs[l % 2]
            b_xnxt = b_y if last else b_xs[l % 2]
            mc = l * 32

            with contextlib.ExitStack() as ph:
                win, b_win = SB(ph, "win", [128, 8, 2000], BF16)
                wgu, b_wgu = SB(ph, "wgu", [16, 256], BF16)
                wqu, b_wqu = SB(ph, "wqu", [128, 2, 768], BF16)
                wkvu, b_wkvu = SB(ph, "wkvu", [128, 1024], BF16)
                qan, b_qan = SB(ph, "qan", [128, 2], F32)
                kvan, b_kvan = SB(ph, "kvan", [128, 1], F32)
                bgate, b_bgate = SB(ph, "bgate", [128, 256], F32)
                gon4, b_gon4 = SB(ph, "gon4", [128, 512], F32)
                qnn, b_qnn = SB(ph, "qnn", [128, 128], F32)
                knn, b_knn = SB(ph, "knn", [128, 128], F32)
                qnr, b_qnr = SB(ph, "qnr", [128, 64], F32)
                knr, b_knr = SB(ph, "knr", [128, 64], F32)
                b_wink = [S.buf("win%d" % k) for k in range(8)]
                for k in range(8):
                    S.dma("pool", win[:, k, :], w_in[l, k * 128:(k + 1) * 128, :], writes=[b_wink[k]], sem="win_ld")
                S.dma("pool", wgu[:], wgu_in[l], writes=[b_wgu])
                S.dma("pool", wqu[:], wqu_in[l].rearrange("(k p) n -> p k n", p=128), writes=[b_wqu])
                S.dma("pool", wkvu[:], wkvu_in[l], writes=[b_wkvu])
                S.dma("sp", qan[:], qan_in[l], writes=[b_qan])
                S.dma("sp", kvan[:], kvan_in[l], writes=[b_kvan])
                S.dma("sp", bgate[:], bgate_in[l], writes=[b_bgate])
                S.dma("sp", gon4[:], gon_in[l], writes=[b_gon4])
                S.dma("sp", qnn[:], qnn_in[l], writes=[b_qnn])
                S.dma("sp", knn[:], knn_in[l], writes=[b_knn])
                S.dma("sp", qnr[:], qnr_in[l], writes=[b_qnr])
                S.dma("sp", knr[:], knr_in[l], writes=[b_knr])
                for c in range(2):
                    ts("dve", wqu[:, c, :], wqu[:, c, :], qan[:, c:c + 1], None, ALU.mult, None, [b_wqu, b_qan], [b_wqu])
                ts("dve", wkvu[:], wkvu[:], kvan[:, 0:1], None, ALU.mult, None, [b_wkvu, b_kvan], [b_wkvu])
                qsc = 192.0 ** -0.5
                ts("dve", qnn[:], qnn[:], qsc, None, ALU.mult, None, [b_qnn], [b_qnn])
                ts("dve", qnr[:], qnr[:], qsc, None, ALU.mult, None, [b_qnr], [b_qnr])

                xt = [SB(ph, "xt%d" % i, [128, D], F32) for i in range(3)]
                cst = [SB(ph, "cst%d" % i, [128, 2, 32], F32) for i in range(3)]
                sq, b_sq = SB(ph, "sq", [128, D], F32)
                ss, b_ss = SB(ph, "ss", [128, 1], F32)
                xn, b_xn = SB(ph, "xn", [128, D], BF16)
                hT, b_hT = SB(ph, "hT", [128, 8, 128], BF16)
                mD, b_mD = SB(ph, "mD", [128, 400], BF16)
                ss3, b_ss3 = SB(ph, "ss3", [128, 3], F32)
                rs3, b_rs3 = SB(ph, "rs3", [128, 3], F32)
                rq2, b_rq2 = SB(ph, "rq2", [128, 3], F32)
                mT, b_mT = SB(ph, "mT", [128, 4, 128], BF16)
                pre, b_pre = SB(ph, "pre", [128, 256], F32)
                lg, b_lg = SB(ph, "lg", [128, 256], F32)
                lgh, b_lgh = SB(ph, "lgh", [128, 256], BF16)
                lgl, b_lgl = SB(ph, "lgl", [128, 256], BF16)
                eb, b_eb = SB(ph, "eb", [128, 256], F32)
                enb, b_enb = SB(ph, "enb", [128, 256], F32)
                ebl, b_ebl = SB(ph, "ebl", [64, 4], F32)
                qg, b_qg = SB(ph, "qg", [128, 256], BF16)
                kg, b_kg = SB(ph, "kg", [128, 256], BF16)
                qkT, b_qkT = SB(ph, "qkT", [64, 8, 128], BF16)
                vsb, b_vsb = SB(ph, "vsb", [128, 512], BF16)
                eo, b_eo = SB(ph, "eo", [128, 512], F32)
                gog, b_gog = SB(ph, "gog", [128, 512], F32)
                ATs, b_ATs = SB(ph, "ATs", [128, 4, 128], BF16)
                stt_, b_st = SB(ph, "gst", [64, 4, 128], F32)
                stb, b_stb = SB(ph, "gstb", [64, 4, 128], BF16)
                sso, b_sso = SB(ph, "sso", [128, 4], F32)
                go, b_go = SB(ph, "go", [128, 512], BF16)
                gT, b_gT = SB(ph, "gT", [128, 4, 128], BF16)
                ss8, b_ss8 = SB(ph, "ss8", [128, 8], F32)
                fac8, b_fac8 = SB(ph, "fac8", [128, 8], F32)
                qn, b_qn = SB(ph, "qn", [128, 4, 128], BF16)
                zall, b_zall = SB(ph, "zall", [128, 5, 64], F32)
                ra, b_ra = SB(ph, "ra", [128, 5, 32], F32)
                rb, b_rb = SB(ph, "rb", [128, 5, 32], F32)
                rope_o, b_ropeo = SB(ph, "rope_o", [128, 5, 64], BF16)
                qT6, b_qT6 = SB(ph, "qT6", [128, 6, 128], BF16)
                kn, b_kn = SB(ph, "kn", [128, 4, 128], BF16)
                vt, b_vt = SB(ph, "vt", [128, 4, 128], BF16)
                kT5, b_kT5 = SB(ph, "kT5", [128, 5, 128], BF16)
                ssk, b_ssk = SB(ph, "ssk", [128, 4], F32)
                fack, b_fack = SB(ph, "fack", [128, 4], F32)

                pT, b_pT = PS(ph, "pT", [128, 8, 128], BF16)
                pA, b_pA = PS(ph, "pA", [128, 512], F32)
                pB, b_pB = PS(ph, "pB", [128, 512], F32)
                pC, b_pC = PS(ph, "pC", [128, 512], F32)
                pD, b_pD = PS(ph, "pD", [128, 512], F32)
                pM, _ = PS(ph, "pM", [128, 8, 128], BF16)
                b_pMa = b_pMb = S.buf("pM")
                pX, b_pX = PS(ph, "pX", [128, 512], F32)
                pY, b_pY = PS(ph, "pY", [128, 512], F32)
                pX4 = pX[:].rearrange("p (h e) -> p h e", e=128)
                pY4 = pY[:].rearrange("p (h e) -> p h e", e=128)

                memset("dve", stt_[:], 0.0, [b_st])
                memset("dve", stb[:], 0.0, [b_stb])

                sq2, b_sq2 = SB(ph, "sq2", [128, 128], F32)
                sq3, b_sq3 = SB(ph, "sq3", [128, 128], F32)
                b_pM = b_pMa
                kprs = [SB(ph, "kpr%d" % i, [128, 64], F32) for i in range(2)]
                rs3s = [SB(ph, "rs3_%d" % i, [128, 3], F32) for i in range(2)]
                rq2s = [SB(ph, "rq2_%d" % i, [128, 3], F32) for i in range(2)]
                mTs = [SB(ph, "mT%d" % i, [128, 4, 128], BF16) for i in range(2)]
                gqks = [SB(ph, "gqk%d" % i, [128, 512], F32) for i in range(2)]
                vsbs = [SB(ph, "vsb%d" % i, [128, 512], BF16) for i in range(2)]
                gogs = [SB(ph, "gog%d" % i, [128, 512], F32) for i in range(2)]
                pW = [(pA, b_pA), (pB, b_pB)]
                pQ = [(pC, b_pC), (pD, b_pD)]

                def drive(gens):
                    alive = list(gens)
                    while alive:
                        for g in list(alive):
                            try:
                                next(g)
                            except StopIteration:
                                alive.remove(g)

                def issue_loads(t):
                    tsl_ = slice(t * 128, (t + 1) * 128)
                    xtt_, b_xt_ = xt[t % 3]
                    cs_t_, b_cst_ = cst[t % 3]
                    S.dma("sp", xtt_[:], x_cur[tsl_, :], reads=([b_xcur] if b_xcur else []), writes=[b_xt_])
                    S.dma("sp", cs_t_[:, 0, :], cos_d[:, t, :], reads=[b_cos], writes=[b_cst_])
                    S.dma("sp", cs_t_[:, 1, :], sin_d[:, t, :], reads=[b_sin], writes=[b_cst_])

                def prologue(t):
                    sl = t % 2
                    tsl = slice(t * 128, (t + 1) * 128)
                    xtt, b_xt = xt[t % 3]
                    cs_t, b_cst = cst[t % 3]
                    kpr, b_kpr = kprs[sl]
                    rs3, b_rs3 = rs3s[sl]
                    rq2, b_rq2 = rq2s[sl]
                    mT, b_mT = mTs[sl]
                    gqk, b_gqk = gqks[sl]
                    vsb, b_vsb = vsbs[sl]
                    gog, b_gog = gogs[sl]
                    act(sq[:], xtt[:], AF.Square, [b_xt], [b_sq, b_ss], accum=ss[:, 0:1])
                    yield
                    rsqrt_cols(ss[:, 0:1], ss[:, 0:1], 1.0 / D, [b_ss], [b_ss])
                    yield
                    ts("dve", xn[:], xtt[:], ss[:, 0:1], None, ALU.mult, None, [b_xt, b_ss], [b_xn])
                    yield
                    for k in range(8):
                        tr(pT[:, k, :], xn[:, k * 128:(k + 1) * 128], ident[:], [b_xn, b_ident], [b_pT])
                    yield
                    for k in range(8):
                        if k % 2 == 0:
                            ts("dve", hT[:, k, :], pT[:, k, :], modcol[:, mc + 8 + k: mc + 9 + k], modcol[:, mc + k: mc + k + 1],
                               ALU.mult, ALU.add, [b_pT, b_modcol], [b_hT])
                        else:
                            act(hT[:, k, :], pT[:, k, :], AF.Identity, [b_pT, b_modcol], [b_hT],
                                scale=modcol[:, mc + 8 + k: mc + 9 + k], bias=modcol[:, mc + k: mc + k + 1])
                        if k % 4 == 3:
                            yield
                    blks = ((1536, 2000), (0, 512), (512, 1024), (1024, 1536))
                    for bi, (c0, c1) in enumerate(blks):
                        pw, b_pw = pW[bi % 2]
                        for k in range(8):
                            mm(pw[:, 0:c1 - c0], hT[:, k, :], win[:, k, c0:c1], k == 0, k == 7, [b_hT, b_wink[k]], [b_pw])
                        yield
                        if bi == 0:
                            cp("act", mD[:], pw[:, 0:400], [b_pw], [b_mD])
                            cp("act", kpr[:], pw[:, 400:464], [b_pw], [b_kpr])
                            yield
                            act(sq[:, 0:256], pw[:, 16:272], AF.Square, [b_pw], [b_sq, b_ss3], accum=ss3[:, 0:1])
                            act(sq[:, 0:128], pw[:, 272:400], AF.Square, [b_pw], [b_sq, b_ss3], accum=ss3[:, 1:2])
                            act(sq[:, 0:64], pw[:, 400:464], AF.Square, [b_pw], [b_sq, b_ss3], accum=ss3[:, 2:3])
                            yield
                        elif bi == 1:
                            cp("act", gqk[:], pw[:], [b_pw], [b_gqk])
                            yield
                        elif bi == 2:
                            cp("act", vsb[:], pw[:], [b_pw], [b_vsb])
                            yield
                        else:
                            act(eo[:], pw[:], AF.Exp, [b_pw], [b_eo], scale=-1.0)
                            yield
                            act(eo[:], eo[:], AF.Ln, [b_eo], [b_eo], bias=1.0)
                            yield
                            act(eo[:], eo[:], AF.Exp, [b_eo], [b_eo], scale=-1.0)
                            yield
                            tt("dve", gog[:], pw[:], eo[:], ALU.mult, [b_pw, b_eo], [b_gog])
                            yield
                            tt("pool", gog[:], gog[:], gon4[:], ALU.mult, [b_gog, b_gon4], [b_gog])
                            yield
                    act(rs3[:, 0:1], ss3[:, 0:1], AF.Ln, [b_ss3], [b_rs3], scale=1.0 / 256, bias=EPS)
                    act(rs3[:, 1:2], ss3[:, 1:2], AF.Ln, [b_ss3], [b_rs3], scale=1.0 / 128, bias=EPS)
                    act(rs3[:, 2:3], ss3[:, 2:3], AF.Ln, [b_ss3], [b_rs3], scale=1.0 / 64, bias=EPS)
                    yield
                    act(rs3[:], rs3[:], AF.Exp, [b_rs3], [b_rs3], scale=-0.5)
                    yield
                    tt("dve", rq2[:], rs3[:], rs3[:], ALU.mult, [b_rs3], [b_rq2])
                    tr(pT[0:16, 0, :], mD[:, 0:16], ident[:], [b_mD, b_ident], [b_pT])
                    tr(pT[:, 1, :], mD[:, 16:144], ident[:], [b_mD, b_ident], [b_pT])
                    tr(pT[:, 2, :], mD[:, 144:272], ident[:], [b_mD, b_ident], [b_pT])
                    tr(pT[:, 3, :], mD[:, 272:400], ident[:], [b_mD, b_ident], [b_pT])
                    yield
                    cp("dve", mT[0:16, 0, :], pT[0:16, 0, :], [b_pT], [b_mT])
                    cp("dve", mT[:, 1:4, :], pT[:, 1:4, :], [b_pT], [b_mT])
                    yield

                def gla_chain(t):
                    sl = t % 2
                    tsl = slice(t * 128, (t + 1) * 128)
                    mT, b_mT = mTs[sl]
                    gqk, b_gqk = gqks[sl]
                    vsb, b_vsb = vsbs[sl]
                    gog, b_gog = gogs[sl]
                    mm(pX[:, 0:256], mT[0:16, 0, :], wgu[:], True, True, [b_mT, b_wgu], [b_pX])
                    tt("dve", pre[:], pX[:, 0:256], bgate[:], ALU.add, [b_pX, b_bgate], [b_pre])
                    yield
                    act(pre[:], pre[:], AF.Exp, [b_pre], [b_pre], scale=-1.0)
                    yield
                    act(lg[:], pre[:], AF.Ln, [b_pre], [b_lg], bias=1.0)
                    yield
                    cp("dve", lgh[:], lg[:], [b_lg], [b_lgh])
                    yield
                    tt("dve", lgl[:], lg[:], lgh[:], ALU.subtract, [b_lg, b_lgh], [b_lgl])
                    yield
                    mm(pX[:, 256:512], maskU4[:, 0, :], lgh[:], True, False, [b_maskU4, b_lgh], [b_pX])
                    mm(pX[:, 256:512], maskU4[:, 0, :], lgl[:], False, True, [b_maskU4, b_lgl], [b_pX])
                    for h in range(4):
                        mm(pY[0:64, h:h + 1], lgh[:, h * 64:(h + 1) * 64], ones_b[:, 0:1], True, False,
                           [b_lgh, b_onesb], [b_pY])
                        mm(pY[0:64, h:h + 1], lgl[:, h * 64:(h + 1) * 64], ones_b[:, 0:1], False, True,
                           [b_lgl, b_onesb], [b_pY])
                    yield
                    act(eb[:], pX[:, 256:512], AF.Exp, [b_pX], [b_eb], scale=-1.0 / 16)
                    act(enb[:], pX[:, 256:512], AF.Exp, [b_pX], [b_enb], scale=1.0 / 16)
                    act(ebl[:], pY[0:64, 0:4], AF.Exp, [b_pY], [b_ebl], scale=-1.0 / 16)
                    yield
                    stt("dve", qg[:], gqk[:, 0:256], 0.125, eb[:], ALU.mult, ALU.mult, [b_gqk, b_eb], [b_qg])
                    tt("dve", kg[:], gqk[:, 256:512], enb[:], ALU.mult, [b_gqk, b_enb], [b_kg])
                    yield
                    for h in range(4):
                        tr(pM[0:64, h, :], qg[:, h * 64:(h + 1) * 64], ident[:], [b_qg, b_ident], [b_pM])
                        tr(pM[0:64, 4 + h, :], kg[:, h * 64:(h + 1) * 64], ident[:], [b_kg, b_ident], [b_pM])
                    yield
                    cp("dve", qkT[:], pM[0:64, :, :], [b_pM], [b_qkT])
                    yield
                    for h in range(4):
                        mm(pY4[:, h, :], qkT[:, 4 + h, :], qkT[:, h, :], True, True, [b_qkT], [b_pY])
                    yield
                    tt("dve", ATs[:], pY4, maskU4[:], ALU.mult, [b_pY, b_maskU4], [b_ATs])
                    yield
                    for h in range(4):
                        mm(pX4[:, h, :], ATs[:, h, :], vsb[:, h * 128:(h + 1) * 128], True, False, [b_ATs, b_vsb], [b_pX])
                        mm(pX4[:, h, :], qkT[:, h, :], stb[:, h, :], False, True, [b_qkT, b_stb], [b_pX])
                    for h in range(4):
                        mm(pY4[0:64, h, :], kg[:, h * 64:(h + 1) * 64], vsb[:, h * 128:(h + 1) * 128], True, True,
                           [b_kg, b_vsb], [b_pY])
                    yield
                    for h in range(4):
                        ts("dve", stt_[:, h, :], stt_[:, h, :], ebl[:, h:h + 1], None, ALU.mult, None, [b_st, b_ebl], [b_st])
                        stt("dve", stt_[:, h, :], pY4[0:64, h, :], ebl[:, h:h + 1], stt_[:, h, :], ALU.mult, ALU.add,
                            [b_pY, b_ebl, b_st], [b_st])
                        if h == 1:
                            yield
                    cp("dve", stb[:], stt_[:], [b_st], [b_stb])
                    yield
                    for h in range(4):
                        act(sq2[:], pX4[:, h, :], AF.Square, [b_pX], [b_sq2, b_sso], accum=sso[:, h:h + 1])
                    yield
                    act(sso[:], sso[:], AF.Ln, [b_sso], [b_sso], scale=1.0 / 128, bias=EPS)
                    yield
                    act(sso[:], sso[:], AF.Exp, [b_sso], [b_sso], scale=-0.5)
                    yield
                    for h in range(4):
                        stt("dve", go[:, h * 128:(h + 1) * 128], pX4[:, h, :], sso[:, h:h + 1], gog[:, h * 128:(h + 1) * 128],
                            ALU.mult, ALU.mult, [b_pX, b_sso, b_gog], [b_go])
                        if h == 1:
                            yield
                    yield
                    for h in range(4):
                        tr(pM[:, h, :], go[:, h * 128:(h + 1) * 128], ident[:], [b_go, b_ident], [b_pM])
                    yield
                    cp("dve", gT[:], pM[:, 0:4, :], [b_pM], [b_gT])
                    yield
                    S.dma("sp", mixT_d[0:4, :, tsl].rearrange("c p s -> p c s"), gT[:], reads=[b_gT], writes=[b_mixT], sem="gT_st")

                qhs, b_qhs = SB(ph, "qhs", [128, 768], F32)
                kvs, b_kvs = SB(ph, "kvs", [128, 1024], F32)
                zk, b_zk = SB(ph, "zk", [128, 64], F32)
                rka, b_rka = SB(ph, "rka", [128, 32], F32)
                rkb, b_rkb = SB(ph, "rkb", [128, 32], F32)
                rope_k, b_ropek = SB(ph, "rope_k", [128, 64], BF16)
                sq4, b_sq4 = SB(ph, "sq4", [128, 128], F32)

                def mla_q(t):
                    sl = t % 2
                    tsl = slice(t * 128, (t + 1) * 128)
                    cs_t, b_cst = cst[t % 3]
                    rs3, b_rs3 = rs3s[sl]
                    rq2, b_rq2 = rq2s[sl]
                    mT, b_mT = mTs[sl]
                    pt_, bpt = pQ[0]
                    for g in range(2):
                        for c in range(2):
                            mm(pt_[:, 0:384], mT[:, 1 + c, :], wqu[:, c, g * 384:(g + 1) * 384], c == 0, c == 1, [b_mT, b_wqu], [bpt])
                        yield
                        cp("act", qhs[:, g * 384:(g + 1) * 384], pt_[:, 0:384], [bpt], [b_qhs])
                        for hh in range(2):
                            h = 2 * g + hh
                            base = hh * 192
                            act(sq3[:, 0:128], pt_[:, base:base + 128], AF.Square, [bpt], [b_sq3, b_ss8], accum=ss8[:, h:h + 1])
                            act(sq3[:, 0:64], pt_[:, base + 128:base + 192], AF.Square, [bpt], [b_sq3, b_ss8], accum=ss8[:, 4 + h:5 + h])
                        yield
                    ts("dve", ss8[:], ss8[:], rq2[:, 0:1], None, ALU.mult, None, [b_ss8, b_rq2], [b_ss8])
                    yield
                    act(fac8[:, 0:4], ss8[:, 0:4], AF.Ln, [b_ss8], [b_fac8], scale=1.0 / 128, bias=EPS)
                    act(fac8[:, 4:8], ss8[:, 4:8], AF.Ln, [b_ss8], [b_fac8], scale=1.0 / 64, bias=EPS)
                    yield
                    act(fac8[:], fac8[:], AF.Exp, [b_fac8], [b_fac8], scale=-0.5)
                    yield
                    ts("dve", fac8[:], fac8[:], rs3[:, 0:1], None, ALU.mult, None, [b_fac8, b_rs3], [b_fac8])
                    yield
                    for h in range(4):
                        base = h * 192
                        stt("dve", qn[:, h, :], qhs[:, base:base + 128], fac8[:, h:h + 1], qnn[:], ALU.mult, ALU.mult,
                            [b_qhs, b_fac8, b_qnn], [b_qn])
                        stt("dve", zall[:, h, :], qhs[:, base + 128:base + 192], fac8[:, 4 + h:5 + h], qnr[:], ALU.mult, ALU.mult,
                            [b_qhs, b_fac8, b_qnr], [b_zall])
                        if h % 2 == 1:
                            yield
                    for h in range(4):
                        tr(pM[:, h, :], qn[:, h, :], ident[:], [b_qn, b_ident], [b_pM])
                    yield
                    cp("dve", qT6[:, 0:4, :], pM[:, 0:4, :], [b_pM], [b_qT6])
                    yield
                    S.dma("sp", QT_d[:, :, tsl].rearrange("c p s -> p c s"), qT6[:, 0:4, :], reads=[b_qT6], writes=[b_QT], sem="qT_st")
                    for hh in range(4):
                        z1, z2 = zall[:, hh, 0:32], zall[:, hh, 32:64]
                        cth, sth = cs_t[:, 0, :], cs_t[:, 1, :]
                        tt("pool", ra[:, hh, :], z1, cth, ALU.mult, [b_zall, b_cst], [b_ra])
                        tt("pool", rb[:, hh, :], z2, sth, ALU.mult, [b_zall, b_cst], [b_rb])
                        tt("pool", rope_o[:, hh, 0:32], ra[:, hh, :], rb[:, hh, :], ALU.subtract, [b_ra, b_rb], [b_ropeo])
                        yield
                        tt("pool", ra[:, hh, :], z2, cth, ALU.mult, [b_zall, b_cst], [b_ra])
                        tt("pool", rb[:, hh, :], z1, sth, ALU.mult, [b_zall, b_cst], [b_rb])
                        tt("pool", rope_o[:, hh, 32:64], ra[:, hh, :], rb[:, hh, :], ALU.add, [b_ra, b_rb], [b_ropeo])
                        yield
                    for hp in range(2):
                        tr(pM[:, 4 + hp, :], rope_o[:, 2 * hp:2 * hp + 2, :].rearrange("p h r -> p (h r)"), ident[:],
                           [b_ropeo, b_ident], [b_pM])
                    yield
                    cp("dve", qT6[:, 4:6, :], pM[:, 4:6, :], [b_pM], [b_qT6])
                    yield
                    S.dma("sp", QPE_d[:, :, tsl].rearrange("c p s -> p c s"), qT6[:, 4:6, :], reads=[b_qT6], writes=[b_QPE], sem="qT_st2")

                def mla_kv(t):
                    sl = t % 2
                    tsl = slice(t * 128, (t + 1) * 128)
                    cs_t, b_cst = cst[t % 3]
                    kpr, b_kpr = kprs[sl]
                    rs3, b_rs3 = rs3s[sl]
                    rq2, b_rq2 = rq2s[sl]
                    mT, b_mT = mTs[sl]
                    pt_, bpt = pQ[1]
                    stt("dve", zk[:], kpr[:], rs3[:, 2:3], knr[:], ALU.mult, ALU.mult, [b_kpr, b_rs3, b_knr], [b_zk])
                    yield
                    for g in range(2):
                        mm(pt_[:], mT[:, 3, :], wkvu[:, g * 512:(g + 1) * 512], True, True, [b_mT, b_wkvu], [bpt])
                        yield
                        cp("act", kvs[:, g * 512:(g + 1) * 512], pt_[:], [bpt], [b_kvs])
                        for hh in range(2):
                            h = 2 * g + hh
                            act(sq4[:, 0:128], pt_[:, hh * 256:hh * 256 + 128], AF.Square, [bpt], [b_sq4, b_ssk], accum=ssk[:, h:h + 1])
                        yield
                    z1, z2 = zk[:, 0:32], zk[:, 32:64]
                    cth, sth = cs_t[:, 0, :], cs_t[:, 1, :]
                    tt("dve", rka[:], z1, cth, ALU.mult, [b_zk, b_cst], [b_rka])
                    tt("dve", rkb[:], z2, sth, ALU.mult, [b_zk, b_cst], [b_rkb])
                    tt("dve", rope_k[:, 0:32], rka[:], rkb[:], ALU.subtract, [b_rka, b_rkb], [b_ropek])
                    yield
                    tt("dve", rka[:], z2, cth, ALU.mult, [b_zk, b_cst], [b_rka])
                    tt("dve", rkb[:], z1, sth, ALU.mult, [b_zk, b_cst], [b_rkb])
                    tt("dve", rope_k[:, 32:64], rka[:], rkb[:], ALU.add, [b_rka, b_rkb], [b_ropek])
                    yield
                    ts("dve", ssk[:], ssk[:], rq2[:, 1:2], None, ALU.mult, None, [b_ssk, b_rq2], [b_ssk])
                    yield
                    act(fack[:], ssk[:], AF.Ln, [b_ssk], [b_fack], scale=1.0 / 128, bias=EPS)
                    yield
                    act(fack[:], fack[:], AF.Exp, [b_fack], [b_fack], scale=-0.5)
                    yield
                    ts("dve", fack[:], fack[:], rs3[:, 1:2], None, ALU.mult, None, [b_fack, b_rs3], [b_fack])
                    yield
                    for h in range(4):
                        base = h * 256
                        stt("dve", kn[:, h, :], kvs[:, base:base + 128], fack[:, h:h + 1], knn[:], ALU.mult, ALU.mult,
                            [b_kvs, b_fack, b_knn], [b_kn])
                        if h % 2 == 1:
                            yield
                    for h in range(4):
                        tr(pM[:, h, :], kn[:, h, :], ident[:], [b_kn, b_ident], [b_pM])
                    tr(pM[0:64, 4, :], rope_k[:], ident[:], [b_ropek, b_ident], [b_pM])
                    yield
                    cp("dve", kT5[:, 0:4, :], pM[:, 0:4, :], [b_pM], [b_kT5])
                    cp("dve", kT5[0:64, 4, :], pM[0:64, 4, :], [b_pM], [b_kT5])
                    yield
                    S.dma("sp", KT_d[:, :, tsl].rearrange("c p s -> p c s"), kT5[:, 0:4, :], reads=[b_kT5], writes=[b_KT], sem="kT_st")
                    S.dma("sp", KPE_d[:, tsl], kT5[0:64, 4, :], reads=[b_kT5], writes=[b_KPE], sem="kT_st2")
                    for h in range(4):
                        base = h * 256
                        ts("dve", vt[:, h, :], kvs[:, base + 128:base + 256], rs3[:, 1:2], None, ALU.mult, None, [b_kvs, b_rs3], [b_vt])
                        if h % 2 == 1:
                            yield
                    S.dma("sp", V_d[tsl, :], vt[:].rearrange("p h e -> p (h e)"), reads=[b_vt], writes=[b_V], sem="vt_st")

                if "p1" not in SKIP:
                    issue_loads(0)
                    if NT > 1:
                        issue_loads(1)
                    drive([prologue(0)])
                for t in range(NT if "p1" not in SKIP else 0):
                    if t + 2 < NT:
                        issue_loads(t + 2)
                    gens = [gla_chain(t), mla_q(t), mla_kv(t)]
                    if t + 1 < NT:
                        gens.append(prologue(t + 1))
                    drive(gens)
            S.barrier()
            if stop and (stop.startswith('p1') or stop.startswith('q') or stop.startswith('c')):
                break

            ph23 = contextlib.ExitStack()
            phw = contextlib.ExitStack()
            uid[0] += 1
            wout = phw.enter_context(nc.sbuf_tensor("wout_s%d" % uid[0], [128, 8, D], BF16, side="right"))
            b_woutk = [S.buf("wout%d" % k) for k in range(8)]
            for k in range(8):
                S.dma("pool", wout[:, k, :], wout_in[l, k * 128:(k + 1) * 128, :], writes=[b_woutk[k]], sem="wout_ld", nobar=True)
            with contextlib.ExitStack() as ph:
                KTs = [SB(ph, "KTs%d" % i, [128, S_LEN], BF16) for i in range(2)]
                Vh = [SB(ph, "Vh%d" % i, [128, NT, 128], BF16) for i in range(2)]
                KPEs, b_KPEs = SB(ph, "KPEs", [64, S_LEN], BF16)
                qns = [SB(ph, "qns%d" % i, [128, 512], BF16) for i in range(2)]
                qps = [SB(ph, "qps%d" % i, [64, 512], BF16) for i in range(2)]
                pTs = [SB(ph, "pTs%d" % i, [128, 512], BF16) for i in range(4)]
                rl, b_rl = SB(ph, "rl", [128, 512], F32)
                ob = [SB(ph, "ob%d" % i, [128, 512], BF16) for i in range(2)]
                pS = [PS(ph, "pS%d" % i, [128, 512], F32) for i in range(4)]
                pO = [PS(ph, "pO%d" % i, [128, 512], F32) for i in range(2)]
                pL = [PS(ph, "pL%d" % i, [128, 512], F32) for i in range(2)]
                S.dma("sp", KPEs[:], KPE_d, reads=[b_KPE], writes=[b_KPEs])
                spl2 = split and last
                if spl2:
                    maskSel, b_maskSel = SB(ph, "maskSel", [128, 8, 512], BF16)
                    for r in range(4):
                        act(maskSel[:, r, :], maskD[:, r, :], AF.Identity, [b_maskD, b_par], [b_maskSel],
                            scale=par[:, 0:1], bias=par[:, 1:2])
                        act(maskSel[:, 4 + r, :], maskD[:, r, :], AF.Identity, [b_maskD, b_par], [b_maskSel], scale=par[:, 1:2])
                    qna = [SB(ph, "qna%d" % i, [128, 512], BF16) for i in range(2)]
                    qnb = [SB(ph, "qnb%d" % i, [128, 512], BF16) for i in range(2)]
                    qpa = [SB(ph, "qpa%d" % i, [64, 512], BF16) for i in range(2)]
                    qpb = [SB(ph, "qpb%d" % i, [64, 512], BF16) for i in range(2)]
                NPOS = NC4 // 2 if spl2 else NC4

                def nkb_of(c):
                    return 8 * c + 8 if spl2 else 4 * c + 4

                chunks = [(h, c) for h in range(4 if "p2" not in SKIP else 0) for c in range(NPOS)]
                blocks = []
                for idx, (h, c) in enumerate(chunks):
                    for kb in range(nkb_of(c)):
                        blocks.append((idx, kb, nkb_of(c)))

                def load_head(h):
                    KTh, b_KTh = KTs[h % 2]
                    Vhh, b_Vhh = Vh[h % 2]
                    S.dma("sp", KTh[:], KT_d[h], reads=[b_KT], writes=[b_KTh])
                    S.dma("sp", Vhh[:], V_d[:, h * 128:(h + 1) * 128].rearrange("(t p) e -> p t e", p=128), reads=[b_V], writes=[b_Vhh])

                def load_q(idx):
                    h, c = chunks[idx]
                    hp, off = h // 2, 64 * (h % 2)
                    if not spl2:
                        csl = slice(c * 512, (c + 1) * 512)
                        S.dma("sp", qns[idx % 2][0][:], QT_d[h, :, csl], reads=[b_QT], writes=[qns[idx % 2][1]])
                        S.dma("sp", qps[idx % 2][0][:], QPE_d[hp, off:off + 64, csl], reads=[b_QPE], writes=[qps[idx % 2][1]])
                        return
                    sla = slice(2 * c * 512, (2 * c + 1) * 512)
                    slb = slice((2 * c + 1) * 512, (2 * c + 2) * 512)
                    qa, b_qa = qna[idx % 2]
                    qb, b_qb = qnb[idx % 2]
                    pa, b_pa = qpa[idx % 2]
                    pb, b_pb = qpb[idx % 2]
                    qs, b_qs = qns[idx % 2]
                    ps_, b_ps = qps[idx % 2]
                    S.dma("sp", qa[:], QT_d[h, :, sla], reads=[b_QT], writes=[b_qa])
                    S.dma("sp", qb[:], QT_d[h, :, slb], reads=[b_QT], writes=[b_qb])
                    S.dma("sp", pa[:], QPE_d[hp, off:off + 64, sla], reads=[b_QPE], writes=[b_pa])
                    S.dma("sp", pb[:], QPE_d[hp, off:off + 64, slb], reads=[b_QPE], writes=[b_pb])
                    ts("dve", qs[:], qa[:], par[:, 0:1], None, ALU.mult, None, [b_qa, b_par], [b_qs])
                    stt("dve", qs[:], qb[:], par[:, 1:2], qs[:], ALU.mult, ALU.add, [b_qb, b_par, b_qs], [b_qs])
                    ts("dve", ps_[:], pa[:], par[0:64, 0:1], None, ALU.mult, None, [b_pa, b_par], [b_ps])
                    stt("dve", ps_[:], pb[:], par[0:64, 1:2], ps_[:], ALU.mult, ALU.add, [b_pb, b_par, b_ps], [b_ps])

                LA = 3
                if chunks:
                    load_head(0)
                    load_q(0)
                for i in range((len(blocks) + LA) if blocks else 0):
                    if i < len(blocks):
                        idx, kb, nkb = blocks[i]
                        h, c = chunks[idx]
                        if kb == 0 and idx + 1 < len(chunks):
                            load_q(idx + 1)
                        KTh, b_KTh = KTs[h % 2]
                        qn_, b_qn_ = qns[idx % 2]
                        qp_, b_qp_ = qps[idx % 2]
                        ksl = slice(kb * 128, (kb + 1) * 128)
                        pSt, b_pSt = pS[i % 4]
                        pTt, b_pTt = pTs[i % 4]
                        mm(pSt[:], KTh[:, ksl], qn_[:], True, False, [b_KTh, b_qn_], [b_pSt])
                        mm(pSt[:], KPEs[:, ksl], qp_[:], False, True, [b_KPEs, b_qp_], [b_pSt])
                        act(pTt[:], pSt[:], AF.Exp, [b_pSt], [b_pTt])
                        if spl2:
                            r = kb - 8 * c
                            if r >= 0:
                                tt("pool", pTt[:], pTt[:], maskSel[:, r, :], ALU.mult, [b_pTt, b_maskSel], [b_pTt])
                        else:
                            r = kb - 4 * c
                            if r >= 0:
                                tt("pool", pTt[:], pTt[:], maskD[:, r, :], ALU.mult, [b_pTt, b_maskD], [b_pTt])
                    if i >= LA:
                        j = i - LA
                        idx, kb, nkb = blocks[j]
                        h, c = chunks[idx]
                        Vhh, b_Vhh = Vh[h % 2]
                        pTt, b_pTt = pTs[j % 4]
                        pOt, b_pOt = pO[idx % 2]
                        pLt, b_pLt = pL[idx % 2]
                        mm(pOt[:], Vhh[:, kb, :], pTt[:], kb == 0, kb == nkb - 1, [b_Vhh, b_pTt], [b_pOt])
                        mm(pLt[:], ones_b[:], pTt[:], kb == 0, kb == nkb - 1, [b_onesb, b_pTt], [b_pLt])
                        if kb == 0 and c == 0 and h + 1 < 4:
                            load_head(h + 1)
                        if kb == nkb - 1:
                            obt, b_obt = ob[idx % 2]
                            csl = slice(c * 512, (c + 1) * 512)
                            recip(rl[:], pLt[:], [b_pLt], [b_rl])
                            tt("dve", obt[:], pOt[:], rl[:], ALU.mult, [b_pOt, b_rl], [b_obt])
                            S.dma("pool", mixT_d[4 + h, :, csl], obt[:], reads=[b_obt], writes=[b_mixT], sem="ob_st%d" % (idx % 2))
            S.barrier()
            if stop == 'p2':
                phw.close()
                ph23.close()
                break
            w1, _ = SB(ph23, "w1", [128, 8, DFF], BF16)
            w2, _ = SB(ph23, "w2", [128, 32, D], BF16)
            b_w1k = [S.buf("w1_%d" % k) for k in range(8)]
            b_w2f = [S.buf("w2_%d" % f) for f in range(32)]
            for k in range(8):
                for q in range(2):
                    S.dma("pool", w1[:, k, q * 2048:(q + 1) * 2048], w1_in[l, k * 128:(k + 1) * 128, q * 2048:(q + 1) * 2048],
                          writes=[b_w1k[k]], sem="w1_ld", nobar=True)
            for f in range(32):
                S.dma("pool", w2[:, f, :], w2_in[l, f * 128:(f + 1) * 128, :], writes=[b_w2f[f]], sem="w2_ld", nobar=True)

            with contextlib.ExitStack() as ph:
                gta, b_gta = SB(ph, "gta", [128, D], F32)
                S.dma("sp", gta[:], gate_d[l, 0], reads=[b_gate], writes=[b_gta])
                mx = [SB(ph, "mx%d" % i, [128, 8, 128], BF16) for i in range(2)]
                xt = [SB(ph, "xt%d" % i, [128, D], F32) for i in range(2)]
                xo = [SB(ph, "xo%d" % i, [128, D], F32) for i in range(2)]
                pW = [PS(ph, "pW%d" % i, [128, 512], F32) for i in range(4)]
                spl = split and last
                if spl:
                    mxb = [SB(ph, "mxb%d" % i, [128, 8, 128], BF16) for i in range(2)]
                    xtb = [SB(ph, "xtb%d" % i, [128, D], F32) for i in range(2)]
                    mxs = [SB(ph, "mxs%d" % i, [128, 8, 128], BF16) for i in range(2)]
                    xss = [SB(ph, "xss%d" % i, [128, D], F32) for i in range(2)]
                NT3 = (NT // 2 if spl else NT) if "p3a" not in SKIP else 0
                for t in range(NT3):
                    tsl = slice(t * 128, (t + 1) * 128)
                    tg = (8 * (t // 4) + t % 4) if spl else t
                    tsl0 = slice(tg * 128, (tg + 1) * 128)
                    tsl1 = slice((tg + 4) * 128, (tg + 5) * 128)
                    mxt, b_mx = mx[t % 2]
                    xtt, b_xt = xt[t % 2]
                    xot, b_xo = xo[t % 2]
                    S.dma("sp", mxt[:], mixT_d[:, :, tsl0].rearrange("c p s -> p c s"), reads=[b_mixT], writes=[b_mx])
                    S.dma("sp", xtt[:], x_cur[tsl0, :], reads=([b_xcur] if b_xcur else []), writes=[b_xt])
                    if spl:
                        mxbt, b_mxb = mxb[t % 2]
                        xtbt, b_xtb = xtb[t % 2]
                        mxst, b_mxs = mxs[t % 2]
                        xsst, b_xss = xss[t % 2]
                        S.dma("sp", mxbt[:], mixT_d[:, :, tsl1].rearrange("c p s -> p c s"), reads=[b_mixT], writes=[b_mxb])
                        S.dma("sp", xtbt[:], x_cur[tsl1, :], reads=([b_xcur] if b_xcur else []), writes=[b_xtb])
                        act(mxst[:, 0:4, :], mxt[:, 0:4, :], AF.Identity, [b_mx, b_par], [b_mxs], scale=par[:, 0:1])
                        stt("dve", mxst[:, 0:4, :], mxbt[:, 0:4, :], par[:, 1:2], mxst[:, 0:4, :], ALU.mult, ALU.add,
                            [b_mxb, b_par, b_mxs], [b_mxs])
                        S.dma("sp", mxst[:, 4:8, :], mixT_d[4:8, :, tsl].rearrange("c p s -> p c s"), reads=[b_mixT], writes=[b_mxs])
                        act(xsst[:], xtt[:], AF.Identity, [b_xt, b_par], [b_xss], scale=par[:, 0:1])
                        stt("dve", xsst[:], xtbt[:], par[:, 1:2], xsst[:], ALU.mult, ALU.add, [b_xtb, b_par, b_xss], [b_xss])
                        mxt, b_mx = mxst, b_mxs
                        xtt, b_xt = xsst, b_xss
                    for half in range(2):
                        pw, b_pw = pW[(t % 2) * 2 + half]
                        hs = slice(half * 512, (half + 1) * 512)
                        for c in range(8):
                            mm(pw[:], mxt[:, c, :], wout[:, c, hs], c == 0, c == 7, [b_mx, b_woutk[c]], [b_pw])
                        tt("dve", xot[:, hs], pw[:], gta[:, hs], ALU.mult, [b_pw, b_gta], [b_xo])
                        tt("pool", xot[:, hs], xot[:, hs], xtt[:, hs], ALU.add, [b_xo, b_xt], [b_xo])
                    S.dma("pool", xmid_d[tsl, :], xot[:], reads=[b_xo], writes=[b_xmid], sem="xo_st%d" % (t % 2))
            S.barrier()
            phw.close()
            if stop == 'p3a':
                ph23.close()
                break

            with contextlib.ExitStack() as ph:
                gtf, b_gtf = SB(ph, "gtf", [128, D], F32)
                S.dma("sp", gtf[:], gate_d[l, 1], reads=[b_gate], writes=[b_gtf])
                xm = [SB(ph, "xm%d" % i, [128, 2, D], F32) for i in range(3)]
                ss, b_ss = SB(ph, "ss", [128, 2], F32)
                xn2 = [SB(ph, "xn2_%d" % i, [128, 2, D], BF16) for i in range(2)]
                h2Ts = [SB(ph, "h2T%d" % i, [128, 8, 256], BF16) for i in range(2)]
                aT, b_aT = SB(ph, "aT", [128, 32, 256], BF16)
                rt = [SB(ph, "rt%d" % i, [128, 256], F32) for i in range(2)]
                yo = [SB(ph, "yo%d" % i, [128, D], F32) for i in range(2)]
                pTs2 = [PS(ph, "pT%d" % i, [128, 8, 128], BF16) for i in range(2)]
                pU = [PS(ph, "pU%d" % i, [128, 256], F32) for i in range(2)]
                pDn = [PS(ph, "pDn%d" % i, [128, 512], F32) for i in range(4)]

                def prep_norm(g):
                    xmt, b_xm = xm[g % 3]
                    xnt, b_xnt = xn2[g % 2]
                    for j in range(2):
                        t = 2 * g + j
                        S.dma("sp", xmt[:, j, :], xmid_d[t * 128:(t + 1) * 128, :], reads=[b_xmid], writes=[b_xm])
                    for j in range(2):
                        act(xnt[:, j, :], xmt[:, j, :], AF.Square, [b_xm], [b_xnt, b_ss], accum=ss[:, j:j + 1])
                    rsqrt_cols(ss[:], ss[:], 1.0 / D, [b_ss], [b_ss])
                    for j in range(2):
                        ts("dve", xnt[:, j, :], xmt[:, j, :], ss[:, j:j + 1], None, ALU.mult, None, [b_xm, b_ss], [b_xnt])

                def prep_tr(g):
                    xnt, b_xnt = xn2[g % 2]
                    h2T, b_h2T = h2Ts[g % 2]
                    for j in range(2):
                        pT, b_pT = pTs2[j]
                        for k in range(8):
                            tr(pT[:, k, :], xnt[:, j, k * 128:(k + 1) * 128], ident[:], [b_xnt, b_ident], [b_pT])
                        for k in range(8):
                            if k % 2 == 0:
                                ts("dve", h2T[:, k, j * 128:(j + 1) * 128], pT[:, k, :], modcol[:, mc + 24 + k: mc + 25 + k],
                                   modcol[:, mc + 16 + k: mc + 17 + k], ALU.mult, ALU.add, [b_pT, b_modcol], [b_h2T])
                            else:
                                act(h2T[:, k, j * 128:(j + 1) * 128], pT[:, k, :], AF.Identity, [b_pT, b_modcol], [b_h2T],
                                    scale=modcol[:, mc + 24 + k: mc + 25 + k], bias=modcol[:, mc + 16 + k: mc + 17 + k])

                yi = 0
                NGE = NG // 2 if (split and last) else NG
                prep_norm(0)
                prep_tr(0)
                for g in range(NGE):
                    xmt, b_xm = xm[g % 3]
                    h2T, b_h2T = h2Ts[g % 2]
                    if g + 1 < NGE:
                        prep_norm(g + 1)
                    for f in range(32):
                        pu, b_pu = pU[f % 2]
                        rtt, b_rt = rt[f % 2]
                        for k in range(8):
                            mm(pu[:], w1[:, k, f * 128:(f + 1) * 128], h2T[:, k, :], k == 0, k == 7, [b_w1k[k], b_h2T], [b_pu])
                        act(rtt[:], pu[:], AF.Relu, [b_pu], [b_rt])
                        tt("dve" if f % 2 == 0 else "pool", aT[:, f, :], rtt[:], rtt[:], ALU.mult, [b_rt], [b_aT])
                    if g + 1 < NGE:
                        prep_tr(g + 1)
                    for j in range(2):
                        t = 2 * g + j
                        tsl = slice(t * 128, (t + 1) * 128)
                        yot, b_yo = yo[yi % 2]
                        yi += 1
                        for half in range(2):
                            pd, b_pd = pDn[j * 2 + half]
                            hs = slice(half * 512, (half + 1) * 512)
                            for f in range(32):
                                mm(pd[:], aT[:, f, j * 128:(j + 1) * 128], w2[:, f, hs], f == 0, f == 31, [b_aT, b_w2f[f]], [b_pd])
                            tt("dve", yot[:, hs], pd[:], gtf[:, hs], ALU.mult, [b_pd, b_gtf], [b_yo])
                            tt("pool", yot[:, hs], yot[:, hs], xmt[:, j, hs], ALU.add, [b_yo, b_xm], [b_yo])
                        d = S.dma("pool", x_nxt[tsl, :], yot[:], reads=[b_yo], writes=[b_xnxt], sem="yo_st%d" % ((yi - 1) % 2))
                        if last:
                            out_dmas.append(d)
            S.barrier()
            ph23.close()

        S.emit(final_waits=out_dmas)
    return nc


def host_inputs(b, S_LEN, DEPTH, x, c, positions, _parity=0, *, w_ada, b_ada, w_in, w_gate_up, b_gate, gla_out_norm, q_a_norm,
                w_q_up, kv_a_norm, w_kv_up, q_norm_nope, k_norm_nope, q_norm_rope, k_norm_rope,
                w_out, w_mlp_up, w_mlp_down):
    f32 = np.float32
    NT = S_LEN // 128
    A = lambda a: np.ascontiguousarray(np.asarray(a))

    def bc(v, n=128):
        v = np.asarray(v, dtype=f32)
        return A(np.broadcast_to(v[:, None, :], (v.shape[0], n, v.shape[1])))

    inv_freq = (10000.0 ** (-np.arange(0, 64, 2, dtype=f32) / f32(64))).astype(f32)
    b_ada = np.asarray(b_ada, dtype=f32)
    d = {
        "x": A(np.asarray(x[b], dtype=f32)),
        "ccol": A(np.asarray(c[b], dtype=f32).reshape(8, 128).T),
        "pos": A(np.asarray(positions[b]).astype(np.int32).reshape(NT, 128).T),
        "invf": A(np.broadcast_to(inv_freq[None, :], (128, 32))),
        "parcol": A(np.broadcast_to(np.array([[1.0 - _parity, float(_parity)]], dtype=f32), (128, 2))),
        "w_ada": A(np.asarray(w_ada, dtype=f32)),
        "bada_col": A(b_ada.reshape(DEPTH, 48, 128).transpose(0, 2, 1)),
        "bada_gate": A(np.broadcast_to(b_ada.reshape(DEPTH, 6, 1, D)[:, [2, 5]], (DEPTH, 2, 128, D))),
        "w_in": A(np.asarray(w_in, dtype=f32)),
        "w_gate_up": A(np.asarray(w_gate_up, dtype=f32)),
        "bgate_bc": bc(b_gate),
        "gon_bc4": bc(np.tile(np.asarray(gla_out_norm, dtype=f32), (1, 4))),
        "qan_col": A(np.asarray(q_a_norm, dtype=f32).reshape(DEPTH, 2, 128).transpose(0, 2, 1)),
        "w_q_up": A(np.asarray(w_q_up, dtype=f32)),
        "kvan_col": A(np.asarray(kv_a_norm, dtype=f32).reshape(DEPTH, 128, 1)),
        "w_kv_up": A(np.asarray(w_kv_up, dtype=f32)),
        "qnn_bc": bc(q_norm_nope), "knn_bc": bc(k_norm_nope),
        "qnr_bc": bc(q_norm_rope), "knr_bc": bc(k_norm_rope),
        "w_out": A(np.asarray(w_out, dtype=f32)),
        "w_mlp_up": A(np.asarray(w_mlp_up, dtype=f32)),
        "w_mlp_down": A(np.asarray(w_mlp_down, dtype=f32)),
    }
    return d


_NC_CACHE = {}


def kernel(**inputs):
    x = np.asarray(inputs["x"])
    B, S_LEN, _ = x.shape
    DEPTH = np.asarray(inputs["w_ada"]).shape[0]
    key = (S_LEN, DEPTH)
    if key not in _NC_CACHE:
        _NC_CACHE[key] = build(S_LEN, DEPTH, split=True)
    nc = _NC_CACHE[key]
    in_maps = []
    for b in range(B):
        base = host_inputs(b, S_LEN, DEPTH, _parity=0, **inputs)
        in_maps.append(base)
        m1 = dict(base)
        m1["parcol"] = host_inputs_par(1)
        in_maps.append(m1)
    res = run_bass_kernel_spmd(nc, in_maps, core_ids=list(range(2 * B)))
    out = np.empty((B, S_LEN, D), dtype=np.float32)
    ov = out.reshape(B, S_LEN // 1024, 2, 512, D)
    for b in range(B):
        for p in range(2):
            ov[b, :, p] = np.asarray(res.results[2 * b + p]["y"], dtype=np.float32).reshape(S_LEN // 1024, 512, D)
    return out


def host_inputs_par(p):
    return np.ascontiguousarray(np.broadcast_to(np.array([[1.0 - p, float(p)]], dtype=np.float32), (128, 2)))
```

```python
import contextlib
import math
import numpy as np
import concourse.bass as bass
import concourse.mybir as mybir
from concourse.bass_utils import run_bass_kernel_spmd

F32 = mybir.dt.float32
BF16 = mybir.dt.bfloat16
I32 = mybir.dt.int32
ALU = mybir.AluOpType
AF = mybir.ActivationFunctionType

ENGS = ("pe", "act", "dve", "pool", "sp")
EST_DUR = {"pe": 0.15, "act": 0.3, "dve": 0.3, "pool": 0.5, "sp": 0.1}
import os as _os
SEM_SHARE = _os.environ.get("SEM_SHARE", "0") == "1"
SEQ_CHAINS = _os.environ.get("SEQ_CHAINS", "0") == "1"
NO_KPR = _os.environ.get("NO_KPR", "0") == "1"
SKIP = set(_os.environ.get("SKIP_PHASES", "").split(","))
OLD_ORDER = _os.environ.get("OLD_ORDER", "0") == "1"
SEM_MAP = {"wm0": "A0", "wm1": "A1", "xt0": "A0", "xt1": "A1", "xt2": "A2", "cst0": "B0", "cst1": "B1", "cst2": "B2",
           "KTs0": "A0", "KTs1": "A1", "Vh0": "B0", "Vh1": "B1", "qns0": "C0", "qns1": "C1",
           "qps0": "D0", "qps1": "D1", "mx0": "B0", "mx1": "B1", "mx2": "B2", "xm0": "A0", "xm1": "A1", "xm2": "A2",
           "ob_st0": "ST0", "ob_st1": "ST1", "xo_st0": "ST0", "xo_st1": "ST1", "yo_st0": "ST0", "yo_st1": "ST1"}


class Buf:
    __slots__ = ("name", "lw", "rd", "rd_dma")

    def __init__(self, name):
        self.name = name
        self.lw = None
        self.rd = {}
        self.rd_dma = []


class Ins:
    __slots__ = ("eng", "fn", "deps", "signal", "cnt", "is_dma", "dsem", "dval", "tfin")

    def __init__(self, eng, fn, is_dma=False):
        self.eng = eng
        self.fn = fn
        self.deps = []
        self.signal = False
        self.cnt = 0
        self.is_dma = is_dma
        self.dsem = None
        self.dval = 0
        self.tfin = 0.0


class Sched:
    def __init__(self, nc):
        self.nc = nc
        self.q = {e: [] for e in ENGS}
        self.dma_sems = {}
        self.all_dma = []
        self.last_on_sem = {}
        self.bar = None
        self.bar_done = {}
        self.nbuf = 0
        self.eng_free = {e: 0.0 for e in ENGS}
        self.step_max = 0.0

    def buf(self, name=None):
        self.nbuf += 1
        return Buf(name or ("b%d" % self.nbuf))

    def _collect(self, ins, reads, writes):
        deps = {}
        pe = (ins.eng == "pe" and not ins.is_dma)

        def add(d):
            if d is None or d is ins:
                return
            if pe and d.eng == "pe" and not d.is_dma:
                return
            if d.is_dma:
                d = self.last_on_sem[d.dsem]
                if d is ins:
                    return
            deps[id(d)] = d

        for b in reads:
            add(b.lw)
        for b in writes:
            add(b.lw)
            for d in b.rd.values():
                add(d)
            for d in b.rd_dma:
                add(d)
        if self.bar is not None and not self.bar_done.get(ins.eng):
            for d in self.bar:
                add(d)
            self.bar_done[ins.eng] = True
        for b in reads:
            if ins.is_dma:
                b.rd_dma.append(ins)
            else:
                b.rd[ins.eng] = ins
        for b in writes:
            b.lw = ins
            b.rd = {}
            b.rd_dma = []
        ins.deps = list(deps.values())
        ready = max([d.tfin for d in ins.deps], default=0.0) + (0.25 if ins.deps else 0.0)
        start = max(ready, self.eng_free[ins.eng])
        if ins.is_dma:
            self.eng_free[ins.eng] = start + 0.1
            ins.tfin = start + 2.5
        else:
            ins.tfin = start + EST_DUR[ins.eng]
            self.eng_free[ins.eng] = ins.tfin
        if ins.tfin > self.step_max:
            self.step_max = ins.tfin

    def op(self, eng, fn, reads=(), writes=()):
        ins = Ins(eng, fn)
        self._collect(ins, reads, writes)
        self.q[eng].append(ins)
        return ins

    def dma(self, eng, out, in_, reads=(), writes=(), sem=None, nobar=False):
        if sem is None:
            sem = (list(writes) + list(reads))[0].name
        if SEM_SHARE:
            sem = SEM_MAP.get(sem, "G_const" if not sem.endswith("_st") else "ST")
        ins = Ins(eng, lambda e: e.dma_start(out=out, in_=in_), is_dma=True)
        tot = self.dma_sems.get(sem, 0) + 16
        self.dma_sems[sem] = tot
        ins.dsem = sem
        ins.dval = tot
        self._collect(ins, reads, writes)
        self.last_on_sem[sem] = ins
        self.q[eng].append(ins)
        if not nobar:
            self.all_dma.append(ins)
        return ins

    def barrier(self):
        deps = []
        for e in ENGS:
            for ins in reversed(self.q[e]):
                if not ins.is_dma:
                    deps.append(ins)
                    break
        deps.extend(self.all_dma)
        self.all_dma = []
        last = {}
        keep = []
        for d in deps:
            if d.is_dma:
                if d.dsem not in last or last[d.dsem].dval < d.dval:
                    last[d.dsem] = d
            else:
                keep.append(d)
        self.bar = keep + list(last.values())
        self.bar_done = {}

    def emit(self, final_waits=()):
        nc = self.nc
        for e in ENGS:
            for ins in self.q[e]:
                for d in ins.deps:
                    if not d.is_dma:
                        d.signal = True
        for e in ENGS:
            c = 0
            for ins in self.q[e]:
                if ins.signal and not ins.is_dma:
                    c += 1
                ins.cnt = c
        with contextlib.ExitStack() as st:
            esem = {e: st.enter_context(nc.semaphore("es_" + e)) for e in ENGS}
            dsem = {k: st.enter_context(nc.semaphore("ds_%d" % i)) for i, k in enumerate(self.dma_sems)}
            block = st.enter_context(nc.Block())
            sched = self

            def run(e, eh):
                waited = {}
                for ins in sched.q[e]:
                    need = {}
                    for d in ins.deps:
                        if d.is_dma:
                            s, v, key = dsem[d.dsem], d.dval, ("d", d.dsem)
                        else:
                            s, v, key = esem[d.eng], d.cnt, ("e", d.eng)
                        if key not in need or need[key][1] < v:
                            need[key] = (s, v)
                    for key, (s, v) in need.items():
                        if waited.get(key, 0) >= v:
                            continue
                        waited[key] = v
                        eh.wait_ge(s, v)
                    bi = ins.fn(eh)
                    if ins.is_dma:
                        bi.then_inc(dsem[ins.dsem], 16)
                    elif ins.signal:
                        bi.then_inc(esem[e], 1)
                if e == "sp":
                    for d in final_waits:
                        eh.wait_ge(dsem[d.dsem], d.dval)

            @block.tensor
            def _(eh):
                run("pe", eh)

            @block.scalar
            def _(eh):
                run("act", eh)

            @block.vector
            def _(eh):
                run("dve", eh)

            @block.gpsimd
            def _(eh):
                run("pool", eh)

            @block.sync
            def _(eh):
                run("sp", eh)


D = 1024
DFF = 4096
EPS = 1e-6
TWO_PI = 2.0 * math.pi
C1 = 6.28125
C2 = TWO_PI - C1


def build(S_LEN, DEPTH, dbg=False, stop=None, split=False):
    NT = S_LEN // 128
    NC4 = S_LEN // 512
    NG = S_LEN // 256
    nc = bass.Bass("TRN2", target_bir_lowering=False)
    S = Sched(nc)

    def din(name, shape, dt=F32):
        return nc.dram_tensor(name, shape, dt, kind="ExternalInput").ap()

    def dscr(name, shape, dt):
        return nc.dram_tensor(name, shape, dt, kind=("ExternalOutput" if dbg else "Internal")).ap()

    x_in = din("x", [S_LEN, D])
    ccol = din("ccol", [128, 8])
    pos_in = din("pos", [128, NT], I32)
    invf_in = din("invf", [128, 32])
    par_in = din("parcol", [128, 2])
    w_ada = din("w_ada", [DEPTH, D, 6 * D])
    bada_col = din("bada_col", [DEPTH, 128, 48])
    bada_gate = din("bada_gate", [DEPTH, 2, 128, D])
    w_in = din("w_in", [DEPTH, D, 2000])
    wgu_in = din("w_gate_up", [DEPTH, 16, 256])
    bgate_in = din("bgate_bc", [DEPTH, 128, 256])
    gon_in = din("gon_bc4", [DEPTH, 128, 512])
    qan_in = din("qan_col", [DEPTH, 128, 2])
    wqu_in = din("w_q_up", [DEPTH, 256, 768])
    kvan_in = din("kvan_col", [DEPTH, 128, 1])
    wkvu_in = din("w_kv_up", [DEPTH, 128, 1024])
    qnn_in = din("qnn_bc", [DEPTH, 128, 128])
    knn_in = din("knn_bc", [DEPTH, 128, 128])
    qnr_in = din("qnr_bc", [DEPTH, 128, 64])
    knr_in = din("knr_bc", [DEPTH, 128, 64])
    wout_in = din("w_out", [DEPTH, D, D])
    w1_in = din("w_mlp_up", [DEPTH, D, DFF])
    w2_in = din("w_mlp_down", [DEPTH, DFF, D])
    y_out = nc.dram_tensor("y", [S_LEN // 2 if split else S_LEN, D], F32, kind="ExternalOutput").ap()

    xs = [dscr("xs%d" % i, [S_LEN, D], F32) for i in range(2)]
    xmid_d = dscr("xmid", [S_LEN, D], F32)
    mixT_d = dscr("mixT", [8, 128, S_LEN], BF16)
    KT_d = dscr("KT", [4, 128, S_LEN], BF16)
    KPE_d = dscr("KPE", [64, S_LEN], BF16)
    V_d = dscr("Vd", [S_LEN, 512], BF16)
    QT_d = dscr("QT", [4, 128, S_LEN], BF16)
    QPE_d = dscr("QPE", [2, 128, S_LEN], BF16)
    cos_d = dscr("cosd", [128, NT, 32], F32)
    sin_d = dscr("sind", [128, NT, 32], F32)
    gate_d = dscr("gated", [DEPTH, 2, 128, D], F32)
    b_xs = [S.buf("xs0"), S.buf("xs1")]
    b_xmid = S.buf("xmid")
    b_mixT = S.buf("mixT")
    b_KT, b_KPE, b_V, b_QT, b_QPE = S.buf("KT"), S.buf("KPE"), S.buf("V"), S.buf("QT"), S.buf("QPE")
    b_cos, b_sin, b_gate = S.buf("cosd"), S.buf("sind"), S.buf("gated")
    b_y = S.buf("y")
    out_dmas = []

    def mm(out, lhsT, rhs, start, stop, R, W):
        S.op("pe", lambda e: e.matmul(out=out, lhsT=lhsT, rhs=rhs, start=start, stop=stop), R, W)

    def tr(out, in_, ident, R, W):
        S.op("pe", lambda e: e.transpose(out=out, in_=in_, identity=ident), R, W)

    def act(out, in_, func, R, W, scale=1.0, bias=0.0, accum=None):
        if accum is None:
            S.op("act", lambda e: e.activation(out=out, in_=in_, func=func, bias=bias, scale=scale), R, W)
        else:
            S.op("act", lambda e: e.activation(out=out, in_=in_, func=func, bias=bias, scale=scale, accum_out=accum), R, W)

    def tt(eng, out, in0, in1, op, R, W):
        S.op(eng, lambda e: e.tensor_tensor(out=out, in0=in0, in1=in1, op=op), R, W)

    def ts(eng, out, in0, s1, s2, op0, op1, R, W):
        if s2 is None:
            S.op(eng, lambda e: e.tensor_scalar(out=out, in0=in0, scalar1=s1, scalar2=None, op0=op0), R, W)
        else:
            S.op(eng, lambda e: e.tensor_scalar(out=out, in0=in0, scalar1=s1, scalar2=s2, op0=op0, op1=op1), R, W)

    def stt(eng, out, in0, scalar, in1, op0, op1, R, W, accum=None):
        if accum is None:
            S.op(eng, lambda e: e.scalar_tensor_tensor(out=out, in0=in0, scalar=scalar, in1=in1, op0=op0, op1=op1), R, W)
        else:
            S.op(eng, lambda e: e.scalar_tensor_tensor(out=out, in0=in0, scalar=scalar, in1=in1, op0=op0, op1=op1, accum_out=accum), R, W)

    def cp(eng, out, in_, R, W):
        if eng == "act":
            S.op("act", lambda e: e.copy(out=out, in_=in_), R, W)
        else:
            S.op(eng, lambda e: e.tensor_copy(out=out, in_=in_), R, W)

    def recip(out, in_, R, W):
        S.op("dve", lambda e: e.reciprocal(out=out, in_=in_), R, W)

    def memset(eng, ap, val, W):
        S.op(eng, lambda e: e.memset(ap, val), (), W)

    def asel(out, in_, pattern, cmp, fill, base, cm, R, W):
        S.op("pool", lambda e: e.affine_select(out=out, in_=in_, pattern=pattern, compare_op=cmp, fill=fill,
                                               base=base, channel_multiplier=cm), R, W)

    def rsqrt_cols(dst, src, scale, R, W):
        act(dst, src, AF.Ln, R, W, scale=scale, bias=EPS)
        act(dst, dst, AF.Exp, W, W, scale=-0.5)

    with contextlib.ExitStack() as top:
        uid = [0]

        def SB(stack, name, shape, dt):
            uid[0] += 1
            t = stack.enter_context(nc.sbuf_tensor("%s_s%d" % (name, uid[0]), shape, dt))
            return t, S.buf(name)

        def PS(stack, name, shape, dt):
            uid[0] += 1
            t = stack.enter_context(nc.psum_tensor("%s_p%d" % (name, uid[0]), shape, dt))
            return t, S.buf(name)

        ident, b_ident = SB(top, "ident", [128, 128], BF16)
        maskU4, b_maskU4 = SB(top, "maskU4", [128, 4, 128], BF16)
        ones_f, b_onesf = SB(top, "ones_f", [128, 128], F32)
        ones_b, b_onesb = SB(top, "ones_b", [128, 128], BF16)
        maskD, b_maskD = SB(top, "maskD", [128, 4, 512], BF16)
        modcol, b_modcol = SB(top, "modcol", [128, DEPTH * 32], F32)

        par, b_par = SB(top, "par", [128, 2], F32)
        S.dma("sp", par[:], par_in, writes=[b_par])
        memset("pool", ident[:], 0.0, [b_ident])
        asel(ident[:], ident[:], [[-1, 128]], ALU.not_equal, 1.0, 0, 1, [b_ident], [b_ident])
        memset("pool", maskU4[:], 1.0, [b_maskU4])
        for h in range(4):
            asel(maskU4[:, h, :], maskU4[:, h, :], [[1, 128]], ALU.is_ge, 0.0, 0, -1, [b_maskU4], [b_maskU4])
        memset("pool", ones_f[:], 1.0, [b_onesf])
        memset("pool", ones_b[:], 1.0, [b_onesb])
        memset("pool", maskD[:], 1.0, [b_maskD])
        for r in range(4):
            asel(maskD[:, r, :], maskD[:, r, :], [[1, 512]], ALU.is_ge, 0.0, -128 * r, -1, [b_maskD], [b_maskD])

        with contextlib.ExitStack() as ph:
            posi, b_posi = SB(ph, "posi", [128, NT], I32)
            posf, b_posf = SB(ph, "posf", [128, NT], F32)
            invf, b_invf = SB(ph, "invf", [128, 32], F32)
            ang, b_ang = SB(ph, "ang", [128, NT, 32], F32)
            uu, b_uu = SB(ph, "uu", [128, NT, 32], F32)
            ni, b_ni = SB(ph, "ni", [128, NT, 32], I32)
            nf, b_nf = SB(ph, "nf", [128, NT, 32], F32)
            mk, b_mk = SB(ph, "mk", [128, NT, 32], F32)
            sn, b_sn = SB(ph, "sn", [128, NT, 32], F32)
            cs, b_cs = SB(ph, "cs", [128, NT, 32], F32)
            S.dma("sp", posi[:], pos_in, writes=[b_posi])
            S.dma("sp", invf[:], invf_in, writes=[b_invf])
            cp("dve", posf[:], posi[:], [b_posi], [b_posf])
            for t in range(NT):
                ts("dve", ang[:, t, :], invf[:], posf[:, t:t + 1], None, ALU.mult, None, [b_invf, b_posf], [b_ang])
            ts("dve", uu[:], ang[:], 1.0 / TWO_PI, None, ALU.mult, None, [b_ang], [b_uu])
            cp("dve", ni[:], uu[:], [b_uu], [b_ni])
            cp("dve", nf[:], ni[:], [b_ni], [b_nf])
            stt("dve", ang[:], nf[:], -C1, ang[:], ALU.mult, ALU.add, [b_nf, b_ang], [b_ang])
            stt("dve", ang[:], nf[:], -C2, ang[:], ALU.mult, ALU.add, [b_nf, b_ang], [b_ang])
            ts("dve", mk[:], ang[:], math.pi, None, ALU.is_gt, None, [b_ang], [b_mk])
            stt("dve", ang[:], mk[:], -TWO_PI, ang[:], ALU.mult, ALU.add, [b_mk, b_ang], [b_ang])
            ts("dve", mk[:], ang[:], -math.pi, None, ALU.is_lt, None, [b_ang], [b_mk])
            stt("dve", ang[:], mk[:], TWO_PI, ang[:], ALU.mult, ALU.add, [b_mk, b_ang], [b_ang])
            ts("dve", ang[:], ang[:], math.pi, -math.pi, ALU.min, ALU.max, [b_ang], [b_ang])
            act(sn[:], ang[:], AF.Sin, [b_ang], [b_sn])
            stt("dve", uu[:], ang[:], -1.0, ang[:], ALU.mult, ALU.max, [b_ang], [b_uu])
            ts("dve", uu[:], uu[:], -1.0, math.pi / 2, ALU.mult, ALU.add, [b_uu], [b_uu])
            act(cs[:], uu[:], AF.Sin, [b_uu], [b_cs])
            S.dma("sp", cos_d, cs[:], reads=[b_cs], writes=[b_cos], sem="cs_st")
            S.dma("sp", sin_d, sn[:], reads=[b_sn], writes=[b_sin], sem="sn_st")

            if stop != 'rope':
                cc, b_cc = SB(ph, "cc", [128, 8], F32)
                ce, b_ce = SB(ph, "ce", [128, 8], F32)
                cond, b_cond = SB(ph, "cond", [128, 8], F32)
                cond_rep, b_crep = SB(ph, "cond_rep", [128, 8, 128], BF16)
                condb, b_condb = SB(ph, "condb", [128, 16], BF16)
                wm = [SB(ph, "wm%d" % i, [128, 8, D], BF16) for i in range(2)]
                bcol, b_bcol = SB(ph, "bcol", [128, DEPTH * 48], F32)
                bgt, b_bgt = SB(ph, "bgt", [128, D], F32)
                gsb, b_gsb = SB(ph, "gsb", [128, D], F32)
                pg = [PS(ph, "pg%d" % i, [128, 512], F32) for i in range(2)]
                pc, b_pc = PS(ph, "pc", [128, 8], F32)
                S.dma("sp", cc[:], ccol, writes=[b_cc])
                for l in range(DEPTH):
                    S.dma("sp", bcol[:, l * 48:(l + 1) * 48], bada_col[l], writes=[b_bcol])
                act(ce[:], cc[:], AF.Exp, [b_cc], [b_ce], scale=-1.0)
                ts("dve", ce[:], ce[:], 1.0, None, ALU.add, None, [b_ce], [b_ce])
                recip(ce[:], ce[:], [b_ce], [b_ce])
                tt("dve", cond[:], cc[:], ce[:], ALU.mult, [b_cc, b_ce], [b_cond])
                memset("dve", condb[:], 0.0, [b_condb])
                cp("dve", condb[:, 0:8], cond[:], [b_cond, b_condb], [b_condb])
                for k in range(8):
                    ts("dve", cond_rep[:, k, :], ones_f[:], cond[:, k:k + 1], None, ALU.mult, None, [b_onesf, b_cond], [b_crep])
                li = 0
                for l in range(DEPTH):
                    for m in range(6):
                        wt, b_wt = wm[li % 2]
                        li += 1
                        S.dma("pool", wt[:], w_ada[l, :, m * D:(m + 1) * D].rearrange("(k p) n -> p k n", p=128), writes=[b_wt])
                        if m in (2, 5):
                            gi = 0 if m == 2 else 1
                            S.dma("sp", bgt[:], bada_gate[l, gi], writes=[b_bgt])
                            for half in range(2):
                                pgt, b_pg = pg[half]
                                for k in range(8):
                                    mm(pgt[:], cond_rep[:, k, :], wt[:, k, half * 512:(half + 1) * 512], k == 0, k == 7,
                                       [b_crep, b_wt], [b_pg])
                                tt("dve", gsb[:, half * 512:(half + 1) * 512], pgt[:], bgt[:, half * 512:(half + 1) * 512],
                                   ALU.add, [b_pg, b_bgt], [b_gsb])
                            S.dma("sp", gate_d[l, gi], gsb[:], reads=[b_gsb], writes=[b_gate], sem="gate_st")
                        else:
                            mi = {0: 0, 1: 1, 3: 2, 4: 3}[m]
                            for ko in range(8):
                                for k in range(8):
                                    mm(pc[:, ko:ko + 1], wt[:, k, ko * 128:(ko + 1) * 128], condb[:, k:k + 1], k == 0, k == 7,
                                       [b_wt, b_condb], [b_pc])
                            dst = modcol[:, l * 32 + mi * 8: l * 32 + mi * 8 + 8]
                            tt("dve", dst, pc[:], bcol[:, l * 48 + m * 8: l * 48 + m * 8 + 8], ALU.add, [b_pc, b_bcol], [b_modcol])
                            if m in (1, 4):
                                ts("dve", dst, dst, 1.0, None, ALU.add, None, [b_modcol], [b_modcol])
        S.barrier()

        for l in range(DEPTH if stop not in ('setup', 'rope') else 0):
            x_cur = x_in if l == 0 else xs[(l - 1) % 2]
            b_xcur = None if l == 0 else b_xs[(l - 1) % 2]
            last = (l == DEPTH - 1)
            x_nxt = y_out if last else xs[l % 2]
            b_xnxt = b_y if last else b_xs[l % 2]
            mc = l * 32

            with contextlib.ExitStack() as ph:
                win, b_win = SB(ph, "win", [128, 8, 2000], BF16)
                wgu, b_wgu = SB(ph, "wgu", [16, 256], BF16)
                wqu, b_wqu = SB(ph, "wqu", [128, 2, 768], BF16)
                wkvu, b_wkvu = SB(ph, "wkvu", [128, 1024], BF16)
                qan, b_qan = SB(ph, "qan", [128, 2], F32)
                kvan, b_kvan = SB(ph, "kvan", [128, 1], F32)
                bgate, b_bgate = SB(ph, "bgate", [128, 256], F32)
                gon4, b_gon4 = SB(ph, "gon4", [128, 512], F32)
                qnn, b_qnn = SB(ph, "qnn", [128, 128], F32)
                knn, b_knn = SB(ph, "knn", [128, 128], F32)
                qnr, b_qnr = SB(ph, "qnr", [128, 64], F32)
                knr, b_knr = SB(ph, "knr", [128, 64], F32)
                b_wink = [S.buf("win%d" % k) for k in range(8)]
                for k in range(8):
                    S.dma("pool", win[:, k, :], w_in[l, k * 128:(k + 1) * 128, :], writes=[b_wink[k]], sem="win_ld")
                S.dma("pool", wgu[:], wgu_in[l], writes=[b_wgu])
                S.dma("pool", wqu[:], wqu_in[l].rearrange("(k p) n -> p k n", p=128), writes=[b_wqu])
                S.dma("pool", wkvu[:], wkvu_in[l], writes=[b_wkvu])
                S.dma("sp", qan[:], qan_in[l], writes=[b_qan])
                S.dma("sp", kvan[:], kvan_in[l], writes=[b_kvan])
                S.dma("sp", bgate[:], bgate_in[l], writes=[b_bgate])
                S.dma("sp", gon4[:], gon_in[l], writes=[b_gon4])
                S.dma("sp", qnn[:], qnn_in[l], writes=[b_qnn])
                S.dma("sp", knn[:], knn_in[l], writes=[b_knn])
                S.dma("sp", qnr[:], qnr_in[l], writes=[b_qnr])
                S.dma("sp", knr[:], knr_in[l], writes=[b_knr])
                for c in range(2):
                    ts("dve", wqu[:, c, :], wqu[:, c, :], qan[:, c:c + 1], None, ALU.mult, None, [b_wqu, b_qan], [b_wqu])
                ts("dve", wkvu[:], wkvu[:], kvan[:, 0:1], None, ALU.mult, None, [b_wkvu, b_kvan], [b_wkvu])
                qsc = 192.0 ** -0.5
                ts("dve", qnn[:], qnn[:], qsc, None, ALU.mult, None, [b_qnn], [b_qnn])
                ts("dve", qnr[:], qnr[:], qsc, None, ALU.mult, None, [b_qnr], [b_qnr])

                xt = [SB(ph, "xt%d" % i, [128, D], F32) for i in range(3)]
                cst = [SB(ph, "cst%d" % i, [128, 2, 32], F32) for i in range(3)]
                sq, b_sq = SB(ph, "sq", [128, D], F32)
                ss, b_ss = SB(ph, "ss", [128, 1], F32)
                xn, b_xn = SB(ph, "xn", [128, D], BF16)
                hT, b_hT = SB(ph, "hT", [128, 8, 128], BF16)
                mD, b_mD = SB(ph, "mD", [128, 400], BF16)
                ss3, b_ss3 = SB(ph, "ss3", [128, 3], F32)
                rs3, b_rs3 = SB(ph, "rs3", [128, 3], F32)
                rq2, b_rq2 = SB(ph, "rq2", [128, 3], F32)
                mT, b_mT = SB(ph, "mT", [128, 4, 128], BF16)
                pre, b_pre = SB(ph, "pre", [128, 256], F32)
                lg, b_lg = SB(ph, "lg", [128, 256], F32)
                lgh, b_lgh = SB(ph, "lgh", [128, 256], BF16)
                lgl, b_lgl = SB(ph, "lgl", [128, 256], BF16)
                eb, b_eb = SB(ph, "eb", [128, 256], F32)
                enb, b_enb = SB(ph, "enb", [128, 256], F32)
                ebl, b_ebl = SB(ph, "ebl", [64, 4], F32)
                qg, b_qg = SB(ph, "qg", [128, 256], BF16)
                kg, b_kg = SB(ph, "kg", [128, 256], BF16)
                qkT, b_qkT = SB(ph, "qkT", [64, 8, 128], BF16)
                vsb, b_vsb = SB(ph, "vsb", [128, 512], BF16)
                eo, b_eo = SB(ph, "eo", [128, 512], F32)
                gog, b_gog = SB(ph, "gog", [128, 512], F32)
                ATs, b_ATs = SB(ph, "ATs", [128, 4, 128], BF16)
                stt_, b_st = SB(ph, "gst", [64, 4, 128], F32)
                stb, b_stb = SB(ph, "gstb", [64, 4, 128], BF16)
                sso, b_sso = SB(ph, "sso", [128, 4], F32)
                go, b_go = SB(ph, "go", [128, 512], BF16)
                gT, b_gT = SB(ph, "gT", [128, 4, 128], BF16)
                ss8, b_ss8 = SB(ph, "ss8", [128, 8], F32)
                fac8, b_fac8 = SB(ph, "fac8", [128, 8], F32)
                qn, b_qn = SB(ph, "qn", [128, 4, 128], BF16)
                zall, b_zall = SB(ph, "zall", [128, 5, 64], F32)
                ra, b_ra = SB(ph, "ra", [128, 5, 32], F32)
                rb, b_rb = SB(ph, "rb", [128, 5, 32], F32)
                rope_o, b_ropeo = SB(ph, "rope_o", [128, 5, 64], BF16)
                qT6, b_qT6 = SB(ph, "qT6", [128, 6, 128], BF16)
                kn, b_kn = SB(ph, "kn", [128, 4, 128], BF16)
                vt, b_vt = SB(ph, "vt", [128, 4, 128], BF16)
                kT5, b_kT5 = SB(ph, "kT5", [128, 5, 128], BF16)
                ssk, b_ssk = SB(ph, "ssk", [128, 4], F32)
                fack, b_fack = SB(ph, "fack", [128, 4], F32)

                pT, b_pT = PS(ph, "pT", [128, 8, 128], BF16)
                pA, b_pA = PS(ph, "pA", [128, 512], F32)
                pB, b_pB = PS(ph, "pB", [128, 512], F32)
                pC, b_pC = PS(ph, "pC", [128, 512], F32)
                pD, b_pD = PS(ph, "pD", [128, 512], F32)
                pM, _ = PS(ph, "pM", [128, 8, 128], BF16)
                b_pMa = b_pMb = S.buf("pM")
                pX, b_pX = PS(ph, "pX", [128, 512], F32)
                pY, b_pY = PS(ph, "pY", [128, 512], F32)
                pX4 = pX[:].rearrange("p (h e) -> p h e", e=128)
                pY4 = pY[:].rearrange("p (h e) -> p h e", e=128)

                memset("dve", stt_[:], 0.0, [b_st])
                memset("dve", stb[:], 0.0, [b_stb])

                sq2, b_sq2 = SB(ph, "sq2", [128, 128], F32)
                sq3, b_sq3 = SB(ph, "sq3", [128, 128], F32)
                b_pM = b_pMa
                kprs = [SB(ph, "kpr%d" % i, [128, 64], F32) for i in range(2)]
                rs3s = [SB(ph, "rs3_%d" % i, [128, 3], F32) for i in range(2)]
                rq2s = [SB(ph, "rq2_%d" % i, [128, 3], F32) for i in range(2)]
                mTs = [SB(ph, "mT%d" % i, [128, 4, 128], BF16) for i in range(2)]
                gqks = [SB(ph, "gqk%d" % i, [128, 512], F32) for i in range(2)]
                vsbs = [SB(ph, "vsb%d" % i, [128, 512], BF16) for i in range(2)]
                gogs = [SB(ph, "gog%d" % i, [128, 512], F32) for i in range(2)]
                pW = [(pA, b_pA), (pB, b_pB)]
                pQ = [(pC, b_pC), (pD, b_pD)]

                def drive(gens):
                    alive = list(gens)
                    while alive:
                        for g in list(alive):
                            try:
                                next(g)
                            except StopIteration:
                                alive.remove(g)

                def issue_loads(t):
                    tsl_ = slice(t * 128, (t + 1) * 128)
                    xtt_, b_xt_ = xt[t % 3]
                    cs_t_, b_cst_ = cst[t % 3]
                    S.dma("sp", xtt_[:], x_cur[tsl_, :], reads=([b_xcur] if b_xcur else []), writes=[b_xt_])
                    S.dma("sp", cs_t_[:, 0, :], cos_d[:, t, :], reads=[b_cos], writes=[b_cst_])
                    S.dma("sp", cs_t_[:, 1, :], sin_d[:, t, :], reads=[b_sin], writes=[b_cst_])

                def prologue(t):
                    sl = t % 2
                    tsl = slice(t * 128, (t + 1) * 128)
                    xtt, b_xt = xt[t % 3]
                    cs_t, b_cst = cst[t % 3]
                    kpr, b_kpr = kprs[sl]
                    rs3, b_rs3 = rs3s[sl]
                    rq2, b_rq2 = rq2s[sl]
                    mT, b_mT = mTs[sl]
                    gqk, b_gqk = gqks[sl]
                    vsb, b_vsb = vsbs[sl]
                    gog, b_gog = gogs[sl]
                    act(sq[:], xtt[:], AF.Square, [b_xt], [b_sq, b_ss], accum=ss[:, 0:1])
                    yield
                    rsqrt_cols(ss[:, 0:1], ss[:, 0:1], 1.0 / D, [b_ss], [b_ss])
                    yield
                    ts("dve", xn[:], xtt[:], ss[:, 0:1], None, ALU.mult, None, [b_xt, b_ss], [b_xn])
                    yield
                    for k in range(8):
                        tr(pT[:, k, :], xn[:, k * 128:(k + 1) * 128], ident[:], [b_xn, b_ident], [b_pT])
                    yield
                    for k in range(8):
                        if k % 2 == 0:
                            ts("dve", hT[:, k, :], pT[:, k, :], modcol[:, mc + 8 + k: mc + 9 + k], modcol[:, mc + k: mc + k + 1],
                               ALU.mult, ALU.add, [b_pT, b_modcol], [b_hT])
                        else:
                            act(hT[:, k, :], pT[:, k, :], AF.Identity, [b_pT, b_modcol], [b_hT],
                                scale=modcol[:, mc + 8 + k: mc + 9 + k], bias=modcol[:, mc + k: mc + k + 1])
                        if k % 4 == 3:
                            yield
                    blks = ((1536, 2000), (0, 512), (512, 1024), (1024, 1536))
                    for bi, (c0, c1) in enumerate(blks):
                        pw, b_pw = pW[bi % 2]
                        for k in range(8):
                            mm(pw[:, 0:c1 - c0], hT[:, k, :], win[:, k, c0:c1], k == 0, k == 7, [b_hT, b_wink[k]], [b_pw])
                        yield
                        if bi == 0:
                            cp("act", mD[:], pw[:, 0:400], [b_pw], [b_mD])
                            cp("act", kpr[:], pw[:, 400:464], [b_pw], [b_kpr])
                            yield
                            act(sq[:, 0:256], pw[:, 16:272], AF.Square, [b_pw], [b_sq, b_ss3], accum=ss3[:, 0:1])
                            act(sq[:, 0:128], pw[:, 272:400], AF.Square, [b_pw], [b_sq, b_ss3], accum=ss3[:, 1:2])
                            act(sq[:, 0:64], pw[:, 400:464], AF.Square, [b_pw], [b_sq, b_ss3], accum=ss3[:, 2:3])
                            yield
                        elif bi == 1:
                            cp("act", gqk[:], pw[:], [b_pw], [b_gqk])
                            yield
                        elif bi == 2:
                            cp("act", vsb[:], pw[:], [b_pw], [b_vsb])
                            yield
                        else:
                            act(eo[:], pw[:], AF.Exp, [b_pw], [b_eo], scale=-1.0)
                            yield
                            act(eo[:], eo[:], AF.Ln, [b_eo], [b_eo], bias=1.0)
                            yield
                            act(eo[:], eo[:], AF.Exp, [b_eo], [b_eo], scale=-1.0)
                            yield
                            tt("dve", gog[:], pw[:], eo[:], ALU.mult, [b_pw, b_eo], [b_gog])
                            yield
                            tt("pool", gog[:], gog[:], gon4[:], ALU.mult, [b_gog, b_gon4], [b_gog])
                            yield
                    act(rs3[:, 0:1], ss3[:, 0:1], AF.Ln, [b_ss3], [b_rs3], scale=1.0 / 256, bias=EPS)
                    act(rs3[:, 1:2], ss3[:, 1:2], AF.Ln, [b_ss3], [b_rs3], scale=1.0 / 128, bias=EPS)
                    act(rs3[:, 2:3], ss3[:, 2:3], AF.Ln, [b_ss3], [b_rs3], scale=1.0 / 64, bias=EPS)
                    yield
                    act(rs3[:], rs3[:], AF.Exp, [b_rs3], [b_rs3], scale=-0.5)
                    yield
                    tt("dve", rq2[:], rs3[:], rs3[:], ALU.mult, [b_rs3], [b_rq2])
                    tr(pT[0:16, 0, :], mD[:, 0:16], ident[:], [b_mD, b_ident], [b_pT])
                    tr(pT[:, 1, :], mD[:, 16:144], ident[:], [b_mD, b_ident], [b_pT])
                    tr(pT[:, 2, :], mD[:, 144:272], ident[:], [b_mD, b_ident], [b_pT])
                    tr(pT[:, 3, :], mD[:, 272:400], ident[:], [b_mD, b_ident], [b_pT])
                    yield
                    cp("dve", mT[0:16, 0, :], pT[0:16, 0, :], [b_pT], [b_mT])
                    cp("dve", mT[:, 1:4, :], pT[:, 1:4, :], [b_pT], [b_mT])
                    yield

                def gla_chain(t):
                    sl = t % 2
                    tsl = slice(t * 128, (t + 1) * 128)
                    mT, b_mT = mTs[sl]
                    gqk, b_gqk = gqks[sl]
                    vsb, b_vsb = vsbs[sl]
                    gog, b_gog = gogs[sl]
                    mm(pX[:, 0:256], mT[0:16, 0, :], wgu[:], True, True, [b_mT, b_wgu], [b_pX])
                    tt("dve", pre[:], pX[:, 0:256], bgate[:], ALU.add, [b_pX, b_bgate], [b_pre])
                    yield
                    act(pre[:], pre[:], AF.Exp, [b_pre], [b_pre], scale=-1.0)
                    yield
                    act(lg[:], pre[:], AF.Ln, [b_pre], [b_lg], bias=1.0)
                    yield
                    cp("dve", lgh[:], lg[:], [b_lg], [b_lgh])
                    yield
                    tt("dve", lgl[:], lg[:], lgh[:], ALU.subtract, [b_lg, b_lgh], [b_lgl])
                    yield
                    mm(pX[:, 256:512], maskU4[:, 0, :], lgh[:], True, False, [b_maskU4, b_lgh], [b_pX])
                    mm(pX[:, 256:512], maskU4[:, 0, :], lgl[:], False, True, [b_maskU4, b_lgl], [b_pX])
                    for h in range(4):
                        mm(pY[0:64, h:h + 1], lgh[:, h * 64:(h + 1) * 64], ones_b[:, 0:1], True, False,
                           [b_lgh, b_onesb], [b_pY])
                        mm(pY[0:64, h:h + 1], lgl[:, h * 64:(h + 1) * 64], ones_b[:, 0:1], False, True,
                           [b_lgl, b_onesb], [b_pY])
                    yield
                    act(eb[:], pX[:, 256:512], AF.Exp, [b_pX], [b_eb], scale=-1.0 / 16)
                    act(enb[:], pX[:, 256:512], AF.Exp, [b_pX], [b_enb], scale=1.0 / 16)
                    act(ebl[:], pY[0:64, 0:4], AF.Exp, [b_pY], [b_ebl], scale=-1.0 / 16)
                    yield
                    stt("dve", qg[:], gqk[:, 0:256], 0.125, eb[:], ALU.mult, ALU.mult, [b_gqk, b_eb], [b_qg])
                    tt("dve", kg[:], gqk[:, 256:512], enb[:], ALU.mult, [b_gqk, b_enb], [b_kg])
                    yield
                    for h in range(4):
                        tr(pM[0:64, h, :], qg[:, h * 64:(h + 1) * 64], ident[:], [b_qg, b_ident], [b_pM])
                        tr(pM[0:64, 4 + h, :], kg[:, h * 64:(h + 1) * 64], ident[:], [b_kg, b_ident], [b_pM])
                    yield
                    cp("dve", qkT[:], pM[0:64, :, :], [b_pM], [b_qkT])
                    yield
                    for h in range(4):
                        mm(pY4[:, h, :], qkT[:, 4 + h, :], qkT[:, h, :], True, True, [b_qkT], [b_pY])
                    yield
                    tt("dve", ATs[:], pY4, maskU4[:], ALU.mult, [b_pY, b_maskU4], [b_ATs])
                    yield
                    for h in range(4):
                        mm(pX4[:, h, :], ATs[:, h, :], vsb[:, h * 128:(h + 1) * 128], True, False, [b_ATs, b_vsb], [b_pX])
                        mm(pX4[:, h, :], qkT[:, h, :], stb[:, h, :], False, True, [b_qkT, b_stb], [b_pX])
                    for h in range(4):
                        mm(pY4[0:64, h, :], kg[:, h * 64:(h + 1) * 64], vsb[:, h * 128:(h + 1) * 128], True, True,
                           [b_kg, b_vsb], [b_pY])
                    yield
                    for h in range(4):
                        ts("dve", stt_[:, h, :], stt_[:, h, :], ebl[:, h:h + 1], None, ALU.mult, None, [b_st, b_ebl], [b_st])
                        stt("dve", stt_[:, h, :], pY4[0:64, h, :], ebl[:, h:h + 1], stt_[:, h, :], ALU.mult, ALU.add,
                            [b_pY, b_ebl, b_st], [b_st])
                        if h == 1:
                            yield
                    cp("dve", stb[:], stt_[:], [b_st], [b_stb])
                    yield
                    for h in range(4):
                        act(sq2[:], pX4[:, h, :], AF.Square, [b_pX], [b_sq2, b_sso], accum=sso[:, h:h + 1])
                    yield
                    act(sso[:], sso[:], AF.Ln, [b_sso], [b_sso], scale=1.0 / 128, bias=EPS)
                    yield
                    act(sso[:], sso[:], AF.Exp, [b_sso], [b_sso], scale=-0.5)
                    yield
                    for h in range(4):
                        stt("dve", go[:, h * 128:(h + 1) * 128], pX4[:, h, :], sso[:, h:h + 1], gog[:, h * 128:(h + 1) * 128],
                            ALU.mult, ALU.mult, [b_pX, b_sso, b_gog], [b_go])
                        if h == 1:
                            yield
                    yield
                    for h in range(4):
                        tr(pM[:, h, :], go[:, h * 128:(h + 1) * 128], ident[:], [b_go, b_ident], [b_pM])
                    yield
                    cp("dve", gT[:], pM[:, 0:4, :], [b_pM], [b_gT])
                    yield
                    S.dma("sp", mixT_d[0:4, :, tsl].rearrange("c p s -> p c s"), gT[:], reads=[b_gT], writes=[b_mixT], sem="gT_st")

                qhs, b_qhs = SB(ph, "qhs", [128, 768], F32)
                kvs, b_kvs = SB(ph, "kvs", [128, 1024], F32)
                zk, b_zk = SB(ph, "zk", [128, 64], F32)
                rka, b_rka = SB(ph, "rka", [128, 32], F32)
                rkb, b_rkb = SB(ph, "rkb", [128, 32], F32)
                rope_k, b_ropek = SB(ph, "rope_k", [128, 64], BF16)
                sq4, b_sq4 = SB(ph, "sq4", [128, 128], F32)

                def mla_q(t):
                    sl = t % 2
                    tsl = slice(t * 128, (t + 1) * 128)
                    cs_t, b_cst = cst[t % 3]
                    rs3, b_rs3 = rs3s[sl]
                    rq2, b_rq2 = rq2s[sl]
                    mT, b_mT = mTs[sl]
                    pt_, bpt = pQ[0]
                    for g in range(2):
                        for c in range(2):
                            mm(pt_[:, 0:384], mT[:, 1 + c, :], wqu[:, c, g * 384:(g + 1) * 384], c == 0, c == 1, [b_mT, b_wqu], [bpt])
                        yield
                        cp("act", qhs[:, g * 384:(g + 1) * 384], pt_[:, 0:384], [bpt], [b_qhs])
                        for hh in range(2):
                            h = 2 * g + hh
                            base = hh * 192
                            act(sq3[:, 0:128], pt_[:, base:base + 128], AF.Square, [bpt], [b_sq3, b_ss8], accum=ss8[:, h:h + 1])
                            act(sq3[:, 0:64], pt_[:, base + 128:base + 192], AF.Square, [bpt], [b_sq3, b_ss8], accum=ss8[:, 4 + h:5 + h])
                        yield
                    ts("dve", ss8[:], ss8[:], rq2[:, 0:1], None, ALU.mult, None, [b_ss8, b_rq2], [b_ss8])
                    yield
                    act(fac8[:, 0:4], ss8[:, 0:4], AF.Ln, [b_ss8], [b_fac8], scale=1.0 / 128, bias=EPS)
                    act(fac8[:, 4:8], ss8[:, 4:8], AF.Ln, [b_ss8], [b_fac8], scale=1.0 / 64, bias=EPS)
                    yield
                    act(fac8[:], fac8[:], AF.Exp, [b_fac8], [b_fac8], scale=-0.5)
                    yield
                    ts("dve", fac8[:], fac8[:], rs3[:, 0:1], None, ALU.mult, None, [b_fac8, b_rs3], [b_fac8])
                    yield
                    for h in range(4):
                        base = h * 192
                        stt("dve", qn[:, h, :], qhs[:, base:base + 128], fac8[:, h:h + 1], qnn[:], ALU.mult, ALU.mult,
                            [b_qhs, b_fac8, b_qnn], [b_qn])
                        stt("dve", zall[:, h, :], qhs[:, base + 128:base + 192], fac8[:, 4 + h:5 + h], qnr[:], ALU.mult, ALU.mult,
                            [b_qhs, b_fac8, b_qnr], [b_zall])
                        if h % 2 == 1:
                            yield
                    for h in range(4):
                        tr(pM[:, h, :], qn[:, h, :], ident[:], [b_qn, b_ident], [b_pM])
                    yield
                    cp("dve", qT6[:, 0:4, :], pM[:, 0:4, :], [b_pM], [b_qT6])
                    yield
                    S.dma("sp", QT_d[:, :, tsl].rearrange("c p s -> p c s"), qT6[:, 0:4, :], reads=[b_qT6], writes=[b_QT], sem="qT_st")
                    for hh in range(4):
                        z1, z2 = zall[:, hh, 0:32], zall[:, hh, 32:64]
                        cth, sth = cs_t[:, 0, :], cs_t[:, 1, :]
                        tt("pool", ra[:, hh, :], z1, cth, ALU.mult, [b_zall, b_cst], [b_ra])
                        tt("pool", rb[:, hh, :], z2, sth, ALU.mult, [b_zall, b_cst], [b_rb])
                        tt("pool", rope_o[:, hh, 0:32], ra[:, hh, :], rb[:, hh, :], ALU.subtract, [b_ra, b_rb], [b_ropeo])
                        yield
                        tt("pool", ra[:, hh, :], z2, cth, ALU.mult, [b_zall, b_cst], [b_ra])
                        tt("pool", rb[:, hh, :], z1, sth, ALU.mult, [b_zall, b_cst], [b_rb])
                        tt("pool", rope_o[:, hh, 32:64], ra[:, hh, :], rb[:, hh, :], ALU.add, [b_ra, b_rb], [b_ropeo])
                        yield
                    for hp in range(2):
                        tr(pM[:, 4 + hp, :], rope_o[:, 2 * hp:2 * hp + 2, :].rearrange("p h r -> p (h r)"), ident[:],
                           [b_ropeo, b_ident], [b_pM])
                    yield
                    cp("dve", qT6[:, 4:6, :], pM[:, 4:6, :], [b_pM], [b_qT6])
                    yield
                    S.dma("sp", QPE_d[:, :, tsl].rearrange("c p s -> p c s"), qT6[:, 4:6, :], reads=[b_qT6], writes=[b_QPE], sem="qT_st2")

                def mla_kv(t):
                    sl = t % 2
                    tsl = slice(t * 128, (t + 1) * 128)
                    cs_t, b_cst = cst[t % 3]
                    kpr, b_kpr = kprs[sl]
                    rs3, b_rs3 = rs3s[sl]
                    rq2, b_rq2 = rq2s[sl]
                    mT, b_mT = mTs[sl]
                    pt_, bpt = pQ[1]
                    stt("dve", zk[:], kpr[:], rs3[:, 2:3], knr[:], ALU.mult, ALU.mult, [b_kpr, b_rs3, b_knr], [b_zk])
                    yield
                    for g in range(2):
                        mm(pt_[:], mT[:, 3, :], wkvu[:, g * 512:(g + 1) * 512], True, True, [b_mT, b_wkvu], [bpt])
                        yield
                        cp("act", kvs[:, g * 512:(g + 1) * 512], pt_[:], [bpt], [b_kvs])
                        for hh in range(2):
                            h = 2 * g + hh
                            act(sq4[:, 0:128], pt_[:, hh * 256:hh * 256 + 128], AF.Square, [bpt], [b_sq4, b_ssk], accum=ssk[:, h:h + 1])
                        yield
                    z1, z2 = zk[:, 0:32], zk[:, 32:64]
                    cth, sth = cs_t[:, 0, :], cs_t[:, 1, :]
                    tt("dve", rka[:], z1, cth, ALU.mult, [b_zk, b_cst], [b_rka])
                    tt("dve", rkb[:], z2, sth, ALU.mult, [b_zk, b_cst], [b_rkb])
                    tt("dve", rope_k[:, 0:32], rka[:], rkb[:], ALU.subtract, [b_rka, b_rkb], [b_ropek])
                    yield
                    tt("dve", rka[:], z2, cth, ALU.mult, [b_zk, b_cst], [b_rka])
                    tt("dve", rkb[:], z1, sth, ALU.mult, [b_zk, b_cst], [b_rkb])
                    tt("dve", rope_k[:, 32:64], rka[:], rkb[:], ALU.add, [b_rka, b_rkb], [b_ropek])
                    yield
                    ts("dve", ssk[:], ssk[:], rq2[:, 1:2], None, ALU.mult, None, [b_ssk, b_rq2], [b_ssk])
                    yield
                    act(fack[:], ssk[:], AF.Ln, [b_ssk], [b_fack], scale=1.0 / 128, bias=EPS)
                    yield
                    act(fack[:], fack[:], AF.Exp, [b_fack], [b_fack], scale=-0.5)
                    yield
                    ts("dve", fack[:], fack[:], rs3[:, 1:2], None, ALU.mult, None, [b_fack, b_rs3], [b_fack])
                    yield
                    for h in range(4):
                        base = h * 256
                        stt("dve", kn[:, h, :], kvs[:, base:base + 128], fack[:, h:h + 1], knn[:], ALU.mult, ALU.mult,
                            [b_kvs, b_fack, b_knn], [b_kn])
                        if h % 2 == 1:
                            yield
                    for h in range(4):
                        tr(pM[:, h, :], kn[:, h, :], ident[:], [b_kn, b_ident], [b_pM])
                    tr(pM[0:64, 4, :], rope_k[:], ident[:], [b_ropek, b_ident], [b_pM])
                    yield
                    cp("dve", kT5[:, 0:4, :], pM[:, 0:4, :], [b_pM], [b_kT5])
                    cp("dve", kT5[0:64, 4, :], pM[0:64, 4, :], [b_pM], [b_kT5])
                    yield
                    S.dma("sp", KT_d[:, :, tsl].rearrange("c p s -> p c s"), kT5[:, 0:4, :], reads=[b_kT5], writes=[b_KT], sem="kT_st")
                    S.dma("sp", KPE_d[:, tsl], kT5[0:64, 4, :], reads=[b_kT5], writes=[b_KPE], sem="kT_st2")
                    for h in range(4):
                        base = h * 256
                        ts("dve", vt[:, h, :], kvs[:, base + 128:base + 256], rs3[:, 1:2], None, ALU.mult, None, [b_kvs, b_rs3], [b_vt])
                        if h % 2 == 1:
                            yield
                    S.dma("sp", V_d[tsl, :], vt[:].rearrange("p h e -> p (h e)"), reads=[b_vt], writes=[b_V], sem="vt_st")

                if "p1" not in SKIP:
                    issue_loads(0)
                    if NT > 1:
                        issue_loads(1)
                    drive([prologue(0)])
                for t in range(NT if "p1" not in SKIP else 0):
                    if t + 2 < NT:
                        issue_loads(t + 2)
                    gens = [gla_chain(t), mla_q(t), mla_kv(t)]
                    if t + 1 < NT:
                        gens.append(prologue(t + 1))
                    drive(gens)
            S.barrier()
            if stop and (stop.startswith('p1') or stop.startswith('q') or stop.startswith('c')):
                break

            ph23 = contextlib.ExitStack()
            phw = contextlib.ExitStack()
            uid[0] += 1
            wout = phw.enter_context(nc.sbuf_tensor("wout_s%d" % uid[0], [128, 8, D], BF16, side="right"))
            b_woutk = [S.buf("wout%d" % k) for k in range(8)]
            for k in range(8):
                S.dma("pool", wout[:, k, :], wout_in[l, k * 128:(k + 1) * 128, :], writes=[b_woutk[k]], sem="wout_ld", nobar=True)
            with contextlib.ExitStack() as ph:
                KTs = [SB(ph, "KTs%d" % i, [128, S_LEN], BF16) for i in range(2)]
                Vh = [SB(ph, "Vh%d" % i, [128, NT, 128], BF16) for i in range(2)]
                KPEs, b_KPEs = SB(ph, "KPEs", [64, S_LEN], BF16)
                qns = [SB(ph, "qns%d" % i, [128, 512], BF16) for i in range(2)]
                qps = [SB(ph, "qps%d" % i, [64, 512], BF16) for i in range(2)]
                pTs = [SB(ph, "pTs%d" % i, [128, 512], BF16) for i in range(4)]
                rl, b_rl = SB(ph, "rl", [128, 512], F32)
                ob = [SB(ph, "ob%d" % i, [128, 512], BF16) for i in range(2)]
                pS = [PS(ph, "pS%d" % i, [128, 512], F32) for i in range(4)]
                pO = [PS(ph, "pO%d" % i, [128, 512], F32) for i in range(2)]
                pL = [PS(ph, "pL%d" % i, [128, 512], F32) for i in range(2)]
                S.dma("sp", KPEs[:], KPE_d, reads=[b_KPE], writes=[b_KPEs])
                spl2 = split and last
                if spl2:
                    maskSel, b_maskSel = SB(ph, "maskSel", [128, 8, 512], BF16)
                    for r in range(4):
                        act(maskSel[:, r, :], maskD[:, r, :], AF.Identity, [b_maskD, b_par], [b_maskSel],
                            scale=par[:, 0:1], bias=par[:, 1:2])
                        act(maskSel[:, 4 + r, :], maskD[:, r, :], AF.Identity, [b_maskD, b_par], [b_maskSel], scale=par[:, 1:2])
                    qna = [SB(ph, "qna%d" % i, [128, 512], BF16) for i in range(2)]
                    qnb = [SB(ph, "qnb%d" % i, [128, 512], BF16) for i in range(2)]
                    qpa = [SB(ph, "qpa%d" % i, [64, 512], BF16) for i in range(2)]
                    qpb = [SB(ph, "qpb%d" % i, [64, 512], BF16) for i in range(2)]
                NPOS = NC4 // 2 if spl2 else NC4

                def nkb_of(c):
                    return 8 * c + 8 if spl2 else 4 * c + 4

                chunks = [(h, c) for h in range(4 if "p2" not in SKIP else 0) for c in range(NPOS)]
                blocks = []
                for idx, (h, c) in enumerate(chunks):
                    for kb in range(nkb_of(c)):
                        blocks.append((idx, kb, nkb_of(c)))

                def load_head(h):
                    KTh, b_KTh = KTs[h % 2]
                    Vhh, b_Vhh = Vh[h % 2]
                    S.dma("sp", KTh[:], KT_d[h], reads=[b_KT], writes=[b_KTh])
                    S.dma("sp", Vhh[:], V_d[:, h * 128:(h + 1) * 128].rearrange("(t p) e -> p t e", p=128), reads=[b_V], writes=[b_Vhh])

                def load_q(idx):
                    h, c = chunks[idx]
                    hp, off = h // 2, 64 * (h % 2)
                    if not spl2:
                        csl = slice(c * 512, (c + 1) * 512)
                        S.dma("sp", qns[idx % 2][0][:], QT_d[h, :, csl], reads=[b_QT], writes=[qns[idx % 2][1]])
                        S.dma("sp", qps[idx % 2][0][:], QPE_d[hp, off:off + 64, csl], reads=[b_QPE], writes=[qps[idx % 2][1]])
                        return
                    sla = slice(2 * c * 512, (2 * c + 1) * 512)
                    slb = slice((2 * c + 1) * 512, (2 * c + 2) * 512)
                    qa, b_qa = qna[idx % 2]
                    qb, b_qb = qnb[idx % 2]
                    pa, b_pa = qpa[idx % 2]
                    pb, b_pb = qpb[idx % 2]
                    qs, b_qs = qns[idx % 2]
                    ps_, b_ps = qps[idx % 2]
                    S.dma("sp", qa[:], QT_d[h, :, sla], reads=[b_QT], writes=[b_qa])
                    S.dma("sp", qb[:], QT_d[h, :, slb], reads=[b_QT], writes=[b_qb])
                    S.dma("sp", pa[:], QPE_d[hp, off:off + 64, sla], reads=[b_QPE], writes=[b_pa])
                    S.dma("sp", pb[:], QPE_d[hp, off:off + 64, slb], reads=[b_QPE], writes=[b_pb])
                    ts("dve", qs[:], qa[:], par[:, 0:1], None, ALU.mult, None, [b_qa, b_par], [b_qs])
                    stt("dve", qs[:], qb[:], par[:, 1:2], qs[:], ALU.mult, ALU.add, [b_qb, b_par, b_qs], [b_qs])
                    ts("dve", ps_[:], pa[:], par[0:64, 0:1], None, ALU.mult, None, [b_pa, b_par], [b_ps])
                    stt("dve", ps_[:], pb[:], par[0:64, 1:2], ps_[:], ALU.mult, ALU.add, [b_pb, b_par, b_ps], [b_ps])

                LA = 3
                if chunks:
                    load_head(0)
                    load_q(0)
                for i in range((len(blocks) + LA) if blocks else 0):
                    if i < len(blocks):
                        idx, kb, nkb = blocks[i]
                        h, c = chunks[idx]
                        if kb == 0 and idx + 1 < len(chunks):
                            load_q(idx + 1)
                        KTh, b_KTh = KTs[h % 2]
                        qn_, b_qn_ = qns[idx % 2]
                        qp_, b_qp_ = qps[idx % 2]
                        ksl = slice(kb * 128, (kb + 1) * 128)
                        pSt, b_pSt = pS[i % 4]
                        pTt, b_pTt = pTs[i % 4]
                        mm(pSt[:], KTh[:, ksl], qn_[:], True, False, [b_KTh, b_qn_], [b_pSt])
                        mm(pSt[:], KPEs[:, ksl], qp_[:], False, True, [b_KPEs, b_qp_], [b_pSt])
                        act(pTt[:], pSt[:], AF.Exp, [b_pSt], [b_pTt])
                        if spl2:
                            r = kb - 8 * c
                            if r >= 0:
                                tt("pool", pTt[:], pTt[:], maskSel[:, r, :], ALU.mult, [b_pTt, b_maskSel], [b_pTt])
                        else:
                            r = kb - 4 * c
                            if r >= 0:
                                tt("pool", pTt[:], pTt[:], maskD[:, r, :], ALU.mult, [b_pTt, b_maskD], [b_pTt])
                    if i >= LA:
                        j = i - LA
                        idx, kb, nkb = blocks[j]
                        h, c = chunks[idx]
                        Vhh, b_Vhh = Vh[h % 2]
                        pTt, b_pTt = pTs[j % 4]
                        pOt, b_pOt = pO[idx % 2]
                        pLt, b_pLt = pL[idx % 2]
                        mm(pOt[:], Vhh[:, kb, :], pTt[:], kb == 0, kb == nkb - 1, [b_Vhh, b_pTt], [b_pOt])
                        mm(pLt[:], ones_b[:], pTt[:], kb == 0, kb == nkb - 1, [b_onesb, b_pTt], [b_pLt])
                        if kb == 0 and c == 0 and h + 1 < 4:
                            load_head(h + 1)
                        if kb == nkb - 1:
                            obt, b_obt = ob[idx % 2]
                            csl = slice(c * 512, (c + 1) * 512)
                            recip(rl[:], pLt[:], [b_pLt], [b_rl])
                            tt("dve", obt[:], pOt[:], rl[:], ALU.mult, [b_pOt, b_rl], [b_obt])
                            S.dma("pool", mixT_d[4 + h, :, csl], obt[:], reads=[b_obt], writes=[b_mixT], sem="ob_st%d" % (idx % 2))
            S.barrier()
            if stop == 'p2':
                phw.close()
                ph23.close()
                break
            w1, _ = SB(ph23, "w1", [128, 8, DFF], BF16)
            w2, _ = SB(ph23, "w2", [128, 32, D], BF16)
            b_w1k = [S.buf("w1_%d" % k) for k in range(8)]
            b_w2f = [S.buf("w2_%d" % f) for f in range(32)]
            for k in range(8):
                for q in range(2):
                    S.dma("pool", w1[:, k, q * 2048:(q + 1) * 2048], w1_in[l, k * 128:(k + 1) * 128, q * 2048:(q + 1) * 2048],
                          writes=[b_w1k[k]], sem="w1_ld", nobar=True)
            for f in range(32):
                S.dma("pool", w2[:, f, :], w2_in[l, f * 128:(f + 1) * 128, :], writes=[b_w2f[f]], sem="w2_ld", nobar=True)

            with contextlib.ExitStack() as ph:
                gta, b_gta = SB(ph, "gta", [128, D], F32)
                S.dma("sp", gta[:], gate_d[l, 0], reads=[b_gate], writes=[b_gta])
                nsl = 2 if (split and last) else 3
                mx = [SB(ph, "mx%d" % i, [128, 8, 128], BF16) for i in range(nsl)]
                xt = [SB(ph, "xt%d" % i, [128, D], F32) for i in range(nsl)]
                xo = [SB(ph, "xo%d" % i, [128, D], F32) for i in range(2)]
                pW = [PS(ph, "pW%d" % i, [128, 512], F32) for i in range(4)]
                spl = split and last
                if spl:
                    mxb = [SB(ph, "mxb%d" % i, [128, 8, 128], BF16) for i in range(2)]
                    xtb = [SB(ph, "xtb%d" % i, [128, D], F32) for i in range(2)]
                    mxs = [SB(ph, "mxs%d" % i, [128, 8, 128], BF16) for i in range(2)]
                    xss = [SB(ph, "xss%d" % i, [128, D], F32) for i in range(2)]
                NT3 = (NT // 2 if spl else NT) if "p3a" not in SKIP else 0
                for t in range(NT3):
                    tsl = slice(t * 128, (t + 1) * 128)
                    tg = (8 * (t // 4) + t % 4) if spl else t
                    tsl0 = slice(tg * 128, (tg + 1) * 128)
                    tsl1 = slice((tg + 4) * 128, (tg + 5) * 128)
                    mxt, b_mx = mx[t % nsl]
                    xtt, b_xt = xt[t % nsl]
                    xot, b_xo = xo[t % 2]
                    S.dma("sp", mxt[:], mixT_d[:, :, tsl0].rearrange("c p s -> p c s"), reads=[b_mixT], writes=[b_mx])
                    S.dma("sp", xtt[:], x_cur[tsl0, :], reads=([b_xcur] if b_xcur else []), writes=[b_xt])
                    if spl:
                        mxbt, b_mxb = mxb[t % 2]
                        xtbt, b_xtb = xtb[t % 2]
                        mxst, b_mxs = mxs[t % 2]
                        xsst, b_xss = xss[t % 2]
                        S.dma("sp", mxbt[:], mixT_d[:, :, tsl1].rearrange("c p s -> p c s"), reads=[b_mixT], writes=[b_mxb])
                        S.dma("sp", xtbt[:], x_cur[tsl1, :], reads=([b_xcur] if b_xcur else []), writes=[b_xtb])
                        act(mxst[:, 0:4, :], mxt[:, 0:4, :], AF.Identity, [b_mx, b_par], [b_mxs], scale=par[:, 0:1])
                        stt("dve", mxst[:, 0:4, :], mxbt[:, 0:4, :], par[:, 1:2], mxst[:, 0:4, :], ALU.mult, ALU.add,
                            [b_mxb, b_par, b_mxs], [b_mxs])
                        S.dma("sp", mxst[:, 4:8, :], mixT_d[4:8, :, tsl].rearrange("c p s -> p c s"), reads=[b_mixT], writes=[b_mxs])
                        act(xsst[:], xtt[:], AF.Identity, [b_xt, b_par], [b_xss], scale=par[:, 0:1])
                        stt("dve", xsst[:], xtbt[:], par[:, 1:2], xsst[:], ALU.mult, ALU.add, [b_xtb, b_par, b_xss], [b_xss])
                        mxt, b_mx = mxst, b_mxs
                        xtt, b_xt = xsst, b_xss
                    for half in range(2):
                        pw, b_pw = pW[(t % 2) * 2 + half]
                        hs = slice(half * 512, (half + 1) * 512)
                        for c in range(8):
                            mm(pw[:], mxt[:, c, :], wout[:, c, hs], c == 0, c == 7, [b_mx, b_woutk[c]], [b_pw])
                        tt("dve", xot[:, hs], pw[:], gta[:, hs], ALU.mult, [b_pw, b_gta], [b_xo])
                        tt("pool", xot[:, hs], xot[:, hs], xtt[:, hs], ALU.add, [b_xo, b_xt], [b_xo])
                    S.dma("pool", xmid_d[tsl, :], xot[:], reads=[b_xo], writes=[b_xmid], sem="xo_st%d" % (t % 2))
            S.barrier()
            phw.close()
            if stop == 'p3a':
                ph23.close()
                break

            with contextlib.ExitStack() as ph:
                gtf, b_gtf = SB(ph, "gtf", [128, D], F32)
                S.dma("sp", gtf[:], gate_d[l, 1], reads=[b_gate], writes=[b_gtf])
                xm = [SB(ph, "xm%d" % i, [128, 2, D], F32) for i in range(3)]
                ss, b_ss = SB(ph, "ss", [128, 2], F32)
                xn2 = [SB(ph, "xn2_%d" % i, [128, 2, D], BF16) for i in range(2)]
                h2Ts = [SB(ph, "h2T%d" % i, [128, 8, 256], BF16) for i in range(2)]
                aT, b_aT = SB(ph, "aT", [128, 32, 256], BF16)
                rt = [SB(ph, "rt%d" % i, [128, 256], F32) for i in range(2)]
                yo = [SB(ph, "yo%d" % i, [128, D], F32) for i in range(2)]
                pTs2 = [PS(ph, "pT%d" % i, [128, 8, 128], BF16) for i in range(2)]
                pU = [PS(ph, "pU%d" % i, [128, 256], F32) for i in range(2)]
                pDn = [PS(ph, "pDn%d" % i, [128, 512], F32) for i in range(4)]

                def prep_norm(g):
                    xmt, b_xm = xm[g % 3]
                    xnt, b_xnt = xn2[g % 2]
                    for j in range(2):
                        t = 2 * g + j
                        S.dma("sp", xmt[:, j, :], xmid_d[t * 128:(t + 1) * 128, :], reads=[b_xmid], writes=[b_xm])
                    for j in range(2):
                        act(xnt[:, j, :], xmt[:, j, :], AF.Square, [b_xm], [b_xnt, b_ss], accum=ss[:, j:j + 1])
                    rsqrt_cols(ss[:], ss[:], 1.0 / D, [b_ss], [b_ss])
                    for j in range(2):
                        ts("dve", xnt[:, j, :], xmt[:, j, :], ss[:, j:j + 1], None, ALU.mult, None, [b_xm, b_ss], [b_xnt])

                def prep_tr(g):
                    xnt, b_xnt = xn2[g % 2]
                    h2T, b_h2T = h2Ts[g % 2]
                    for j in range(2):
                        pT, b_pT = pTs2[j]
                        for k in range(8):
                            tr(pT[:, k, :], xnt[:, j, k * 128:(k + 1) * 128], ident[:], [b_xnt, b_ident], [b_pT])
                        for k in range(8):
                            if k % 2 == 0:
                                ts("dve", h2T[:, k, j * 128:(j + 1) * 128], pT[:, k, :], modcol[:, mc + 24 + k: mc + 25 + k],
                                   modcol[:, mc + 16 + k: mc + 17 + k], ALU.mult, ALU.add, [b_pT, b_modcol], [b_h2T])
                            else:
                                act(h2T[:, k, j * 128:(j + 1) * 128], pT[:, k, :], AF.Identity, [b_pT, b_modcol], [b_h2T],
                                    scale=modcol[:, mc + 24 + k: mc + 25 + k], bias=modcol[:, mc + 16 + k: mc + 17 + k])

                yi = 0
                NGE = NG // 2 if (split and last) else NG
                prep_norm(0)
                prep_tr(0)
                for g in range(NGE):
                    xmt, b_xm = xm[g % 3]
                    h2T, b_h2T = h2Ts[g % 2]
                    if g + 1 < NGE:
                        prep_norm(g + 1)
                    for f in range(32):
                        pu, b_pu = pU[f % 2]
                        rtt, b_rt = rt[f % 2]
                        for k in range(8):
                            mm(pu[:], w1[:, k, f * 128:(f + 1) * 128], h2T[:, k, :], k == 0, k == 7, [b_w1k[k], b_h2T], [b_pu])
                        act(rtt[:], pu[:], AF.Relu, [b_pu], [b_rt])
                        tt("dve" if f % 2 == 0 else "pool", aT[:, f, :], rtt[:], rtt[:], ALU.mult, [b_rt], [b_aT])
                    if g + 1 < NGE:
                        prep_tr(g + 1)
                    for j in range(2):
                        t = 2 * g + j
                        tsl = slice(t * 128, (t + 1) * 128)
                        yot, b_yo = yo[yi % 2]
                        yi += 1
                        for half in range(2):
                            pd, b_pd = pDn[j * 2 + half]
                            hs = slice(half * 512, (half + 1) * 512)
                            for f in range(32):
                                mm(pd[:], aT[:, f, j * 128:(j + 1) * 128], w2[:, f, hs], f == 0, f == 31, [b_aT, b_w2f[f]], [b_pd])
                            tt("dve", yot[:, hs], pd[:], gtf[:, hs], ALU.mult, [b_pd, b_gtf], [b_yo])
                            tt("pool", yot[:, hs], yot[:, hs], xmt[:, j, hs], ALU.add, [b_yo, b_xm], [b_yo])
                        d = S.dma("pool", x_nxt[tsl, :], yot[:], reads=[b_yo], writes=[b_xnxt], sem="yo_st%d" % ((yi - 1) % 2))
                        if last:
                            out_dmas.append(d)
            S.barrier()
            ph23.close()

        S.emit(final_waits=out_dmas)
    return nc


def host_inputs(b, S_LEN, DEPTH, x, c, positions, _parity=0, *, w_ada, b_ada, w_in, w_gate_up, b_gate, gla_out_norm, q_a_norm,
                w_q_up, kv_a_norm, w_kv_up, q_norm_nope, k_norm_nope, q_norm_rope, k_norm_rope,
                w_out, w_mlp_up, w_mlp_down):
    f32 = np.float32
    NT = S_LEN // 128
    A = lambda a: np.ascontiguousarray(np.asarray(a))

    def bc(v, n=128):
        v = np.asarray(v, dtype=f32)
        return A(np.broadcast_to(v[:, None, :], (v.shape[0], n, v.shape[1])))

    inv_freq = (10000.0 ** (-np.arange(0, 64, 2, dtype=f32) / f32(64))).astype(f32)
    b_ada = np.asarray(b_ada, dtype=f32)
    d = {
        "x": A(np.asarray(x[b], dtype=f32)),
        "ccol": A(np.asarray(c[b], dtype=f32).reshape(8, 128).T),
        "pos": A(np.asarray(positions[b]).astype(np.int32).reshape(NT, 128).T),
        "invf": A(np.broadcast_to(inv_freq[None, :], (128, 32))),
        "parcol": A(np.broadcast_to(np.array([[1.0 - _parity, float(_parity)]], dtype=f32), (128, 2))),
        "w_ada": A(np.asarray(w_ada, dtype=f32)),
        "bada_col": A(b_ada.reshape(DEPTH, 48, 128).transpose(0, 2, 1)),
        "bada_gate": A(np.broadcast_to(b_ada.reshape(DEPTH, 6, 1, D)[:, [2, 5]], (DEPTH, 2, 128, D))),
        "w_in": A(np.asarray(w_in, dtype=f32)),
        "w_gate_up": A(np.asarray(w_gate_up, dtype=f32)),
        "bgate_bc": bc(b_gate),
        "gon_bc4": bc(np.tile(np.asarray(gla_out_norm, dtype=f32), (1, 4))),
        "qan_col": A(np.asarray(q_a_norm, dtype=f32).reshape(DEPTH, 2, 128).transpose(0, 2, 1)),
        "w_q_up": A(np.asarray(w_q_up, dtype=f32)),
        "kvan_col": A(np.asarray(kv_a_norm, dtype=f32).reshape(DEPTH, 128, 1)),
        "w_kv_up": A(np.asarray(w_kv_up, dtype=f32)),
        "qnn_bc": bc(q_norm_nope), "knn_bc": bc(k_norm_nope),
        "qnr_bc": bc(q_norm_rope), "knr_bc": bc(k_norm_rope),
        "w_out": A(np.asarray(w_out, dtype=f32)),
        "w_mlp_up": A(np.asarray(w_mlp_up, dtype=f32)),
        "w_mlp_down": A(np.asarray(w_mlp_down, dtype=f32)),
    }
    return d


_NC_CACHE = {}


def kernel(**inputs):
    x = np.asarray(inputs["x"])
    B, S_LEN, _ = x.shape
    DEPTH = np.asarray(inputs["w_ada"]).shape[0]
    key = (S_LEN, DEPTH)
    if key not in _NC_CACHE:
        _NC_CACHE[key] = build(S_LEN, DEPTH, split=True)
    nc = _NC_CACHE[key]
    in_maps = []
    for b in range(B):
        base = host_inputs(b, S_LEN, DEPTH, _parity=0, **inputs)
        in_maps.append(base)
        m1 = dict(base)
        m1["parcol"] = host_inputs_par(1)
        in_maps.append(m1)
    res = run_bass_kernel_spmd(nc, in_maps, core_ids=list(range(2 * B)))
    out = np.empty((B, S_LEN, D), dtype=np.float32)
    ov = out.reshape(B, S_LEN // 1024, 2, 512, D)
    for b in range(B):
        for p in range(2):
            ov[b, :, p] = np.asarray(res.results[2 * b + p]["y"], dtype=np.float32).reshape(S_LEN // 1024, 512, D)
    return out


def host_inputs_par(p):
    return np.ascontiguousarray(np.broadcast_to(np.array([[1.0 - p, float(p)]], dtype=np.float32), (128, 2)))
```

```python
import contextlib
import math
import numpy as np
import concourse.bass as bass
import concourse.mybir as mybir
from concourse.bass_utils import run_bass_kernel_spmd

F32 = mybir.dt.float32
BF16 = mybir.dt.bfloat16
I32 = mybir.dt.int32
ALU = mybir.AluOpType
AF = mybir.ActivationFunctionType

ENGS = ("pe", "act", "dve", "pool", "sp")
EST_DUR = {"pe": 0.15, "act": 0.3, "dve": 0.3, "pool": 0.5, "sp": 0.1}
import os as _os
SEM_SHARE = _os.environ.get("SEM_SHARE", "0") == "1"
SEQ_CHAINS = _os.environ.get("SEQ_CHAINS", "0") == "1"
NO_KPR = _os.environ.get("NO_KPR", "0") == "1"
SKIP = set(_os.environ.get("SKIP_PHASES", "").split(","))
OLD_ORDER = _os.environ.get("OLD_ORDER", "0") == "1"
SEM_MAP = {"wm0": "A0", "wm1": "A1", "xt0": "A0", "xt1": "A1", "xt2": "A2", "cst0": "B0", "cst1": "B1", "cst2": "B2",
           "KTs0": "A0", "KTs1": "A1", "Vh0": "B0", "Vh1": "B1", "qns0": "C0", "qns1": "C1",
           "qps0": "D0", "qps1": "D1", "mx0": "B0", "mx1": "B1", "xm0": "A0", "xm1": "A1", "xm2": "A2",
           "ob_st0": "ST0", "ob_st1": "ST1", "xo_st0": "ST0", "xo_st1": "ST1", "yo_st0": "ST0", "yo_st1": "ST1"}


class Buf:
    __slots__ = ("name", "lw", "rd", "rd_dma")

    def __init__(self, name):
        self.name = name
        self.lw = None
        self.rd = {}
        self.rd_dma = []


class Ins:
    __slots__ = ("eng", "fn", "deps", "signal", "cnt", "is_dma", "dsem", "dval", "tfin")

    def __init__(self, eng, fn, is_dma=False):
        self.eng = eng
        self.fn = fn
        self.deps = []
        self.signal = False
        self.cnt = 0
        self.is_dma = is_dma
        self.dsem = None
        self.dval = 0
        self.tfin = 0.0


class Sched:
    def __init__(self, nc):
        self.nc = nc
        self.q = {e: [] for e in ENGS}
        self.dma_sems = {}
        self.all_dma = []
        self.last_on_sem = {}
        self.bar = None
        self.bar_done = {}
        self.nbuf = 0
        self.eng_free = {e: 0.0 for e in ENGS}
        self.step_max = 0.0

    def buf(self, name=None):
        self.nbuf += 1
        return Buf(name or ("b%d" % self.nbuf))

    def _collect(self, ins, reads, writes):
        deps = {}
        pe = (ins.eng == "pe" and not ins.is_dma)

        def add(d):
            if d is None or d is ins:
                return
            if pe and d.eng == "pe" and not d.is_dma:
                return
            if d.is_dma:
                d = self.last_on_sem[d.dsem]
                if d is ins:
                    return
            deps[id(d)] = d

        for b in reads:
            add(b.lw)
        for b in writes:
            add(b.lw)
            for d in b.rd.values():
                add(d)
            for d in b.rd_dma:
                add(d)
        if self.bar is not None and not self.bar_done.get(ins.eng):
            for d in self.bar:
                add(d)
            self.bar_done[ins.eng] = True
        for b in reads:
            if ins.is_dma:
                b.rd_dma.append(ins)
            else:
                b.rd[ins.eng] = ins
        for b in writes:
            b.lw = ins
            b.rd = {}
            b.rd_dma = []
        ins.deps = list(deps.values())
        ready = max([d.tfin for d in ins.deps], default=0.0) + (0.25 if ins.deps else 0.0)
        start = max(ready, self.eng_free[ins.eng])
        if ins.is_dma:
            self.eng_free[ins.eng] = start + 0.1
            ins.tfin = start + 2.5
        else:
            ins.tfin = start + EST_DUR[ins.eng]
            self.eng_free[ins.eng] = ins.tfin
        if ins.tfin > self.step_max:
            self.step_max = ins.tfin

    def op(self, eng, fn, reads=(), writes=()):
        ins = Ins(eng, fn)
        self._collect(ins, reads, writes)
        self.q[eng].append(ins)
        return ins

    def dma(self, eng, out, in_, reads=(), writes=(), sem=None, nobar=False):
        if sem is None:
            sem = (list(writes) + list(reads))[0].name
        if SEM_SHARE:
            sem = SEM_MAP.get(sem, "G_const" if not sem.endswith("_st") else "ST")
        ins = Ins(eng, lambda e: e.dma_start(out=out, in_=in_), is_dma=True)
        tot = self.dma_sems.get(sem, 0) + 16
        self.dma_sems[sem] = tot
        ins.dsem = sem
        ins.dval = tot
        self._collect(ins, reads, writes)
        self.last_on_sem[sem] = ins
        self.q[eng].append(ins)
        if not nobar:
            self.all_dma.append(ins)
        return ins

    def barrier(self):
        deps = []
        for e in ENGS:
            for ins in reversed(self.q[e]):
                if not ins.is_dma:
                    deps.append(ins)
                    break
        deps.extend(self.all_dma)
        self.all_dma = []
        last = {}
        keep = []
        for d in deps:
            if d.is_dma:
                if d.dsem not in last or last[d.dsem].dval < d.dval:
                    last[d.dsem] = d
            else:
                keep.append(d)
        self.bar = keep + list(last.values())
        self.bar_done = {}

    def emit(self, final_waits=()):
        nc = self.nc
        for e in ENGS:
            for ins in self.q[e]:
                for d in ins.deps:
                    if not d.is_dma:
                        d.signal = True
        for e in ENGS:
            c = 0
            for ins in self.q[e]:
                if ins.signal and not ins.is_dma:
                    c += 1
                ins.cnt = c
        with contextlib.ExitStack() as st:
            esem = {e: st.enter_context(nc.semaphore("es_" + e)) for e in ENGS}
            dsem = {k: st.enter_context(nc.semaphore("ds_%d" % i)) for i, k in enumerate(self.dma_sems)}
            block = st.enter_context(nc.Block())
            sched = self

            def run(e, eh):
                waited = {}
                for ins in sched.q[e]:
                    need = {}
                    for d in ins.deps:
                        if d.is_dma:
                            s, v, key = dsem[d.dsem], d.dval, ("d", d.dsem)
                        else:
                            s, v, key = esem[d.eng], d.cnt, ("e", d.eng)
                        if key not in need or need[key][1] < v:
                            need[key] = (s, v)
                    for key, (s, v) in need.items():
                        if waited.get(key, 0) >= v:
                            continue
                        waited[key] = v
                        eh.wait_ge(s, v)
                    bi = ins.fn(eh)
                    if ins.is_dma:
                        bi.then_inc(dsem[ins.dsem], 16)
                    elif ins.signal:
                        bi.then_inc(esem[e], 1)
                if e == "sp":
                    for d in final_waits:
                        eh.wait_ge(dsem[d.dsem], d.dval)

            @block.tensor
            def _(eh):
                run("pe", eh)

            @block.scalar
            def _(eh):
                run("act", eh)

            @block.vector
            def _(eh):
                run("dve", eh)

            @block.gpsimd
            def _(eh):
                run("pool", eh)

            @block.sync
            def _(eh):
                run("sp", eh)


D = 1024
DFF = 4096
EPS = 1e-6
TWO_PI = 2.0 * math.pi
C1 = 6.28125
C2 = TWO_PI - C1


def build(S_LEN, DEPTH, dbg=False, stop=None, split=False):
    NT = S_LEN // 128
    NC4 = S_LEN // 512
    NG = S_LEN // 256
    nc = bass.Bass("TRN2", target_bir_lowering=False)
    S = Sched(nc)

    def din(name, shape, dt=F32):
        return nc.dram_tensor(name, shape, dt, kind="ExternalInput").ap()

    def dscr(name, shape, dt):
        return nc.dram_tensor(name, shape, dt, kind=("ExternalOutput" if dbg else "Internal")).ap()

    x_in = din("x", [S_LEN, D])
    ccol = din("ccol", [128, 8])
    pos_in = din("pos", [128, NT], I32)
    invf_in = din("invf", [128, 32])
    par_in = din("parcol", [128, 2])
    w_ada = din("w_ada", [DEPTH, D, 6 * D])
    bada_col = din("bada_col", [DEPTH, 128, 48])
    bada_gate = din("bada_gate", [DEPTH, 2, 128, D])
    w_in = din("w_in", [DEPTH, D, 2000])
    wgu_in = din("w_gate_up", [DEPTH, 16, 256])
    bgate_in = din("bgate_bc", [DEPTH, 128, 256])
    gon_in = din("gon_bc4", [DEPTH, 128, 512])
    qan_in = din("qan_col", [DEPTH, 128, 2])
    wqu_in = din("w_q_up", [DEPTH, 256, 768])
    kvan_in = din("kvan_col", [DEPTH, 128, 1])
    wkvu_in = din("w_kv_up", [DEPTH, 128, 1024])
    qnn_in = din("qnn_bc", [DEPTH, 128, 128])
    knn_in = din("knn_bc", [DEPTH, 128, 128])
    qnr_in = din("qnr_bc", [DEPTH, 128, 64])
    knr_in = din("knr_bc", [DEPTH, 128, 64])
    wout_in = din("w_out", [DEPTH, D, D])
    w1_in = din("w_mlp_up", [DEPTH, D, DFF])
    w2_in = din("w_mlp_down", [DEPTH, DFF, D])
    y_out = nc.dram_tensor("y", [S_LEN // 2 if split else S_LEN, D], F32, kind="ExternalOutput").ap()

    xs = [dscr("xs%d" % i, [S_LEN, D], F32) for i in range(2)]
    xmid_d = dscr("xmid", [S_LEN, D], F32)
    mixT_d = dscr("mixT", [8, 128, S_LEN], BF16)
    KT_d = dscr("KT", [4, 128, S_LEN], BF16)
    KPE_d = dscr("KPE", [64, S_LEN], BF16)
    V_d = dscr("Vd", [S_LEN, 512], BF16)
    QT_d = dscr("QT", [4, 128, S_LEN], BF16)
    QPE_d = dscr("QPE", [2, 128, S_LEN], BF16)
    cos_d = dscr("cosd", [128, NT, 32], F32)
    sin_d = dscr("sind", [128, NT, 32], F32)
    gate_d = dscr("gated", [DEPTH, 2, 128, D], F32)
    b_xs = [S.buf("xs0"), S.buf("xs1")]
    b_xmid = S.buf("xmid")
    b_mixT = S.buf("mixT")
    b_KT, b_KPE, b_V, b_QT, b_QPE = S.buf("KT"), S.buf("KPE"), S.buf("V"), S.buf("QT"), S.buf("QPE")
    b_cos, b_sin, b_gate = S.buf("cosd"), S.buf("sind"), S.buf("gated")
    b_y = S.buf("y")
    out_dmas = []

    def mm(out, lhsT, rhs, start, stop, R, W):
        S.op("pe", lambda e: e.matmul(out=out, lhsT=lhsT, rhs=rhs, start=start, stop=stop), R, W)

    def tr(out, in_, ident, R, W):
        S.op("pe", lambda e: e.transpose(out=out, in_=in_, identity=ident), R, W)

    def act(out, in_, func, R, W, scale=1.0, bias=0.0, accum=None):
        if accum is None:
            S.op("act", lambda e: e.activation(out=out, in_=in_, func=func, bias=bias, scale=scale), R, W)
        else:
            S.op("act", lambda e: e.activation(out=out, in_=in_, func=func, bias=bias, scale=scale, accum_out=accum), R, W)

    def tt(eng, out, in0, in1, op, R, W):
        S.op(eng, lambda e: e.tensor_tensor(out=out, in0=in0, in1=in1, op=op), R, W)

    def ts(eng, out, in0, s1, s2, op0, op1, R, W):
        if s2 is None:
            S.op(eng, lambda e: e.tensor_scalar(out=out, in0=in0, scalar1=s1, scalar2=None, op0=op0), R, W)
        else:
            S.op(eng, lambda e: e.tensor_scalar(out=out, in0=in0, scalar1=s1, scalar2=s2, op0=op0, op1=op1), R, W)

    def stt(eng, out, in0, scalar, in1, op0, op1, R, W, accum=None):
        if accum is None:
            S.op(eng, lambda e: e.scalar_tensor_tensor(out=out, in0=in0, scalar=scalar, in1=in1, op0=op0, op1=op1), R, W)
        else:
            S.op(eng, lambda e: e.scalar_tensor_tensor(out=out, in0=in0, scalar=scalar, in1=in1, op0=op0, op1=op1, accum_out=accum), R, W)

    def cp(eng, out, in_, R, W):
        if eng == "act":
            S.op("act", lambda e: e.copy(out=out, in_=in_), R, W)
        else:
            S.op(eng, lambda e: e.tensor_copy(out=out, in_=in_), R, W)

    def recip(out, in_, R, W):
        S.op("dve", lambda e: e.reciprocal(out=out, in_=in_), R, W)

    def memset(eng, ap, val, W):
        S.op(eng, lambda e: e.memset(ap, val), (), W)

    def asel(out, in_, pattern, cmp, fill, base, cm, R, W):
        S.op("pool", lambda e: e.affine_select(out=out, in_=in_, pattern=pattern, compare_op=cmp, fill=fill,
                                               base=base, channel_multiplier=cm), R, W)

    def rsqrt_cols(dst, src, scale, R, W):
        act(dst, src, AF.Ln, R, W, scale=scale, bias=EPS)
        act(dst, dst, AF.Exp, W, W, scale=-0.5)

    with contextlib.ExitStack() as top:
        uid = [0]

        def SB(stack, name, shape, dt):
            uid[0] += 1
            t = stack.enter_context(nc.sbuf_tensor("%s_s%d" % (name, uid[0]), shape, dt))
            return t, S.buf(name)

        def PS(stack, name, shape, dt):
            uid[0] += 1
            t = stack.enter_context(nc.psum_tensor("%s_p%d" % (name, uid[0]), shape, dt))
            return t, S.buf(name)

        ident, b_ident = SB(top, "ident", [128, 128], BF16)
        maskU4, b_maskU4 = SB(top, "maskU4", [128, 4, 128], BF16)
        ones_f, b_onesf = SB(top, "ones_f", [128, 128], F32)
        ones_b, b_onesb = SB(top, "ones_b", [128, 128], BF16)
        maskD, b_maskD = SB(top, "maskD", [128, 4, 512], BF16)
        modcol, b_modcol = SB(top, "modcol", [128, DEPTH * 32], F32)

        par, b_par = SB(top, "par", [128, 2], F32)
        S.dma("sp", par[:], par_in, writes=[b_par])
        memset("pool", ident[:], 0.0, [b_ident])
        asel(ident[:], ident[:], [[-1, 128]], ALU.not_equal, 1.0, 0, 1, [b_ident], [b_ident])
        memset("pool", maskU4[:], 1.0, [b_maskU4])
        for h in range(4):
            asel(maskU4[:, h, :], maskU4[:, h, :], [[1, 128]], ALU.is_ge, 0.0, 0, -1, [b_maskU4], [b_maskU4])
        memset("pool", ones_f[:], 1.0, [b_onesf])
        memset("pool", ones_b[:], 1.0, [b_onesb])
        memset("pool", maskD[:], 1.0, [b_maskD])
        for r in range(4):
            asel(maskD[:, r, :], maskD[:, r, :], [[1, 512]], ALU.is_ge, 0.0, -128 * r, -1, [b_maskD], [b_maskD])

        with contextlib.ExitStack() as ph:
            posi, b_posi = SB(ph, "posi", [128, NT], I32)
            posf, b_posf = SB(ph, "posf", [128, NT], F32)
            invf, b_invf = SB(ph, "invf", [128, 32], F32)
            ang, b_ang = SB(ph, "ang", [128, NT, 32], F32)
            uu, b_uu = SB(ph, "uu", [128, NT, 32], F32)
            ni, b_ni = SB(ph, "ni", [128, NT, 32], I32)
            nf, b_nf = SB(ph, "nf", [128, NT, 32], F32)
            mk, b_mk = SB(ph, "mk", [128, NT, 32], F32)
            sn, b_sn = SB(ph, "sn", [128, NT, 32], F32)
            cs, b_cs = SB(ph, "cs", [128, NT, 32], F32)
            S.dma("sp", posi[:], pos_in, writes=[b_posi])
            S.dma("sp", invf[:], invf_in, writes=[b_invf])
            cp("dve", posf[:], posi[:], [b_posi], [b_posf])
            for t in range(NT):
                ts("dve", ang[:, t, :], invf[:], posf[:, t:t + 1], None, ALU.mult, None, [b_invf, b_posf], [b_ang])
            ts("dve", uu[:], ang[:], 1.0 / TWO_PI, None, ALU.mult, None, [b_ang], [b_uu])
            cp("dve", ni[:], uu[:], [b_uu], [b_ni])
            cp("dve", nf[:], ni[:], [b_ni], [b_nf])
            stt("dve", ang[:], nf[:], -C1, ang[:], ALU.mult, ALU.add, [b_nf, b_ang], [b_ang])
            stt("dve", ang[:], nf[:], -C2, ang[:], ALU.mult, ALU.add, [b_nf, b_ang], [b_ang])
            ts("dve", mk[:], ang[:], math.pi, None, ALU.is_gt, None, [b_ang], [b_mk])
            stt("dve", ang[:], mk[:], -TWO_PI, ang[:], ALU.mult, ALU.add, [b_mk, b_ang], [b_ang])
            ts("dve", mk[:], ang[:], -math.pi, None, ALU.is_lt, None, [b_ang], [b_mk])
            stt("dve", ang[:], mk[:], TWO_PI, ang[:], ALU.mult, ALU.add, [b_mk, b_ang], [b_ang])
            ts("dve", ang[:], ang[:], math.pi, -math.pi, ALU.min, ALU.max, [b_ang], [b_ang])
            act(sn[:], ang[:], AF.Sin, [b_ang], [b_sn])
            stt("dve", uu[:], ang[:], -1.0, ang[:], ALU.mult, ALU.max, [b_ang], [b_uu])
            ts("dve", uu[:], uu[:], -1.0, math.pi / 2, ALU.mult, ALU.add, [b_uu], [b_uu])
            act(cs[:], uu[:], AF.Sin, [b_uu], [b_cs])
            S.dma("sp", cos_d, cs[:], reads=[b_cs], writes=[b_cos], sem="cs_st")
            S.dma("sp", sin_d, sn[:], reads=[b_sn], writes=[b_sin], sem="sn_st")

            if stop != 'rope':
                cc, b_cc = SB(ph, "cc", [128, 8], F32)
                ce, b_ce = SB(ph, "ce", [128, 8], F32)
                cond, b_cond = SB(ph, "cond", [128, 8], F32)
                cond_rep, b_crep = SB(ph, "cond_rep", [128, 8, 128], BF16)
                condb, b_condb = SB(ph, "condb", [128, 16], BF16)
                wm = [SB(ph, "wm%d" % i, [128, 8, D], BF16) for i in range(2)]
                bcol, b_bcol = SB(ph, "bcol", [128, DEPTH * 48], F32)
                bgt, b_bgt = SB(ph, "bgt", [128, D], F32)
                gsb, b_gsb = SB(ph, "gsb", [128, D], F32)
                pg = [PS(ph, "pg%d" % i, [128, 512], F32) for i in range(2)]
                pc, b_pc = PS(ph, "pc", [128, 8], F32)
                S.dma("sp", cc[:], ccol, writes=[b_cc])
                for l in range(DEPTH):
                    S.dma("sp", bcol[:, l * 48:(l + 1) * 48], bada_col[l], writes=[b_bcol])
                act(ce[:], cc[:], AF.Exp, [b_cc], [b_ce], scale=-1.0)
                ts("dve", ce[:], ce[:], 1.0, None, ALU.add, None, [b_ce], [b_ce])
                recip(ce[:], ce[:], [b_ce], [b_ce])
                tt("dve", cond[:], cc[:], ce[:], ALU.mult, [b_cc, b_ce], [b_cond])
                memset("dve", condb[:], 0.0, [b_condb])
                cp("dve", condb[:, 0:8], cond[:], [b_cond, b_condb], [b_condb])
                for k in range(8):
                    ts("dve", cond_rep[:, k, :], ones_f[:], cond[:, k:k + 1], None, ALU.mult, None, [b_onesf, b_cond], [b_crep])
                li = 0
                for l in range(DEPTH):
                    for m in range(6):
                        wt, b_wt = wm[li % 2]
                        li += 1
                        S.dma("pool", wt[:], w_ada[l, :, m * D:(m + 1) * D].rearrange("(k p) n -> p k n", p=128), writes=[b_wt])
                        if m in (2, 5):
                            gi = 0 if m == 2 else 1
                            S.dma("sp", bgt[:], bada_gate[l, gi], writes=[b_bgt])
                            for half in range(2):
                                pgt, b_pg = pg[half]
                                for k in range(8):
                                    mm(pgt[:], cond_rep[:, k, :], wt[:, k, half * 512:(half + 1) * 512], k == 0, k == 7,
                                       [b_crep, b_wt], [b_pg])
                                tt("dve", gsb[:, half * 512:(half + 1) * 512], pgt[:], bgt[:, half * 512:(half + 1) * 512],
                                   ALU.add, [b_pg, b_bgt], [b_gsb])
                            S.dma("sp", gate_d[l, gi], gsb[:], reads=[b_gsb], writes=[b_gate], sem="gate_st")
                        else:
                            mi = {0: 0, 1: 1, 3: 2, 4: 3}[m]
                            for ko in range(8):
                                for k in range(8):
                                    mm(pc[:, ko:ko + 1], wt[:, k, ko * 128:(ko + 1) * 128], condb[:, k:k + 1], k == 0, k == 7,
                                       [b_wt, b_condb], [b_pc])
                            dst = modcol[:, l * 32 + mi * 8: l * 32 + mi * 8 + 8]
                            tt("dve", dst, pc[:], bcol[:, l * 48 + m * 8: l * 48 + m * 8 + 8], ALU.add, [b_pc, b_bcol], [b_modcol])
                            if m in (1, 4):
                                ts("dve", dst, dst, 1.0, None, ALU.add, None, [b_modcol], [b_modcol])
        S.barrier()

        for l in range(DEPTH if stop not in ('setup', 'rope') else 0):
            x_cur = x_in if l == 0 else xs[(l - 1) % 2]
            b_xcur = None if l == 0 else b_xs[(l - 1) % 2]
            last = (l == DEPTH - 1)
            x_nxt = y_out if last else xs[l % 2]
            b_xnxt = b_y if last else b_xs[l % 2]
            mc = l * 32

            with contextlib.ExitStack() as ph:
                win, b_win = SB(ph, "win", [128, 8, 2000], BF16)
                wgu, b_wgu = SB(ph, "wgu", [16, 256], BF16)
                wqu, b_wqu = SB(ph, "wqu", [128, 2, 768], BF16)
                wkvu, b_wkvu = SB(ph, "wkvu", [128, 1024], BF16)
                qan, b_qan = SB(ph, "qan", [128, 2], F32)
                kvan, b_kvan = SB(ph, "kvan", [128, 1], F32)
                bgate, b_bgate = SB(ph, "bgate", [128, 256], F32)
                gon4, b_gon4 = SB(ph, "gon4", [128, 512], F32)
                qnn, b_qnn = SB(ph, "qnn", [128, 128], F32)
                knn, b_knn = SB(ph, "knn", [128, 128], F32)
                qnr, b_qnr = SB(ph, "qnr", [128, 64], F32)
                knr, b_knr = SB(ph, "knr", [128, 64], F32)
                b_wink = [S.buf("win%d" % k) for k in range(8)]
                for k in range(8):
                    S.dma("pool", win[:, k, :], w_in[l, k * 128:(k + 1) * 128, :], writes=[b_wink[k]], sem="win_ld")
                S.dma("pool", wgu[:], wgu_in[l], writes=[b_wgu])
                S.dma("pool", wqu[:], wqu_in[l].rearrange("(k p) n -> p k n", p=128), writes=[b_wqu])
                S.dma("pool", wkvu[:], wkvu_in[l], writes=[b_wkvu])
                S.dma("sp", qan[:], qan_in[l], writes=[b_qan])
                S.dma("sp", kvan[:], kvan_in[l], writes=[b_kvan])
                S.dma("sp", bgate[:], bgate_in[l], writes=[b_bgate])
                S.dma("sp", gon4[:], gon_in[l], writes=[b_gon4])
                S.dma("sp", qnn[:], qnn_in[l], writes=[b_qnn])
                S.dma("sp", knn[:], knn_in[l], writes=[b_knn])
                S.dma("sp", qnr[:], qnr_in[l], writes=[b_qnr])
                S.dma("sp", knr[:], knr_in[l], writes=[b_knr])
                for c in range(2):
                    ts("dve", wqu[:, c, :], wqu[:, c, :], qan[:, c:c + 1], None, ALU.mult, None, [b_wqu, b_qan], [b_wqu])
                ts("dve", wkvu[:], wkvu[:], kvan[:, 0:1], None, ALU.mult, None, [b_wkvu, b_kvan], [b_wkvu])
                qsc = 192.0 ** -0.5
                ts("dve", qnn[:], qnn[:], qsc, None, ALU.mult, None, [b_qnn], [b_qnn])
                ts("dve", qnr[:], qnr[:], qsc, None, ALU.mult, None, [b_qnr], [b_qnr])

                xt = [SB(ph, "xt%d" % i, [128, D], F32) for i in range(3)]
                cst = [SB(ph, "cst%d" % i, [128, 2, 32], F32) for i in range(3)]
                sq, b_sq = SB(ph, "sq", [128, D], F32)
                ss, b_ss = SB(ph, "ss", [128, 1], F32)
                xn, b_xn = SB(ph, "xn", [128, D], BF16)
                hT, b_hT = SB(ph, "hT", [128, 8, 128], BF16)
                mD, b_mD = SB(ph, "mD", [128, 400], BF16)
                ss3, b_ss3 = SB(ph, "ss3", [128, 3], F32)
                rs3, b_rs3 = SB(ph, "rs3", [128, 3], F32)
                rq2, b_rq2 = SB(ph, "rq2", [128, 3], F32)
                mT, b_mT = SB(ph, "mT", [128, 4, 128], BF16)
                pre, b_pre = SB(ph, "pre", [128, 256], F32)
                lg, b_lg = SB(ph, "lg", [128, 256], F32)
                lgh, b_lgh = SB(ph, "lgh", [128, 256], BF16)
                lgl, b_lgl = SB(ph, "lgl", [128, 256], BF16)
                eb, b_eb = SB(ph, "eb", [128, 256], F32)
                enb, b_enb = SB(ph, "enb", [128, 256], F32)
                ebl, b_ebl = SB(ph, "ebl", [64, 4], F32)
                qg, b_qg = SB(ph, "qg", [128, 256], BF16)
                kg, b_kg = SB(ph, "kg", [128, 256], BF16)
                qkT, b_qkT = SB(ph, "qkT", [64, 8, 128], BF16)
                vsb, b_vsb = SB(ph, "vsb", [128, 512], BF16)
                eo, b_eo = SB(ph, "eo", [128, 512], F32)
                gog, b_gog = SB(ph, "gog", [128, 512], F32)
                ATs, b_ATs = SB(ph, "ATs", [128, 4, 128], BF16)
                stt_, b_st = SB(ph, "gst", [64, 4, 128], F32)
                stb, b_stb = SB(ph, "gstb", [64, 4, 128], BF16)
                sso, b_sso = SB(ph, "sso", [128, 4], F32)
                go, b_go = SB(ph, "go", [128, 512], BF16)
                gT4 = [SB(ph, "gTq%d" % i, [128, 4, 512], BF16) for i in range(2)]
                ss8, b_ss8 = SB(ph, "ss8", [128, 8], F32)
                fac8, b_fac8 = SB(ph, "fac8", [128, 8], F32)
                qn, b_qn = SB(ph, "qn", [128, 4, 128], BF16)
                zall, b_zall = SB(ph, "zall", [128, 5, 64], F32)
                ra, b_ra = SB(ph, "ra", [128, 5, 32], F32)
                rb, b_rb = SB(ph, "rb", [128, 5, 32], F32)
                rope_o, b_ropeo = SB(ph, "rope_o", [128, 5, 64], BF16)
                qT64 = [SB(ph, "qT6q%d" % i, [128, 6, 512], BF16) for i in range(2)]
                kn, b_kn = SB(ph, "kn", [128, 4, 128], BF16)
                vt, b_vt = SB(ph, "vt", [128, 4, 128], BF16)
                kT54 = [SB(ph, "kT5q%d" % i, [128, 5, 512], BF16) for i in range(2)]
                ssk, b_ssk = SB(ph, "ssk", [128, 4], F32)
                fack, b_fack = SB(ph, "fack", [128, 4], F32)

                pT, b_pT = PS(ph, "pT", [128, 8, 128], BF16)
                pA, b_pA = PS(ph, "pA", [128, 512], F32)
                pB, b_pB = PS(ph, "pB", [128, 512], F32)
                pC, b_pC = PS(ph, "pC", [128, 512], F32)
                pD, b_pD = PS(ph, "pD", [128, 512], F32)
                pM, _ = PS(ph, "pM", [128, 8, 128], BF16)
                b_pMa = b_pMb = S.buf("pM")
                pX, b_pX = PS(ph, "pX", [128, 512], F32)
                pY, b_pY = PS(ph, "pY", [128, 512], F32)
                pX4 = pX[:].rearrange("p (h e) -> p h e", e=128)
                pY4 = pY[:].rearrange("p (h e) -> p h e", e=128)

                memset("dve", stt_[:], 0.0, [b_st])
                memset("dve", stb[:], 0.0, [b_stb])

                sq2, b_sq2 = SB(ph, "sq2", [128, 128], F32)
                sq3, b_sq3 = SB(ph, "sq3", [128, 128], F32)
                b_pM = b_pMa
                kprs = [SB(ph, "kpr%d" % i, [128, 64], F32) for i in range(2)]
                rs3s = [SB(ph, "rs3_%d" % i, [128, 3], F32) for i in range(2)]
                rq2s = [SB(ph, "rq2_%d" % i, [128, 3], F32) for i in range(2)]
                mTs = [SB(ph, "mT%d" % i, [128, 4, 128], BF16) for i in range(2)]
                gqks = [SB(ph, "gqk%d" % i, [128, 512], F32) for i in range(2)]
                vsbs = [SB(ph, "vsb%d" % i, [128, 512], BF16) for i in range(2)]
                gogs = [SB(ph, "gog%d" % i, [128, 512], F32) for i in range(2)]
                pW = [(pA, b_pA), (pB, b_pB)]
                pQ = [(pC, b_pC), (pD, b_pD)]

                def drive(gens):
                    alive = list(gens)
                    while alive:
                        for g in list(alive):
                            try:
                                next(g)
                            except StopIteration:
                                alive.remove(g)

                def issue_loads(t):
                    tsl_ = slice(t * 128, (t + 1) * 128)
                    xtt_, b_xt_ = xt[t % 3]
                    cs_t_, b_cst_ = cst[t % 3]
                    S.dma("sp", xtt_[:], x_cur[tsl_, :], reads=([b_xcur] if b_xcur else []), writes=[b_xt_])
                    S.dma("sp", cs_t_[:, 0, :], cos_d[:, t, :], reads=[b_cos], writes=[b_cst_])
                    S.dma("sp", cs_t_[:, 1, :], sin_d[:, t, :], reads=[b_sin], writes=[b_cst_])

                def prologue(t):
                    sl = t % 2
                    tsl = slice(t * 128, (t + 1) * 128)
                    xtt, b_xt = xt[t % 3]
                    cs_t, b_cst = cst[t % 3]
                    kpr, b_kpr = kprs[sl]
                    rs3, b_rs3 = rs3s[sl]
                    rq2, b_rq2 = rq2s[sl]
                    mT, b_mT = mTs[sl]
                    gqk, b_gqk = gqks[sl]
                    vsb, b_vsb = vsbs[sl]
                    gog, b_gog = gogs[sl]
                    act(sq[:], xtt[:], AF.Square, [b_xt], [b_sq, b_ss], accum=ss[:, 0:1])
                    yield
                    rsqrt_cols(ss[:, 0:1], ss[:, 0:1], 1.0 / D, [b_ss], [b_ss])
                    yield
                    ts("dve", xn[:], xtt[:], ss[:, 0:1], None, ALU.mult, None, [b_xt, b_ss], [b_xn])
                    yield
                    for k in range(8):
                        tr(pT[:, k, :], xn[:, k * 128:(k + 1) * 128], ident[:], [b_xn, b_ident], [b_pT])
                    yield
                    for k in range(8):
                        if k % 2 == 0:
                            ts("dve", hT[:, k, :], pT[:, k, :], modcol[:, mc + 8 + k: mc + 9 + k], modcol[:, mc + k: mc + k + 1],
                               ALU.mult, ALU.add, [b_pT, b_modcol], [b_hT])
                        else:
                            act(hT[:, k, :], pT[:, k, :], AF.Identity, [b_pT, b_modcol], [b_hT],
                                scale=modcol[:, mc + 8 + k: mc + 9 + k], bias=modcol[:, mc + k: mc + k + 1])
                        if k % 4 == 3:
                            yield
                    blks = ((1536, 2000), (0, 512), (512, 1024), (1024, 1536))
                    for bi, (c0, c1) in enumerate(blks):
                        pw, b_pw = pW[bi % 2]
                        for k in range(8):
                            mm(pw[:, 0:c1 - c0], hT[:, k, :], win[:, k, c0:c1], k == 0, k == 7, [b_hT, b_wink[k]], [b_pw])
                        yield
                        if bi == 0:
                            cp("act", mD[:], pw[:, 0:400], [b_pw], [b_mD])
                            cp("act", kpr[:], pw[:, 400:464], [b_pw], [b_kpr])
                            yield
                            act(sq[:, 0:256], pw[:, 16:272], AF.Square, [b_pw], [b_sq, b_ss3], accum=ss3[:, 0:1])
                            act(sq[:, 0:128], pw[:, 272:400], AF.Square, [b_pw], [b_sq, b_ss3], accum=ss3[:, 1:2])
                            act(sq[:, 0:64], pw[:, 400:464], AF.Square, [b_pw], [b_sq, b_ss3], accum=ss3[:, 2:3])
                            yield
                        elif bi == 1:
                            cp("act", gqk[:], pw[:], [b_pw], [b_gqk])
                            yield
                        elif bi == 2:
                            cp("act", vsb[:], pw[:], [b_pw], [b_vsb])
                            yield
                        else:
                            act(eo[:], pw[:], AF.Exp, [b_pw], [b_eo], scale=-1.0)
                            yield
                            act(eo[:], eo[:], AF.Ln, [b_eo], [b_eo], bias=1.0)
                            yield
                            act(eo[:], eo[:], AF.Exp, [b_eo], [b_eo], scale=-1.0)
                            yield
                            tt("dve", gog[:], pw[:], eo[:], ALU.mult, [b_pw, b_eo], [b_gog])
                            yield
                            tt("pool", gog[:], gog[:], gon4[:], ALU.mult, [b_gog, b_gon4], [b_gog])
                            yield
                    act(rs3[:, 0:1], ss3[:, 0:1], AF.Ln, [b_ss3], [b_rs3], scale=1.0 / 256, bias=EPS)
                    act(rs3[:, 1:2], ss3[:, 1:2], AF.Ln, [b_ss3], [b_rs3], scale=1.0 / 128, bias=EPS)
                    act(rs3[:, 2:3], ss3[:, 2:3], AF.Ln, [b_ss3], [b_rs3], scale=1.0 / 64, bias=EPS)
                    yield
                    act(rs3[:], rs3[:], AF.Exp, [b_rs3], [b_rs3], scale=-0.5)
                    yield
                    tt("dve", rq2[:], rs3[:], rs3[:], ALU.mult, [b_rs3], [b_rq2])
                    tr(pT[0:16, 0, :], mD[:, 0:16], ident[:], [b_mD, b_ident], [b_pT])
                    tr(pT[:, 1, :], mD[:, 16:144], ident[:], [b_mD, b_ident], [b_pT])
                    tr(pT[:, 2, :], mD[:, 144:272], ident[:], [b_mD, b_ident], [b_pT])
                    tr(pT[:, 3, :], mD[:, 272:400], ident[:], [b_mD, b_ident], [b_pT])
                    yield
                    cp("dve", mT[0:16, 0, :], pT[0:16, 0, :], [b_pT], [b_mT])
                    cp("dve", mT[:, 1:4, :], pT[:, 1:4, :], [b_pT], [b_mT])
                    yield

                def gla_chain(t):
                    sl = t % 2
                    tsl = slice(t * 128, (t + 1) * 128)
                    mT, b_mT = mTs[sl]
                    gqk, b_gqk = gqks[sl]
                    vsb, b_vsb = vsbs[sl]
                    gog, b_gog = gogs[sl]
                    mm(pX[:, 0:256], mT[0:16, 0, :], wgu[:], True, True, [b_mT, b_wgu], [b_pX])
                    tt("dve", pre[:], pX[:, 0:256], bgate[:], ALU.add, [b_pX, b_bgate], [b_pre])
                    yield
                    act(pre[:], pre[:], AF.Exp, [b_pre], [b_pre], scale=-1.0)
                    yield
                    act(lg[:], pre[:], AF.Ln, [b_pre], [b_lg], bias=1.0)
                    yield
                    cp("dve", lgh[:], lg[:], [b_lg], [b_lgh])
                    yield
                    tt("dve", lgl[:], lg[:], lgh[:], ALU.subtract, [b_lg, b_lgh], [b_lgl])
                    yield
                    mm(pX[:, 256:512], maskU4[:, 0, :], lgh[:], True, False, [b_maskU4, b_lgh], [b_pX])
                    mm(pX[:, 256:512], maskU4[:, 0, :], lgl[:], False, True, [b_maskU4, b_lgl], [b_pX])
                    for h in range(4):
                        mm(pY[0:64, h:h + 1], lgh[:, h * 64:(h + 1) * 64], ones_b[:, 0:1], True, False,
                           [b_lgh, b_onesb], [b_pY])
                        mm(pY[0:64, h:h + 1], lgl[:, h * 64:(h + 1) * 64], ones_b[:, 0:1], False, True,
                           [b_lgl, b_onesb], [b_pY])
                    yield
                    act(eb[:], pX[:, 256:512], AF.Exp, [b_pX], [b_eb], scale=-1.0 / 16)
                    act(enb[:], pX[:, 256:512], AF.Exp, [b_pX], [b_enb], scale=1.0 / 16)
                    act(ebl[:], pY[0:64, 0:4], AF.Exp, [b_pY], [b_ebl], scale=-1.0 / 16)
                    yield
                    stt("dve", qg[:], gqk[:, 0:256], 0.125, eb[:], ALU.mult, ALU.mult, [b_gqk, b_eb], [b_qg])
                    tt("dve", kg[:], gqk[:, 256:512], enb[:], ALU.mult, [b_gqk, b_enb], [b_kg])
                    yield
                    for h in range(4):
                        tr(pM[0:64, h, :], qg[:, h * 64:(h + 1) * 64], ident[:], [b_qg, b_ident], [b_pM])
                        tr(pM[0:64, 4 + h, :], kg[:, h * 64:(h + 1) * 64], ident[:], [b_kg, b_ident], [b_pM])
                    yield
                    cp("dve", qkT[:], pM[0:64, :, :], [b_pM], [b_qkT])
                    yield
                    for h in range(4):
                        mm(pY4[:, h, :], qkT[:, 4 + h, :], qkT[:, h, :], True, True, [b_qkT], [b_pY])
                    yield
                    tt("dve", ATs[:], pY4, maskU4[:], ALU.mult, [b_pY, b_maskU4], [b_ATs])
                    yield
                    for h in range(4):
                        mm(pX4[:, h, :], ATs[:, h, :], vsb[:, h * 128:(h + 1) * 128], True, False, [b_ATs, b_vsb], [b_pX])
                        mm(pX4[:, h, :], qkT[:, h, :], stb[:, h, :], False, True, [b_qkT, b_stb], [b_pX])
                    for h in range(4):
                        mm(pY4[0:64, h, :], kg[:, h * 64:(h + 1) * 64], vsb[:, h * 128:(h + 1) * 128], True, True,
                           [b_kg, b_vsb], [b_pY])
                    yield
                    for h in range(4):
                        ts("dve", stt_[:, h, :], stt_[:, h, :], ebl[:, h:h + 1], None, ALU.mult, None, [b_st, b_ebl], [b_st])
                        stt("dve", stt_[:, h, :], pY4[0:64, h, :], ebl[:, h:h + 1], stt_[:, h, :], ALU.mult, ALU.add,
                            [b_pY, b_ebl, b_st], [b_st])
                        if h == 1:
                            yield
                    cp("dve", stb[:], stt_[:], [b_st], [b_stb])
                    yield
                    for h in range(4):
                        act(sq2[:], pX4[:, h, :], AF.Square, [b_pX], [b_sq2, b_sso], accum=sso[:, h:h + 1])
                    yield
                    act(sso[:], sso[:], AF.Ln, [b_sso], [b_sso], scale=1.0 / 128, bias=EPS)
                    yield
                    act(sso[:], sso[:], AF.Exp, [b_sso], [b_sso], scale=-0.5)
                    yield
                    for h in range(4):
                        stt("dve", go[:, h * 128:(h + 1) * 128], pX4[:, h, :], sso[:, h:h + 1], gog[:, h * 128:(h + 1) * 128],
                            ALU.mult, ALU.mult, [b_pX, b_sso, b_gog], [b_go])
                        if h == 1:
                            yield
                    yield
                    for h in range(4):
                        tr(pM[:, h, :], go[:, h * 128:(h + 1) * 128], ident[:], [b_go, b_ident], [b_pM])
                    yield
                    gT, b_gT = gT4[(t // 4) % 2]
                    q4 = slice((t % 4) * 128, (t % 4 + 1) * 128)
                    cp("dve", gT[:, :, q4], pM[:, 0:4, :], [b_pM], [b_gT])
                    yield
                    if t % 4 == 3 or t == NT - 1:
                        g0 = (t // 4) * 512
                        n4 = (t % 4 + 1) * 128
                        S.dma("sp", mixT_d[0:4, :, g0:g0 + n4].rearrange("c p s -> p c s"), gT[:, :, 0:n4], reads=[b_gT], writes=[b_mixT],
                              sem="gT_st%d" % ((t // 4) % 2))

                qhs, b_qhs = SB(ph, "qhs", [128, 768], F32)
                kvs, b_kvs = SB(ph, "kvs", [128, 1024], F32)
                zk, b_zk = SB(ph, "zk", [128, 64], F32)
                rka, b_rka = SB(ph, "rka", [128, 32], F32)
                rkb, b_rkb = SB(ph, "rkb", [128, 32], F32)
                rope_k, b_ropek = SB(ph, "rope_k", [128, 64], BF16)
                sq4, b_sq4 = SB(ph, "sq4", [128, 128], F32)

                def mla_q(t):
                    sl = t % 2
                    tsl = slice(t * 128, (t + 1) * 128)
                    cs_t, b_cst = cst[t % 3]
                    rs3, b_rs3 = rs3s[sl]
                    rq2, b_rq2 = rq2s[sl]
                    mT, b_mT = mTs[sl]
                    pt_, bpt = pQ[0]
                    for g in range(2):
                        for c in range(2):
                            mm(pt_[:, 0:384], mT[:, 1 + c, :], wqu[:, c, g * 384:(g + 1) * 384], c == 0, c == 1, [b_mT, b_wqu], [bpt])
                        yield
                        cp("act", qhs[:, g * 384:(g + 1) * 384], pt_[:, 0:384], [bpt], [b_qhs])
                        for hh in range(2):
                            h = 2 * g + hh
                            base = hh * 192
                            act(sq3[:, 0:128], pt_[:, base:base + 128], AF.Square, [bpt], [b_sq3, b_ss8], accum=ss8[:, h:h + 1])
                            act(sq3[:, 0:64], pt_[:, base + 128:base + 192], AF.Square, [bpt], [b_sq3, b_ss8], accum=ss8[:, 4 + h:5 + h])
                        yield
                    ts("dve", ss8[:], ss8[:], rq2[:, 0:1], None, ALU.mult, None, [b_ss8, b_rq2], [b_ss8])
                    yield
                    act(fac8[:, 0:4], ss8[:, 0:4], AF.Ln, [b_ss8], [b_fac8], scale=1.0 / 128, bias=EPS)
                    act(fac8[:, 4:8], ss8[:, 4:8], AF.Ln, [b_ss8], [b_fac8], scale=1.0 / 64, bias=EPS)
                    yield
                    act(fac8[:], fac8[:], AF.Exp, [b_fac8], [b_fac8], scale=-0.5)
                    yield
                    ts("dve", fac8[:], fac8[:], rs3[:, 0:1], None, ALU.mult, None, [b_fac8, b_rs3], [b_fac8])
                    yield
                    for h in range(4):
                        base = h * 192
                        stt("dve", qn[:, h, :], qhs[:, base:base + 128], fac8[:, h:h + 1], qnn[:], ALU.mult, ALU.mult,
                            [b_qhs, b_fac8, b_qnn], [b_qn])
                        stt("dve", zall[:, h, :], qhs[:, base + 128:base + 192], fac8[:, 4 + h:5 + h], qnr[:], ALU.mult, ALU.mult,
                            [b_qhs, b_fac8, b_qnr], [b_zall])
                        if h % 2 == 1:
                            yield
                    for h in range(4):
                        tr(pM[:, h, :], qn[:, h, :], ident[:], [b_qn, b_ident], [b_pM])
                    yield
                    qT6, b_qT6 = qT64[(t // 4) % 2]
                    q4 = slice((t % 4) * 128, (t % 4 + 1) * 128)
                    cp("dve", qT6[:, 0:4, q4], pM[:, 0:4, :], [b_pM], [b_qT6])
                    yield
                    for hh in range(4):
                        z1, z2 = zall[:, hh, 0:32], zall[:, hh, 32:64]
                        cth, sth = cs_t[:, 0, :], cs_t[:, 1, :]
                        tt("pool", ra[:, hh, :], z1, cth, ALU.mult, [b_zall, b_cst], [b_ra])
                        tt("pool", rb[:, hh, :], z2, sth, ALU.mult, [b_zall, b_cst], [b_rb])
                        tt("pool", rope_o[:, hh, 0:32], ra[:, hh, :], rb[:, hh, :], ALU.subtract, [b_ra, b_rb], [b_ropeo])
                        yield
                        tt("pool", ra[:, hh, :], z2, cth, ALU.mult, [b_zall, b_cst], [b_ra])
                        tt("pool", rb[:, hh, :], z1, sth, ALU.mult, [b_zall, b_cst], [b_rb])
                        tt("pool", rope_o[:, hh, 32:64], ra[:, hh, :], rb[:, hh, :], ALU.add, [b_ra, b_rb], [b_ropeo])
                        yield
                    for hp in range(2):
                        tr(pM[:, 4 + hp, :], rope_o[:, 2 * hp:2 * hp + 2, :].rearrange("p h r -> p (h r)"), ident[:],
                           [b_ropeo, b_ident], [b_pM])
                    yield
                    cp("dve", qT6[:, 4:6, q4], pM[:, 4:6, :], [b_pM], [b_qT6])
                    yield
                    if t % 4 == 3 or t == NT - 1:
                        g0 = (t // 4) * 512
                        n4 = (t % 4 + 1) * 128
                        S.dma("sp", QT_d[:, :, g0:g0 + n4].rearrange("c p s -> p c s"), qT6[:, 0:4, 0:n4], reads=[b_qT6], writes=[b_QT],
                              sem="qT_st%d" % ((t // 4) % 2))
                        S.dma("sp", QPE_d[:, :, g0:g0 + n4].rearrange("c p s -> p c s"), qT6[:, 4:6, 0:n4], reads=[b_qT6], writes=[b_QPE],
                              sem="qT_stb%d" % ((t // 4) % 2))

                def mla_kv(t):
                    sl = t % 2
                    tsl = slice(t * 128, (t + 1) * 128)
                    cs_t, b_cst = cst[t % 3]
                    kpr, b_kpr = kprs[sl]
                    rs3, b_rs3 = rs3s[sl]
                    rq2, b_rq2 = rq2s[sl]
                    mT, b_mT = mTs[sl]
                    pt_, bpt = pQ[1]
                    stt("dve", zk[:], kpr[:], rs3[:, 2:3], knr[:], ALU.mult, ALU.mult, [b_kpr, b_rs3, b_knr], [b_zk])
                    yield
                    for g in range(2):
                        mm(pt_[:], mT[:, 3, :], wkvu[:, g * 512:(g + 1) * 512], True, True, [b_mT, b_wkvu], [bpt])
                        yield
                        cp("act", kvs[:, g * 512:(g + 1) * 512], pt_[:], [bpt], [b_kvs])
                        for hh in range(2):
                            h = 2 * g + hh
                            act(sq4[:, 0:128], pt_[:, hh * 256:hh * 256 + 128], AF.Square, [bpt], [b_sq4, b_ssk], accum=ssk[:, h:h + 1])
                        yield
                    z1, z2 = zk[:, 0:32], zk[:, 32:64]
                    cth, sth = cs_t[:, 0, :], cs_t[:, 1, :]
                    tt("dve", rka[:], z1, cth, ALU.mult, [b_zk, b_cst], [b_rka])
                    tt("dve", rkb[:], z2, sth, ALU.mult, [b_zk, b_cst], [b_rkb])
                    tt("dve", rope_k[:, 0:32], rka[:], rkb[:], ALU.subtract, [b_rka, b_rkb], [b_ropek])
                    yield
                    tt("dve", rka[:], z2, cth, ALU.mult, [b_zk, b_cst], [b_rka])
                    tt("dve", rkb[:], z1, sth, ALU.mult, [b_zk, b_cst], [b_rkb])
                    tt("dve", rope_k[:, 32:64], rka[:], rkb[:], ALU.add, [b_rka, b_rkb], [b_ropek])
                    yield
                    ts("dve", ssk[:], ssk[:], rq2[:, 1:2], None, ALU.mult, None, [b_ssk, b_rq2], [b_ssk])
                    yield
                    act(fack[:], ssk[:], AF.Ln, [b_ssk], [b_fack], scale=1.0 / 128, bias=EPS)
                    yield
                    act(fack[:], fack[:], AF.Exp, [b_fack], [b_fack], scale=-0.5)
                    yield
                    ts("dve", fack[:], fack[:], rs3[:, 1:2], None, ALU.mult, None, [b_fack, b_rs3], [b_fack])
                    yield
                    for h in range(4):
                        base = h * 256
                        stt("dve", kn[:, h, :], kvs[:, base:base + 128], fack[:, h:h + 1], knn[:], ALU.mult, ALU.mult,
                            [b_kvs, b_fack, b_knn], [b_kn])
                        if h % 2 == 1:
                            yield
                    for h in range(4):
                        tr(pM[:, h, :], kn[:, h, :], ident[:], [b_kn, b_ident], [b_pM])
                    tr(pM[0:64, 4, :], rope_k[:], ident[:], [b_ropek, b_ident], [b_pM])
                    yield
                    kT5, b_kT5 = kT54[(t // 4) % 2]
                    q4 = slice((t % 4) * 128, (t % 4 + 1) * 128)
                    cp("dve", kT5[:, 0:4, q4], pM[:, 0:4, :], [b_pM], [b_kT5])
                    cp("dve", kT5[0:64, 4, q4], pM[0:64, 4, :], [b_pM], [b_kT5])
                    yield
                    if t % 4 == 3 or t == NT - 1:
                        g0 = (t // 4) * 512
                        n4 = (t % 4 + 1) * 128
                        S.dma("sp", KT_d[:, :, g0:g0 + n4].rearrange("c p s -> p c s"), kT5[:, 0:4, 0:n4], reads=[b_kT5], writes=[b_KT],
                              sem="kT_st%d" % ((t // 4) % 2))
                        S.dma("sp", KPE_d[:, g0:g0 + n4], kT5[0:64, 4, 0:n4], reads=[b_kT5], writes=[b_KPE], sem="kT_stb%d" % ((t // 4) % 2))
                    for h in range(4):
                        base = h * 256
                        ts("dve", vt[:, h, :], kvs[:, base + 128:base + 256], rs3[:, 1:2], None, ALU.mult, None, [b_kvs, b_rs3], [b_vt])
                        if h % 2 == 1:
                            yield
                    S.dma("sp", V_d[tsl, :], vt[:].rearrange("p h e -> p (h e)"), reads=[b_vt], writes=[b_V], sem="vt_st")

                if "p1" not in SKIP:
                    issue_loads(0)
                    if NT > 1:
                        issue_loads(1)
                    drive([prologue(0)])
                for t in range(NT if "p1" not in SKIP else 0):
                    if t + 2 < NT:
                        issue_loads(t + 2)
                    gens = [gla_chain(t), mla_q(t), mla_kv(t)]
                    if t + 1 < NT:
                        gens.append(prologue(t + 1))
                    drive(gens)
            S.barrier()
            if stop and (stop.startswith('p1') or stop.startswith('q') or stop.startswith('c')):
                break

            ph23 = contextlib.ExitStack()
            phw = contextlib.ExitStack()
            uid[0] += 1
            wout = phw.enter_context(nc.sbuf_tensor("wout_s%d" % uid[0], [128, 8, D], BF16, side="right"))
            b_woutk = [S.buf("wout%d" % k) for k in range(8)]
            for k in range(8):
                S.dma("pool", wout[:, k, :], wout_in[l, k * 128:(k + 1) * 128, :], writes=[b_woutk[k]], sem="wout_ld", nobar=True)
            with contextlib.ExitStack() as ph:
                KTs = [SB(ph, "KTs%d" % i, [128, S_LEN], BF16) for i in range(2)]
                Vh = [SB(ph, "Vh%d" % i, [128, NT, 128], BF16) for i in range(2)]
                KPEs, b_KPEs = SB(ph, "KPEs", [64, S_LEN], BF16)
                qns = [SB(ph, "qns%d" % i, [128, 512], BF16) for i in range(2)]
                qps = [SB(ph, "qps%d" % i, [64, 512], BF16) for i in range(2)]
                pTs = [SB(ph, "pTs%d" % i, [128, 512], BF16) for i in range(4)]
                rl, b_rl = SB(ph, "rl", [128, 512], F32)
                ob = [SB(ph, "ob%d" % i, [128, 512], BF16) for i in range(2)]
                pS = [PS(ph, "pS%d" % i, [128, 512], F32) for i in range(4)]
                pO = [PS(ph, "pO%d" % i, [128, 512], F32) for i in range(2)]
                pL = [PS(ph, "pL%d" % i, [128, 512], F32) for i in range(2)]
                S.dma("sp", KPEs[:], KPE_d, reads=[b_KPE], writes=[b_KPEs])
                spl2 = split and last
                if spl2:
                    maskSel, b_maskSel = SB(ph, "maskSel", [128, 8, 512], BF16)
                    for r in range(4):
                        act(maskSel[:, r, :], maskD[:, r, :], AF.Identity, [b_maskD, b_par], [b_maskSel],
                            scale=par[:, 0:1], bias=par[:, 1:2])
                        act(maskSel[:, 4 + r, :], maskD[:, r, :], AF.Identity, [b_maskD, b_par], [b_maskSel], scale=par[:, 1:2])
                    qna = [SB(ph, "qna%d" % i, [128, 512], BF16) for i in range(2)]
                    qnb = [SB(ph, "qnb%d" % i, [128, 512], BF16) for i in range(2)]
                    qpa = [SB(ph, "qpa%d" % i, [64, 512], BF16) for i in range(2)]
                    qpb = [SB(ph, "qpb%d" % i, [64, 512], BF16) for i in range(2)]
                NPOS = NC4 // 2 if spl2 else NC4

                def nkb_of(c):
                    return 8 * c + 8 if spl2 else 4 * c + 4

                chunks = [(h, c) for h in range(4 if "p2" not in SKIP else 0) for c in range(NPOS)]
                blocks = []
                for idx, (h, c) in enumerate(chunks):
                    for kb in range(nkb_of(c)):
                        blocks.append((idx, kb, nkb_of(c)))

                def load_head(h):
                    KTh, b_KTh = KTs[h % 2]
                    Vhh, b_Vhh = Vh[h % 2]
                    S.dma("sp", KTh[:], KT_d[h], reads=[b_KT], writes=[b_KTh])
                    S.dma("sp", Vhh[:], V_d[:, h * 128:(h + 1) * 128].rearrange("(t p) e -> p t e", p=128), reads=[b_V], writes=[b_Vhh])

                def load_q(idx):
                    h, c = chunks[idx]
                    hp, off = h // 2, 64 * (h % 2)
                    if not spl2:
                        csl = slice(c * 512, (c + 1) * 512)
                        S.dma("sp", qns[idx % 2][0][:], QT_d[h, :, csl], reads=[b_QT], writes=[qns[idx % 2][1]])
                        S.dma("sp", qps[idx % 2][0][:], QPE_d[hp, off:off + 64, csl], reads=[b_QPE], writes=[qps[idx % 2][1]])
                        return
                    sla = slice(2 * c * 512, (2 * c + 1) * 512)
                    slb = slice((2 * c + 1) * 512, (2 * c + 2) * 512)
                    qa, b_qa = qna[idx % 2]
                    qb, b_qb = qnb[idx % 2]
                    pa, b_pa = qpa[idx % 2]
                    pb, b_pb = qpb[idx % 2]
                    qs, b_qs = qns[idx % 2]
                    ps_, b_ps = qps[idx % 2]
                    S.dma("sp", qa[:], QT_d[h, :, sla], reads=[b_QT], writes=[b_qa])
                    S.dma("sp", qb[:], QT_d[h, :, slb], reads=[b_QT], writes=[b_qb])
                    S.dma("sp", pa[:], QPE_d[hp, off:off + 64, sla], reads=[b_QPE], writes=[b_pa])
                    S.dma("sp", pb[:], QPE_d[hp, off:off + 64, slb], reads=[b_QPE], writes=[b_pb])
                    ts("dve", qs[:], qa[:], par[:, 0:1], None, ALU.mult, None, [b_qa, b_par], [b_qs])
                    stt("dve", qs[:], qb[:], par[:, 1:2], qs[:], ALU.mult, ALU.add, [b_qb, b_par, b_qs], [b_qs])
                    ts("dve", ps_[:], pa[:], par[0:64, 0:1], None, ALU.mult, None, [b_pa, b_par], [b_ps])
                    stt("dve", ps_[:], pb[:], par[0:64, 1:2], ps_[:], ALU.mult, ALU.add, [b_pb, b_par, b_ps], [b_ps])

                LA = 3
                if chunks:
                    load_head(0)
                    load_q(0)
                for i in range((len(blocks) + LA) if blocks else 0):
                    if i < len(blocks):
                        idx, kb, nkb = blocks[i]
                        h, c = chunks[idx]
                        if kb == 0 and idx + 1 < len(chunks):
                            load_q(idx + 1)
                        KTh, b_KTh = KTs[h % 2]
                        qn_, b_qn_ = qns[idx % 2]
                        qp_, b_qp_ = qps[idx % 2]
                        ksl = slice(kb * 128, (kb + 1) * 128)
                        pSt, b_pSt = pS[i % 4]
                        pTt, b_pTt = pTs[i % 4]
                        mm(pSt[:], KTh[:, ksl], qn_[:], True, False, [b_KTh, b_qn_], [b_pSt])
                        mm(pSt[:], KPEs[:, ksl], qp_[:], False, True, [b_KPEs, b_qp_], [b_pSt])
                        act(pTt[:], pSt[:], AF.Exp, [b_pSt], [b_pTt])
                        if spl2:
                            r = kb - 8 * c
                            if r >= 0:
                                tt("pool", pTt[:], pTt[:], maskSel[:, r, :], ALU.mult, [b_pTt, b_maskSel], [b_pTt])
                        else:
                            r = kb - 4 * c
                            if r >= 0:
                                tt("pool", pTt[:], pTt[:], maskD[:, r, :], ALU.mult, [b_pTt, b_maskD], [b_pTt])
                    if i >= LA:
                        j = i - LA
                        idx, kb, nkb = blocks[j]
                        h, c = chunks[idx]
                        Vhh, b_Vhh = Vh[h % 2]
                        pTt, b_pTt = pTs[j % 4]
                        pOt, b_pOt = pO[idx % 2]
                        pLt, b_pLt = pL[idx % 2]
                        mm(pOt[:], Vhh[:, kb, :], pTt[:], kb == 0, kb == nkb - 1, [b_Vhh, b_pTt], [b_pOt])
                        mm(pLt[:], ones_b[:], pTt[:], kb == 0, kb == nkb - 1, [b_onesb, b_pTt], [b_pLt])
                        if kb == 0 and c == 0 and h + 1 < 4:
                            load_head(h + 1)
                        if kb == nkb - 1:
                            obt, b_obt = ob[idx % 2]
                            csl = slice(c * 512, (c + 1) * 512)
                            recip(rl[:], pLt[:], [b_pLt], [b_rl])
                            tt("dve", obt[:], pOt[:], rl[:], ALU.mult, [b_pOt, b_rl], [b_obt])
                            S.dma("pool", mixT_d[4 + h, :, csl], obt[:], reads=[b_obt], writes=[b_mixT], sem="ob_st%d" % (idx % 2))
            S.barrier()
            if stop == 'p2':
                phw.close()
                ph23.close()
                break
            w1, _ = SB(ph23, "w1", [128, 8, DFF], BF16)
            w2, _ = SB(ph23, "w2", [128, 32, D], BF16)
            b_w1k = [S.buf("w1_%d" % k) for k in range(8)]
            b_w2f = [S.buf("w2_%d" % f) for f in range(32)]
            for k in range(8):
                for q in range(2):
                    S.dma("pool", w1[:, k, q * 2048:(q + 1) * 2048], w1_in[l, k * 128:(k + 1) * 128, q * 2048:(q + 1) * 2048],
                          writes=[b_w1k[k]], sem="w1_ld", nobar=True)
            for f in range(32):
                S.dma("pool", w2[:, f, :], w2_in[l, f * 128:(f + 1) * 128, :], writes=[b_w2f[f]], sem="w2_ld", nobar=True)

            with contextlib.ExitStack() as ph:
                gta, b_gta = SB(ph, "gta", [128, D], F32)
                S.dma("sp", gta[:], gate_d[l, 0], reads=[b_gate], writes=[b_gta])
                mx = [SB(ph, "mx%d" % i, [128, 8, 128], BF16) for i in range(2)]
                xt = [SB(ph, "xt%d" % i, [128, D], F32) for i in range(2)]
                xo = [SB(ph, "xo%d" % i, [128, D], F32) for i in range(2)]
                pW = [PS(ph, "pW%d" % i, [128, 512], F32) for i in range(4)]
                spl = split and last
                if spl:
                    mxb = [SB(ph, "mxb%d" % i, [128, 8, 128], BF16) for i in range(2)]
                    xtb = [SB(ph, "xtb%d" % i, [128, D], F32) for i in range(2)]
                    mxs = [SB(ph, "mxs%d" % i, [128, 8, 128], BF16) for i in range(2)]
                    xss = [SB(ph, "xss%d" % i, [128, D], F32) for i in range(2)]
                NT3 = (NT // 2 if spl else NT) if "p3a" not in SKIP else 0
                for t in range(NT3):
                    tsl = slice(t * 128, (t + 1) * 128)
                    tg = (8 * (t // 4) + t % 4) if spl else t
                    tsl0 = slice(tg * 128, (tg + 1) * 128)
                    tsl1 = slice((tg + 4) * 128, (tg + 5) * 128)
                    mxt, b_mx = mx[t % 2]
                    xtt, b_xt = xt[t % 2]
                    xot, b_xo = xo[t % 2]
                    S.dma("sp", mxt[:], mixT_d[:, :, tsl0].rearrange("c p s -> p c s"), reads=[b_mixT], writes=[b_mx])
                    S.dma("sp", xtt[:], x_cur[tsl0, :], reads=([b_xcur] if b_xcur else []), writes=[b_xt])
                    if spl:
                        mxbt, b_mxb = mxb[t % 2]
                        xtbt, b_xtb = xtb[t % 2]
                        mxst, b_mxs = mxs[t % 2]
                        xsst, b_xss = xss[t % 2]
                        S.dma("sp", mxbt[:], mixT_d[:, :, tsl1].rearrange("c p s -> p c s"), reads=[b_mixT], writes=[b_mxb])
                        S.dma("sp", xtbt[:], x_cur[tsl1, :], reads=([b_xcur] if b_xcur else []), writes=[b_xtb])
                        act(mxst[:, 0:4, :], mxt[:, 0:4, :], AF.Identity, [b_mx, b_par], [b_mxs], scale=par[:, 0:1])
                        stt("dve", mxst[:, 0:4, :], mxbt[:, 0:4, :], par[:, 1:2], mxst[:, 0:4, :], ALU.mult, ALU.add,
                            [b_mxb, b_par, b_mxs], [b_mxs])
                        S.dma("sp", mxst[:, 4:8, :], mixT_d[4:8, :, tsl].rearrange("c p s -> p c s"), reads=[b_mixT], writes=[b_mxs])
                        act(xsst[:], xtt[:], AF.Identity, [b_xt, b_par], [b_xss], scale=par[:, 0:1])
                        stt("dve", xsst[:], xtbt[:], par[:, 1:2], xsst[:], ALU.mult, ALU.add, [b_xtb, b_par, b_xss], [b_xss])
                        mxt, b_mx = mxst, b_mxs
                        xtt, b_xt = xsst, b_xss
                    for half in range(2):
                        pw, b_pw = pW[(t % 2) * 2 + half]
                        hs = slice(half * 512, (half + 1) * 512)
                        for c in range(8):
                            mm(pw[:], mxt[:, c, :], wout[:, c, hs], c == 0, c == 7, [b_mx, b_woutk[c]], [b_pw])
                        tt("dve", xot[:, hs], pw[:], gta[:, hs], ALU.mult, [b_pw, b_gta], [b_xo])
                        tt("pool", xot[:, hs], xot[:, hs], xtt[:, hs], ALU.add, [b_xo, b_xt], [b_xo])
                    S.dma("pool", xmid_d[tsl, :], xot[:], reads=[b_xo], writes=[b_xmid], sem="xo_st%d" % (t % 2))
            S.barrier()
            phw.close()
            if stop == 'p3a':
                ph23.close()
                break

            with contextlib.ExitStack() as ph:
                gtf, b_gtf = SB(ph, "gtf", [128, D], F32)
                S.dma("sp", gtf[:], gate_d[l, 1], reads=[b_gate], writes=[b_gtf])
                xm = [SB(ph, "xm%d" % i, [128, 2, D], F32) for i in range(3)]
                ss, b_ss = SB(ph, "ss", [128, 2], F32)
                xn2 = [SB(ph, "xn2_%d" % i, [128, 2, D], BF16) for i in range(2)]
                h2Ts = [SB(ph, "h2T%d" % i, [128, 8, 256], BF16) for i in range(2)]
                aT, b_aT = SB(ph, "aT", [128, 32, 256], BF16)
                rt = [SB(ph, "rt%d" % i, [128, 256], F32) for i in range(2)]
                yo = [SB(ph, "yo%d" % i, [128, D], F32) for i in range(2)]
                pTs2 = [PS(ph, "pT%d" % i, [128, 8, 128], BF16) for i in range(2)]
                pU = [PS(ph, "pU%d" % i, [128, 256], F32) for i in range(2)]
                pDn = [PS(ph, "pDn%d" % i, [128, 512], F32) for i in range(4)]

                def prep_norm(g):
                    xmt, b_xm = xm[g % 3]
                    xnt, b_xnt = xn2[g % 2]
                    for j in range(2):
                        t = 2 * g + j
                        S.dma("sp", xmt[:, j, :], xmid_d[t * 128:(t + 1) * 128, :], reads=[b_xmid], writes=[b_xm])
                    for j in range(2):
                        act(xnt[:, j, :], xmt[:, j, :], AF.Square, [b_xm], [b_xnt, b_ss], accum=ss[:, j:j + 1])
                    rsqrt_cols(ss[:], ss[:], 1.0 / D, [b_ss], [b_ss])
                    for j in range(2):
                        ts("dve", xnt[:, j, :], xmt[:, j, :], ss[:, j:j + 1], None, ALU.mult, None, [b_xm, b_ss], [b_xnt])

                def prep_tr(g):
                    xnt, b_xnt = xn2[g % 2]
                    h2T, b_h2T = h2Ts[g % 2]
                    for j in range(2):
                        pT, b_pT = pTs2[j]
                        for k in range(8):
                            tr(pT[:, k, :], xnt[:, j, k * 128:(k + 1) * 128], ident[:], [b_xnt, b_ident], [b_pT])
                        for k in range(8):
                            if k % 2 == 0:
                                ts("dve", h2T[:, k, j * 128:(j + 1) * 128], pT[:, k, :], modcol[:, mc + 24 + k: mc + 25 + k],
                                   modcol[:, mc + 16 + k: mc + 17 + k], ALU.mult, ALU.add, [b_pT, b_modcol], [b_h2T])
                            else:
                                act(h2T[:, k, j * 128:(j + 1) * 128], pT[:, k, :], AF.Identity, [b_pT, b_modcol], [b_h2T],
                                    scale=modcol[:, mc + 24 + k: mc + 25 + k], bias=modcol[:, mc + 16 + k: mc + 17 + k])

                yi = 0
                NGE = NG // 2 if (split and last) else NG
                prep_norm(0)
                prep_tr(0)
                for g in range(NGE):
                    xmt, b_xm = xm[g % 3]
                    h2T, b_h2T = h2Ts[g % 2]
                    if g + 1 < NGE:
                        prep_norm(g + 1)
                    for f in range(32):
                        pu, b_pu = pU[f % 2]
                        rtt, b_rt = rt[f % 2]
                        for k in range(8):
                            mm(pu[:], w1[:, k, f * 128:(f + 1) * 128], h2T[:, k, :], k == 0, k == 7, [b_w1k[k], b_h2T], [b_pu])
                        act(rtt[:], pu[:], AF.Relu, [b_pu], [b_rt])
                        tt("dve" if f % 2 == 0 else "pool", aT[:, f, :], rtt[:], rtt[:], ALU.mult, [b_rt], [b_aT])
                    if g + 1 < NGE:
                        prep_tr(g + 1)
                    for j in range(2):
                        t = 2 * g + j
                        tsl = slice(t * 128, (t + 1) * 128)
                        yot, b_yo = yo[yi % 2]
                        yi += 1
                        for half in range(2):
                            pd, b_pd = pDn[j * 2 + half]
                            hs = slice(half * 512, (half + 1) * 512)
                            for f in range(32):
                                mm(pd[:], aT[:, f, j * 128:(j + 1) * 128], w2[:, f, hs], f == 0, f == 31, [b_aT, b_w2f[f]], [b_pd])
                            tt("dve", yot[:, hs], pd[:], gtf[:, hs], ALU.mult, [b_pd, b_gtf], [b_yo])
                            tt("pool", yot[:, hs], yot[:, hs], xmt[:, j, hs], ALU.add, [b_yo, b_xm], [b_yo])
                        d = S.dma("pool", x_nxt[tsl, :], yot[:], reads=[b_yo], writes=[b_xnxt], sem="yo_st%d" % ((yi - 1) % 2))
                        if last:
                            out_dmas.append(d)
            S.barrier()
            ph23.close()

        S.emit(final_waits=out_dmas)
    return nc


def host_inputs(b, S_LEN, DEPTH, x, c, positions, _parity=0, *, w_ada, b_ada, w_in, w_gate_up, b_gate, gla_out_norm, q_a_norm,
                w_q_up, kv_a_norm, w_kv_up, q_norm_nope, k_norm_nope, q_norm_rope, k_norm_rope,
                w_out, w_mlp_up, w_mlp_down):
    f32 = np.float32
    NT = S_LEN // 128
    A = lambda a: np.ascontiguousarray(np.asarray(a))

    def bc(v, n=128):
        v = np.asarray(v, dtype=f32)
        return A(np.broadcast_to(v[:, None, :], (v.shape[0], n, v.shape[1])))

    inv_freq = (10000.0 ** (-np.arange(0, 64, 2, dtype=f32) / f32(64))).astype(f32)
    b_ada = np.asarray(b_ada, dtype=f32)
    d = {
        "x": A(np.asarray(x[b], dtype=f32)),
        "ccol": A(np.asarray(c[b], dtype=f32).reshape(8, 128).T),
        "pos": A(np.asarray(positions[b]).astype(np.int32).reshape(NT, 128).T),
        "invf": A(np.broadcast_to(inv_freq[None, :], (128, 32))),
        "parcol": A(np.broadcast_to(np.array([[1.0 - _parity, float(_parity)]], dtype=f32), (128, 2))),
        "w_ada": A(np.asarray(w_ada, dtype=f32)),
        "bada_col": A(b_ada.reshape(DEPTH, 48, 128).transpose(0, 2, 1)),
        "bada_gate": A(np.broadcast_to(b_ada.reshape(DEPTH, 6, 1, D)[:, [2, 5]], (DEPTH, 2, 128, D))),
        "w_in": A(np.asarray(w_in, dtype=f32)),
        "w_gate_up": A(np.asarray(w_gate_up, dtype=f32)),
        "bgate_bc": bc(b_gate),
        "gon_bc4": bc(np.tile(np.asarray(gla_out_norm, dtype=f32), (1, 4))),
        "qan_col": A(np.asarray(q_a_norm, dtype=f32).reshape(DEPTH, 2, 128).transpose(0, 2, 1)),
        "w_q_up": A(np.asarray(w_q_up, dtype=f32)),
        "kvan_col": A(np.asarray(kv_a_norm, dtype=f32).reshape(DEPTH, 128, 1)),
        "w_kv_up": A(np.asarray(w_kv_up, dtype=f32)),
        "qnn_bc": bc(q_norm_nope), "knn_bc": bc(k_norm_nope),
        "qnr_bc": bc(q_norm_rope), "knr_bc": bc(k_norm_rope),
        "w_out": A(np.asarray(w_out, dtype=f32)),
        "w_mlp_up": A(np.asarray(w_mlp_up, dtype=f32)),
        "w_mlp_down": A(np.asarray(w_mlp_down, dtype=f32)),
    }
    return d


_NC_CACHE = {}


def kernel(**inputs):
    x = np.asarray(inputs["x"])
    B, S_LEN, _ = x.shape
    DEPTH = np.asarray(inputs["w_ada"]).shape[0]
    key = (S_LEN, DEPTH)
    if key not in _NC_CACHE:
        _NC_CACHE[key] = build(S_LEN, DEPTH, split=True)
    nc = _NC_CACHE[key]
    in_maps = []
    for b in range(B):
        base = host_inputs(b, S_LEN, DEPTH, _parity=0, **inputs)
        in_maps.append(base)
        m1 = dict(base)
        m1["parcol"] = host_inputs_par(1)
        in_maps.append(m1)
    res = run_bass_kernel_spmd(nc, in_maps, core_ids=list(range(2 * B)))
    out = np.empty((B, S_LEN, D), dtype=np.float32)
    ov = out.reshape(B, S_LEN // 1024, 2, 512, D)
    for b in range(B):
        for p in range(2):
            ov[b, :, p] = np.asarray(res.results[2 * b + p]["y"], dtype=np.float32).reshape(S_LEN // 1024, 512, D)
    return out


def host_inputs_par(p):
    return np.ascontiguousarray(np.broadcast_to(np.array([[1.0 - p, float(p)]], dtype=np.float32), (128, 2)))
```

```python
import contextlib
import math
import numpy as np
import concourse.bass as bass
import concourse.mybir as mybir
from concourse.bass_utils import run_bass_kernel_spmd

F32 = mybir.dt.float32
BF16 = mybir.dt.bfloat16
I32 = mybir.dt.int32
ALU = mybir.AluOpType
AF = mybir.ActivationFunctionType

ENGS = ("pe", "act", "dve", "pool", "sp")
EST_DUR = {"pe": 0.15, "act": 0.3, "dve": 0.3, "pool": 0.5, "sp": 0.1}
import os as _os
SEM_SHARE = _os.environ.get("SEM_SHARE", "0") == "1"
SEQ_CHAINS = _os.environ.get("SEQ_CHAINS", "0") == "1"
NO_KPR = _os.environ.get("NO_KPR", "0") == "1"
SKIP = set(_os.environ.get("SKIP_PHASES", "").split(","))
OLD_ORDER = _os.environ.get("OLD_ORDER", "0") == "1"
SEM_MAP = {"wm0": "A0", "wm1": "A1", "wm2": "A2", "wm3": "A3", "xt0": "A0", "xt1": "A1", "xt2": "A2", "cst0": "B0", "cst1": "B1", "cst2": "B2",
           "KTs0": "A0", "KTs1": "A1", "Vh0": "B0", "Vh1": "B1", "qns0": "C0", "qns1": "C1",
           "qps0": "D0", "qps1": "D1", "mx0": "B0", "mx1": "B1", "xm0": "A0", "xm1": "A1", "xm2": "A2",
           "ob_st0": "ST0", "ob_st1": "ST1", "xo_st0": "ST0", "xo_st1": "ST1", "yo_st0": "ST0", "yo_st1": "ST1"}


class Buf:
    __slots__ = ("name", "lw", "rd", "rd_dma")

    def __init__(self, name):
        self.name = name
        self.lw = None
        self.rd = {}
        self.rd_dma = []


class Ins:
    __slots__ = ("eng", "fn", "deps", "signal", "cnt", "is_dma", "dsem", "dval", "tfin")

    def __init__(self, eng, fn, is_dma=False):
        self.eng = eng
        self.fn = fn
        self.deps = []
        self.signal = False
        self.cnt = 0
        self.is_dma = is_dma
        self.dsem = None
        self.dval = 0
        self.tfin = 0.0


class Sched:
    def __init__(self, nc):
        self.nc = nc
        self.q = {e: [] for e in ENGS}
        self.dma_sems = {}
        self.all_dma = []
        self.last_on_sem = {}
        self.bar = None
        self.bar_done = {}
        self.nbuf = 0
        self.eng_free = {e: 0.0 for e in ENGS}
        self.step_max = 0.0

    def buf(self, name=None):
        self.nbuf += 1
        return Buf(name or ("b%d" % self.nbuf))

    def _collect(self, ins, reads, writes):
        deps = {}
        pe = (ins.eng == "pe" and not ins.is_dma)

        def add(d):
            if d is None or d is ins:
                return
            if pe and d.eng == "pe" and not d.is_dma:
                return
            if d.is_dma:
                d = self.last_on_sem[d.dsem]
                if d is ins:
                    return
            deps[id(d)] = d

        for b in reads:
            add(b.lw)
        for b in writes:
            add(b.lw)
            for d in b.rd.values():
                add(d)
            for d in b.rd_dma:
                add(d)
        if self.bar is not None and not self.bar_done.get(ins.eng):
            for d in self.bar:
                add(d)
            self.bar_done[ins.eng] = True
        for b in reads:
            if ins.is_dma:
                b.rd_dma.append(ins)
            else:
                b.rd[ins.eng] = ins
        for b in writes:
            b.lw = ins
            b.rd = {}
            b.rd_dma = []
        ins.deps = list(deps.values())
        ready = max([d.tfin for d in ins.deps], default=0.0) + (0.25 if ins.deps else 0.0)
        start = max(ready, self.eng_free[ins.eng])
        if ins.is_dma:
            self.eng_free[ins.eng] = start + 0.1
            ins.tfin = start + 2.5
        else:
            ins.tfin = start + EST_DUR[ins.eng]
            self.eng_free[ins.eng] = ins.tfin
        if ins.tfin > self.step_max:
            self.step_max = ins.tfin

    def op(self, eng, fn, reads=(), writes=()):
        ins = Ins(eng, fn)
        self._collect(ins, reads, writes)
        self.q[eng].append(ins)
        return ins

    def dma(self, eng, out, in_, reads=(), writes=(), sem=None, nobar=False):
        if sem is None:
            sem = (list(writes) + list(reads))[0].name
        if SEM_SHARE:
            sem = SEM_MAP.get(sem, "G_const" if not sem.endswith("_st") else "ST")
        ins = Ins(eng, lambda e: e.dma_start(out=out, in_=in_), is_dma=True)
        tot = self.dma_sems.get(sem, 0) + 16
        self.dma_sems[sem] = tot
        ins.dsem = sem
        ins.dval = tot
        self._collect(ins, reads, writes)
        self.last_on_sem[sem] = ins
        self.q[eng].append(ins)
        if not nobar:
            self.all_dma.append(ins)
        return ins

    def barrier(self):
        deps = []
        for e in ENGS:
            for ins in reversed(self.q[e]):
                if not ins.is_dma:
                    deps.append(ins)
                    break
        deps.extend(self.all_dma)
        self.all_dma = []
        last = {}
        keep = []
        for d in deps:
            if d.is_dma:
                if d.dsem not in last or last[d.dsem].dval < d.dval:
                    last[d.dsem] = d
            else:
                keep.append(d)
        self.bar = keep + list(last.values())
        self.bar_done = {}

    def emit(self, final_waits=()):
        nc = self.nc
        for e in ENGS:
            for ins in self.q[e]:
                for d in ins.deps:
                    if not d.is_dma:
                        d.signal = True
        for e in ENGS:
            c = 0
            for ins in self.q[e]:
                if ins.signal and not ins.is_dma:
                    c += 1
                ins.cnt = c
        with contextlib.ExitStack() as st:
            esem = {e: st.enter_context(nc.semaphore("es_" + e)) for e in ENGS}
            dsem = {k: st.enter_context(nc.semaphore("ds_%d" % i)) for i, k in enumerate(self.dma_sems)}
            block = st.enter_context(nc.Block())
            sched = self

            def run(e, eh):
                waited = {}
                for ins in sched.q[e]:
                    need = {}
                    for d in ins.deps:
                        if d.is_dma:
                            s, v, key = dsem[d.dsem], d.dval, ("d", d.dsem)
                        else:
                            s, v, key = esem[d.eng], d.cnt, ("e", d.eng)
                        if key not in need or need[key][1] < v:
                            need[key] = (s, v)
                    for key, (s, v) in need.items():
                        if waited.get(key, 0) >= v:
                            continue
                        waited[key] = v
                        eh.wait_ge(s, v)
                    bi = ins.fn(eh)
                    if ins.is_dma:
                        bi.then_inc(dsem[ins.dsem], 16)
                    elif ins.signal:
                        bi.then_inc(esem[e], 1)
                if e == "sp":
                    for d in final_waits:
                        eh.wait_ge(dsem[d.dsem], d.dval)

            @block.tensor
            def _(eh):
                run("pe", eh)

            @block.scalar
            def _(eh):
                run("act", eh)

            @block.vector
            def _(eh):
                run("dve", eh)

            @block.gpsimd
            def _(eh):
                run("pool", eh)

            @block.sync
            def _(eh):
                run("sp", eh)


D = 1024
DFF = 4096
EPS = 1e-6
TWO_PI = 2.0 * math.pi
C1 = 6.28125
C2 = TWO_PI - C1


def build(S_LEN, DEPTH, dbg=False, stop=None, split=False):
    NT = S_LEN // 128
    NC4 = S_LEN // 512
    NG = S_LEN // 256
    nc = bass.Bass("TRN2", target_bir_lowering=False)
    S = Sched(nc)

    def din(name, shape, dt=F32):
        return nc.dram_tensor(name, shape, dt, kind="ExternalInput").ap()

    def dscr(name, shape, dt):
        return nc.dram_tensor(name, shape, dt, kind=("ExternalOutput" if dbg else "Internal")).ap()

    x_in = din("x", [S_LEN, D])
    ccol = din("ccol", [128, 8])
    pos_in = din("pos", [128, NT], I32)
    invf_in = din("invf", [128, 32])
    par_in = din("parcol", [128, 2])
    w_ada = din("w_ada", [DEPTH, D, 6 * D])
    bada_col = din("bada_col", [DEPTH, 128, 48])
    bada_gate = din("bada_gate", [DEPTH, 2, 128, D])
    w_in = din("w_in", [DEPTH, D, 2000])
    wgu_in = din("w_gate_up", [DEPTH, 16, 256])
    bgate_in = din("bgate_bc", [DEPTH, 128, 256])
    gon_in = din("gon_bc4", [DEPTH, 128, 512])
    qan_in = din("qan_col", [DEPTH, 128, 2])
    wqu_in = din("w_q_up", [DEPTH, 256, 768])
    kvan_in = din("kvan_col", [DEPTH, 128, 1])
    wkvu_in = din("w_kv_up", [DEPTH, 128, 1024])
    qnn_in = din("qnn_bc", [DEPTH, 128, 128])
    knn_in = din("knn_bc", [DEPTH, 128, 128])
    qnr_in = din("qnr_bc", [DEPTH, 128, 64])
    knr_in = din("knr_bc", [DEPTH, 128, 64])
    wout_in = din("w_out", [DEPTH, D, D])
    w1_in = din("w_mlp_up", [DEPTH, D, DFF])
    w2_in = din("w_mlp_down", [DEPTH, DFF, D])
    y_out = nc.dram_tensor("y", [S_LEN // 2 if split else S_LEN, D], F32, kind="ExternalOutput").ap()

    xs = [dscr("xs%d" % i, [S_LEN, D], F32) for i in range(2)]
    xmid_d = dscr("xmid", [S_LEN, D], F32)
    mixT_d = dscr("mixT", [8, 128, S_LEN], BF16)
    KT_d = dscr("KT", [4, 128, S_LEN], BF16)
    KPE_d = dscr("KPE", [64, S_LEN], BF16)
    V_d = dscr("Vd", [S_LEN, 512], BF16)
    QT_d = dscr("QT", [4, 128, S_LEN], BF16)
    QPE_d = dscr("QPE", [2, 128, S_LEN], BF16)
    cos_d = dscr("cosd", [128, NT, 32], F32)
    sin_d = dscr("sind", [128, NT, 32], F32)
    gate_d = dscr("gated", [DEPTH, 2, 128, D], F32)
    b_xs = [S.buf("xs0"), S.buf("xs1")]
    b_xmid = S.buf("xmid")
    b_mixT = S.buf("mixT")
    b_KT, b_KPE, b_V, b_QT, b_QPE = S.buf("KT"), S.buf("KPE"), S.buf("V"), S.buf("QT"), S.buf("QPE")
    b_cos, b_sin, b_gate = S.buf("cosd"), S.buf("sind"), S.buf("gated")
    b_y = S.buf("y")
    out_dmas = []

    def mm(out, lhsT, rhs, start, stop, R, W):
        S.op("pe", lambda e: e.matmul(out=out, lhsT=lhsT, rhs=rhs, start=start, stop=stop), R, W)

    def tr(out, in_, ident, R, W):
        S.op("pe", lambda e: e.transpose(out=out, in_=in_, identity=ident), R, W)

    def act(out, in_, func, R, W, scale=1.0, bias=0.0, accum=None):
        if accum is None:
            S.op("act", lambda e: e.activation(out=out, in_=in_, func=func, bias=bias, scale=scale), R, W)
        else:
            S.op("act", lambda e: e.activation(out=out, in_=in_, func=func, bias=bias, scale=scale, accum_out=accum), R, W)

    def tt(eng, out, in0, in1, op, R, W):
        S.op(eng, lambda e: e.tensor_tensor(out=out, in0=in0, in1=in1, op=op), R, W)

    def ts(eng, out, in0, s1, s2, op0, op1, R, W):
        if s2 is None:
            S.op(eng, lambda e: e.tensor_scalar(out=out, in0=in0, scalar1=s1, scalar2=None, op0=op0), R, W)
        else:
            S.op(eng, lambda e: e.tensor_scalar(out=out, in0=in0, scalar1=s1, scalar2=s2, op0=op0, op1=op1), R, W)

    def stt(eng, out, in0, scalar, in1, op0, op1, R, W, accum=None):
        if accum is None:
            S.op(eng, lambda e: e.scalar_tensor_tensor(out=out, in0=in0, scalar=scalar, in1=in1, op0=op0, op1=op1), R, W)
        else:
            S.op(eng, lambda e: e.scalar_tensor_tensor(out=out, in0=in0, scalar=scalar, in1=in1, op0=op0, op1=op1, accum_out=accum), R, W)

    def cp(eng, out, in_, R, W):
        if eng == "act":
            S.op("act", lambda e: e.copy(out=out, in_=in_), R, W)
        else:
            S.op(eng, lambda e: e.tensor_copy(out=out, in_=in_), R, W)

    def recip(out, in_, R, W):
        S.op("dve", lambda e: e.reciprocal(out=out, in_=in_), R, W)

    def memset(eng, ap, val, W):
        S.op(eng, lambda e: e.memset(ap, val), (), W)

    def asel(out, in_, pattern, cmp, fill, base, cm, R, W):
        S.op("pool", lambda e: e.affine_select(out=out, in_=in_, pattern=pattern, compare_op=cmp, fill=fill,
                                               base=base, channel_multiplier=cm), R, W)

    def rsqrt_cols(dst, src, scale, R, W):
        act(dst, src, AF.Ln, R, W, scale=scale, bias=EPS)
        act(dst, dst, AF.Exp, W, W, scale=-0.5)

    with contextlib.ExitStack() as top:
        uid = [0]

        def SB(stack, name, shape, dt):
            uid[0] += 1
            t = stack.enter_context(nc.sbuf_tensor("%s_s%d" % (name, uid[0]), shape, dt))
            return t, S.buf(name)

        def PS(stack, name, shape, dt):
            uid[0] += 1
            t = stack.enter_context(nc.psum_tensor("%s_p%d" % (name, uid[0]), shape, dt))
            return t, S.buf(name)

        ident, b_ident = SB(top, "ident", [128, 128], BF16)
        maskU4, b_maskU4 = SB(top, "maskU4", [128, 4, 128], BF16)
        ones_f, b_onesf = SB(top, "ones_f", [128, 128], F32)
        ones_b, b_onesb = SB(top, "ones_b", [128, 128], BF16)
        maskD, b_maskD = SB(top, "maskD", [128, 4, 512], BF16)
        modcol, b_modcol = SB(top, "modcol", [128, DEPTH * 32], F32)

        par, b_par = SB(top, "par", [128, 2], F32)
        S.dma("sp", par[:], par_in, writes=[b_par])
        memset("pool", ident[:], 0.0, [b_ident])
        asel(ident[:], ident[:], [[-1, 128]], ALU.not_equal, 1.0, 0, 1, [b_ident], [b_ident])
        memset("pool", maskU4[:], 1.0, [b_maskU4])
        for h in range(4):
            asel(maskU4[:, h, :], maskU4[:, h, :], [[1, 128]], ALU.is_ge, 0.0, 0, -1, [b_maskU4], [b_maskU4])
        memset("pool", ones_f[:], 1.0, [b_onesf])
        memset("pool", ones_b[:], 1.0, [b_onesb])
        memset("pool", maskD[:], 1.0, [b_maskD])
        for r in range(4):
            asel(maskD[:, r, :], maskD[:, r, :], [[1, 512]], ALU.is_ge, 0.0, -128 * r, -1, [b_maskD], [b_maskD])

        with contextlib.ExitStack() as ph:
            posi, b_posi = SB(ph, "posi", [128, NT], I32)
            posf, b_posf = SB(ph, "posf", [128, NT], F32)
            invf, b_invf = SB(ph, "invf", [128, 32], F32)
            ang, b_ang = SB(ph, "ang", [128, NT, 32], F32)
            uu, b_uu = SB(ph, "uu", [128, NT, 32], F32)
            ni, b_ni = SB(ph, "ni", [128, NT, 32], I32)
            nf, b_nf = SB(ph, "nf", [128, NT, 32], F32)
            mk, b_mk = SB(ph, "mk", [128, NT, 32], F32)
            sn, b_sn = SB(ph, "sn", [128, NT, 32], F32)
            cs, b_cs = SB(ph, "cs", [128, NT, 32], F32)
            S.dma("sp", posi[:], pos_in, writes=[b_posi])
            S.dma("sp", invf[:], invf_in, writes=[b_invf])
            cp("dve", posf[:], posi[:], [b_posi], [b_posf])
            for t in range(NT):
                ts("dve", ang[:, t, :], invf[:], posf[:, t:t + 1], None, ALU.mult, None, [b_invf, b_posf], [b_ang])
            ts("dve", uu[:], ang[:], 1.0 / TWO_PI, None, ALU.mult, None, [b_ang], [b_uu])
            cp("dve", ni[:], uu[:], [b_uu], [b_ni])
            cp("dve", nf[:], ni[:], [b_ni], [b_nf])
            stt("dve", ang[:], nf[:], -C1, ang[:], ALU.mult, ALU.add, [b_nf, b_ang], [b_ang])
            stt("dve", ang[:], nf[:], -C2, ang[:], ALU.mult, ALU.add, [b_nf, b_ang], [b_ang])
            ts("dve", mk[:], ang[:], math.pi, None, ALU.is_gt, None, [b_ang], [b_mk])
            stt("dve", ang[:], mk[:], -TWO_PI, ang[:], ALU.mult, ALU.add, [b_mk, b_ang], [b_ang])
            ts("dve", mk[:], ang[:], -math.pi, None, ALU.is_lt, None, [b_ang], [b_mk])
            stt("dve", ang[:], mk[:], TWO_PI, ang[:], ALU.mult, ALU.add, [b_mk, b_ang], [b_ang])
            ts("dve", ang[:], ang[:], math.pi, -math.pi, ALU.min, ALU.max, [b_ang], [b_ang])
            act(sn[:], ang[:], AF.Sin, [b_ang], [b_sn])
            stt("dve", uu[:], ang[:], -1.0, ang[:], ALU.mult, ALU.max, [b_ang], [b_uu])
            ts("dve", uu[:], uu[:], -1.0, math.pi / 2, ALU.mult, ALU.add, [b_uu], [b_uu])
            act(cs[:], uu[:], AF.Sin, [b_uu], [b_cs])
            S.dma("sp", cos_d, cs[:], reads=[b_cs], writes=[b_cos], sem="cs_st")
            S.dma("sp", sin_d, sn[:], reads=[b_sn], writes=[b_sin], sem="sn_st")

            if stop != 'rope':
                cc, b_cc = SB(ph, "cc", [128, 8], F32)
                ce, b_ce = SB(ph, "ce", [128, 8], F32)
                cond, b_cond = SB(ph, "cond", [128, 8], F32)
                cond_rep, b_crep = SB(ph, "cond_rep", [128, 8, 128], BF16)
                condb, b_condb = SB(ph, "condb", [128, 16], BF16)
                wm = [SB(ph, "wm%d" % i, [128, 8, D], BF16) for i in range(4)]
                bcol, b_bcol = SB(ph, "bcol", [128, DEPTH * 48], F32)
                bgt, b_bgt = SB(ph, "bgt", [128, D], F32)
                gsb, b_gsb = SB(ph, "gsb", [128, D], F32)
                pg = [PS(ph, "pg%d" % i, [128, 512], F32) for i in range(2)]
                pc, b_pc = PS(ph, "pc", [128, 8], F32)
                S.dma("sp", cc[:], ccol, writes=[b_cc])
                for l in range(DEPTH):
                    S.dma("sp", bcol[:, l * 48:(l + 1) * 48], bada_col[l], writes=[b_bcol])
                act(ce[:], cc[:], AF.Exp, [b_cc], [b_ce], scale=-1.0)
                ts("dve", ce[:], ce[:], 1.0, None, ALU.add, None, [b_ce], [b_ce])
                recip(ce[:], ce[:], [b_ce], [b_ce])
                tt("dve", cond[:], cc[:], ce[:], ALU.mult, [b_cc, b_ce], [b_cond])
                memset("dve", condb[:], 0.0, [b_condb])
                cp("dve", condb[:, 0:8], cond[:], [b_cond, b_condb], [b_condb])
                for k in range(8):
                    ts("dve", cond_rep[:, k, :], ones_f[:], cond[:, k:k + 1], None, ALU.mult, None, [b_onesf, b_cond], [b_crep])
                li = 0
                for l in range(DEPTH):
                    for m in range(6):
                        wt, b_wt = wm[li % 4]
                        li += 1
                        S.dma("pool", wt[:], w_ada[l, :, m * D:(m + 1) * D].rearrange("(k p) n -> p k n", p=128), writes=[b_wt])
                        if m in (2, 5):
                            gi = 0 if m == 2 else 1
                            S.dma("sp", bgt[:], bada_gate[l, gi], writes=[b_bgt])
                            for half in range(2):
                                pgt, b_pg = pg[half]
                                for k in range(8):
                                    mm(pgt[:], cond_rep[:, k, :], wt[:, k, half * 512:(half + 1) * 512], k == 0, k == 7,
                                       [b_crep, b_wt], [b_pg])
                                tt("dve", gsb[:, half * 512:(half + 1) * 512], pgt[:], bgt[:, half * 512:(half + 1) * 512],
                                   ALU.add, [b_pg, b_bgt], [b_gsb])
                            S.dma("sp", gate_d[l, gi], gsb[:], reads=[b_gsb], writes=[b_gate], sem="gate_st")
                        else:
                            mi = {0: 0, 1: 1, 3: 2, 4: 3}[m]
                            for ko in range(8):
                                for k in range(8):
                                    mm(pc[:, ko:ko + 1], wt[:, k, ko * 128:(ko + 1) * 128], condb[:, k:k + 1], k == 0, k == 7,
                                       [b_wt, b_condb], [b_pc])
                            dst = modcol[:, l * 32 + mi * 8: l * 32 + mi * 8 + 8]
                            tt("dve", dst, pc[:], bcol[:, l * 48 + m * 8: l * 48 + m * 8 + 8], ALU.add, [b_pc, b_bcol], [b_modcol])
                            if m in (1, 4):
                                ts("dve", dst, dst, 1.0, None, ALU.add, None, [b_modcol], [b_modcol])
        S.barrier()

        for l in range(DEPTH if stop not in ('setup', 'rope') else 0):
            x_cur = x_in if l == 0 else xs[(l - 1) % 2]
            b_xcur = None if l == 0 else b_xs[(l - 1) % 2]
            last = (l == DEPTH - 1)
            x_nxt = y_out if last else xs[l % 2]
            b_xnxt = b_y if last else b_xs[l % 2]
            mc = l * 32

            with contextlib.ExitStack() as ph:
                win, b_win = SB(ph, "win", [128, 8, 2000], BF16)
                wgu, b_wgu = SB(ph, "wgu", [16, 256], BF16)
                wqu, b_wqu = SB(ph, "wqu", [128, 2, 768], BF16)
                wkvu, b_wkvu = SB(ph, "wkvu", [128, 1024], BF16)
                qan, b_qan = SB(ph, "qan", [128, 2], F32)
                kvan, b_kvan = SB(ph, "kvan", [128, 1], F32)
                bgate, b_bgate = SB(ph, "bgate", [128, 256], F32)
                gon4, b_gon4 = SB(ph, "gon4", [128, 512], F32)
                qnn, b_qnn = SB(ph, "qnn", [128, 128], F32)
                knn, b_knn = SB(ph, "knn", [128, 128], F32)
                qnr, b_qnr = SB(ph, "qnr", [128, 64], F32)
                knr, b_knr = SB(ph, "knr", [128, 64], F32)
                b_wink = [S.buf("win%d" % k) for k in range(8)]
                for k in range(8):
                    S.dma("pool", win[:, k, :], w_in[l, k * 128:(k + 1) * 128, :], writes=[b_wink[k]], sem="win_ld")
                S.dma("pool", wgu[:], wgu_in[l], writes=[b_wgu])
                S.dma("pool", wqu[:], wqu_in[l].rearrange("(k p) n -> p k n", p=128), writes=[b_wqu])
                S.dma("pool", wkvu[:], wkvu_in[l], writes=[b_wkvu])
                S.dma("sp", qan[:], qan_in[l], writes=[b_qan])
                S.dma("sp", kvan[:], kvan_in[l], writes=[b_kvan])
                S.dma("sp", bgate[:], bgate_in[l], writes=[b_bgate])
                S.dma("sp", gon4[:], gon_in[l], writes=[b_gon4])
                S.dma("sp", qnn[:], qnn_in[l], writes=[b_qnn])
                S.dma("sp", knn[:], knn_in[l], writes=[b_knn])
                S.dma("sp", qnr[:], qnr_in[l], writes=[b_qnr])
                S.dma("sp", knr[:], knr_in[l], writes=[b_knr])
                for c in range(2):
                    ts("dve", wqu[:, c, :], wqu[:, c, :], qan[:, c:c + 1], None, ALU.mult, None, [b_wqu, b_qan], [b_wqu])
                ts("dve", wkvu[:], wkvu[:], kvan[:, 0:1], None, ALU.mult, None, [b_wkvu, b_kvan], [b_wkvu])
                qsc = 192.0 ** -0.5
                ts("dve", qnn[:], qnn[:], qsc, None, ALU.mult, None, [b_qnn], [b_qnn])
                ts("dve", qnr[:], qnr[:], qsc, None, ALU.mult, None, [b_qnr], [b_qnr])

                xt = [SB(ph, "xt%d" % i, [128, D], F32) for i in range(3)]
                cst = [SB(ph, "cst%d" % i, [128, 2, 32], F32) for i in range(3)]
                sq, b_sq = SB(ph, "sq", [128, D], F32)
                ss, b_ss = SB(ph, "ss", [128, 1], F32)
                xn, b_xn = SB(ph, "xn", [128, D], BF16)
                hT, b_hT = SB(ph, "hT", [128, 8, 128], BF16)
                mD, b_mD = SB(ph, "mD", [128, 400], BF16)
                ss3, b_ss3 = SB(ph, "ss3", [128, 3], F32)
                rs3, b_rs3 = SB(ph, "rs3", [128, 3], F32)
                rq2, b_rq2 = SB(ph, "rq2", [128, 3], F32)
                mT, b_mT = SB(ph, "mT", [128, 4, 128], BF16)
                pre, b_pre = SB(ph, "pre", [128, 256], F32)
                lg, b_lg = SB(ph, "lg", [128, 256], F32)
                lgh, b_lgh = SB(ph, "lgh", [128, 256], BF16)
                lgl, b_lgl = SB(ph, "lgl", [128, 256], BF16)
                eb, b_eb = SB(ph, "eb", [128, 256], F32)
                enb, b_enb = SB(ph, "enb", [128, 256], F32)
                ebl, b_ebl = SB(ph, "ebl", [64, 4], F32)
                qg, b_qg = SB(ph, "qg", [128, 256], BF16)
                kg, b_kg = SB(ph, "kg", [128, 256], BF16)
                qkT, b_qkT = SB(ph, "qkT", [64, 8, 128], BF16)
                vsb, b_vsb = SB(ph, "vsb", [128, 512], BF16)
                eo, b_eo = SB(ph, "eo", [128, 512], F32)
                gog, b_gog = SB(ph, "gog", [128, 512], F32)
                ATs, b_ATs = SB(ph, "ATs", [128, 4, 128], BF16)
                stt_, b_st = SB(ph, "gst", [64, 4, 128], F32)
                stb, b_stb = SB(ph, "gstb", [64, 4, 128], BF16)
                sso, b_sso = SB(ph, "sso", [128, 4], F32)
                go, b_go = SB(ph, "go", [128, 512], BF16)
                gT, b_gT = SB(ph, "gT", [128, 4, 128], BF16)
                ss8, b_ss8 = SB(ph, "ss8", [128, 8], F32)
                fac8, b_fac8 = SB(ph, "fac8", [128, 8], F32)
                qn, b_qn = SB(ph, "qn", [128, 4, 128], BF16)
                zall, b_zall = SB(ph, "zall", [128, 5, 64], F32)
                ra, b_ra = SB(ph, "ra", [128, 5, 32], F32)
                rb, b_rb = SB(ph, "rb", [128, 5, 32], F32)
                rope_o, b_ropeo = SB(ph, "rope_o", [128, 5, 64], BF16)
                qT6, b_qT6 = SB(ph, "qT6", [128, 6, 128], BF16)
                kn, b_kn = SB(ph, "kn", [128, 4, 128], BF16)
                vt, b_vt = SB(ph, "vt", [128, 4, 128], BF16)
                kT5, b_kT5 = SB(ph, "kT5", [128, 5, 128], BF16)
                ssk, b_ssk = SB(ph, "ssk", [128, 4], F32)
                fack, b_fack = SB(ph, "fack", [128, 4], F32)

                pT, b_pT = PS(ph, "pT", [128, 8, 128], BF16)
                pA, b_pA = PS(ph, "pA", [128, 512], F32)
                pB, b_pB = PS(ph, "pB", [128, 512], F32)
                pC, b_pC = PS(ph, "pC", [128, 512], F32)
                pD, b_pD = PS(ph, "pD", [128, 512], F32)
                pM, _ = PS(ph, "pM", [128, 8, 128], BF16)
                b_pMa = b_pMb = S.buf("pM")
                pX, b_pX = PS(ph, "pX", [128, 512], F32)
                pY, b_pY = PS(ph, "pY", [128, 512], F32)
                pX4 = pX[:].rearrange("p (h e) -> p h e", e=128)
                pY4 = pY[:].rearrange("p (h e) -> p h e", e=128)

                memset("dve", stt_[:], 0.0, [b_st])
                memset("dve", stb[:], 0.0, [b_stb])

                sq2, b_sq2 = SB(ph, "sq2", [128, 128], F32)
                sq3, b_sq3 = SB(ph, "sq3", [128, 128], F32)
                b_pM = b_pMa
                kprs = [SB(ph, "kpr%d" % i, [128, 64], F32) for i in range(2)]
                rs3s = [SB(ph, "rs3_%d" % i, [128, 3], F32) for i in range(2)]
                rq2s = [SB(ph, "rq2_%d" % i, [128, 3], F32) for i in range(2)]
                mTs = [SB(ph, "mT%d" % i, [128, 4, 128], BF16) for i in range(2)]
                gqks = [SB(ph, "gqk%d" % i, [128, 512], F32) for i in range(2)]
                vsbs = [SB(ph, "vsb%d" % i, [128, 512], BF16) for i in range(2)]
                gogs = [SB(ph, "gog%d" % i, [128, 512], F32) for i in range(2)]
                pW = [(pA, b_pA), (pB, b_pB)]
                pQ = [(pC, b_pC), (pD, b_pD)]

                def drive(gens):
                    alive = list(gens)
                    while alive:
                        for g in list(alive):
                            try:
                                next(g)
                            except StopIteration:
                                alive.remove(g)

                def issue_loads(t):
                    tsl_ = slice(t * 128, (t + 1) * 128)
                    xtt_, b_xt_ = xt[t % 3]
                    cs_t_, b_cst_ = cst[t % 3]
                    S.dma("sp", xtt_[:], x_cur[tsl_, :], reads=([b_xcur] if b_xcur else []), writes=[b_xt_])
                    S.dma("sp", cs_t_[:, 0, :], cos_d[:, t, :], reads=[b_cos], writes=[b_cst_])
                    S.dma("sp", cs_t_[:, 1, :], sin_d[:, t, :], reads=[b_sin], writes=[b_cst_])

                def prologue(t):
                    sl = t % 2
                    tsl = slice(t * 128, (t + 1) * 128)
                    xtt, b_xt = xt[t % 3]
                    cs_t, b_cst = cst[t % 3]
                    kpr, b_kpr = kprs[sl]
                    rs3, b_rs3 = rs3s[sl]
                    rq2, b_rq2 = rq2s[sl]
                    mT, b_mT = mTs[sl]
                    gqk, b_gqk = gqks[sl]
                    vsb, b_vsb = vsbs[sl]
                    gog, b_gog = gogs[sl]
                    act(sq[:], xtt[:], AF.Square, [b_xt], [b_sq, b_ss], accum=ss[:, 0:1])
                    yield
                    rsqrt_cols(ss[:, 0:1], ss[:, 0:1], 1.0 / D, [b_ss], [b_ss])
                    yield
                    ts("dve", xn[:], xtt[:], ss[:, 0:1], None, ALU.mult, None, [b_xt, b_ss], [b_xn])
                    yield
                    for k in range(8):
                        tr(pT[:, k, :], xn[:, k * 128:(k + 1) * 128], ident[:], [b_xn, b_ident], [b_pT])
                    yield
                    for k in range(8):
                        if k % 2 == 0:
                            ts("dve", hT[:, k, :], pT[:, k, :], modcol[:, mc + 8 + k: mc + 9 + k], modcol[:, mc + k: mc + k + 1],
                               ALU.mult, ALU.add, [b_pT, b_modcol], [b_hT])
                        else:
                            act(hT[:, k, :], pT[:, k, :], AF.Identity, [b_pT, b_modcol], [b_hT],
                                scale=modcol[:, mc + 8 + k: mc + 9 + k], bias=modcol[:, mc + k: mc + k + 1])
                        if k % 4 == 3:
                            yield
                    blks = ((1536, 2000), (0, 512), (512, 1024), (1024, 1536))
                    for bi, (c0, c1) in enumerate(blks):
                        pw, b_pw = pW[bi % 2]
                        for k in range(8):
                            mm(pw[:, 0:c1 - c0], hT[:, k, :], win[:, k, c0:c1], k == 0, k == 7, [b_hT, b_wink[k]], [b_pw])
                        yield
                        if bi == 0:
                            cp("act", mD[:], pw[:, 0:400], [b_pw], [b_mD])
                            cp("act", kpr[:], pw[:, 400:464], [b_pw], [b_kpr])
                            yield
                            act(sq[:, 0:256], pw[:, 16:272], AF.Square, [b_pw], [b_sq, b_ss3], accum=ss3[:, 0:1])
                            act(sq[:, 0:128], pw[:, 272:400], AF.Square, [b_pw], [b_sq, b_ss3], accum=ss3[:, 1:2])
                            act(sq[:, 0:64], pw[:, 400:464], AF.Square, [b_pw], [b_sq, b_ss3], accum=ss3[:, 2:3])
                            yield
                        elif bi == 1:
                            cp("act", gqk[:], pw[:], [b_pw], [b_gqk])
                            yield
                        elif bi == 2:
                            cp("act", vsb[:], pw[:], [b_pw], [b_vsb])
                            yield
                        else:
                            act(eo[:], pw[:], AF.Exp, [b_pw], [b_eo], scale=-1.0)
                            yield
                            act(eo[:], eo[:], AF.Ln, [b_eo], [b_eo], bias=1.0)
                            yield
                            act(eo[:], eo[:], AF.Exp, [b_eo], [b_eo], scale=-1.0)
                            yield
                            tt("dve", gog[:], pw[:], eo[:], ALU.mult, [b_pw, b_eo], [b_gog])
                            yield
                            tt("pool", gog[:], gog[:], gon4[:], ALU.mult, [b_gog, b_gon4], [b_gog])
                            yield
                    act(rs3[:, 0:1], ss3[:, 0:1], AF.Ln, [b_ss3], [b_rs3], scale=1.0 / 256, bias=EPS)
                    act(rs3[:, 1:2], ss3[:, 1:2], AF.Ln, [b_ss3], [b_rs3], scale=1.0 / 128, bias=EPS)
                    act(rs3[:, 2:3], ss3[:, 2:3], AF.Ln, [b_ss3], [b_rs3], scale=1.0 / 64, bias=EPS)
                    yield
                    act(rs3[:], rs3[:], AF.Exp, [b_rs3], [b_rs3], scale=-0.5)
                    yield
                    tt("dve", rq2[:], rs3[:], rs3[:], ALU.mult, [b_rs3], [b_rq2])
                    tr(pT[0:16, 0, :], mD[:, 0:16], ident[:], [b_mD, b_ident], [b_pT])
                    tr(pT[:, 1, :], mD[:, 16:144], ident[:], [b_mD, b_ident], [b_pT])
                    tr(pT[:, 2, :], mD[:, 144:272], ident[:], [b_mD, b_ident], [b_pT])
                    tr(pT[:, 3, :], mD[:, 272:400], ident[:], [b_mD, b_ident], [b_pT])
                    yield
                    cp("dve", mT[0:16, 0, :], pT[0:16, 0, :], [b_pT], [b_mT])
                    cp("dve", mT[:, 1:4, :], pT[:, 1:4, :], [b_pT], [b_mT])
                    yield

                def gla_chain(t):
                    sl = t % 2
                    tsl = slice(t * 128, (t + 1) * 128)
                    mT, b_mT = mTs[sl]
                    gqk, b_gqk = gqks[sl]
                    vsb, b_vsb = vsbs[sl]
                    gog, b_gog = gogs[sl]
                    mm(pX[:, 0:256], mT[0:16, 0, :], wgu[:], True, True, [b_mT, b_wgu], [b_pX])
                    tt("dve", pre[:], pX[:, 0:256], bgate[:], ALU.add, [b_pX, b_bgate], [b_pre])
                    yield
                    act(pre[:], pre[:], AF.Exp, [b_pre], [b_pre], scale=-1.0)
                    yield
                    act(lg[:], pre[:], AF.Ln, [b_pre], [b_lg], bias=1.0)
                    yield
                    cp("dve", lgh[:], lg[:], [b_lg], [b_lgh])
                    yield
                    tt("dve", lgl[:], lg[:], lgh[:], ALU.subtract, [b_lg, b_lgh], [b_lgl])
                    yield
                    mm(pX[:, 256:512], maskU4[:, 0, :], lgh[:], True, False, [b_maskU4, b_lgh], [b_pX])
                    mm(pX[:, 256:512], maskU4[:, 0, :], lgl[:], False, True, [b_maskU4, b_lgl], [b_pX])
                    for h in range(4):
                        mm(pY[0:64, h:h + 1], lgh[:, h * 64:(h + 1) * 64], ones_b[:, 0:1], True, False,
                           [b_lgh, b_onesb], [b_pY])
                        mm(pY[0:64, h:h + 1], lgl[:, h * 64:(h + 1) * 64], ones_b[:, 0:1], False, True,
                           [b_lgl, b_onesb], [b_pY])
                    yield
                    act(eb[:], pX[:, 256:512], AF.Exp, [b_pX], [b_eb], scale=-1.0 / 16)
                    act(enb[:], pX[:, 256:512], AF.Exp, [b_pX], [b_enb], scale=1.0 / 16)
                    act(ebl[:], pY[0:64, 0:4], AF.Exp, [b_pY], [b_ebl], scale=-1.0 / 16)
                    yield
                    stt("dve", qg[:], gqk[:, 0:256], 0.125, eb[:], ALU.mult, ALU.mult, [b_gqk, b_eb], [b_qg])
                    tt("dve", kg[:], gqk[:, 256:512], enb[:], ALU.mult, [b_gqk, b_enb], [b_kg])
                    yield
                    for h in range(4):
                        tr(pM[0:64, h, :], qg[:, h * 64:(h + 1) * 64], ident[:], [b_qg, b_ident], [b_pM])
                        tr(pM[0:64, 4 + h, :], kg[:, h * 64:(h + 1) * 64], ident[:], [b_kg, b_ident], [b_pM])
                    yield
                    cp("dve", qkT[:], pM[0:64, :, :], [b_pM], [b_qkT])
                    yield
                    for h in range(4):
                        mm(pY4[:, h, :], qkT[:, 4 + h, :], qkT[:, h, :], True, True, [b_qkT], [b_pY])
                    yield
                    tt("dve", ATs[:], pY4, maskU4[:], ALU.mult, [b_pY, b_maskU4], [b_ATs])
                    yield
                    for h in range(4):
                        mm(pX4[:, h, :], ATs[:, h, :], vsb[:, h * 128:(h + 1) * 128], True, False, [b_ATs, b_vsb], [b_pX])
                        mm(pX4[:, h, :], qkT[:, h, :], stb[:, h, :], False, True, [b_qkT, b_stb], [b_pX])
                    for h in range(4):
                        mm(pY4[0:64, h, :], kg[:, h * 64:(h + 1) * 64], vsb[:, h * 128:(h + 1) * 128], True, True,
                           [b_kg, b_vsb], [b_pY])
                    yield
                    for h in range(4):
                        ts("dve", stt_[:, h, :], stt_[:, h, :], ebl[:, h:h + 1], None, ALU.mult, None, [b_st, b_ebl], [b_st])
                        stt("dve", stt_[:, h, :], pY4[0:64, h, :], ebl[:, h:h + 1], stt_[:, h, :], ALU.mult, ALU.add,
                            [b_pY, b_ebl, b_st], [b_st])
                        if h == 1:
                            yield
                    cp("dve", stb[:], stt_[:], [b_st], [b_stb])
                    yield
                    for h in range(4):
                        act(sq2[:], pX4[:, h, :], AF.Square, [b_pX], [b_sq2, b_sso], accum=sso[:, h:h + 1])
                    yield
                    act(sso[:], sso[:], AF.Ln, [b_sso], [b_sso], scale=1.0 / 128, bias=EPS)
                    yield
                    act(sso[:], sso[:], AF.Exp, [b_sso], [b_sso], scale=-0.5)
                    yield
                    for h in range(4):
                        stt("dve", go[:, h * 128:(h + 1) * 128], pX4[:, h, :], sso[:, h:h + 1], gog[:, h * 128:(h + 1) * 128],
                            ALU.mult, ALU.mult, [b_pX, b_sso, b_gog], [b_go])
                        if h == 1:
                            yield
                    yield
                    for h in range(4):
                        tr(pM[:, h, :], go[:, h * 128:(h + 1) * 128], ident[:], [b_go, b_ident], [b_pM])
                    yield
                    cp("dve", gT[:], pM[:, 0:4, :], [b_pM], [b_gT])
                    yield
                    S.dma("sp", mixT_d[0:4, :, tsl].rearrange("c p s -> p c s"), gT[:], reads=[b_gT], writes=[b_mixT], sem="gT_st")

                qhs, b_qhs = SB(ph, "qhs", [128, 768], F32)
                kvs, b_kvs = SB(ph, "kvs", [128, 1024], F32)
                zk, b_zk = SB(ph, "zk", [128, 64], F32)
                rka, b_rka = SB(ph, "rka", [128, 32], F32)
                rkb, b_rkb = SB(ph, "rkb", [128, 32], F32)
                rope_k, b_ropek = SB(ph, "rope_k", [128, 64], BF16)
                sq4, b_sq4 = SB(ph, "sq4", [128, 128], F32)

                def mla_q(t):
                    sl = t % 2
                    tsl = slice(t * 128, (t + 1) * 128)
                    cs_t, b_cst = cst[t % 3]
                    rs3, b_rs3 = rs3s[sl]
                    rq2, b_rq2 = rq2s[sl]
                    mT, b_mT = mTs[sl]
                    pt_, bpt = pQ[0]
                    for g in range(2):
                        for c in range(2):
                            mm(pt_[:, 0:384], mT[:, 1 + c, :], wqu[:, c, g * 384:(g + 1) * 384], c == 0, c == 1, [b_mT, b_wqu], [bpt])
                        yield
                        cp("act", qhs[:, g * 384:(g + 1) * 384], pt_[:, 0:384], [bpt], [b_qhs])
                        for hh in range(2):
                            h = 2 * g + hh
                            base = hh * 192
                            act(sq3[:, 0:128], pt_[:, base:base + 128], AF.Square, [bpt], [b_sq3, b_ss8], accum=ss8[:, h:h + 1])
                            act(sq3[:, 0:64], pt_[:, base + 128:base + 192], AF.Square, [bpt], [b_sq3, b_ss8], accum=ss8[:, 4 + h:5 + h])
                        yield
                    ts("dve", ss8[:], ss8[:], rq2[:, 0:1], None, ALU.mult, None, [b_ss8, b_rq2], [b_ss8])
                    yield
                    act(fac8[:, 0:4], ss8[:, 0:4], AF.Ln, [b_ss8], [b_fac8], scale=1.0 / 128, bias=EPS)
                    act(fac8[:, 4:8], ss8[:, 4:8], AF.Ln, [b_ss8], [b_fac8], scale=1.0 / 64, bias=EPS)
                    yield
                    act(fac8[:], fac8[:], AF.Exp, [b_fac8], [b_fac8], scale=-0.5)
                    yield
                    ts("dve", fac8[:], fac8[:], rs3[:, 0:1], None, ALU.mult, None, [b_fac8, b_rs3], [b_fac8])
                    yield
                    for h in range(4):
                        base = h * 192
                        stt("dve", qn[:, h, :], qhs[:, base:base + 128], fac8[:, h:h + 1], qnn[:], ALU.mult, ALU.mult,
                            [b_qhs, b_fac8, b_qnn], [b_qn])
                        stt("dve", zall[:, h, :], qhs[:, base + 128:base + 192], fac8[:, 4 + h:5 + h], qnr[:], ALU.mult, ALU.mult,
                            [b_qhs, b_fac8, b_qnr], [b_zall])
                        if h % 2 == 1:
                            yield
                    for h in range(4):
                        tr(pM[:, h, :], qn[:, h, :], ident[:], [b_qn, b_ident], [b_pM])
                    yield
                    cp("dve", qT6[:, 0:4, :], pM[:, 0:4, :], [b_pM], [b_qT6])
                    yield
                    S.dma("sp", QT_d[:, :, tsl].rearrange("c p s -> p c s"), qT6[:, 0:4, :], reads=[b_qT6], writes=[b_QT], sem="qT_st")
                    for hh in range(4):
                        z1, z2 = zall[:, hh, 0:32], zall[:, hh, 32:64]
                        cth, sth = cs_t[:, 0, :], cs_t[:, 1, :]
                        tt("pool", ra[:, hh, :], z1, cth, ALU.mult, [b_zall, b_cst], [b_ra])
                        tt("pool", rb[:, hh, :], z2, sth, ALU.mult, [b_zall, b_cst], [b_rb])
                        tt("pool", rope_o[:, hh, 0:32], ra[:, hh, :], rb[:, hh, :], ALU.subtract, [b_ra, b_rb], [b_ropeo])
                        yield
                        tt("pool", ra[:, hh, :], z2, cth, ALU.mult, [b_zall, b_cst], [b_ra])
                        tt("pool", rb[:, hh, :], z1, sth, ALU.mult, [b_zall, b_cst], [b_rb])
                        tt("pool", rope_o[:, hh, 32:64], ra[:, hh, :], rb[:, hh, :], ALU.add, [b_ra, b_rb], [b_ropeo])
                        yield
                    for hp in range(2):
                        tr(pM[:, 4 + hp, :], rope_o[:, 2 * hp:2 * hp + 2, :].rearrange("p h r -> p (h r)"), ident[:],
                           [b_ropeo, b_ident], [b_pM])
                    yield
                    cp("dve", qT6[:, 4:6, :], pM[:, 4:6, :], [b_pM], [b_qT6])
                    yield
                    S.dma("sp", QPE_d[:, :, tsl].rearrange("c p s -> p c s"), qT6[:, 4:6, :], reads=[b_qT6], writes=[b_QPE], sem="qT_st2")

                def mla_kv(t):
                    sl = t % 2
                    tsl = slice(t * 128, (t + 1) * 128)
                    cs_t, b_cst = cst[t % 3]
                    kpr, b_kpr = kprs[sl]
                    rs3, b_rs3 = rs3s[sl]
                    rq2, b_rq2 = rq2s[sl]
                    mT, b_mT = mTs[sl]
                    pt_, bpt = pQ[1]
                    stt("dve", zk[:], kpr[:], rs3[:, 2:3], knr[:], ALU.mult, ALU.mult, [b_kpr, b_rs3, b_knr], [b_zk])
                    yield
                    for g in range(2):
                        mm(pt_[:], mT[:, 3, :], wkvu[:, g * 512:(g + 1) * 512], True, True, [b_mT, b_wkvu], [bpt])
                        yield
                        cp("act", kvs[:, g * 512:(g + 1) * 512], pt_[:], [bpt], [b_kvs])
                        for hh in range(2):
                            h = 2 * g + hh
                            act(sq4[:, 0:128], pt_[:, hh * 256:hh * 256 + 128], AF.Square, [bpt], [b_sq4, b_ssk], accum=ssk[:, h:h + 1])
                        yield
                    z1, z2 = zk[:, 0:32], zk[:, 32:64]
                    cth, sth = cs_t[:, 0, :], cs_t[:, 1, :]
                    tt("dve", rka[:], z1, cth, ALU.mult, [b_zk, b_cst], [b_rka])
                    tt("dve", rkb[:], z2, sth, ALU.mult, [b_zk, b_cst], [b_rkb])
                    tt("dve", rope_k[:, 0:32], rka[:], rkb[:], ALU.subtract, [b_rka, b_rkb], [b_ropek])
                    yield
                    tt("dve", rka[:], z2, cth, ALU.mult, [b_zk, b_cst], [b_rka])
                    tt("dve", rkb[:], z1, sth, ALU.mult, [b_zk, b_cst], [b_rkb])
                    tt("dve", rope_k[:, 32:64], rka[:], rkb[:], ALU.add, [b_rka, b_rkb], [b_ropek])
                    yield
                    ts("dve", ssk[:], ssk[:], rq2[:, 1:2], None, ALU.mult, None, [b_ssk, b_rq2], [b_ssk])
                    yield
                    act(fack[:], ssk[:], AF.Ln, [b_ssk], [b_fack], scale=1.0 / 128, bias=EPS)
                    yield
                    act(fack[:], fack[:], AF.Exp, [b_fack], [b_fack], scale=-0.5)
                    yield
                    ts("dve", fack[:], fack[:], rs3[:, 1:2], None, ALU.mult, None, [b_fack, b_rs3], [b_fack])
                    yield
                    for h in range(4):
                        base = h * 256
                        stt("dve", kn[:, h, :], kvs[:, base:base + 128], fack[:, h:h + 1], knn[:], ALU.mult, ALU.mult,
                            [b_kvs, b_fack, b_knn], [b_kn])
                        if h % 2 == 1:
                            yield
                    for h in range(4):
                        tr(pM[:, h, :], kn[:, h, :], ident[:], [b_kn, b_ident], [b_pM])
                    tr(pM[0:64, 4, :], rope_k[:], ident[:], [b_ropek, b_ident], [b_pM])
                    yield
                    cp("dve", kT5[:, 0:4, :], pM[:, 0:4, :], [b_pM], [b_kT5])
                    cp("dve", kT5[0:64, 4, :], pM[0:64, 4, :], [b_pM], [b_kT5])
                    yield
                    S.dma("sp", KT_d[:, :, tsl].rearrange("c p s -> p c s"), kT5[:, 0:4, :], reads=[b_kT5], writes=[b_KT], sem="kT_st")
                    S.dma("sp", KPE_d[:, tsl], kT5[0:64, 4, :], reads=[b_kT5], writes=[b_KPE], sem="kT_st2")
                    for h in range(4):
                        base = h * 256
                        ts("dve", vt[:, h, :], kvs[:, base + 128:base + 256], rs3[:, 1:2], None, ALU.mult, None, [b_kvs, b_rs3], [b_vt])
                        if h % 2 == 1:
                            yield
                    S.dma("sp", V_d[tsl, :], vt[:].rearrange("p h e -> p (h e)"), reads=[b_vt], writes=[b_V], sem="vt_st")

                if "p1" not in SKIP:
                    issue_loads(0)
                    if NT > 1:
                        issue_loads(1)
                    drive([prologue(0)])
                for t in range(NT if "p1" not in SKIP else 0):
                    if t + 2 < NT:
                        issue_loads(t + 2)
                    gens = [mla_q(t), gla_chain(t), mla_kv(t)]
                    if t + 1 < NT:
                        gens.append(prologue(t + 1))
                    drive(gens)
            S.barrier()
            if stop and (stop.startswith('p1') or stop.startswith('q') or stop.startswith('c')):
                break

            ph23 = contextlib.ExitStack()
            phw = contextlib.ExitStack()
            uid[0] += 1
            wout = phw.enter_context(nc.sbuf_tensor("wout_s%d" % uid[0], [128, 8, D], BF16, side="right"))
            b_woutk = [S.buf("wout%d" % k) for k in range(8)]
            for k in range(8):
                S.dma("pool", wout[:, k, :], wout_in[l, k * 128:(k + 1) * 128, :], writes=[b_woutk[k]], sem="wout_ld", nobar=True)
            with contextlib.ExitStack() as ph:
                KTs = [SB(ph, "KTs%d" % i, [128, S_LEN], BF16) for i in range(2)]
                Vh = [SB(ph, "Vh%d" % i, [128, NT, 128], BF16) for i in range(2)]
                KPEs, b_KPEs = SB(ph, "KPEs", [64, S_LEN], BF16)
                qns = [SB(ph, "qns%d" % i, [128, 512], BF16) for i in range(2)]
                qps = [SB(ph, "qps%d" % i, [64, 512], BF16) for i in range(2)]
                pTs = [SB(ph, "pTs%d" % i, [128, 512], BF16) for i in range(4)]
                rl, b_rl = SB(ph, "rl", [128, 512], F32)
                ob = [SB(ph, "ob%d" % i, [128, 512], BF16) for i in range(2)]
                pS = [PS(ph, "pS%d" % i, [128, 512], F32) for i in range(4)]
                pO = [PS(ph, "pO%d" % i, [128, 512], F32) for i in range(2)]
                pL = [PS(ph, "pL%d" % i, [128, 512], F32) for i in range(2)]
                S.dma("sp", KPEs[:], KPE_d, reads=[b_KPE], writes=[b_KPEs])
                spl2 = split and last
                if spl2:
                    maskSel, b_maskSel = SB(ph, "maskSel", [128, 8, 512], BF16)
                    for r in range(4):
                        act(maskSel[:, r, :], maskD[:, r, :], AF.Identity, [b_maskD, b_par], [b_maskSel],
                            scale=par[:, 0:1], bias=par[:, 1:2])
                        act(maskSel[:, 4 + r, :], maskD[:, r, :], AF.Identity, [b_maskD, b_par], [b_maskSel], scale=par[:, 1:2])
                    qna = [SB(ph, "qna%d" % i, [128, 512], BF16) for i in range(2)]
                    qnb = [SB(ph, "qnb%d" % i, [128, 512], BF16) for i in range(2)]
                    qpa = [SB(ph, "qpa%d" % i, [64, 512], BF16) for i in range(2)]
                    qpb = [SB(ph, "qpb%d" % i, [64, 512], BF16) for i in range(2)]
                NPOS = NC4 // 2 if spl2 else NC4

                def nkb_of(c):
                    return 8 * c + 8 if spl2 else 4 * c + 4

                chunks = [(h, c) for h in range(4 if "p2" not in SKIP else 0) for c in range(NPOS)]
                blocks = []
                for idx, (h, c) in enumerate(chunks):
                    for kb in range(nkb_of(c)):
                        blocks.append((idx, kb, nkb_of(c)))

                def load_head(h):
                    KTh, b_KTh = KTs[h % 2]
                    Vhh, b_Vhh = Vh[h % 2]
                    S.dma("sp", KTh[:], KT_d[h], reads=[b_KT], writes=[b_KTh])
                    S.dma("sp", Vhh[:], V_d[:, h * 128:(h + 1) * 128].rearrange("(t p) e -> p t e", p=128), reads=[b_V], writes=[b_Vhh])

                def load_q(idx):
                    h, c = chunks[idx]
                    hp, off = h // 2, 64 * (h % 2)
                    if not spl2:
                        csl = slice(c * 512, (c + 1) * 512)
                        S.dma("sp", qns[idx % 2][0][:], QT_d[h, :, csl], reads=[b_QT], writes=[qns[idx % 2][1]])
                        S.dma("sp", qps[idx % 2][0][:], QPE_d[hp, off:off + 64, csl], reads=[b_QPE], writes=[qps[idx % 2][1]])
                        return
                    sla = slice(2 * c * 512, (2 * c + 1) * 512)
                    slb = slice((2 * c + 1) * 512, (2 * c + 2) * 512)
                    qa, b_qa = qna[idx % 2]
                    qb, b_qb = qnb[idx % 2]
                    pa, b_pa = qpa[idx % 2]
                    pb, b_pb = qpb[idx % 2]
                    qs, b_qs = qns[idx % 2]
                    ps_, b_ps = qps[idx % 2]
                    S.dma("sp", qa[:], QT_d[h, :, sla], reads=[b_QT], writes=[b_qa])
                    S.dma("sp", qb[:], QT_d[h, :, slb], reads=[b_QT], writes=[b_qb])
                    S.dma("sp", pa[:], QPE_d[hp, off:off + 64, sla], reads=[b_QPE], writes=[b_pa])
                    S.dma("sp", pb[:], QPE_d[hp, off:off + 64, slb], reads=[b_QPE], writes=[b_pb])
                    ts("dve", qs[:], qa[:], par[:, 0:1], None, ALU.mult, None, [b_qa, b_par], [b_qs])
                    stt("dve", qs[:], qb[:], par[:, 1:2], qs[:], ALU.mult, ALU.add, [b_qb, b_par, b_qs], [b_qs])
                    ts("dve", ps_[:], pa[:], par[0:64, 0:1], None, ALU.mult, None, [b_pa, b_par], [b_ps])
                    stt("dve", ps_[:], pb[:], par[0:64, 1:2], ps_[:], ALU.mult, ALU.add, [b_pb, b_par, b_ps], [b_ps])

                LA = 3
                if chunks:
                    load_head(0)
                    load_q(0)
                for i in range((len(blocks) + LA) if blocks else 0):
                    if i < len(blocks):
                        idx, kb, nkb = blocks[i]
                        h, c = chunks[idx]
                        if kb == 0 and idx + 1 < len(chunks):
                            load_q(idx + 1)
                        KTh, b_KTh = KTs[h % 2]
                        qn_, b_qn_ = qns[idx % 2]
                        qp_, b_qp_ = qps[idx % 2]
                        ksl = slice(kb * 128, (kb + 1) * 128)
                        pSt, b_pSt = pS[i % 4]
                        pTt, b_pTt = pTs[i % 4]
                        mm(pSt[:], KTh[:, ksl], qn_[:], True, False, [b_KTh, b_qn_], [b_pSt])
                        mm(pSt[:], KPEs[:, ksl], qp_[:], False, True, [b_KPEs, b_qp_], [b_pSt])
                        act(pTt[:], pSt[:], AF.Exp, [b_pSt], [b_pTt])
                        if spl2:
                            r = kb - 8 * c
                            if r >= 0:
                                tt("pool", pTt[:], pTt[:], maskSel[:, r, :], ALU.mult, [b_pTt, b_maskSel], [b_pTt])
                        else:
                            r = kb - 4 * c
                            if r >= 0:
                                tt("pool", pTt[:], pTt[:], maskD[:, r, :], ALU.mult, [b_pTt, b_maskD], [b_pTt])
                    if i >= LA:
                        j = i - LA
                        idx, kb, nkb = blocks[j]
                        h, c = chunks[idx]
                        Vhh, b_Vhh = Vh[h % 2]
                        pTt, b_pTt = pTs[j % 4]
                        pOt, b_pOt = pO[idx % 2]
                        pLt, b_pLt = pL[idx % 2]
                        mm(pOt[:], Vhh[:, kb, :], pTt[:], kb == 0, kb == nkb - 1, [b_Vhh, b_pTt], [b_pOt])
                        mm(pLt[:], ones_b[:], pTt[:], kb == 0, kb == nkb - 1, [b_onesb, b_pTt], [b_pLt])
                        if kb == 0 and c == 0 and h + 1 < 4:
                            load_head(h + 1)
                        if kb == nkb - 1:
                            obt, b_obt = ob[idx % 2]
                            csl = slice(c * 512, (c + 1) * 512)
                            recip(rl[:], pLt[:], [b_pLt], [b_rl])
                            tt("dve", obt[:], pOt[:], rl[:], ALU.mult, [b_pOt, b_rl], [b_obt])
                            S.dma("pool", mixT_d[4 + h, :, csl], obt[:], reads=[b_obt], writes=[b_mixT], sem="ob_st%d" % (idx % 2))
            S.barrier()
            if stop == 'p2':
                phw.close()
                ph23.close()
                break
            w1, _ = SB(ph23, "w1", [128, 8, DFF], BF16)
            w2, _ = SB(ph23, "w2", [128, 32, D], BF16)
            b_w1k = [S.buf("w1_%d" % k) for k in range(8)]
            b_w2f = [S.buf("w2_%d" % f) for f in range(32)]
            for k in range(8):
                for q in range(2):
                    S.dma("pool", w1[:, k, q * 2048:(q + 1) * 2048], w1_in[l, k * 128:(k + 1) * 128, q * 2048:(q + 1) * 2048],
                          writes=[b_w1k[k]], sem="w1_ld", nobar=True)
            for f in range(32):
                S.dma("pool", w2[:, f, :], w2_in[l, f * 128:(f + 1) * 128, :], writes=[b_w2f[f]], sem="w2_ld", nobar=True)

            with contextlib.ExitStack() as ph:
                gta, b_gta = SB(ph, "gta", [128, D], F32)
                S.dma("sp", gta[:], gate_d[l, 0], reads=[b_gate], writes=[b_gta])
                mx = [SB(ph, "mx%d" % i, [128, 8, 128], BF16) for i in range(2)]
                xt = [SB(ph, "xt%d" % i, [128, D], F32) for i in range(2)]
                xo = [SB(ph, "xo%d" % i, [128, D], F32) for i in range(2)]
                pW = [PS(ph, "pW%d" % i, [128, 512], F32) for i in range(4)]
                spl = split and last
                if spl:
                    mxb = [SB(ph, "mxb%d" % i, [128, 8, 128], BF16) for i in range(2)]
                    xtb = [SB(ph, "xtb%d" % i, [128, D], F32) for i in range(2)]
                    mxs = [SB(ph, "mxs%d" % i, [128, 8, 128], BF16) for i in range(2)]
                    xss = [SB(ph, "xss%d" % i, [128, D], F32) for i in range(2)]
                NT3 = (NT // 2 if spl else NT) if "p3a" not in SKIP else 0
                for t in range(NT3):
                    tsl = slice(t * 128, (t + 1) * 128)
                    tg = (8 * (t // 4) + t % 4) if spl else t
                    tsl0 = slice(tg * 128, (tg + 1) * 128)
                    tsl1 = slice((tg + 4) * 128, (tg + 5) * 128)
                    mxt, b_mx = mx[t % 2]
                    xtt, b_xt = xt[t % 2]
                    xot, b_xo = xo[t % 2]
                    S.dma("sp", mxt[:], mixT_d[:, :, tsl0].rearrange("c p s -> p c s"), reads=[b_mixT], writes=[b_mx])
                    S.dma("sp", xtt[:], x_cur[tsl0, :], reads=([b_xcur] if b_xcur else []), writes=[b_xt])
                    if spl:
                        mxbt, b_mxb = mxb[t % 2]
                        xtbt, b_xtb = xtb[t % 2]
                        mxst, b_mxs = mxs[t % 2]
                        xsst, b_xss = xss[t % 2]
                        S.dma("sp", mxbt[:], mixT_d[:, :, tsl1].rearrange("c p s -> p c s"), reads=[b_mixT], writes=[b_mxb])
                        S.dma("sp", xtbt[:], x_cur[tsl1, :], reads=([b_xcur] if b_xcur else []), writes=[b_xtb])
                        act(mxst[:, 0:4, :], mxt[:, 0:4, :], AF.Identity, [b_mx, b_par], [b_mxs], scale=par[:, 0:1])
                        stt("dve", mxst[:, 0:4, :], mxbt[:, 0:4, :], par[:, 1:2], mxst[:, 0:4, :], ALU.mult, ALU.add,
                            [b_mxb, b_par, b_mxs], [b_mxs])
                        S.dma("sp", mxst[:, 4:8, :], mixT_d[4:8, :, tsl].rearrange("c p s -> p c s"), reads=[b_mixT], writes=[b_mxs])
                        act(xsst[:], xtt[:], AF.Identity, [b_xt, b_par], [b_xss], scale=par[:, 0:1])
                        stt("dve", xsst[:], xtbt[:], par[:, 1:2], xsst[:], ALU.mult, ALU.add, [b_xtb, b_par, b_xss], [b_xss])
                        mxt, b_mx = mxst, b_mxs
                        xtt, b_xt = xsst, b_xss
                    for half in range(2):
                        pw, b_pw = pW[(t % 2) * 2 + half]
                        hs = slice(half * 512, (half + 1) * 512)
                        for c in range(8):
                            mm(pw[:], mxt[:, c, :], wout[:, c, hs], c == 0, c == 7, [b_mx, b_woutk[c]], [b_pw])
                        tt("dve", xot[:, hs], pw[:], gta[:, hs], ALU.mult, [b_pw, b_gta], [b_xo])
                        tt("pool", xot[:, hs], xot[:, hs], xtt[:, hs], ALU.add, [b_xo, b_xt], [b_xo])
                    S.dma("pool", xmid_d[tsl, :], xot[:], reads=[b_xo], writes=[b_xmid], sem="xo_st%d" % (t % 2))
            S.barrier()
            phw.close()
            if stop == 'p3a':
                ph23.close()
                break

            with contextlib.ExitStack() as ph:
                gtf, b_gtf = SB(ph, "gtf", [128, D], F32)
                S.dma("sp", gtf[:], gate_d[l, 1], reads=[b_gate], writes=[b_gtf])
                xm = [SB(ph, "xm%d" % i, [128, 2, D], F32) for i in range(3)]
                ss, b_ss = SB(ph, "ss", [128, 2], F32)
                xn2 = [SB(ph, "xn2_%d" % i, [128, 2, D], BF16) for i in range(2)]
                h2Ts = [SB(ph, "h2T%d" % i, [128, 8, 256], BF16) for i in range(2)]
                aT, b_aT = SB(ph, "aT", [128, 32, 256], BF16)
                rt = [SB(ph, "rt%d" % i, [128, 256], F32) for i in range(2)]
                yo = [SB(ph, "yo%d" % i, [128, D], F32) for i in range(2)]
                pTs2 = [PS(ph, "pT%d" % i, [128, 8, 128], BF16) for i in range(2)]
                pU = [PS(ph, "pU%d" % i, [128, 256], F32) for i in range(2)]
                pDn = [PS(ph, "pDn%d" % i, [128, 512], F32) for i in range(4)]

                def prep_norm(g):
                    xmt, b_xm = xm[g % 3]
                    xnt, b_xnt = xn2[g % 2]
                    for j in range(2):
                        t = 2 * g + j
                        S.dma("sp", xmt[:, j, :], xmid_d[t * 128:(t + 1) * 128, :], reads=[b_xmid], writes=[b_xm])
                    for j in range(2):
                        act(xnt[:, j, :], xmt[:, j, :], AF.Square, [b_xm], [b_xnt, b_ss], accum=ss[:, j:j + 1])
                    rsqrt_cols(ss[:], ss[:], 1.0 / D, [b_ss], [b_ss])
                    for j in range(2):
                        ts("dve", xnt[:, j, :], xmt[:, j, :], ss[:, j:j + 1], None, ALU.mult, None, [b_xm, b_ss], [b_xnt])

                def prep_tr(g):
                    xnt, b_xnt = xn2[g % 2]
                    h2T, b_h2T = h2Ts[g % 2]
                    for j in range(2):
                        pT, b_pT = pTs2[j]
                        for k in range(8):
                            tr(pT[:, k, :], xnt[:, j, k * 128:(k + 1) * 128], ident[:], [b_xnt, b_ident], [b_pT])
                        for k in range(8):
                            if k % 2 == 0:
                                ts("dve", h2T[:, k, j * 128:(j + 1) * 128], pT[:, k, :], modcol[:, mc + 24 + k: mc + 25 + k],
                                   modcol[:, mc + 16 + k: mc + 17 + k], ALU.mult, ALU.add, [b_pT, b_modcol], [b_h2T])
                            else:
                                act(h2T[:, k, j * 128:(j + 1) * 128], pT[:, k, :], AF.Identity, [b_pT, b_modcol], [b_h2T],
                                    scale=modcol[:, mc + 24 + k: mc + 25 + k], bias=modcol[:, mc + 16 + k: mc + 17 + k])

                yi = 0
                NGE = NG // 2 if (split and last) else NG
                prep_norm(0)
                prep_tr(0)
                for g in range(NGE):
                    xmt, b_xm = xm[g % 3]
                    h2T, b_h2T = h2Ts[g % 2]
                    if g + 1 < NGE:
                        prep_norm(g + 1)
                    for f in range(32):
                        pu, b_pu = pU[f % 2]
                        rtt, b_rt = rt[f % 2]
                        for k in range(8):
                            mm(pu[:], w1[:, k, f * 128:(f + 1) * 128], h2T[:, k, :], k == 0, k == 7, [b_w1k[k], b_h2T], [b_pu])
                        act(rtt[:], pu[:], AF.Relu, [b_pu], [b_rt])
                        tt("dve" if f % 2 == 0 else "pool", aT[:, f, :], rtt[:], rtt[:], ALU.mult, [b_rt], [b_aT])
                    if g + 1 < NGE:
                        prep_tr(g + 1)
                    for j in range(2):
                        t = 2 * g + j
                        tsl = slice(t * 128, (t + 1) * 128)
                        yot, b_yo = yo[yi % 2]
                        yi += 1
                        for half in range(2):
                            pd, b_pd = pDn[j * 2 + half]
                            hs = slice(half * 512, (half + 1) * 512)
                            for f in range(32):
                                mm(pd[:], aT[:, f, j * 128:(j + 1) * 128], w2[:, f, hs], f == 0, f == 31, [b_aT, b_w2f[f]], [b_pd])
                            tt("dve", yot[:, hs], pd[:], gtf[:, hs], ALU.mult, [b_pd, b_gtf], [b_yo])
                            tt("pool", yot[:, hs], yot[:, hs], xmt[:, j, hs], ALU.add, [b_yo, b_xm], [b_yo])
                        d = S.dma("pool", x_nxt[tsl, :], yot[:], reads=[b_yo], writes=[b_xnxt], sem="yo_st%d" % ((yi - 1) % 2))
                        if last:
                            out_dmas.append(d)
            S.barrier()
            ph23.close()

        S.emit(final_waits=out_dmas)
    return nc


def host_inputs(b, S_LEN, DEPTH, x, c, positions, _parity=0, *, w_ada, b_ada, w_in, w_gate_up, b_gate, gla_out_norm, q_a_norm,
                w_q_up, kv_a_norm, w_kv_up, q_norm_nope, k_norm_nope, q_norm_rope, k_norm_rope,
                w_out, w_mlp_up, w_mlp_down):
    f32 = np.float32
    NT = S_LEN // 128
    A = lambda a: np.ascontiguousarray(np.asarray(a))

    def bc(v, n=128):
        v = np.asarray(v, dtype=f32)
        return A(np.broadcast_to(v[:, None, :], (v.shape[0], n, v.shape[1])))

    inv_freq = (10000.0 ** (-np.arange(0, 64, 2, dtype=f32) / f32(64))).astype(f32)
    b_ada = np.asarray(b_ada, dtype=f32)
    d = {
        "x": A(np.asarray(x[b], dtype=f32)),
        "ccol": A(np.asarray(c[b], dtype=f32).reshape(8, 128).T),
        "pos": A(np.asarray(positions[b]).astype(np.int32).reshape(NT, 128).T),
        "invf": A(np.broadcast_to(inv_freq[None, :], (128, 32))),
        "parcol": A(np.broadcast_to(np.array([[1.0 - _parity, float(_parity)]], dtype=f32), (128, 2))),
        "w_ada": A(np.asarray(w_ada, dtype=f32)),
        "bada_col": A(b_ada.reshape(DEPTH, 48, 128).transpose(0, 2, 1)),
        "bada_gate": A(np.broadcast_to(b_ada.reshape(DEPTH, 6, 1, D)[:, [2, 5]], (DEPTH, 2, 128, D))),
        "w_in": A(np.asarray(w_in, dtype=f32)),
        "w_gate_up": A(np.asarray(w_gate_up, dtype=f32)),
        "bgate_bc": bc(b_gate),
        "gon_bc4": bc(np.tile(np.asarray(gla_out_norm, dtype=f32), (1, 4))),
        "qan_col": A(np.asarray(q_a_norm, dtype=f32).reshape(DEPTH, 2, 128).transpose(0, 2, 1)),
        "w_q_up": A(np.asarray(w_q_up, dtype=f32)),
        "kvan_col": A(np.asarray(kv_a_norm, dtype=f32).reshape(DEPTH, 128, 1)),
        "w_kv_up": A(np.asarray(w_kv_up, dtype=f32)),
        "qnn_bc": bc(q_norm_nope), "knn_bc": bc(k_norm_nope),
        "qnr_bc": bc(q_norm_rope), "knr_bc": bc(k_norm_rope),
        "w_out": A(np.asarray(w_out, dtype=f32)),
        "w_mlp_up": A(np.asarray(w_mlp_up, dtype=f32)),
        "w_mlp_down": A(np.asarray(w_mlp_down, dtype=f32)),
    }
    return d


_NC_CACHE = {}


def kernel(**inputs):
    x = np.asarray(inputs["x"])
    B, S_LEN, _ = x.shape
    DEPTH = np.asarray(inputs["w_ada"]).shape[0]
    key = (S_LEN, DEPTH)
    if key not in _NC_CACHE:
        _NC_CACHE[key] = build(S_LEN, DEPTH, split=True)
    nc = _NC_CACHE[key]
    in_maps = []
    for b in range(B):
        base = host_inputs(b, S_LEN, DEPTH, _parity=0, **inputs)
        in_maps.append(base)
        m1 = dict(base)
        m1["parcol"] = host_inputs_par(1)
        in_maps.append(m1)
    res = run_bass_kernel_spmd(nc, in_maps, core_ids=list(range(2 * B)))
    out = np.empty((B, S_LEN, D), dtype=np.float32)
    ov = out.reshape(B, S_LEN // 1024, 2, 512, D)
    for b in range(B):
        for p in range(2):
            ov[b, :, p] = np.asarray(res.results[2 * b + p]["y"], dtype=np.float32).reshape(S_LEN // 1024, 512, D)
    return out


def host_inputs_par(p):
    return np.ascontiguousarray(np.broadcast_to(np.array([[1.0 - p, float(p)]], dtype=np.float32), (128, 2)))
```
